# Optimizing a Trainium2 kernel written in Bass

```python
import math
import jax
import jax.numpy as jnp
from jax import lax
import numpy as np

D_MODEL = 1024
BATCH = 16
SEQ = 4096
DEPTH = 2
DEC_BATCH = 8
DEC_SEQ = 2048
PAST_LEN = 128

ROPE_THETA = 10000.0
NORM_EPS = 1e-6
HEAD_NORM_EPS = 1e-5
CHUNK = 128
Q_BLOCK = 128
N_BRANCH = 4
BRANCH_WIDTH = 512

RET_HEADS = 4
RET_QK_DIM = 64
RET_V_DIM = 128
MLA_HEADS = 8
MLA_NOPE_DIM = 64
MLA_ROPE_DIM = 32
MLA_V_DIM = 64
MLA_Q_RANK = 256
MLA_KV_RANK = 128
DIFF_HEADS = 4
DIFF_HEAD_DIM = 64
DIFF_V_DIM = 2 * DIFF_HEAD_DIM
SSM_HEADS = 8
SSM_HEAD_DIM = 64
SSM_GROUPS = 2
SSM_STATE = 128
SSM_CONV = 3
SSM_INNER = SSM_HEADS * SSM_HEAD_DIM
SSM_CONV_CH = SSM_INNER + 2 * SSM_GROUPS * SSM_STATE
D_FF = 2816
FFN_CONV = 3

IN_SPLIT_SIZES = (
    RET_HEADS * RET_QK_DIM,
    RET_HEADS * RET_QK_DIM,
    RET_HEADS * RET_V_DIM,
    RET_HEADS * RET_V_DIM,
    MLA_Q_RANK,
    MLA_KV_RANK,
    MLA_ROPE_DIM,
    DIFF_HEADS * 2 * DIFF_HEAD_DIM,
    DIFF_HEADS * 2 * DIFF_HEAD_DIM,
    DIFF_HEADS * DIFF_V_DIM,
    SSM_INNER,
    SSM_CONV_CH,
    2 * SSM_HEADS,
    N_BRANCH * D_MODEL,
)
IN_COLS = sum(IN_SPLIT_SIZES)

kernel_name = "hybrid_bidir_encoder_gated_merge"


def rms_norm(x, w, eps=NORM_EPS):
    xf = x.astype(jnp.float32)
    y = xf * lax.rsqrt(jnp.mean(xf * xf, axis=-1, keepdims=True) + eps)
    return (y * w.astype(jnp.float32)).astype(x.dtype)


def split_cols(t, sizes):
    return jnp.split(t, np.cumsum(sizes)[:-1].tolist(), axis=-1)


def seq_flip(t):
    return jnp.flip(t, axis=1)


def rope_tables(seq, dim):
    inv_freq = 1.0 / (ROPE_THETA ** (jnp.arange(0, dim, 2, dtype=jnp.float32) / dim))
    ang = jnp.arange(seq, dtype=jnp.float32)[:, None] * inv_freq[None, :]
    return jnp.cos(ang), jnp.sin(ang)


def apply_rope(x, cos, sin):
    shape = (1, x.shape[1]) + (1,) * (x.ndim - 3) + (cos.shape[-1],)
    c = cos.reshape(shape).astype(x.dtype)
    sn = sin.reshape(shape).astype(x.dtype)
    x1, x2 = jnp.split(x, 2, axis=-1)
    return jnp.concatenate([x1 * c - x2 * sn, x2 * c + x1 * sn], axis=-1)


def depthwise_conv_centred(x, w, bias):
    k, c = w.shape
    pad = k // 2
    y = lax.conv_general_dilated(x, w.astype(x.dtype)[:, None, :], window_strides=(1,),
                                 padding=[(pad, pad)], dimension_numbers=("NWC", "WIO", "NWC"),
                                 feature_group_count=c)
    return y + bias.astype(x.dtype)


def split_query_blocks(t):
    b, s, h, d = t.shape
    return t.reshape(b, s // Q_BLOCK, Q_BLOCK, h, d).transpose(1, 0, 2, 3, 4)


def merge_query_blocks(t):
    n, b, qb, h, d = t.shape
    return t.transpose(1, 0, 2, 3, 4).reshape(b, n * qb, h, d)


def softmax_attention(q, k, v, scale):
    def one_block(qb):
        sc = jnp.einsum("bqhd,bkhd->bhqk", qb, k, preferred_element_type=jnp.float32) * scale
        pr = jax.nn.softmax(sc, axis=-1)
        return jnp.einsum("bhqk,bkhe->bqhe", pr.astype(v.dtype), v)
    return merge_query_blocks(lax.map(one_block, split_query_blocks(q)))


def diff_attention(q1, q2, k1, k2, v, lam, scale):
    def one_block(qs):
        qa, qb = qs
        s1 = jnp.einsum("bqhd,bkhd->bhqk", qa, k1, preferred_element_type=jnp.float32) * scale
        s2 = jnp.einsum("bqhd,bkhd->bhqk", qb, k2, preferred_element_type=jnp.float32) * scale
        amap = jax.nn.softmax(s1, axis=-1) - lam * jax.nn.softmax(s2, axis=-1)
        return jnp.einsum("bhqk,bkhe->bqhe", amap.astype(v.dtype), v)
    return merge_query_blocks(lax.map(one_block, (split_query_blocks(q1), split_query_blocks(q2))))


def retention_causal(q, k, v, log_gamma):
    b, s, h, dk = q.shape
    dv = v.shape[-1]
    n = s // CHUNK
    f32 = jnp.float32
    lg = log_gamma.astype(f32)
    pos = jnp.arange(CHUNK, dtype=f32)
    rel = pos[:, None] - pos[None, :]
    intra = jnp.where(rel >= 0, jnp.exp(jnp.maximum(rel, 0.0)[None] * lg[:, None, None]), 0.0)
    q_dec = jnp.exp((pos + 1.0)[:, None] * lg)
    k_dec = jnp.exp((CHUNK - 1.0 - pos)[:, None] * lg)
    chunk_dec = jnp.exp(CHUNK * lg)
    qc = q.astype(f32).reshape(b, n, CHUNK, h, dk)
    kc = k.astype(f32).reshape(b, n, CHUNK, h, dk)
    vc = v.astype(f32).reshape(b, n, CHUNK, h, dv)
    scores = jnp.einsum("bnihd,bnjhd->bnhij", qc, kc) * intra
    y_intra = jnp.einsum("bnhij,bnjhe->bnihe", scores, vc)
    kv = jnp.einsum("bnjhd,bnjhe->nbhde", kc * k_dec[:, :, None], vc)

    def step(state, kv_c):
        return chunk_dec[:, None, None] * state + kv_c, state

    _, prev = lax.scan(step, jnp.zeros((b, h, dk, dv), f32), kv)
    y_cross = jnp.einsum("bnihd,nbhde->bnihe", qc * q_dec[:, :, None], prev)
    return (y_intra + y_cross).reshape(b, s, h, dv)


def ssd_causal(x, dt, a, bmat, cmat):
    b, s, h, p = x.shape
    g, nst = bmat.shape[-2], bmat.shape[-1]
    r = h // g
    n = s // CHUNK
    f32 = jnp.float32
    xc = x.astype(f32).reshape(b, n, CHUNK, g, r, p)
    dtc = dt.astype(f32).reshape(b, n, CHUNK, g, r)
    bc = bmat.astype(f32).reshape(b, n, CHUNK, g, nst)
    cc = cmat.astype(f32).reshape(b, n, CHUNK, g, nst)
    cum = jnp.cumsum(dtc * a.astype(f32).reshape(g, r), axis=2)
    cum_t = jnp.moveaxis(cum, 2, -1)
    causal = jnp.tril(jnp.ones((CHUNK, CHUNK), dtype=bool))
    seg = jnp.exp(jnp.where(causal, cum_t[..., :, None] - cum_t[..., None, :], -jnp.inf))
    cb = jnp.einsum("bnigk,bnjgk->bngij", cc, bc)
    w = cb[:, :, :, None] * seg * jnp.moveaxis(dtc, 2, -1)[..., None, :]
    y_diag = jnp.einsum("bngrij,bnjgrp->bnigrp", w, xc)
    decay_to_end = jnp.exp(cum[:, :, -1:] - cum)
    xw = xc * (decay_to_end * dtc)[..., None]
    states = jnp.einsum("bnjgk,bnjgrp->nbgrpk", bc, xw)
    chunk_decay = jnp.moveaxis(jnp.exp(cum[:, :, -1]), 1, 0)

    def step(hstate, inp):
        st, dec = inp
        return dec[..., None, None] * hstate + st, hstate

    _, prev = lax.scan(step, jnp.zeros((b, g, r, p, nst), f32), (states, chunk_decay))
    y_off = jnp.einsum("bnigk,nbgrpk->bnigrp", cc, prev) * jnp.exp(cum)[..., None]
    return (y_diag + y_off).reshape(b, s, h, p)


def retention_branch(q, k, v, g, log_decay, norm_w, cos, sin):
    b, s, _ = q.shape
    q = apply_rope(q.reshape(b, s, RET_HEADS, RET_QK_DIM), cos, sin)
    k = apply_rope(k.reshape(b, s, RET_HEADS, RET_QK_DIM), cos, sin) * (RET_QK_DIM ** -0.5)
    v = v.reshape(b, s, RET_HEADS, RET_V_DIM)
    y = (retention_causal(q, k, v, log_decay[0])
         + seq_flip(retention_causal(seq_flip(q), seq_flip(k), seq_flip(v), log_decay[1])))
    y = rms_norm(y, norm_w.reshape(RET_HEADS, RET_V_DIM), HEAD_NORM_EPS).reshape(b, s, -1)
    return jax.nn.silu(g) * y.astype(g.dtype)


def mla_branch(c_q, c_kv, k_rope, q_norm_w, w_uq, kv_norm_w, w_ukv, cos, sin):
    b, s, _ = c_q.shape
    q = (rms_norm(c_q, q_norm_w) @ w_uq).reshape(b, s, MLA_HEADS, MLA_NOPE_DIM + MLA_ROPE_DIM)
    q_nope, q_rope = jnp.split(q, [MLA_NOPE_DIM], axis=-1)
    q_rope = apply_rope(q_rope, cos, sin)
    kv = (rms_norm(c_kv, kv_norm_w) @ w_ukv).reshape(b, s, MLA_HEADS, MLA_NOPE_DIM + MLA_V_DIM)
    k_nope, v = jnp.split(kv, [MLA_NOPE_DIM], axis=-1)
    k_r = apply_rope(k_rope[:, :, None, :], cos, sin)
    q = jnp.concatenate([q_nope, q_rope], axis=-1)
    k = jnp.concatenate([k_nope, jnp.broadcast_to(k_r, (b, s, MLA_HEADS, MLA_ROPE_DIM))], axis=-1)
    out = softmax_attention(q, k, v, (MLA_NOPE_DIM + MLA_ROPE_DIM) ** -0.5)
    return out.reshape(b, s, -1)


def diff_branch(q, k, v, lam_vec, norm_w, lam_init, cos, sin):
    b, s, _ = q.shape
    q = apply_rope(q.reshape(b, s, DIFF_HEADS, 2, DIFF_HEAD_DIM), cos, sin)
    k = apply_rope(k.reshape(b, s, DIFF_HEADS, 2, DIFF_HEAD_DIM), cos, sin)
    v = v.reshape(b, s, DIFF_HEADS, DIFF_V_DIM)
    lv = lam_vec.astype(jnp.float32)
    lam = jnp.exp(jnp.sum(lv[0] * lv[1])) - jnp.exp(jnp.sum(lv[2] * lv[3])) + lam_init
    out = diff_attention(q[:, :, :, 0], q[:, :, :, 1], k[:, :, :, 0], k[:, :, :, 1], v, lam,
                         DIFF_HEAD_DIM ** -0.5)
    out = rms_norm(out, norm_w, HEAD_NORM_EPS) * (1.0 - lam_init)
    return out.reshape(b, s, -1)


def ssm_branch(z, xbc, dt_raw, conv_w, conv_b, dt_bias, a_log, d_skip, norm_w):
    b, s, _ = z.shape
    f32 = jnp.float32
    xbc = jax.nn.silu(depthwise_conv_centred(xbc, conv_w, conv_b))
    xs, bm, cm = split_cols(xbc, (SSM_INNER, SSM_GROUPS * SSM_STATE, SSM_GROUPS * SSM_STATE))
    xs = xs.reshape(b, s, SSM_HEADS, SSM_HEAD_DIM)
    bm = bm.reshape(b, s, SSM_GROUPS, SSM_STATE)
    cm = cm.reshape(b, s, SSM_GROUPS, SSM_STATE)
    dt = jax.nn.softplus(dt_raw.astype(f32).reshape(b, s, 2, SSM_HEADS) + dt_bias.astype(f32))
    a = -jnp.exp(a_log.astype(f32))
    y_f = ssd_causal(xs, dt[:, :, 0], a[0], bm, cm)
    y_b = seq_flip(ssd_causal(seq_flip(xs), seq_flip(dt[:, :, 1]), a[1], seq_flip(bm), seq_flip(cm)))
    y = y_f + y_b + d_skip.astype(f32)[:, None] * xs.astype(f32)
    y = y.reshape(b, s, SSM_INNER) * jax.nn.silu(z.astype(f32))
    y = rms_norm(y.reshape(b, s, SSM_GROUPS, SSM_INNER // SSM_GROUPS),
                 norm_w.reshape(SSM_GROUPS, SSM_INNER // SSM_GROUPS), HEAD_NORM_EPS)
    return y.reshape(b, s, SSM_INNER).astype(z.dtype)


def encoder_layer(x, l, p, rope_ret, rope_mla, rope_diff):
    b, s, _ = x.shape
    h = rms_norm(x, p["norm_mix_w"][l])
    proj = h @ p["w_in"][l]
    (ret_q, ret_k, ret_v, ret_g, mla_cq, mla_ckv, mla_kr, diff_q, diff_k, diff_v,
     ssm_z, ssm_xbc, ssm_dt, gate_logits) = split_cols(proj, IN_SPLIT_SIZES)
    lam_init = 0.8 - 0.6 * math.exp(-0.3 * l)
    outs = (
        retention_branch(ret_q, ret_k, ret_v, ret_g, p["ret_log_decay"][l], p["ret_norm_w"][l], *rope_ret),
        mla_branch(mla_cq, mla_ckv, mla_kr, p["mla_q_norm_w"][l], p["mla_w_uq"][l],
                   p["mla_kv_norm_w"][l], p["mla_w_ukv"][l], *rope_mla),
        diff_branch(diff_q, diff_k, diff_v, p["diff_lambda"][l], p["diff_norm_w"][l], lam_init, *rope_diff),
        ssm_branch(ssm_z, ssm_xbc, ssm_dt, p["ssm_conv_w"][l], p["ssm_conv_b"][l], p["ssm_dt_bias"][l],
                   p["ssm_a_log"][l], p["ssm_d"][l], p["ssm_norm_w"][l]),
    )
    gates = jax.nn.sigmoid(gate_logits.reshape(b, s, N_BRANCH, D_MODEL))
    merged = sum(gates[:, :, i] * (outs[i].astype(x.dtype) @ p["w_branch"][l, i]) for i in range(N_BRANCH))
    x = x + merged @ p["w_out"][l]
    h = rms_norm(x, p["norm_ffn_w"][l])
    u = depthwise_conv_centred(h @ p["ffn_w_gate"][l], p["ffn_conv_w"][l], p["ffn_conv_b"][l])
    x = x + (jax.nn.silu(u) * (h @ p["ffn_w_up"][l])) @ p["ffn_w_down"][l]
    return x


def trunk(x, p):
    s = x.shape[1]
    rope_ret = rope_tables(s, RET_QK_DIM)
    rope_mla = rope_tables(s, MLA_ROPE_DIM)
    rope_diff = rope_tables(s, DIFF_HEAD_DIM)
    for l in range(DEPTH):
        x = encoder_layer(x, l, p, rope_ret, rope_mla, rope_diff)
    return rms_norm(x, p["final_norm_w"])


def setup_inputs(seed: int = 0) -> dict:
    key = jax.random.key(seed)
    ks = jax.random.split(key, 27)
    f32 = jnp.float32

    def nrm(k, shape, scale):
        return jax.random.normal(k, shape, f32) * scale

    def gain(k, shape):
        return 1.0 + 0.02 * jax.random.normal(k, shape, f32)

    base_decay = jnp.log(1.0 - 2.0 ** (-5.0 - jnp.arange(RET_HEADS, dtype=f32)))
    dt0 = jnp.exp(jax.random.uniform(ks[14], (DEPTH, 2, SSM_HEADS), f32, math.log(1e-3), math.log(1e-1)))
    return {
        "x_prompt": nrm(ks[0], (BATCH, SEQ, D_MODEL), 1.0),
        "x_sample": nrm(ks[1], (DEC_BATCH, DEC_SEQ, D_MODEL), 1.0),
        "norm_mix_w": gain(ks[2], (DEPTH, D_MODEL)),
        "w_in": nrm(ks[3], (DEPTH, D_MODEL, IN_COLS), D_MODEL ** -0.5),
        "ret_log_decay": base_decay * (1.0 + 0.05 * jax.random.normal(ks[4], (DEPTH, 2, RET_HEADS), f32)),
        "ret_norm_w": gain(ks[5], (DEPTH, RET_HEADS * RET_V_DIM)),
        "mla_q_norm_w": gain(ks[6], (DEPTH, MLA_Q_RANK)),
        "mla_w_uq": nrm(ks[7], (DEPTH, MLA_Q_RANK, MLA_HEADS * (MLA_NOPE_DIM + MLA_ROPE_DIM)), MLA_Q_RANK ** -0.5),
        "mla_kv_norm_w": gain(ks[8], (DEPTH, MLA_KV_RANK)),
        "mla_w_ukv": nrm(ks[9], (DEPTH, MLA_KV_RANK, MLA_HEADS * (MLA_NOPE_DIM + MLA_V_DIM)), MLA_KV_RANK ** -0.5),
        "diff_lambda": nrm(ks[10], (DEPTH, 4, DIFF_HEAD_DIM), 0.1),
        "diff_norm_w": gain(ks[11], (DEPTH, DIFF_V_DIM)),
        "ssm_conv_w": nrm(ks[12], (DEPTH, SSM_CONV, SSM_CONV_CH), SSM_CONV ** -0.5),
        "ssm_conv_b": nrm(ks[13], (DEPTH, SSM_CONV_CH), 0.02),
        "ssm_dt_bias": dt0 + jnp.log(-jnp.expm1(-dt0)),
        "ssm_a_log": jnp.log(jax.random.uniform(ks[15], (DEPTH, 2, SSM_HEADS), f32, 1.0, 16.0)),
        "ssm_d": gain(ks[16], (DEPTH, SSM_HEADS)),
        "ssm_norm_w": gain(ks[17], (DEPTH, SSM_INNER)),
        "w_branch": nrm(ks[18], (DEPTH, N_BRANCH, BRANCH_WIDTH, D_MODEL), BRANCH_WIDTH ** -0.5),
        "w_out": nrm(ks[19], (DEPTH, D_MODEL, D_MODEL), D_MODEL ** -0.5),
        "norm_ffn_w": gain(ks[20], (DEPTH, D_MODEL)),
        "ffn_w_gate": nrm(ks[21], (DEPTH, D_MODEL, D_FF), D_MODEL ** -0.5),
        "ffn_w_up": nrm(ks[22], (DEPTH, D_MODEL, D_FF), D_MODEL ** -0.5),
        "ffn_conv_w": nrm(ks[23], (DEPTH, FFN_CONV, D_FF), FFN_CONV ** -0.5),
        "ffn_conv_b": nrm(ks[24], (DEPTH, D_FF), 0.02),
        "ffn_w_down": nrm(ks[25], (DEPTH, D_FF, D_MODEL), D_FF ** -0.5),
        "final_norm_w": gain(ks[26], (D_MODEL,)),
    }


def reference(x_prompt, x_sample, norm_mix_w, w_in, ret_log_decay, ret_norm_w, mla_q_norm_w, mla_w_uq,
              mla_kv_norm_w, mla_w_ukv, diff_lambda, diff_norm_w, ssm_conv_w, ssm_conv_b, ssm_dt_bias,
              ssm_a_log, ssm_d, ssm_norm_w, w_branch, w_out, norm_ffn_w, ffn_w_gate, ffn_w_up, ffn_conv_w,
              ffn_conv_b, ffn_w_down, final_norm_w):
    p = {
        "norm_mix_w": norm_mix_w, "w_in": w_in, "ret_log_decay": ret_log_decay, "ret_norm_w": ret_norm_w,
        "mla_q_norm_w": mla_q_norm_w, "mla_w_uq": mla_w_uq, "mla_kv_norm_w": mla_kv_norm_w,
        "mla_w_ukv": mla_w_ukv, "diff_lambda": diff_lambda, "diff_norm_w": diff_norm_w,
        "ssm_conv_w": ssm_conv_w, "ssm_conv_b": ssm_conv_b, "ssm_dt_bias": ssm_dt_bias,
        "ssm_a_log": ssm_a_log, "ssm_d": ssm_d, "ssm_norm_w": ssm_norm_w, "w_branch": w_branch,
        "w_out": w_out, "norm_ffn_w": norm_ffn_w, "ffn_w_gate": ffn_w_gate, "ffn_w_up": ffn_w_up,
        "ffn_conv_w": ffn_conv_w, "ffn_conv_b": ffn_conv_b, "ffn_w_down": ffn_w_down,
        "final_norm_w": final_norm_w,
    }
    y_prompt = trunk(x_prompt, p)
    y_sample = trunk(x_sample, p)
    return (y_prompt, y_sample)
```

```python
import math
from contextlib import ExitStack

import numpy as np
import ml_dtypes

import concourse.bass as bass
import concourse.mybir as mybir
from concourse.bass_utils import run_bass_kernel_spmd

F32 = mybir.dt.float32
BF16 = mybir.dt.bfloat16
AF = mybir.ActivationFunctionType
ALU = mybir.AluOpType
AX = mybir.AxisListType

PE, ACT, DVE, POOL, SP = "pe", "act", "dve", "pool", "sp"
ENGS = (PE, ACT, DVE, POOL, SP)

D_MODEL = 1024
IN_COLS = 9136
D_FF = 2816
NFF = 22
NORM_EPS = 1e-6
HEAD_EPS = 1e-5
ROPE_THETA = 10000.0

C_RQ, C_RK, C_RV, C_RG = 0, 256, 512, 1024
C_CQ, C_CKV, C_KR = 1536, 1792, 1920
C_DQ, C_DK, C_DV = 1952, 2464, 2976
C_SZ, C_XBC, C_DT, C_GATE = 3488, 4000, 5024, 5040
NA = 5040


class Op:
    __slots__ = ("eng", "fn", "deps", "is_dma", "sig", "slot", "dval", "idx", "sigval")


class Sched:
    def __init__(self, nc, ndsem=12):
        self.nc = nc
        self.ops = {e: [] for e in ENGS}
        self.res = {}
        self.dma_n = {e: 0 for e in ENGS}
        self.dma_last = {e: {} for e in ENGS}
        import os
        self.ndsem = int(os.environ.get("NDSEM", ndsem))
        rr = int(os.environ.get("NROT", "1"))
        self.nrot = {PE: 4 * rr, ACT: 2 * rr, DVE: 2 * rr, POOL: 2 * rr, SP: 1}
        self.last_real = {e: None for e in ENGS}

    def add(self, eng, fn, reads=(), writes=(), dma=False, extra=(), real=True):
        op = Op()
        op.eng = eng
        op.fn = fn
        op.is_dma = dma
        op.sig = False
        op.idx = len(self.ops[eng])
        op.slot = None
        deps = set(extra)
        res = self.res
        if any(type(k) is tuple and k[0] == "ps" for k in reads):
            writes = list(writes) + [k for k in reads if type(k) is tuple and k[0] == "ps"]
            reads = [k for k in reads if not (type(k) is tuple and k[0] == "ps")]
        for k in reads:
            st = res.get(k)
            if st is not None and st[0] is not None:
                deps.add(st[0])
        for k in writes:
            st = res.get(k)
            if st is not None:
                if st[0] is not None:
                    deps.add(st[0])
                deps.update(st[1].values())
        if dma:
            i = self.dma_n[eng]
            self.dma_n[eng] = i + 1
            op.slot = i % self.ndsem
            op.dval = 16 * (i // self.ndsem + 1)
            prev = self.dma_last[eng].get(op.slot)
            if prev is not None:
                deps.add(prev)
            self.dma_last[eng][op.slot] = op
        rk = (eng, op.slot) if dma else eng
        for k in reads:
            st = res.get(k)
            if st is None:
                st = [None, {}]
                res[k] = st
            st[1][rk] = op
        for k in writes:
            res[k] = [op, {}]
        fd = []
        work = list(deps)
        seen = set()
        while work:
            d = work.pop()
            if d is op or d is None or id(d) in seen:
                continue
            seen.add(id(d))
            if d.fn is None:
                if d.eng != eng:
                    work.extend(d.deps)
                continue
            if (not d.is_dma) and (not dma) and d.eng == eng and eng == PE:
                continue
            fd.append(d)
            if not d.is_dma:
                d.sig = True
        op.deps = fd
        self.ops[eng].append(op)
        if real and not dma:
            self.last_real[eng] = op
        return op

    def pe(self, fn, reads=(), writes=()):
        return self.add(PE, fn, reads, writes)

    def act(self, fn, reads=(), writes=()):
        return self.add(ACT, fn, reads, writes)

    def dve(self, fn, reads=(), writes=()):
        return self.add(DVE, fn, reads, writes)

    def pool(self, fn, reads=(), writes=()):
        return self.add(POOL, fn, reads, writes)

    def dma(self, out, in_, reads=(), writes=(), q=SP, **kw):
        return self.add(q, lambda e: e.dma_start(out=out, in_=in_, **kw), reads, writes, dma=True)

    def barrier(self):
        lasts = [self.last_real[e] for e in ENGS if self.last_real[e] is not None]
        dmas = []
        for e in ENGS:
            dmas.extend(self.dma_last[e].values())
        for e in ENGS:
            if not self.ops[e] and e not in (PE, ACT, DVE, POOL, SP):
                continue
            self.add(e, None, extra=[o for o in lasts if o.eng != e] + dmas, real=False)
        self.res = {}

    def emit(self, es):
        nc = self.nc
        engobj = {PE: nc.tensor, ACT: nc.scalar, DVE: nc.vector, POOL: nc.gpsimd, SP: nc.sync}
        csem = {}
        for e in ENGS:
            if any(o.sig for o in self.ops[e]):
                csem[e] = [es.enter_context(nc.semaphore(f"c_{e}_{r}")) for r in range(self.nrot[e])]
        dsem = {}
        for e in ENGS:
            if self.dma_n[e] > 0:
                dsem[e] = [es.enter_context(nc.semaphore(f"d_{e}_{r}"))
                           for r in range(min(self.ndsem, self.dma_n[e]))]
        for e in ENGS:
            n = 0
            R = self.nrot[e]
            for o in self.ops[e]:
                if o.sig:
                    o.sigval = (n % R, n // R + 1)
                    n += 1
        block = es.enter_context(nc.Block())
        self.nwaits = 0
        self.ninst = 0

        def emit_engine(e):
            eo = engobj[e]
            waited = {}
            for o in self.ops[e]:
                for d in o.deps:
                    if d.is_dma:
                        key = ("d", d.eng, d.slot)
                        val = d.dval
                        sem = dsem[d.eng][d.slot]
                    else:
                        r, val = d.sigval
                        key = ("c", d.eng, r)
                        sem = csem[d.eng][r]
                    if waited.get(key, 0) >= val:
                        continue
                    waited[key] = val
                    eo.wait_ge(sem, val)
                    self.nwaits += 1
                if o.fn is None:
                    continue
                inst = o.fn(eo)
                self.ninst += 1
                if o.is_dma:
                    inst.then_inc(dsem[e][o.slot], 16)
                elif o.sig:
                    r, _ = o.sigval
                    inst.then_inc(csem[e][r], 1)
            for slot, o in self.dma_last[e].items():
                key = ("d", e, slot)
                if waited.get(key, 0) < o.dval:
                    eo.wait_ge(dsem[e][slot], o.dval)

        @block.tensor
        def _(t):
            emit_engine(PE)

        @block.scalar
        def _(t):
            emit_engine(ACT)

        @block.vector
        def _(t):
            emit_engine(DVE)

        @block.gpsimd
        def _(t):
            emit_engine(POOL)

        @block.sync
        def _(t):
            emit_engine(SP)


class Rot:
    def __init__(self, items):
        self.items = items
        self.i = 0

    def next(self):
        it = self.items[self.i % len(self.items)]
        self.i += 1
        return it


PP = {}
_off = 0
for _n, _w in [("wn", 8), ("nf", 8), ("fin", 8), ("scw", 24), ("scb", 8), ("qnw", 2), ("kvnw", 1),
               ("rnw", 4), ("dnw", 1), ("fcw", 66), ("fcb", 22),
               ("lgd", 8), ("dtb", 16), ("alog", 16), ("ssd", 8), ("snw", 512), ("dlam", 256)]:
    PP[_n] = (_off, _w)
    _off += _w
NPP = _off

CP = {}
_off = 0
for _n, _w in [("Uf", 128), ("Ub", 128), ("SLf", 128), ("SLb", 128), ("ones", 128), ("Dpos", 128), ("Dneg", 128),
               ("Mgt", 128), ("Mlt", 128), ("I2", 128), ("Ip1", 128), ("Cmi", 128), ("Cm1j", 64), ("Jj", 64),
               ("C128", 128)]:
    CP[_n] = (_off, _w)
    _off += _w
NCP = _off

BP = {}
_off = 0
for _n, _w in [("ident", 128), ("perm64", 128), ("perm32", 128), ("ones", 128)]:
    BP[_n] = (_off, _w)
    _off += _w
NBP = _off


def make_cpack():
    c = np.zeros((128, NCP), np.float32)
    j = np.arange(128)[:, None].astype(np.float32)
    i = np.arange(128)[None, :].astype(np.float32)

    def put(name, a):
        o, w = CP[name]
        c[:, o:o + w] = a

    put("Uf", (j <= i))
    put("Ub", (j >= i))
    put("SLf", (j > i))
    put("SLb", (j < i))
    put("ones", np.ones((128, 128)))
    put("Dpos", np.maximum(i - j, 0))
    put("Dneg", np.maximum(j - i, 0))
    put("Mgt", (i > j))
    put("Mlt", (j > i))
    put("I2", 2.0 * (i == j))
    put("Ip1", np.broadcast_to(i + 1, (128, 128)))
    put("Cmi", np.broadcast_to(128 - i, (128, 128)))
    put("Cm1j", np.broadcast_to(127 - j, (128, 64)))
    put("Jj", np.broadcast_to(j, (128, 64)))
    put("C128", np.full((128, 128), 128.0))
    return c


def make_bpack():
    b = np.zeros((128, NBP), np.float32)
    k = np.arange(128)[:, None]
    m = np.arange(128)[None, :]

    def put(name, a):
        o, w = BP[name]
        b[:, o:o + w] = a

    put("ident", (k == m))
    put("perm64", (k == (m ^ 32)))
    put("perm32", (k == (m ^ 16)))
    put("ones", np.ones((128, 128)))
    return b.astype(ml_dtypes.bfloat16)


def make_rope(smax):
    out = np.zeros((4, 128, smax), np.float32)
    pos = np.arange(smax, dtype=np.float32)
    for ti, dim in ((0, 64), (2, 32)):
        inv = (1.0 / (np.float32(ROPE_THETA) ** (np.arange(0, dim, 2, dtype=np.float32) / np.float32(dim)))).astype(np.float32)
        ang = pos[:, None] * inv[None, :]
        cos = np.cos(ang).astype(np.float32)
        sin = np.sin(ang).astype(np.float32)
        for f in range(128):
            d = f % dim
            jx = d % (dim // 2)
            out[ti, f] = cos[:, jx]
            out[ti + 1, f] = sin[:, jx] * (-1.0 if d < dim // 2 else 1.0)
    return out


def make_ppack(inp, l):
    p = np.zeros((128, NPP), np.float32)

    def put(name, a):
        o, w = PP[name]
        p[:, o:o + w] = np.asarray(a, np.float32).reshape(128, w)

    def fm(v, nch):
        return np.asarray(v, np.float32).reshape(nch, 128).T

    def bc(v):
        v = np.asarray(v, np.float32).reshape(-1)
        return np.broadcast_to(v[None, :], (128, v.size))

    put("wn", fm(inp["norm_mix_w"][l], 8))
    put("nf", fm(inp["norm_ffn_w"][l], 8))
    put("fin", fm(inp["final_norm_w"], 8))
    put("scw", np.concatenate([fm(inp["ssm_conv_w"][l][t], 8) for t in range(3)], axis=1))
    put("scb", fm(inp["ssm_conv_b"][l], 8))
    put("qnw", fm(inp["mla_q_norm_w"][l], 2))
    put("kvnw", fm(inp["mla_kv_norm_w"][l], 1))
    put("rnw", fm(inp["ret_norm_w"][l], 4))
    put("dnw", fm(inp["diff_norm_w"][l], 1))
    put("fcw", np.concatenate([fm(inp["ffn_conv_w"][l][t], NFF) for t in range(3)], axis=1))
    put("fcb", fm(inp["ffn_conv_b"][l], NFF))
    put("lgd", bc(inp["ret_log_decay"][l]))
    put("dtb", bc(inp["ssm_dt_bias"][l]))
    put("alog", bc(inp["ssm_a_log"][l]))
    put("ssd", bc(inp["ssm_d"][l]))
    put("snw", bc(inp["ssm_norm_w"][l]))
    put("dlam", bc(inp["diff_lambda"][l]))
    return p


class Arena:
    def __init__(self, tensor, nwords):
        self.t = tensor
        self.n = nwords
        self.off = 0
        self.base = 0

    def mark(self):
        self.base = self.off

    def reset(self):
        self.off = self.base

    def alloc(self, shape, dt=F32):
        n = int(np.prod(shape))
        words = n if dt == F32 else (n + 1) // 2
        words = (words + 7) // 8 * 8
        assert self.off + words <= self.n, f"SBUF arena overflow {self.off}+{words}>{self.n}"
        ap = self.t[:, self.off:self.off + words]
        self.off += words
        if dt != F32:
            ap = ap.bitcast(dt)
        ap = ap[:, 0:n]
        if len(shape) == 2:
            ap = ap.rearrange("p (a b) -> p a b", a=shape[0])
        elif len(shape) == 3:
            ap = ap.rearrange("p (a b c) -> p a b c", a=shape[0], b=shape[1])
        return ap


class MK:
    def __init__(self, seqs, depth, debug=(), stop_after=None):
        self.seqs = list(seqs)
        self.T = sum(seqs)
        self.depth = depth
        self.smax = max(seqs)
        self.debug = set(debug)
        self.stop_after = stop_after
        import os
        self.dbgbar = int(os.environ.get('DBGBAR', '0'))
        T = self.T
        nc = bass.Bass("TRN2", target_bir_lowering=False)
        self.nc = nc

        def din(name, shape, dt=F32):
            return nc.dram_tensor(name, shape, dt, kind="ExternalInput").ap()

        self.xT = din("xT", [1024, T])
        self.w_in = din("w_in", [depth, 1024, IN_COLS])
        self.w_uq = din("mla_w_uq", [depth, 256, 768])
        self.w_ukv = din("mla_w_ukv", [depth, 128, 1024])
        self.w_branch = din("w_branch", [depth, 4, 512, 1024])
        self.w_out = din("w_out", [depth, 1024, 1024])
        self.w_gate = din("ffn_w_gate", [depth, 1024, D_FF])
        self.w_up = din("ffn_w_up", [depth, 1024, D_FF])
        self.w_down = din("ffn_w_down", [depth, D_FF, 1024])
        self.ppack = din("ppack", [depth, 128, NPP])
        self.cpack = din("cpack", [128, NCP])
        self.bpack = din("bpack", [128, NBP], BF16)
        self.rope = din("rope", [4, 128, self.smax])
        self.yT = nc.dram_tensor("yT", [1024, T], F32, kind="ExternalOutput").ap()
        self.scr = {}

        def scr(name, shape, dt=BF16):
            kind = "ExternalOutput" if name in self.debug else "Internal"
            self.scr[name] = nc.dram_tensor(name, shape, dt, kind=kind).ap()
            return self.scr[name]

        scr("xres", [1024, T], F32)
        scr("xmid", [1024, T], F32)
        scr("rqT", [256, T]); scr("rkT", [256, T]); scr("rv", [T, 512]); scr("rgT", [512, T])
        scr("mqnT", [512, T]); scr("mqrT", [256, T]); scr("mknT", [512, T]); scr("mkrT", [32, T])
        scr("mva", [T, 1024])
        scr("dqT", [512, T]); scr("dkT", [512, T]); scr("dv", [T, 512])
        scr("sz", [T, 512]); scr("sxB", [T, 768]); scr("sBT", [256, T]); scr("sCT", [256, T])
        scr("sdt", [T, 16], F32)
        scr("boT", [2048, T])
        self.groups = []
        base = 0
        for si, S_ in enumerate(self.seqs):
            for g in range(S_ // 512):
                self.groups.append(dict(s=si, t0=base + g * 512, pos0=g * 512, S=S_, sbase=base))
            base += S_

    def build(self):
        nc = self.nc
        with ExitStack() as es:
            self.es = es
            self.S = Sched(nc)
            ARENA_WORDS = 51 * 1024
            at = es.enter_context(nc.sbuf_tensor("arena", [128, ARENA_WORDS], F32))
            self.A = Arena(at, ARENA_WORDS)
            self.banks = [es.enter_context(nc.psum_tensor(f"bank{i}", [128, 512], F32)) for i in range(8)]
            A = self.A
            S = self.S
            self.bp = A.alloc([NBP], BF16)
            self.pp = A.alloc([NPP], F32)
            S.dma(self.bp, self.bpack, writes=["bp"])
            A.mark()
            S.barrier()
            order = ["p1", "ret", "mla", "diff", "ssd", "p3", "p4"]
            sa = self.stop_after
            only = getattr(self, "only", None)
            done = False
            for l in range(self.depth):
                S.dma(self.pp, self.ppack[l], writes=["pp"])
                S.barrier()
                for ph in order:
                    run = True
                    if sa not in (None, "all") and l == self.depth - 1:
                        if sa in ("p3", "p4"):
                            run = order.index(ph) <= order.index(sa)
                        else:
                            run = ph in ("p1", sa)
                    if run:
                        {"p1": self.phase1, "ret": self.phase_ret, "mla": self.phase_mla, "diff": self.phase_diff,
                         "ssd": self.phase_ssd, "p3": self.phase3, "p4": self.phase4}[ph](l)
                        S.barrier()
            if sa in (None, "all"):
                self.phase5()
            S.emit(es)
        return nc

    def dbg(self, name, ap, reads):
        if name not in self.debug:
            return
        shp = [int(x) for x in ap.shape]
        t = self.nc.dram_tensor(name, shp, ap.dtype, kind="ExternalOutput").ap()
        self.S.dma(t, ap, reads=reads)

    def cpv(self, name):
        o, w = CP[name]
        return self.cp[:, o:o + w]

    def bpv(self, name):
        o, w = BP[name]
        return self.bp[:, o:o + w]

    def ppv(self, name, i=0, n=1):
        o, w = PP[name]
        return self.pp[:, o + i:o + i + n]

    def bank(self, i, dt=F32):
        b = self.banks[i][:]
        if dt != F32:
            b = b.bitcast(dt)
        return b

    def load_cast(self, name, pieces, stgR):
        S = self.S
        keys = []
        for i, pc in enumerate(pieces):
            dst, src = pc[0], pc[1]
            st, sk = stgR.next()
            shp = list(src.shape)
            n = int(np.prod(shp[1:]))
            stv = st[:, 0:n]
            if len(shp) == 3:
                stv = stv.rearrange("p (a b) -> p a b", a=shp[1])
            S.dma(stv, src, writes=[sk])
            if len(pc) > 2:
                stv = pc[2](stv)
            k = (name, i)
            keys.append(k)
            eng = (ACT, DVE, POOL)[i % 3]
            if eng == ACT:
                S.act(lambda e, dst=dst, stv=stv: e.activation(out=dst, in_=stv, func=AF.Copy), reads=[sk], writes=[k])
            else:
                S.add(eng, lambda e, dst=dst, stv=stv: e.tensor_copy(out=dst, in_=stv), reads=[sk], writes=[k])
        S.add(PE, None, reads=keys, writes=[name], real=False)

    def mmg(self, out, okey, pairs, reads):
        S = self.S
        n = len(pairs)
        for i, (lhsT, rhs) in enumerate(pairs):
            S.pe(lambda e, lhsT=lhsT, rhs=rhs, i=i: e.matmul(out, lhsT=lhsT, rhs=rhs, start=(i == 0), stop=(i == n - 1)),
                 reads=reads, writes=[okey])

    def rstd_from(self, ps_list, inv_n, eps, lnv, rstd, rkey, lkey):
        S = self.S
        for ps, pk, sl in ps_list:
            S.act(lambda e, ps=ps, sl=sl: e.activation(out=lnv[:, sl], in_=ps, func=AF.Ln, scale=inv_n, bias=eps),
                  reads=[pk], writes=[lkey])
        S.act(lambda e: e.activation(out=rstd, in_=lnv, func=AF.Exp, scale=-0.5), reads=[lkey], writes=[rkey])

    def phase1(self, l):
        S, A, nc = self.S, self.A, self.nc
        A.reset()
        sc = self.scr
        x_in = (self.xT if l == 0 else sc["xres"]).rearrange("(c p) t -> p c t", p=128)
        wA = A.alloc([8, NA], BF16)
        wuqn = A.alloc([2, 8, 64], BF16)
        wuqr = A.alloc([2, 8, 32], BF16)
        wukn = A.alloc([8, 64], BF16)
        wuv = A.alloc([8, 64], BF16)
        stgR = Rot([(A.alloc([1536], F32), ("stg", i)) for i in range(2)])
        win = self.w_in[l].rearrange("(c p) n -> p c n", p=128)
        pieces = []
        for k in range(8):
            for hf in range(4):
                pieces.append((wA[:, k, hf * 1260:(hf + 1) * 1260], win[:, k, hf * 1260:(hf + 1) * 1260]))
        uq = self.w_uq[l].rearrange("(c p) n -> p c n", p=128)
        pieces.append((wuqn, uq, lambda v: v.rearrange("p k (h d) -> p k h d", h=8)[:, :, :, 0:64]))
        pieces.append((wuqr, uq, lambda v: v.rearrange("p k (h d) -> p k h d", h=8)[:, :, :, 64:96]))
        pieces.append((wukn, self.w_ukv[l], lambda v: v.rearrange("p (h d) -> p h d", h=8)[:, :, 0:64]))
        pieces.append((wuv, self.w_ukv[l], lambda v: v.rearrange("p (h d) -> p h d", h=8)[:, :, 64:128]))
        self.load_cast("wA", pieces, stgR)

        xs = A.alloc([8, 514], F32)
        hTs = [A.alloc([8, 514], BF16) for _ in range(2)]
        sqR = Rot([(A.alloc([514], BF16), ("sq", i)) for i in range(2)])
        lnv = A.alloc([514], F32)
        rstd = A.alloc([514], F32)
        tab = A.alloc([4, 512], F32)
        tk = "tab"
        xsbR = Rot([(A.alloc([512], BF16), ("xsb", i)) for i in range(3)])
        t1R = Rot([(A.alloc([512], F32), ("t1", i)) for i in range(2)])
        t2R = Rot([(A.alloc([512], F32), ("t2", i)) for i in range(2)])
        roR = Rot([(A.alloc([512], BF16), ("ro", i)) for i in range(3)])
        soR = Rot([(A.alloc([512], BF16), ("so", i)) for i in range(3)])
        GR = Rot([(A.alloc([514], F32), ("G", i)) for i in range(2)])
        cvR = Rot([(A.alloc([512], F32), ("cv", i)) for i in range(1)])
        cqf = A.alloc([2, 512], F32)
        ckvf = A.alloc([512], F32)
        cqn = A.alloc([2, 512], BF16)
        ckvn = A.alloc([512], BF16)
        lnq = lnv[:, 0:512]
        rsq = rstd[:, 0:512]
        tokR = Rot([(A.alloc([512], BF16), ("tok", i)) for i in range(3)])
        vaugs = [A.alloc([8, 128], BF16) for _ in range(2)]
        xTok = A.alloc([4, 768], BF16)
        dtx = A.alloc([4, 16], F32)
        dte = A.alloc([4, 16], F32)
        dts = A.alloc([4, 16], F32)
        for i, va in enumerate(vaugs):
            S.pool(lambda e, va=va: e.memset(va, 1.0), writes=[("vaug", i)])
        mainR = Rot([(self.bank(i), ("ps", i)) for i in range(4)])
        psA = self.bank(4)
        psH = self.bank(5)
        permR = Rot([(self.bank(6), ("ps", 6))])
        psT = self.bank(7, BF16)
        onesb = self.bpv("ones")
        ident = self.bpv("ident")
        ropeT = self.rope.rearrange("a p t -> p a t")
        tcount = [0]

        pend = []

        def flush():
            while pend:
                pend.pop(0)()

        def MM(out, okey, pairs, reads):
            self.mmg(out, okey, pairs, reads)
            flush()

        def rope_unit(ps, pk, M, scale, ci, perm, dst):
            tab, tk = self.cur_tab
            Ct = tab[0:M, ci, :]
            St = tab[0:M, ci + 1, :]
            xsb, k1 = xsbR.next()
            S.act(lambda e: e.activation(out=xsb, in_=ps, func=AF.Copy, scale=scale),
                  reads=[pk], writes=[k1])
            t1, kt1 = t1R.next()
            S.pool(lambda e: e.tensor_tensor(out=t1[0:M, :], in0=xsb[0:M, :], in1=Ct, op=ALU.mult),
                   reads=[k1, tk], writes=[kt1])

            def part_b():
                pp_, pk2 = permR.next()
                S.pe(lambda e: e.matmul(pp_, lhsT=perm, rhs=xsb, start=True, stop=True),
                     reads=[k1, "bp"], writes=[pk2])
                t2, kt2 = t2R.next()
                S.dve(lambda e: e.tensor_tensor(out=t2[0:M, :], in0=pp_[0:M, :], in1=St, op=ALU.mult),
                      reads=[pk2, tk], writes=[kt2])
                ro, kro = roR.next()
                S.dve(lambda e: e.tensor_tensor(out=ro[0:M, :], in0=t1[0:M, :], in1=t2[0:M, :], op=ALU.add),
                      reads=[kt1, kt2], writes=[kro])
                S.dma(dst, ro[0:M, :], reads=[kro])

            pend.append(part_b)

        def load_x(gi_):
            if gi_ >= len(self.groups):
                return
            g_ = self.groups[gi_]
            hasL = g_["pos0"] > 0
            hasR = g_["pos0"] + 512 < g_["S"]
            if not hasL:
                S.pool(lambda e: e.memset(xs[:, :, 0:1], 0.0), writes=["xsh"])
            if not hasR:
                S.pool(lambda e: e.memset(xs[:, :, 513:514], 0.0), writes=["xsh"])
            lo = g_["t0"] - 1 if hasL else g_["t0"]
            hi = g_["t0"] + 513 if hasR else g_["t0"] + 512
            c0 = 0 if hasL else 1
            S.dma(xs[:, :, c0:c0 + (hi - lo)], x_in[:, :, lo:hi], writes=["xs"])

        def load_tab(gi_):
            if gi_ >= len(self.groups):
                return
            p0_ = self.groups[gi_]["pos0"]
            S.dma(tab, ropeT[:, :, p0_:p0_ + 512], writes=[tk])

        def group_body(gi, g):
            t0, pos0, Sq = g["t0"], g["pos0"], g["S"]
            hT = hTs[gi % 2]
            hk = [("hT", gi % 2, k) for k in range(8)]
            if gi == 0:
                load_x(0)
                load_tab(0)
            self.cur_tab = (tab, tk)
            for k in range(8):
                sq, sqk = sqR.next()
                S.act(lambda e, sq=sq, k=k: e.activation(out=sq, in_=xs[:, k, :], func=AF.Square), reads=["xs", "xsh"], writes=[sqk])
                S.pe(lambda e, sq=sq, k=k: e.matmul(psA, lhsT=onesb, rhs=sq[:, 0:512], start=(k == 0), stop=(k == 7)),
                     reads=[sqk, "bp"], writes=[("ps", 4)])
                S.pe(lambda e, sq=sq, k=k: e.matmul(psH[:, 0:2], lhsT=onesb, rhs=sq[:, 512:514], start=(k == 0), stop=(k == 7)),
                     reads=[sqk, "bp"], writes=[("ps", 5)])
            self.rstd_from([(psA, ("ps", 4), slice(0, 512)), (psH[:, 0:2], ("ps", 5), slice(512, 514))],
                           1.0 / 1024, NORM_EPS, lnv, rstd, "rstd", "lnv")
            for k in range(8):
                S.dve(lambda e, k=k: e.scalar_tensor_tensor(out=hT[:, k, :], in0=xs[:, k, :], scalar=self.ppv("wn", k),
                                                             in1=rstd, op0=ALU.mult, op1=ALU.mult),
                      reads=["xs", "xsh", "rstd", "pp"], writes=[hk[k]])
            self.dbg(f"dbg_h{gi}", hT, hk)
            self.dbg(f"dbg_x{gi}", xs, ["xs"])
            self.dbg(f"dbg_r{gi}", rstd, ["rstd"])
            if self.dbgbar:
                S.barrier()
            load_x(gi + 1)
            if self.dbgbar > 1:
                S.barrier()

            def fm_mm(c0_, M):
                ps, pk = mainR.next()
                MM(ps[0:M, :], pk, [(wA[:, k, c0_:c0_ + M], hT[:, k, 1:513]) for k in range(8)], hk + ["wA"])
                return ps, pk

            def ret_unit(c0_, j, scale, dname):
                ps, pk = fm_mm(c0_ + j * 128, 128)
                rope_unit(ps, pk, 128, scale, 0, self.bpv("perm64"), sc[dname][j * 128:(j + 1) * 128, t0:t0 + 512])

            for j in range(2):
                ps, pk = fm_mm(C_CQ + j * 128, 128)
                S.act(lambda e, ps=ps, j=j: e.activation(out=cqf[:, j, :], in_=ps, func=AF.Copy), reads=[pk], writes=[("cqf", j)])
                sq, sqk = sqR.next()
                S.act(lambda e, sq=sq, j=j: e.activation(out=sq[:, 0:512], in_=cqf[:, j, :], func=AF.Square),
                      reads=[("cqf", j)], writes=[sqk])
                pend.append(lambda sq=sq, j=j, sqk=sqk: S.pe(
                    lambda e: e.matmul(psA, lhsT=onesb, rhs=sq[:, 0:512], start=(j == 0), stop=(j == 1)),
                    reads=[sqk, "bp"], writes=[("ps", 4)]))
            ret_unit(C_RQ, 0, 1.0, "rqT")
            self.rstd_from([(psA, ("ps", 4), slice(0, 512))], 1.0 / 256, NORM_EPS, lnq, rsq, "rstd", "lnv")
            for j in range(2):
                S.dve(lambda e, j=j: e.scalar_tensor_tensor(out=cqn[:, j, :], in0=cqf[:, j, :], scalar=self.ppv("qnw", j),
                                                             in1=rsq, op0=ALU.mult, op1=ALU.mult),
                      reads=[("cqf", j), "rstd", "pp"], writes=[("cqn", j)])
            cqk = [("cqn", 0), ("cqn", 1)]
            ps, pk = fm_mm(C_CKV, 128)
            S.act(lambda e, ps=ps: e.activation(out=ckvf, in_=ps, func=AF.Copy), reads=[pk], writes=["ckvf"])
            sq, sqk = sqR.next()
            S.act(lambda e, sq=sq: e.activation(out=sq[:, 0:512], in_=ckvf, func=AF.Square), reads=["ckvf"], writes=[sqk])
            pend.append(lambda sq=sq, sqk=sqk: S.pe(lambda e: e.matmul(psA, lhsT=onesb, rhs=sq[:, 0:512], start=True, stop=True),
                                                    reads=[sqk, "bp"], writes=[("ps", 4)]))
            ret_unit(C_RQ, 1, 1.0, "rqT")
            self.rstd_from([(psA, ("ps", 4), slice(0, 512))], 1.0 / 128, NORM_EPS, lnq, rsq, "rstd", "lnv")
            S.dve(lambda e: e.scalar_tensor_tensor(out=ckvn, in0=ckvf, scalar=self.ppv("kvnw", 0), in1=rsq,
                                                   op0=ALU.mult, op1=ALU.mult),
                  reads=["ckvf", "rstd", "pp"], writes=["ckvn"])
            ret_unit(C_RK, 0, 0.125, "rkT")
            ret_unit(C_RK, 1, 0.125, "rkT")
            for i in range(4):
                ps, pk = mainR.next()
                MM(ps, pk, [(wuqn[:, k, 2 * i:2 * i + 2, :].rearrange("p a b -> p (a b)"), cqn[:, k, :]) for k in range(2)], cqk + ["wA"])
                so, sk = soR.next()
                S.act(lambda e, ps=ps, so=so: e.activation(out=so, in_=ps, func=AF.Copy), reads=[pk], writes=[sk])
                S.dma(sc["mqnT"][i * 128:(i + 1) * 128, t0:t0 + 512], so, reads=[sk])
            for i in range(2):
                ps, pk = mainR.next()
                MM(ps, pk, [(wuqr[:, k, 4 * i:4 * i + 4, :].rearrange("p a b -> p (a b)"), cqn[:, k, :]) for k in range(2)], cqk + ["wA"])
                rope_unit(ps, pk, 128, 1.0, 2, self.bpv("perm32"), sc["mqrT"][i * 128:(i + 1) * 128, t0:t0 + 512])
            for i in range(4):
                ps, pk = mainR.next()
                MM(ps, pk, [(wukn[:, 2 * i:2 * i + 2, :].rearrange("p a b -> p (a b)"), ckvn)], ["ckvn", "wA"])
                so, sk = soR.next()
                S.act(lambda e, ps=ps, so=so: e.activation(out=so, in_=ps, func=AF.Copy), reads=[pk], writes=[sk])
                S.dma(sc["mknT"][i * 128:(i + 1) * 128, t0:t0 + 512], so, reads=[sk])
            for tt in range(4):
                ps, pk = mainR.next()
                MM(ps, pk, [(ckvn[:, tt * 128:(tt + 1) * 128], wuv.rearrange("p a b -> p (a b)"))], ["ckvn", "wA"])
                va = vaugs[tt % 2]
                vk = ("vaug", tt % 2)
                S.act(lambda e, ps=ps, va=va: e.activation(out=va[:, :, 0:64], in_=ps.rearrange("p (h d) -> p h d", h=8),
                                                           func=AF.Copy), reads=[pk], writes=[vk])
                S.dma(sc["mva"][t0 + tt * 128:t0 + (tt + 1) * 128, :], va.rearrange("p h d -> p (h d)"), reads=[vk])
            ps, pk = fm_mm(C_KR, 128)
            rope_unit(ps, pk, 32, 1.0, 2, self.bpv("perm32"), sc["mkrT"][0:32, t0:t0 + 512])
            for tt in range(4):
                self.mmg(psH[:, 32 + tt * 16:48 + tt * 16], ("ps", 5),
                         [(hT[:, k, 1 + tt * 128:1 + (tt + 1) * 128], wA[:, k, C_DT:C_DT + 16]) for k in range(8)], hk + ["wA"])
            o_, w_ = PP["dtb"]
            dtb_b = self.pp[:, o_:o_ + 16].unsqueeze(1).broadcast_to([128, 4, 16])
            S.dve(lambda e: e.tensor_tensor(out=dtx, in0=psH[:, 32:96].rearrange("p (a b) -> p a b", a=4), in1=dtb_b, op=ALU.add),
                  reads=[("ps", 5), "pp"], writes=["dtx"])
            S.act(lambda e: e.activation(out=dte, in_=dtx, func=AF.Exp), reads=["dtx"], writes=["dte"])
            S.act(lambda e: e.activation(out=dts, in_=dte, func=AF.Ln, bias=1.0), reads=["dte"], writes=["dts"])
            S.dma(sc["sdt"][t0:t0 + 512, :].rearrange("(a p) c -> p a c", p=128), dts, reads=["dts"])
            for (c0_, nch, scale, dname) in ((C_DQ, 4, 1.0, "dqT"), (C_DK, 4, 1.0, "dkT")):
                for j in range(nch):
                    ret_unit(c0_, j, scale, dname)
            flush()
            load_tab(gi + 1)
            for (c0_, dname) in ((C_RV, "rv"), (C_DV, "dv")):
                for tt in range(4):
                    ps, pk = mainR.next()
                    MM(ps, pk, [(hT[:, k, 1 + tt * 128:1 + (tt + 1) * 128], wA[:, k, c0_:c0_ + 512]) for k in range(8)], hk + ["wA"])
                    tk_, tkk = tokR.next()
                    S.act(lambda e, ps=ps, tk_=tk_: e.activation(out=tk_, in_=ps, func=AF.Copy), reads=[pk], writes=[tkk])
                    S.dma(sc[dname][t0 + tt * 128:t0 + (tt + 1) * 128, :], tk_, reads=[tkk])
            for j in range(4):
                ps, pk = fm_mm(C_RG + j * 128, 128)
                so, sk = soR.next()
                S.act(lambda e, ps=ps, so=so: e.activation(out=so, in_=ps, func=AF.Silu), reads=[pk], writes=[sk])
                S.dma(sc["rgT"][j * 128:(j + 1) * 128, t0:t0 + 512], so, reads=[sk])
            for tt in range(4):
                ps, pk = mainR.next()
                MM(ps, pk, [(hT[:, k, 1 + tt * 128:1 + (tt + 1) * 128], wA[:, k, C_SZ:C_SZ + 512]) for k in range(8)], hk + ["wA"])
                tk_, tkk = tokR.next()
                S.act(lambda e, ps=ps, tk_=tk_: e.activation(out=tk_, in_=ps, func=AF.Silu), reads=[pk], writes=[tkk])
                S.dma(sc["sz"][t0 + tt * 128:t0 + (tt + 1) * 128, :], tk_, reads=[tkk])
            for j in range(8):
                cc = C_XBC + j * 128
                ps, pk = fm_mm(cc, 128)
                hcol = 2 + 2 * j
                hkey = ("ps", 5)
                for k in range(8):
                    S.pe(lambda e, k=k, cc=cc, hcol=hcol: e.matmul(psH[:, hcol:hcol + 2], lhsT=wA[:, k, cc:cc + 128],
                                                                    rhs=hT[:, k, 0:514:513], start=(k == 0), stop=(k == 7)),
                         reads=hk + ["wA"], writes=[hkey])
                G, gk = GR.next()
                S.act(lambda e, ps=ps, G=G: e.activation(out=G[:, 1:513], in_=ps, func=AF.Copy), reads=[pk], writes=[gk])
                S.dve(lambda e, G=G, hcol=hcol: e.tensor_copy(out=G[:, 0:514:513], in_=psH[:, hcol:hcol + 2]), reads=[hkey], writes=[(gk, "h")])
                cv, ck = cvR.next()
                o_w = PP["scw"][0]
                o_b = PP["scb"][0]
                S.dve(lambda e, G=G, cv=cv, j=j: e.tensor_scalar(out=cv, in0=G[:, 1:513], scalar1=self.pp[:, o_w + 8 + j:o_w + 9 + j],
                                                                  scalar2=self.pp[:, o_b + j:o_b + j + 1], op0=ALU.mult, op1=ALU.add),
                      reads=[gk, "pp"], writes=[ck])
                S.dve(lambda e, G=G, cv=cv, j=j: e.scalar_tensor_tensor(out=cv, in0=G[:, 0:512], scalar=self.pp[:, o_w + j:o_w + j + 1],
                                                                         in1=cv, op0=ALU.mult, op1=ALU.add),
                      reads=[gk, (gk, "h"), "pp", ck], writes=[ck])
                S.dve(lambda e, G=G, cv=cv, j=j: e.scalar_tensor_tensor(out=cv, in0=G[:, 2:514], scalar=self.pp[:, o_w + 16 + j:o_w + 17 + j],
                                                                         in1=cv, op0=ALU.mult, op1=ALU.add),
                      reads=[gk, (gk, "h"), "pp", ck], writes=[ck])
                so, sk = soR.next()
                S.act(lambda e, cv=cv, so=so: e.activation(out=so, in_=cv, func=AF.Silu), reads=[ck], writes=[sk])
                if j >= 4:
                    dname, r0 = ("sBT", (j - 4) * 128) if j < 6 else ("sCT", (j - 6) * 128)
                    S.dma(sc[dname][r0:r0 + 128, t0:t0 + 512], so, reads=[sk])
                if j < 6:
                    def tr_part(so=so, sk=sk, j=j):
                        half = tcount[0] % 2
                        tcount[0] += 1
                        pT = psT[:, half * 512:(half + 1) * 512]
                        ptk = ("ps", 7)
                        for tt in range(4):
                            S.pe(lambda e, tt=tt: e.transpose(out=pT[:, tt * 128:(tt + 1) * 128], in_=so[:, tt * 128:(tt + 1) * 128],
                                                              identity=ident), reads=[sk, "bp"], writes=[ptk])
                        S.dve(lambda e: e.tensor_copy(out=xTok[:, :, j * 128:(j + 1) * 128],
                                                      in_=pT.rearrange("p (a b) -> p a b", a=4)),
                              reads=[ptk], writes=[("xTok", j)])
                    pend.append(tr_part)
            flush()
            S.dma(sc["sxB"][t0:t0 + 512, :].rearrange("(a p) c -> p a c", p=128), xTok, reads=[("xTok", j) for j in range(6)])

        for gi, g in enumerate(self.groups):
            group_body(gi, g)


def common_inputs(inp, depth, smax):
    f = lambda n: np.ascontiguousarray(np.asarray(inp[n], np.float32))
    m = {
        "w_in": f("w_in"), "mla_w_uq": f("mla_w_uq"), "mla_w_ukv": f("mla_w_ukv"), "w_branch": f("w_branch"),
        "w_out": f("w_out"), "ffn_w_gate": f("ffn_w_gate"), "ffn_w_up": f("ffn_w_up"), "ffn_w_down": f("ffn_w_down"),
        "ppack": np.stack([make_ppack(inp, l) for l in range(depth)]),
        "cpack": make_cpack(), "bpack": make_bpack(), "rope": make_rope(smax),
    }
    return m


def _seq_list(mk):
    out = []
    base = 0
    for S_ in mk.seqs:
        out.append((base, S_))
        base += S_
    return out


def phase_mla(self, l):
    S, A = self.S, self.A
    A.reset()
    sc = self.scr
    smax = self.smax
    NKCm = smax // 128
    Vaug = A.alloc([NKCm, 8, 128], BF16)
    KTs = [A.alloc([smax], BF16) for _ in range(2)]
    QTs = [A.alloc([smax], BF16) for _ in range(2)]
    pTR = Rot([(A.alloc([512], BF16), ("pT", i)) for i in range(4)])
    rsR = Rot([(A.alloc([512], F32), ("rs", i)) for i in range(2)])
    obR = Rot([(A.alloc([512], BF16), ("ob", i)) for i in range(2)])
    scR = Rot([(self.bank(i), ("ps", i)) for i in range(4)])
    accR = Rot([(self.bank(4 + i), ("ps", 4 + i)) for i in range(3)])
    scale = 96.0 ** -0.5
    LA = 2
    for (base, Sq) in _seq_list(self):
        NKC = Sq // 128
        NQG = Sq // 512
        for c0 in range(0, NKC, 4):
            S.dma(Vaug[:, c0:c0 + 4, :, :].rearrange("p c h d -> p c (h d)"),
                  sc["mva"][base + c0 * 128:base + (c0 + 4) * 128, :].rearrange("(c p) e -> p c e", p=128),
                  writes=[("V", c0)])
        vkeys = [("V", c0) for c0 in range(0, NKC, 4)]

        def load_head(h, base=base, Sq=Sq):
            KT, QT = KTs[h % 2], QTs[h % 2]
            S.dma(KT[0:64, 0:Sq], sc["mknT"][h * 64:(h + 1) * 64, base:base + Sq], writes=[("KT", h % 2, 0)])
            S.dma(KT[64:96, 0:Sq], sc["mkrT"][0:32, base:base + Sq], writes=[("KT", h % 2, 1)])
            S.dma(QT[0:64, 0:Sq], sc["mqnT"][h * 64:(h + 1) * 64, base:base + Sq], writes=[("QT", h % 2, 0)])
            S.dma(QT[64:96, 0:Sq], sc["mqrT"][h * 32:(h + 1) * 32, base:base + Sq], writes=[("QT", h % 2, 1)])

        items = [(h, qg, kc) for h in range(8) for qg in range(NQG) for kc in range(NKC)]
        state = {}

        def do_S(it):
            h, qg, kc = it
            if qg == 0 and kc == 0:
                if h == 0:
                    load_head(0)
                if h + 1 < 8:
                    load_head(h + 1)
            KT, QT = KTs[h % 2], QTs[h % 2]
            ps, pk = scR.next()
            state[it] = (ps, pk)
            S.pe(lambda e: e.matmul(ps, lhsT=KT[0:96, kc * 128:(kc + 1) * 128], rhs=QT[0:96, qg * 512:(qg + 1) * 512],
                                    start=True, stop=True),
                 reads=[("KT", h % 2, 0), ("KT", h % 2, 1), ("QT", h % 2, 0), ("QT", h % 2, 1)], writes=[pk])
            pT, ptk = pTR.next()
            S.act(lambda e: e.activation(out=pT, in_=ps, func=AF.Exp, scale=scale), reads=[pk], writes=[ptk])
            state[it] = (pT, ptk)

        def do_PV(it, base=base):
            h, qg, kc = it
            pT, ptk = state.pop(it)
            if kc == 0:
                state[("acc", h, qg)] = accR.next()
            acc, ak = state[("acc", h, qg)]
            S.pe(lambda e: e.matmul(acc, lhsT=Vaug[:, kc, h, :], rhs=pT, start=(kc == 0), stop=(kc == NKC - 1)),
                 reads=[ptk] + vkeys, writes=[ak])
            if kc == NKC - 1:
                rs, rk = rsR.next()
                S.dve(lambda e: e.reciprocal(out=rs[0:64, :], in_=acc[64:128, :]), reads=[ak], writes=[rk])
                ob, ok = obR.next()
                S.dve(lambda e: e.tensor_tensor(out=ob[0:64, :], in0=acc[0:64, :], in1=rs[0:64, :], op=ALU.mult),
                      reads=[ak, rk], writes=[ok])
                t0 = base + qg * 512
                S.dma(sc["boT"][512 + h * 64:512 + (h + 1) * 64, t0:t0 + 512], ob[0:64, :], reads=[ok])
                del state[("acc", h, qg)]

        n = len(items)
        for i in range(n + LA):
            if i < n:
                do_S(items[i])
            if i - LA >= 0:
                do_PV(items[i - LA])


def phase_diff(self, l):
    S, A = self.S, self.A
    A.reset()
    sc = self.scr
    smax = self.smax
    NKCm = smax // 128
    lam_init = 0.8 - 0.6 * math.exp(-0.3 * l)
    V = A.alloc([NKCm, 512], BF16)
    KTz = [[A.alloc([smax], BF16) for _ in range(2)] for _ in range(2)]
    QTs = [A.alloc([smax], BF16) for _ in range(2)]
    for b_ in range(2):
        S.pool(lambda e, b_=b_: e.memset(KTz[b_][0][64:128, :], 0.0), writes=[("KT", b_, 0)])
        S.pool(lambda e, b_=b_: e.memset(KTz[b_][1][0:64, :], 0.0), writes=[("KT", b_, 1)])
    pTR = Rot([(A.alloc([512], BF16), ("pT", i)) for i in range(4)])
    rR = Rot([(A.alloc([512], F32), ("r", i)) for i in range(2)])
    tR = [A.alloc([512], F32) for _ in range(2)]
    dbuf = A.alloc([512], F32)
    sqb = A.alloc([512], BF16)
    lnb = A.alloc([512], F32)
    rsb = A.alloc([512], F32)
    obR = Rot([(A.alloc([512], BF16), ("ob", i)) for i in range(2)])
    junk = A.alloc([64], F32)
    sv = A.alloc([8], F32)
    scR = Rot([(self.bank(i), ("ps", i)) for i in range(3)])
    accO = [self.bank(3), self.bank(5)]
    accS = [self.bank(4), self.bank(6)]
    psN = self.bank(7)
    onesb = self.bpv("ones")
    o_, _w = PP["dlam"]
    dl = self.pp[:, o_:o_ + 256]
    for i in range(2):
        S.dve(lambda e, i=i: e.scalar_tensor_tensor(out=junk, in0=dl[:, i * 128:i * 128 + 64], scalar=1.0,
                                                    in1=dl[:, i * 128 + 64:i * 128 + 128], op0=ALU.mult, op1=ALU.mult,
                                                    accum_out=sv[:, i:i + 1]), reads=["pp", "junk"], writes=["junk", ("sv", i)])
    S.act(lambda e: e.activation(out=sv[:, 2:4], in_=sv[:, 0:2], func=AF.Exp), reads=[("sv", 0), ("sv", 1)], writes=["sve"])
    S.dve(lambda e: e.tensor_tensor(out=sv[:, 4:5], in0=sv[:, 3:4], in1=sv[:, 2:3], op=ALU.subtract), reads=["sve"], writes=["nl0"])
    S.dve(lambda e: e.tensor_scalar(out=sv[:, 5:6], in0=sv[:, 4:5], scalar1=-lam_init, scalar2=None, op0=ALU.add),
          reads=["nl0"], writes=["neglam"])
    S.dve(lambda e: e.tensor_scalar(out=sv[:, 6:7], in0=self.ppv("dnw", 0), scalar1=(1.0 - lam_init), scalar2=None, op0=ALU.mult),
          reads=["pp"], writes=["wd"])
    neglam = sv[:, 5:6]
    wd = sv[:, 6:7]
    LA = 2
    for (base, Sq) in _seq_list(self):
        NKC = Sq // 128
        NQG = Sq // 512
        for c0 in range(0, NKC, 8):
            S.dma(V[:, c0:c0 + 8, :], sc["dv"][base + c0 * 128:base + (c0 + 8) * 128, :].rearrange("(c p) e -> p c e", p=128),
                  writes=[("V", c0)])
        vkeys = [("V", c0) for c0 in range(0, NKC, 8)]

        def load_head(h, base=base, Sq=Sq):
            S.dma(KTz[h % 2][0][0:64, 0:Sq], sc["dkT"][h * 128:h * 128 + 64, base:base + Sq], writes=[("KT", h % 2, 0)])
            S.dma(KTz[h % 2][1][64:128, 0:Sq], sc["dkT"][h * 128 + 64:(h + 1) * 128, base:base + Sq], writes=[("KT", h % 2, 1)])
            S.dma(QTs[h % 2][:, 0:Sq], sc["dqT"][h * 128:(h + 1) * 128, base:base + Sq], writes=[("QT", h % 2)])

        items = [(h, qg, m, kc) for h in range(4) for qg in range(NQG) for m in range(2) for kc in range(NKC)]
        state = {}

        def do_S(it):
            h, qg, m, kc = it
            if qg == 0 and kc == 0 and m == 0:
                if h == 0:
                    load_head(0)
                if h + 1 < 4:
                    load_head(h + 1)
            KT, QT = KTz[h % 2][m], QTs[h % 2]
            ps, pk = scR.next()
            S.pe(lambda e: e.matmul(ps, lhsT=KT[:, kc * 128:(kc + 1) * 128],
                                    rhs=QT[:, qg * 512:(qg + 1) * 512], start=True, stop=True),
                 reads=[("KT", h % 2, m), ("QT", h % 2)], writes=[pk])
            pT, ptk = pTR.next()
            S.act(lambda e: e.activation(out=pT, in_=ps, func=AF.Exp, scale=0.125), reads=[pk], writes=[ptk])
            state[it] = (pT, ptk)

        def do_PV(it, base=base):
            h, qg, m, kc = it
            pT, ptk = state.pop(it)
            aO, aS = accO[m], accS[m]
            kO, kS = ("ps", 3 + 2 * m), ("ps", 4 + 2 * m)
            S.pe(lambda e: e.matmul(aO, lhsT=V[:, kc, h * 128:(h + 1) * 128], rhs=pT, start=(kc == 0), stop=(kc == NKC - 1)),
                 reads=[ptk] + vkeys, writes=[kO])
            S.pe(lambda e: e.matmul(aS, lhsT=onesb, rhs=pT, start=(kc == 0), stop=(kc == NKC - 1)),
                 reads=[ptk, "bp"], writes=[kS])
            if kc == NKC - 1:
                r, rk = rR.next()
                S.dve(lambda e: e.reciprocal(out=r, in_=aS), reads=[kS], writes=[rk])
                t = tR[m]
                S.dve(lambda e: e.tensor_tensor(out=t, in0=aO, in1=r, op=ALU.mult), reads=[kO, rk], writes=[("t", m)])
                if m == 1:
                    S.dve(lambda e: e.scalar_tensor_tensor(out=dbuf, in0=tR[1], scalar=neglam, in1=tR[0], op0=ALU.mult, op1=ALU.add),
                          reads=[("t", 0), ("t", 1), "neglam"], writes=["dbuf"])
                    S.dve(lambda e: e.tensor_tensor(out=sqb, in0=dbuf, in1=dbuf, op=ALU.mult), reads=["dbuf"], writes=["sqb"])
                    S.pe(lambda e: e.matmul(psN, lhsT=onesb, rhs=sqb, start=True, stop=True), reads=["sqb", "bp"], writes=[("ps", 7)])
                    self.rstd_from([(psN, ("ps", 7), slice(0, 512))], 1.0 / 128, HEAD_EPS, lnb, rsb, "rsb", "lnb")
                    ob, ok = obR.next()
                    S.dve(lambda e: e.scalar_tensor_tensor(out=ob, in0=dbuf, scalar=wd, in1=rsb, op0=ALU.mult, op1=ALU.mult),
                          reads=["dbuf", "rsb", "wd"], writes=[ok])
                    t0 = base + qg * 512
                    S.dma(sc["boT"][1024 + h * 128:1024 + (h + 1) * 128, t0:t0 + 512], ob, reads=[ok])

        n = len(items)
        for i in range(n + LA):
            if i < n:
                do_S(items[i])
            if i - LA >= 0:
                do_PV(items[i - LA])


MK.phase_mla = phase_mla
MK.phase_diff = phase_diff


def phase_ret(self, l):
    S, A = self.S, self.A
    A.reset()
    self.load_cp()
    sc = self.scr
    smax = self.smax
    Nm = smax // 128
    qT = A.alloc([2, smax], BF16)
    kT = A.alloc([2, smax], BF16)
    vR = Rot([(A.alloc([4, 512], BF16), ("v", i)) for i in range(3)])
    Sfb = A.alloc([Nm, 512], BF16)
    Sbb = A.alloc([Nm, 512], BF16)
    kvbs = A.alloc([Nm, 512], BF16)
    Sf = A.alloc([512], F32)
    Sb = A.alloc([512], F32)
    MT = A.alloc([4, 128], F32)
    e1t = A.alloc([128], F32)
    Qdf = A.alloc([2, 128], F32)
    Qdb = A.alloc([2, 128], F32)
    Kdf = A.alloc([4, 64], F32)
    Kdb = A.alloc([4, 64], F32)
    Gcf = A.alloc([4, 128], F32)
    Gcb = A.alloc([4, 128], F32)
    kdfR = Rot([(A.alloc([256], BF16), ("kdf", i)) for i in range(2)])
    kdbR = Rot([(A.alloc([256], BF16), ("kdb", i)) for i in range(2)])
    qdR = Rot([(A.alloc([512], BF16), ("qd", i)) for i in range(4)])
    aTR = Rot([(A.alloc([512], BF16), ("aT", i)) for i in range(2)])
    gR = Rot([(A.alloc([512], BF16), ("g", i)) for i in range(2)])
    sqb = A.alloc([512], BF16)
    lnb = A.alloc([512], F32)
    rsb = A.alloc([512], F32)
    tb = A.alloc([512], F32)
    obR = Rot([(A.alloc([512], BF16), ("ob", i)) for i in range(2)])
    ident = self.bpv("ident")
    onesb = self.bpv("ones")
    o_lg = PP["lgd"][0]

    def lg(d, h, p0=0, p1=128):
        return self.pp[p0:p1, o_lg + d * 4 + h:o_lg + d * 4 + h + 1]

    for h in range(4):
        S.act(lambda e, h=h: e.activation(out=MT[:, h, :], in_=self.cpv("Dpos"), func=AF.Exp, scale=lg(0, h)), reads=["pp"], writes=[("MT", h)])
        S.act(lambda e, h=h: e.activation(out=e1t, in_=self.cpv("Dneg"), func=AF.Exp, scale=lg(1, h)), reads=["pp"], writes=["e1t"])
        S.dve(lambda e, h=h: e.tensor_tensor(out=MT[:, h, :], in0=MT[:, h, :], in1=self.cpv("Mgt"), op=ALU.mult), reads=[("MT", h)], writes=[("MT", h)])
        S.dve(lambda e, h=h: e.tensor_tensor(out=e1t, in0=e1t, in1=self.cpv("Mlt"), op=ALU.mult), reads=["e1t"], writes=["e1t"])
        S.dve(lambda e, h=h: e.tensor_tensor(out=MT[:, h, :], in0=MT[:, h, :], in1=e1t, op=ALU.add), reads=[("MT", h), "e1t"], writes=[("MT", h)])
        S.dve(lambda e, h=h: e.tensor_tensor(out=MT[:, h, :], in0=MT[:, h, :], in1=self.cpv("I2"), op=ALU.add), reads=[("MT", h)], writes=[("MT", h)])
        c, r0 = h // 2, (h % 2) * 64
        S.act(lambda e, h=h, c=c, r0=r0: e.activation(out=Qdf[r0:r0 + 64, c, :], in_=self.cpv("Ip1")[r0:r0 + 64, :], func=AF.Exp,
                                                       scale=lg(0, h, r0, r0 + 64)), reads=["pp"], writes=[("Qd", h)])
        S.act(lambda e, h=h, c=c, r0=r0: e.activation(out=Qdb[r0:r0 + 64, c, :], in_=self.cpv("Cmi")[r0:r0 + 64, :], func=AF.Exp,
                                                       scale=lg(1, h, r0, r0 + 64)), reads=["pp"], writes=[("Qd", h)])
        S.act(lambda e, h=h: e.activation(out=Kdf[:, h, :], in_=self.cpv("Cm1j"), func=AF.Exp, scale=lg(0, h)), reads=["pp"], writes=[("Kd", h)])
        S.act(lambda e, h=h: e.activation(out=Kdb[:, h, :], in_=self.cpv("Jj"), func=AF.Exp, scale=lg(1, h)), reads=["pp"], writes=[("Kd", h)])
        S.act(lambda e, h=h: e.activation(out=Gcf[0:64, h, :], in_=self.cpv("C128")[0:64, :], func=AF.Exp, scale=lg(0, h, 0, 64)), reads=["pp"], writes=[("Gc", h)])
        S.act(lambda e, h=h: e.activation(out=Gcb[0:64, h, :], in_=self.cpv("C128")[0:64, :], func=AF.Exp, scale=lg(1, h, 0, 64)), reads=["pp"], writes=[("Gc", h)])
    tabkeys = [("MT", h) for h in range(4)] + [("Qd", h) for h in range(4)] + [("Kd", h) for h in range(4)] + [("Gc", h) for h in range(4)]
    psT = self.bank(0, BF16)
    kvfB, kvbB = self.bank(1), self.bank(2)
    Gcf2 = Gcf.rearrange("p a b -> p (a b)")
    Gcb2 = Gcb.rearrange("p a b -> p (a b)")
    Kdf2 = Kdf.rearrange("p a b -> p (a b)")
    Kdb2 = Kdb.rearrange("p a b -> p (a b)")
    kd_keys = [("Kd", h) for h in range(4)]
    gc_keys = [("Gc", h) for h in range(4)]

    for (base, Sq) in _seq_list(self):
        N = Sq // 128
        S.dma(qT[:, :, 0:Sq], sc["rqT"][:, base:base + Sq].rearrange("(c p) t -> p c t", p=128), writes=["qT"])
        S.dma(kT[:, :, 0:Sq], sc["rkT"][:, base:base + Sq].rearrange("(c p) t -> p c t", p=128), writes=["kT"])

        def load_v(G, base=base):
            vt, vk = vR.next()
            S.dma(vt, sc["rv"][base + G * 512:base + (G + 1) * 512, :].rearrange("(c p) e -> p c e", p=128), writes=[vk])
            return vt, vk
        S.pool(lambda e: e.memset(Sf[0:64, :], 0.0), writes=["Sf"])
        S.pool(lambda e: e.memset(Sb[0:64, :], 0.0), writes=["Sb"])

        def pass1(n, vt, vk):
            vkeys = [vk]
            half = n % 2
            pT_ = psT[:, half * 512:half * 512 + 256]
            ptk = ("ps", 0)
            for c in range(2):
                S.pe(lambda e, c=c: e.transpose(out=pT_[:, c * 128:(c + 1) * 128], in_=kT[:, c, n * 128:(n + 1) * 128], identity=ident),
                     reads=["kT", "bp"], writes=[ptk])
            kdf, kfk = kdfR.next()
            kdb, kbk = kdbR.next()
            S.dve(lambda e: e.tensor_tensor(out=kdf, in0=pT_, in1=Kdf2, op=ALU.mult), reads=[ptk] + kd_keys, writes=[kfk])
            S.dve(lambda e: e.tensor_tensor(out=kdb, in0=pT_, in1=Kdb2, op=ALU.mult), reads=[ptk] + kd_keys, writes=[kbk])
            for h in range(4):
                S.pe(lambda e, h=h: e.matmul(kvfB[0:64, h * 128:(h + 1) * 128], lhsT=kdf[:, h * 64:(h + 1) * 64],
                                             rhs=vt[:, n % 4, h * 128:(h + 1) * 128], start=True, stop=True), reads=[kfk] + vkeys, writes=[("ps", 1)])
            for h in range(4):
                S.pe(lambda e, h=h: e.matmul(kvbB[0:64, h * 128:(h + 1) * 128], lhsT=kdb[:, h * 64:(h + 1) * 64],
                                             rhs=vt[:, n % 4, h * 128:(h + 1) * 128], start=True, stop=True), reads=[kbk] + vkeys, writes=[("ps", 2)])
            S.act(lambda e: e.activation(out=kvbs[0:64, n, :], in_=kvbB[0:64, :], func=AF.Copy), reads=[("ps", 2)], writes=[("kvbs", n)])
            S.pool(lambda e: e.tensor_copy(out=Sfb[0:64, n, :], in_=Sf[0:64, :]), reads=["Sf"], writes=[("Sfb", n)])
            S.pool(lambda e: e.tensor_tensor(out=Sf[0:64, :], in0=Sf[0:64, :], in1=Gcf2[0:64, :], op=ALU.mult), reads=["Sf"] + gc_keys, writes=["Sf"])
            S.dve(lambda e: e.tensor_tensor(out=Sf[0:64, :], in0=Sf[0:64, :], in1=kvfB[0:64, :], op=ALU.add), reads=["Sf", ("ps", 1)], writes=["Sf"])

        for G in range(N // 4):
            vt, vk = load_v(G)
            for ci in range(4):
                pass1(G * 4 + ci, vt, vk)

        def pass1b(n):
            S.pool(lambda e: e.tensor_copy(out=Sbb[0:64, n, :], in_=Sb[0:64, :]), reads=["Sb"], writes=[("Sbb", n)])
            S.pool(lambda e: e.tensor_tensor(out=Sb[0:64, :], in0=Sb[0:64, :], in1=Gcb2[0:64, :], op=ALU.mult), reads=["Sb"] + gc_keys, writes=["Sb"])
            S.pool(lambda e: e.tensor_tensor(out=Sb[0:64, :], in0=Sb[0:64, :], in1=kvbs[0:64, n, :], op=ALU.add), reads=["Sb", ("kvbs", n)], writes=["Sb"])


        scR = Rot([(self.bank(i), ("ps", i)) for i in (0, 1)])
        yR = Rot([(self.bank(i), ("ps", i)) for i in (2, 3, 4)])
        psN = self.bank(5)

        def pass2(G, h, vt, vk, base=base):
            vkeys = [vk]
            c, r0 = h // 2, (h % 2) * 64
            t0 = base + G * 512
            g, gk = gR.next()
            S.dma(g, sc["rgT"][h * 128:(h + 1) * 128, t0:t0 + 512], writes=[gk])
            qdf, qfk = qdR.next()
            qdb, qbk = qdR.next()
            qv = qT[r0:r0 + 64, c, G * 512:(G + 1) * 512].rearrange("p (a b) -> p a b", a=4)
            S.dve(lambda e: e.tensor_tensor(out=qdf[0:64, :].rearrange("p (a b) -> p a b", a=4), in0=qv,
                                            in1=Qdf[r0:r0 + 64, c, :].unsqueeze(1).broadcast_to([64, 4, 128]), op=ALU.mult),
                  reads=["qT", ("Qd", h)], writes=[qfk])
            S.dve(lambda e: e.tensor_tensor(out=qdb[0:64, :].rearrange("p (a b) -> p a b", a=4), in0=qv,
                                            in1=Qdb[r0:r0 + 64, c, :].unsqueeze(1).broadcast_to([64, 4, 128]), op=ALU.mult),
                  reads=["qT", ("Qd", h)], writes=[qbk])
            ps, pk = scR.next()
            for ci in range(4):
                n = G * 4 + ci
                S.pe(lambda e, ci=ci, n=n: e.matmul(ps[:, ci * 128:(ci + 1) * 128], lhsT=kT[r0:r0 + 64, c, n * 128:(n + 1) * 128],
                                                     rhs=qT[r0:r0 + 64, c, n * 128:(n + 1) * 128], start=True, stop=True),
                     reads=["kT", "qT"], writes=[pk])
            aT, ak = aTR.next()
            S.dve(lambda e: e.tensor_tensor(out=aT.rearrange("p (a b) -> p a b", a=4), in0=ps.rearrange("p (a b) -> p a b", a=4),
                                            in1=MT[:, h, :].unsqueeze(1).broadcast_to([128, 4, 128]), op=ALU.mult),
                  reads=[pk, ("MT", h)], writes=[ak])
            y, yk = yR.next()
            for ci in range(4):
                n = G * 4 + ci
                ysl = y[:, ci * 128:(ci + 1) * 128]
                S.pe(lambda e, ci=ci, n=n, ysl=ysl: e.matmul(ysl, lhsT=vt[:, ci, h * 128:(h + 1) * 128], rhs=aT[:, ci * 128:(ci + 1) * 128],
                                                              start=True, stop=False), reads=[ak] + vkeys, writes=[yk])
                S.pe(lambda e, ci=ci, n=n, ysl=ysl: e.matmul(ysl, lhsT=Sfb[0:64, n, h * 128:(h + 1) * 128], rhs=qdf[0:64, ci * 128:(ci + 1) * 128],
                                                              start=False, stop=False), reads=[qfk, ("Sfb", n)], writes=[yk])
                S.pe(lambda e, ci=ci, n=n, ysl=ysl: e.matmul(ysl, lhsT=Sbb[0:64, n, h * 128:(h + 1) * 128], rhs=qdb[0:64, ci * 128:(ci + 1) * 128],
                                                              start=False, stop=True), reads=[qbk, ("Sbb", n)], writes=[yk])
            def epi():
                S.act(lambda e: e.activation(out=sqb, in_=y, func=AF.Square), reads=[yk], writes=["sqb"])
                S.pe(lambda e: e.matmul(psN, lhsT=onesb, rhs=sqb, start=True, stop=True), reads=["sqb", "bp"], writes=[("ps", 5)])
                self.rstd_from([(psN, ("ps", 5), slice(0, 512))], 1.0 / 128, HEAD_EPS, lnb, rsb, "rsb", "lnb")
                S.dve(lambda e: e.tensor_tensor(out=tb, in0=y, in1=rsb, op=ALU.mult), reads=[yk, "rsb"], writes=["tb"])
                ob, ok = obR.next()
                S.dve(lambda e: e.scalar_tensor_tensor(out=ob, in0=tb, scalar=self.ppv("rnw", h), in1=g, op0=ALU.mult, op1=ALU.mult),
                      reads=["tb", gk, "pp"], writes=[ok])
                S.dma(sc["boT"][h * 128:(h + 1) * 128, t0:t0 + 512], ob, reads=[ok])
            return epi

        prev_epi = None
        for G in range(Sq // 512 - 1, -1, -1):
            for ci in range(3, -1, -1):
                pass1b(G * 4 + ci)
            vt, vk = load_v(G)
            for h in range(4):
                epi = pass2(G, h, vt, vk)
                if prev_epi is not None:
                    prev_epi()
                prev_epi = epi
        prev_epi()


MK.phase_ret = phase_ret


def phase_ssd(self, l):
    S, A = self.S, self.A
    A.reset()
    self.load_cp()
    sc = self.scr
    smax = self.smax
    Nm = smax // 128
    BT = A.alloc([2, smax], BF16)
    CT = A.alloc([2, smax], BF16)
    dtA = A.alloc([Nm, 16], F32)
    dta = A.alloc([Nm, 16], F32)
    cumS = A.alloc([Nm, 16], F32)
    ecum = A.alloc([Nm, 16], F32)
    sdte = A.alloc([Nm, 16], F32)
    etot = A.alloc([Nm, 16], F32)
    A16 = A.alloc([16], F32)
    prevs = [A.alloc([Nm, 512], BF16) for _ in range(2)]
    Hs = [A.alloc([512], F32) for _ in range(2)]
    xBR = Rot([(A.alloc([4, 768], BF16), ("xB", i)) for i in range(4)])
    zR = Rot([(A.alloc([4, 512], BF16), ("z", i)) for i in range(2)])
    xwR = Rot([(A.alloc([512], BF16), ("xw", i)) for i in range(2)])
    xdtR = Rot([(A.alloc([512], BF16), ("xdt", i)) for i in range(4)])
    LhR = Rot([(A.alloc([4, 128], F32), ("Lh", i)) for i in range(2)])
    ER = Rot([(A.alloc([4, 128], F32), ("E", i)) for i in range(2)])
    WR = Rot([(A.alloc([4, 128], BF16), ("W", i)) for i in range(2)])
    CBm = [A.alloc([2, 128], F32) for _ in range(2)]
    tbs = [A.alloc([512], F32) for _ in range(2)]
    ubs = [A.alloc([512], F32) for _ in range(2)]
    t2b = A.alloc([512], F32)
    junk = A.alloc([256], F32)
    ssbs = [A.alloc([4], F32) for _ in range(2)]
    yoR = Rot([(A.alloc([512], BF16), ("yo", i)) for i in range(2)])
    oTR = Rot([(A.alloc([4, 512], BF16), ("oT", i)) for i in range(2)])
    ident = self.bpv("ident")
    Uf32, Ub32 = self.cpv("Uf"), self.cpv("Ub")
    SL = [self.cpv("SLf"), self.cpv("SLb")]
    U = [Uf32, Ub32]
    ones32 = self.cpv("ones")
    o_a = PP["alog"][0]
    o_d = PP["ssd"][0]
    o_n = PP["snw"][0]
    S.act(lambda e: e.activation(out=A16, in_=self.pp[:, o_a:o_a + 16], func=AF.Exp), reads=["pp"], writes=["A16"])
    S.dve(lambda e: e.tensor_scalar(out=A16, in0=A16, scalar1=-1.0, scalar2=None, op0=ALU.mult), reads=["A16"], writes=["A16"])
    Dsk = self.pp[:, o_d:o_d + 8].unsqueeze(2).broadcast_to([128, 8, 64])
    snw = self.pp[:, o_n:o_n + 512]

    for (base, Sq) in _seq_list(self):
        N = Sq // 128
        NG = Sq // 512
        S.dma(BT[:, :, 0:Sq], sc["sBT"][:, base:base + Sq].rearrange("(c p) t -> p c t", p=128), writes=["BT"])
        S.dma(CT[:, :, 0:Sq], sc["sCT"][:, base:base + Sq].rearrange("(c p) t -> p c t", p=128), writes=["CT"])
        S.dma(dtA[:, 0:N, :], sc["sdt"][base:base + Sq, :].rearrange("(c p) k -> p c k", p=128), writes=["dtA"])
        S.dve(lambda e, N=N: e.tensor_tensor(out=dta[:, 0:N, :], in0=dtA[:, 0:N, :], in1=A16.unsqueeze(1).broadcast_to([128, N, 16]), op=ALU.mult),
              reads=["dtA", "A16"], writes=["dta"])
        psC = self.bank(0)
        psTt = self.bank(1)
        psC3 = psC.rearrange("p (n c) -> p n c", c=16)
        S.pe(lambda e, N=N: e.matmul(psC3[:, 0:N, 0:8], lhsT=Uf32, rhs=dta[:, 0:N, 0:8], start=True, stop=True), reads=["dta"], writes=[("ps", 0)])
        S.pe(lambda e, N=N: e.matmul(psC3[:, 0:N, 8:16], lhsT=Ub32, rhs=dta[:, 0:N, 8:16], start=True, stop=True), reads=["dta"], writes=[("ps", 0)])
        S.pe(lambda e, N=N: e.matmul(psTt[:, 0:N * 16], lhsT=ones32, rhs=dta[:, 0:N, :].rearrange("p n c -> p (n c)"), start=True, stop=True),
             reads=["dta"], writes=[("ps", 1)])
        cs2 = cumS.rearrange("p n c -> p (n c)")
        S.act(lambda e, N=N: e.activation(out=cs2[:, 0:N * 16], in_=psC[:, 0:N * 16], func=AF.Copy), reads=[("ps", 0)], writes=["cumS"])
        S.act(lambda e, N=N: e.activation(out=ecum.rearrange("p n c -> p (n c)")[:, 0:N * 16], in_=cs2[:, 0:N * 16], func=AF.Exp),
              reads=["cumS"], writes=["ecum"])
        sd2 = sdte.rearrange("p n c -> p (n c)")
        S.dve(lambda e, N=N: e.tensor_tensor(out=sd2[:, 0:N * 16], in0=psTt[:, 0:N * 16], in1=cs2[:, 0:N * 16], op=ALU.subtract),
              reads=[("ps", 1), "cumS"], writes=["sdte"])
        S.act(lambda e, N=N: e.activation(out=sd2[:, 0:N * 16], in_=sd2[:, 0:N * 16], func=AF.Exp), reads=["sdte"], writes=["sdte"])
        S.dve(lambda e, N=N: e.tensor_tensor(out=sd2[:, 0:N * 16], in0=sd2[:, 0:N * 16], in1=dtA.rearrange("p n c -> p (n c)")[:, 0:N * 16], op=ALU.mult),
              reads=["sdte", "dtA"], writes=["sdte"])
        S.act(lambda e, N=N: e.activation(out=etot.rearrange("p n c -> p (n c)")[:, 0:N * 16], in_=psTt[:, 0:N * 16], func=AF.Exp),
              reads=[("ps", 1)], writes=["etot"])
        for d in range(2):
            S.pool(lambda e, d=d: e.memset(Hs[d], 0.0), writes=[("H", d)])

        def load_x(G, base=base):
            xB, xk = xBR.next()
            t0 = base + G * 512
            S.dma(xB, sc["sxB"][t0:t0 + 512, :].rearrange("(c p) e -> p c e", p=128), writes=[xk])
            return xB, xk

        stR = Rot([(self.bank(i), ("ps", i)) for i in (2, 3)])

        def state_step(d, n, xB, xk):
            ci = n % 4
            H, prev = Hs[d], prevs[d]
            S.pool(lambda e: e.tensor_copy(out=prev[:, n, :], in_=H), reads=[("H", d)], writes=[("prev", d, n)])
            xw, wk = xwR.next()
            S.dve(lambda e: e.tensor_tensor(out=xw.rearrange("p (h q) -> p h q", h=8), in0=xB[:, ci, 0:512].rearrange("p (h q) -> p h q", h=8),
                                            in1=sdte[:, n, d * 8:(d + 1) * 8].unsqueeze(2).broadcast_to([128, 8, 64]), op=ALU.mult),
                  reads=[xk, "sdte"], writes=[wk])
            ps, pk = stR.next()
            for g in range(2):
                S.pe(lambda e, g=g: e.matmul(ps[:, g * 256:(g + 1) * 256], lhsT=xB[:, ci, 512 + g * 128:512 + (g + 1) * 128],
                                             rhs=xw[:, g * 256:(g + 1) * 256], start=True, stop=True), reads=[xk, wk], writes=[pk])
            S.pool(lambda e: e.tensor_tensor(out=H.rearrange("p (h q) -> p h q", h=8), in0=H.rearrange("p (h q) -> p h q", h=8),
                                             in1=etot[:, n, d * 8:(d + 1) * 8].unsqueeze(2).broadcast_to([128, 8, 64]), op=ALU.mult),
                   reads=[("H", d), "etot"], writes=[("H", d)])
            S.dve(lambda e: e.tensor_tensor(out=H, in0=H, in1=ps, op=ALU.add), reads=[("H", d), pk], writes=[("H", d)])

        curf = curb = None
        for step in range(N):
            nf, nb_ = step, N - 1 - step
            if nf % 4 == 0:
                curf = load_x(nf // 4)
            if nb_ % 4 == 3:
                curb = load_x(nb_ // 4)
            state_step(0, nf, curf[0], curf[1])
            state_step(1, nb_, curb[0], curb[1])

        psCB = self.bank(0)
        psDR = Rot([(self.bank(i), ("ps", i)) for i in (1, 2)])
        Yd = [self.bank(3), self.bank(4)]
        Yo = [self.bank(5), self.bank(6)]
        psT = self.bank(7, BF16)

        pend_tail = [None]
        store_jobs = []

        def out_chunk(n, xB, xk, z, zk, oT, otk):
            ci = n % 4
            tsl = slice(n * 128, (n + 1) * 128)
            for g in range(2):
                S.pe(lambda e, g=g: e.matmul(psCB[:, g * 128:(g + 1) * 128], lhsT=BT[:, g, tsl], rhs=CT[:, g, tsl], start=True, stop=True),
                     reads=["BT", "CT"], writes=[("ps", 0)])
            for d in range(2):
                S.dve(lambda e, d=d: e.tensor_tensor(out=CBm[d], in0=psCB[:, 0:256].rearrange("p (g i) -> p g i", g=2),
                                                     in1=U[d].unsqueeze(1).broadcast_to([128, 2, 128]), op=ALU.mult),
                      reads=[("ps", 0)], writes=[("CBm", d)])
            for d in range(2):
                xdt, xdk = xdtR.next()
                S.dve(lambda e, d=d, xdt=xdt: e.tensor_tensor(out=xdt.rearrange("p (h q) -> p h q", h=8), in0=xB[:, ci, 0:512].rearrange("p (h q) -> p h q", h=8),
                                                               in1=dtA[:, n, d * 8:(d + 1) * 8].unsqueeze(2).broadcast_to([128, 8, 64]), op=ALU.mult),
                      reads=[xk, "dtA"], writes=[xdk])
                for g in range(2):
                    c0 = d * 8 + g * 4
                    Lh, lk = LhR.next()
                    for hh in range(4):
                        S.act(lambda e, d=d, c0=c0, Lh=Lh, hh=hh: e.activation(out=Lh[:, hh, :], in_=SL[d], func=AF.Copy,
                                                                                 scale=dta[:, n, c0 + hh:c0 + hh + 1]),
                              reads=["dta"], writes=[(lk, hh)])
                    psD, pdk = psDR.next()
                    for hh in range(4):
                        S.pe(lambda e, d=d, hh=hh, Lh=Lh, psD=psD: e.matmul(psD[:, hh * 128:(hh + 1) * 128], lhsT=Lh[:, hh, :], rhs=U[d], start=True, stop=True),
                             reads=[(lk, hh)], writes=[pdk])
                    E, ek = ER.next()
                    S.act(lambda e, E=E, psD=psD: e.activation(out=E.rearrange("p a b -> p (a b)"), in_=psD, func=AF.Exp), reads=[pdk], writes=[ek])
                    W, wk = WR.next()
                    S.dve(lambda e, d=d, g=g, E=E, W=W: e.tensor_tensor(out=W, in0=E, in1=CBm[d][:, g, :].unsqueeze(1).broadcast_to([128, 4, 128]), op=ALU.mult),
                          reads=[ek, ("CBm", d)], writes=[wk])
                    for hh in range(4):
                        h = g * 4 + hh
                        S.pe(lambda e, d=d, hh=hh, h=h, W=W, xdt=xdt: e.matmul(Yd[d][:, h * 64:(h + 1) * 64], lhsT=W[:, hh, :], rhs=xdt[:, h * 64:(h + 1) * 64],
                                                                                 start=True, stop=True), reads=[wk, xdk], writes=[("ps", 3 + d)])
                    S.pe(lambda e, d=d, g=g: e.matmul(Yo[d][:, g * 256:(g + 1) * 256], lhsT=CT[:, g, tsl], rhs=prevs[d][:, n, g * 256:(g + 1) * 256],
                                                      start=True, stop=True), reads=["CT", ("prev", d, n)], writes=[("ps", 5 + d)])
            tb, ub, ssb = tbs[n % 2], ubs[n % 2], ssbs[n % 2]
            tbk, ubk = ("tb", n % 2), ("ub", n % 2)
            t3 = tb.rearrange("p (h q) -> p h q", h=8)
            u3 = ub.rearrange("p (h q) -> p h q", h=8)
            S.dve(lambda e: e.tensor_tensor(out=t3, in0=Yo[0].rearrange("p (h q) -> p h q", h=8),
                                            in1=ecum[:, n, 0:8].unsqueeze(2).broadcast_to([128, 8, 64]), op=ALU.mult),
                  reads=[("ps", 5), "ecum"], writes=[tbk])
            S.dve(lambda e: e.tensor_tensor(out=tb, in0=tb, in1=Yd[0], op=ALU.add), reads=[tbk, ("ps", 3)], writes=[tbk])
            S.dve(lambda e: e.tensor_tensor(out=u3, in0=Yo[1].rearrange("p (h q) -> p h q", h=8),
                                            in1=ecum[:, n, 8:16].unsqueeze(2).broadcast_to([128, 8, 64]), op=ALU.mult),
                  reads=[("ps", 6), "ecum"], writes=[ubk])
            S.dve(lambda e: e.tensor_tensor(out=ub, in0=ub, in1=Yd[1], op=ALU.add), reads=[ubk, ("ps", 4)], writes=[ubk])
            def tail_b():
                S.pool(lambda e: e.tensor_tensor(out=tb, in0=tb, in1=ub, op=ALU.add), reads=[tbk, ubk], writes=[tbk])
                S.pool(lambda e: e.tensor_tensor(out=t2b.rearrange("p (h q) -> p h q", h=8), in0=xB[:, ci, 0:512].rearrange("p (h q) -> p h q", h=8),
                                                 in1=Dsk, op=ALU.mult), reads=[xk, "pp"], writes=["t2b"])
                S.pool(lambda e: e.tensor_tensor(out=tb, in0=tb, in1=t2b, op=ALU.add), reads=[tbk, "t2b"], writes=[tbk])
                S.pool(lambda e: e.tensor_tensor(out=tb, in0=tb, in1=z[:, ci, :], op=ALU.mult), reads=[tbk, zk], writes=[tbk])
                for g in range(2):
                    S.act(lambda e, g=g: e.activation(out=junk, in_=tb[:, g * 256:(g + 1) * 256], func=AF.Square, accum_out=ssb[:, g:g + 1]),
                          reads=[tbk, "junk"], writes=["junk", ("ss", n % 2, g)])
                S.act(lambda e: e.activation(out=ssb[:, 2:4], in_=ssb[:, 0:2], func=AF.Ln, scale=1.0 / 256, bias=HEAD_EPS),
                      reads=[("ss", n % 2, 0), ("ss", n % 2, 1)], writes=[("ssl", n % 2)])
                S.act(lambda e: e.activation(out=ssb[:, 0:2], in_=ssb[:, 2:4], func=AF.Exp, scale=-0.5), reads=[("ssl", n % 2)], writes=[("ss", n % 2, 0), ("ss", n % 2, 1), ("rs2", n % 2)])
                yo, yk = yoR.next()
                for g in range(2):
                    S.dve(lambda e, g=g, yo=yo: e.scalar_tensor_tensor(out=yo[:, g * 256:(g + 1) * 256], in0=tb[:, g * 256:(g + 1) * 256], scalar=ssb[:, g:g + 1],
                                                                        in1=snw[:, g * 256:(g + 1) * 256], op0=ALU.mult, op1=ALU.mult),
                          reads=[tbk, ("rs2", n % 2), "pp"], writes=[(yk, g)])
                half = n % 2
                pT_ = psT[:, half * 512:(half + 1) * 512]
                ptk = ("ps", 7)
                for cc in range(4):
                    S.pe(lambda e, cc=cc, yo=yo: e.transpose(out=pT_[:, cc * 128:(cc + 1) * 128], in_=yo[:, cc * 128:(cc + 1) * 128], identity=ident),
                         reads=[(yk, 0), (yk, 1), "bp"], writes=[ptk])
                S.dve(lambda e: e.tensor_copy(out=oT[:, :, ci * 128:(ci + 1) * 128], in_=pT_.rearrange("p (a b) -> p a b", a=4)),
                      reads=[ptk], writes=[(otk, ci)])


            return tail_b

        for G in range(NG):
            xB, xk = load_x(G)
            z, zk = zR.next()
            t0 = base + G * 512
            S.dma(z, sc["sz"][t0:t0 + 512, :].rearrange("(c p) e -> p c e", p=128), writes=[zk])
            oT, otk = oTR.next()
            for ci in range(4):
                tb_fn = out_chunk(G * 4 + ci, xB, xk, z, zk, oT, otk)
                if pend_tail[0] is not None:
                    pend_tail[0]()
                pend_tail[0] = tb_fn
            store_jobs.append((t0, oT, otk))
            if len(store_jobs) > 1:
                t0_, oT_, otk_ = store_jobs.pop(0)
                S.dma(sc["boT"][1536:2048, t0_:t0_ + 512].rearrange("(c p) t -> p c t", p=128), oT_, reads=[(otk_, ci) for ci in range(4)])
        pend_tail[0]()
        pend_tail[0] = None
        while store_jobs:
            t0_, oT_, otk_ = store_jobs.pop(0)
            S.dma(sc["boT"][1536:2048, t0_:t0_ + 512].rearrange("(c p) t -> p c t", p=128), oT_, reads=[(otk_, ci) for ci in range(4)])


MK.phase_ssd = phase_ssd


def load_cp(self):
    self.cp = self.A.alloc([NCP], F32)
    self.S.dma(self.cp, self.cpack, writes=["cp"])
    for e in (PE, ACT, DVE, POOL):
        self.S.add(e, None, reads=["cp"], real=False)


def rms_group(self, xs, ncols, wname, hT, hk, sqR, lnv, rstd, psA, psH0, kA=None, kH=None):
    S = self.S
    onesb = self.bpv("ones")
    for k in range(8):
        sq, sqk = sqR.next()
        S.act(lambda e, sq=sq, k=k: e.activation(out=sq[:, 0:ncols], in_=xs[:, k, :], func=AF.Square),
              reads=["xs", "xsh"], writes=[sqk])
        S.pe(lambda e, sq=sq, k=k: e.matmul(psA, lhsT=onesb, rhs=sq[:, 0:512], start=(k == 0), stop=(k == 7)),
             reads=[sqk, "bp"], writes=[kA])
        if ncols > 512:
            S.pe(lambda e, sq=sq, k=k: e.matmul(psH0, lhsT=onesb, rhs=sq[:, 512:ncols], start=(k == 0), stop=(k == 7)),
                 reads=[sqk, "bp"], writes=[kH])
    lst = [(psA, kA, slice(0, 512))]
    if ncols > 512:
        lst.append((psH0, kH, slice(512, ncols)))
    self.rstd_from(lst, 1.0 / 1024, NORM_EPS, lnv[:, 0:ncols], rstd[:, 0:ncols], "rstd", "lnv")
    if hT is not None:
        for k in range(8):
            S.dve(lambda e, k=k: e.scalar_tensor_tensor(out=hT[:, k, :], in0=xs[:, k, :], scalar=self.ppv(wname, k),
                                                         in1=rstd[:, 0:ncols], op0=ALU.mult, op1=ALU.mult),
                  reads=["xs", "xsh", "rstd", "pp"], writes=[hk[k]])


def phase3(self, l):
    S, A = self.S, self.A
    A.reset()
    sc = self.scr
    x_in = (self.xT if l == 0 else sc["xres"]).rearrange("(c p) t -> p c t", p=128)
    boT = sc["boT"].rearrange("(c p) t -> p c t", p=128)
    wG = A.alloc([8, 4096], BF16)
    wB = A.alloc([16, 1024], BF16)
    wO = A.alloc([8, 1024], BF16)
    bo = A.alloc([16, 512], BF16)
    bo_w = bo.rearrange("p a b -> p (a b)").bitcast(F32)
    stgR = Rot([(bo_w[:, i * 2048:(i + 1) * 2048], ("stg", i)) for i in range(2)])
    win = self.w_in[l].rearrange("(c p) n -> p c n", p=128)
    pieces = []
    for k in range(8):
        for q in range(2):
            pieces.append((wG[:, k, q * 2048:(q + 1) * 2048], win[:, k, C_GATE + q * 2048:C_GATE + (q + 1) * 2048]))
    wbr = self.w_branch[l].rearrange("b (c p) n -> p (b c) n", p=128)
    for i in range(16):
        pieces.append((wB[:, i, :], wbr[:, i, :]))
    wo = self.w_out[l].rearrange("(c p) n -> p c n", p=128)
    for k in range(8):
        pieces.append((wO[:, k, :], wo[:, k, :]))
    self.load_cast("w3", pieces, stgR)
    xs = A.alloc([8, 512], F32)
    hTs = [A.alloc([8, 512], BF16) for _ in range(2)]
    sqR = Rot([(A.alloc([512], BF16), ("sq", i)) for i in range(2)])
    lnv = A.alloc([512], F32)
    rstd = A.alloc([512], F32)
    merged = A.alloc([8, 512], BF16)
    gsR = Rot([(A.alloc([512], F32), ("gs", i)) for i in range(3)])
    tmR = Rot([(A.alloc([512], F32), ("tm", i)) for i in range(2)])
    mR = Rot([(A.alloc([512], F32), ("m", i)) for i in range(2)])
    xoR = Rot([(A.alloc([512], F32), ("xo", i)) for i in range(2)])
    psGR = Rot([(self.bank(i), ("ps", i)) for i in (0, 1, 2)])
    psPR = Rot([(self.bank(i), ("ps", i)) for i in (3, 4)])
    psOR = Rot([(self.bank(i), ("ps", i)) for i in (5, 6)])
    psA = self.bank(7)
    ng = len(self.groups)

    def load_x(gi):
        if gi >= ng:
            return
        t0 = self.groups[gi]["t0"]
        S.dma(xs, x_in[:, :, t0:t0 + 512], writes=["xs"])

    def load_bo(gi):
        if gi >= ng:
            return
        t0 = self.groups[gi]["t0"]
        for q in range(4):
            S.dma(bo[:, q * 4:(q + 1) * 4, :], boT[:, q * 4:(q + 1) * 4, t0:t0 + 512], writes=[("bo", q), ("stg", 0), ("stg", 1)])

    def body(gi, g):
        t0 = g["t0"]
        hT = hTs[gi % 2]
        hk = [("hT", gi % 2, k) for k in range(8)]
        if gi == 0:
            load_x(0)
            load_bo(0)
        rms_group(self, xs, 512, "wn", hT, hk, sqR, lnv, rstd, psA, None, kA=("ps", 7))
        load_x(gi + 1)
        for j in range(8):
            m, mk_ = mR.next()
            for b in range(4):
                psG, pgk = psGR.next()
                cg = b * 1024 + j * 128
                self.mmg(psG, pgk, [(wG[:, k, cg:cg + 128], hT[:, k, :]) for k in range(8)], hk + ["w3"])
                gs, gk = gsR.next()
                S.act(lambda e, psG=psG, gs=gs: e.activation(out=gs, in_=psG, func=AF.Sigmoid), reads=[pgk], writes=[gk])
                psP, ppk = psPR.next()
                self.mmg(psP, ppk, [(wB[:, b * 4 + kc, j * 128:(j + 1) * 128], bo[:, b * 4 + kc, :]) for kc in range(4)], [("bo", b), "w3"])
                if b == 0:
                    S.dve(lambda e, psP=psP, gs=gs, m=m: e.tensor_tensor(out=m, in0=psP, in1=gs, op=ALU.mult), reads=[ppk, gk], writes=[mk_])
                else:
                    tm, tk = tmR.next()
                    S.dve(lambda e, psP=psP, gs=gs, tm=tm: e.tensor_tensor(out=tm, in0=psP, in1=gs, op=ALU.mult), reads=[ppk, gk], writes=[tk])
                    dst = merged[:, j, :] if b == 3 else m
                    dk = ("mg", j) if b == 3 else mk_
                    S.pool(lambda e, tm=tm, m=m, dst=dst: e.tensor_tensor(out=dst, in0=m, in1=tm, op=ALU.add), reads=[mk_, tk], writes=[dk])
        load_bo(gi + 1)
        mgk = [("mg", j) for j in range(8)]
        for jo in range(8):
            psO, pok = psOR.next()
            self.mmg(psO, pok, [(wO[:, k, jo * 128:(jo + 1) * 128], merged[:, k, :]) for k in range(8)], mgk + ["w3"])
            xo, xk = xoR.next()
            S.dma(xo, x_in[:, jo, t0:t0 + 512], writes=[xk])
            S.dve(lambda e, psO=psO, xo=xo: e.tensor_tensor(out=xo, in0=psO, in1=xo, op=ALU.add), reads=[pok, xk], writes=[xk])
            S.dma(sc["xmid"][jo * 128:(jo + 1) * 128, t0:t0 + 512], xo, reads=[xk])

    for gi, g in enumerate(self.groups):
        body(gi, g)


def phase4(self, l):
    S, A = self.S, self.A
    A.reset()
    sc = self.scr
    x_in = sc["xmid"].rearrange("(c p) t -> p c t", p=128)
    wg = A.alloc([8, D_FF], BF16)
    wu = A.alloc([8, D_FF], BF16)
    wd = A.alloc([NFF, 1024], BF16)
    act = A.alloc([NFF, 512], BF16)
    act_w = act.rearrange("p a b -> p (a b)").bitcast(F32)
    stgR = Rot([(act_w[:, i * 2816:(i + 1) * 2816], ("stg", i)) for i in range(2)])
    pieces = []
    g_ = self.w_gate[l].rearrange("(c p) n -> p c n", p=128)
    u_ = self.w_up[l].rearrange("(c p) n -> p c n", p=128)
    for k in range(8):
        pieces.append((wg[:, k, :], g_[:, k, :]))
        pieces.append((wu[:, k, :], u_[:, k, :]))
    d_ = self.w_down[l].rearrange("(c p) n -> p c n", p=128)
    for c in range(0, NFF, 2):
        pieces.append((wd[:, c:c + 2, :], d_[:, c:c + 2, :]))
    self.load_cast("w4", pieces, stgR)
    xs = A.alloc([8, 514], F32)
    hTs = [A.alloc([8, 514], BF16) for _ in range(2)]
    sqR = Rot([(A.alloc([514], BF16), ("sq", i)) for i in range(2)])
    lnv = A.alloc([514], F32)
    rstd = lnv
    GR = Rot([(A.alloc([514], F32), ("G", i)) for i in range(1)])
    cvR = Rot([(A.alloc([512], F32), ("cv", i)) for i in range(1)])
    xrR = Rot([(A.alloc([512], F32), ("xr", i)) for i in range(1)])
    psgR = Rot([(self.bank(i), ("ps", i)) for i in (0, 1)])
    psuR = Rot([(self.bank(i), ("ps", i)) for i in (2, 3)])
    psOR = Rot([(self.bank(i), ("ps", i)) for i in (4, 5)])
    psA = self.bank(6)
    psH = self.bank(7)
    ng = len(self.groups)
    o_w = PP["fcw"][0]
    o_b = PP["fcb"][0]
    x_out = sc["xres"]

    def load_x(gi_):
        if gi_ >= ng:
            return
        g_i = self.groups[gi_]
        hasL = g_i["pos0"] > 0
        hasR = g_i["pos0"] + 512 < g_i["S"]
        if not hasL:
            S.pool(lambda e: e.memset(xs[:, :, 0:1], 0.0), writes=["xsh"])
        if not hasR:
            S.pool(lambda e: e.memset(xs[:, :, 513:514], 0.0), writes=["xsh"])
        lo = g_i["t0"] - 1 if hasL else g_i["t0"]
        hi = g_i["t0"] + 513 if hasR else g_i["t0"] + 512
        c0 = 0 if hasL else 1
        S.dma(xs[:, :, c0:c0 + (hi - lo)], x_in[:, :, lo:hi], writes=["xs"])

    def body(gi, g):
        t0 = g["t0"]
        hT = hTs[gi % 2]
        hk = [("hT", gi % 2, k) for k in range(8)]
        if gi == 0:
            load_x(0)
        rms_group(self, xs, 514, "nf", hT, hk, sqR, lnv, rstd, psA, psH[:, 0:2], kA=("ps", 6), kH=("ps", 7))
        load_x(gi + 1)
        for c in range(NFF):
            cs = slice(c * 128, (c + 1) * 128)
            psg, pgk = psgR.next()
            self.mmg(psg, pgk, [(wg[:, k, cs], hT[:, k, 1:513]) for k in range(8)], hk + ["w4"])
            hcol = 2 + 2 * c
            hkey = ("ps", 7)
            for k in range(8):
                S.pe(lambda e, k=k, cs=cs, hcol=hcol: e.matmul(psH[:, hcol:hcol + 2], lhsT=wg[:, k, cs], rhs=hT[:, k, 0:514:513],
                                                                start=(k == 0), stop=(k == 7)), reads=hk + ["w4"], writes=[hkey])
            psu, puk = psuR.next()
            self.mmg(psu, puk, [(wu[:, k, cs], hT[:, k, 1:513]) for k in range(8)], hk + ["w4"])
            G, gk = GR.next()
            S.act(lambda e, psg=psg, G=G: e.activation(out=G[:, 1:513], in_=psg, func=AF.Copy), reads=[pgk], writes=[gk])
            S.dve(lambda e, G=G, hcol=hcol: e.tensor_copy(out=G[:, 0:514:513], in_=psH[:, hcol:hcol + 2]), reads=[hkey], writes=[(gk, "h")])
            cv, ck = cvR.next()
            S.dve(lambda e, G=G, cv=cv, c=c: e.tensor_scalar(out=cv, in0=G[:, 1:513], scalar1=self.pp[:, o_w + NFF + c:o_w + NFF + c + 1],
                                                              scalar2=self.pp[:, o_b + c:o_b + c + 1], op0=ALU.mult, op1=ALU.add),
                  reads=[gk, "pp"], writes=[ck])
            S.dve(lambda e, G=G, cv=cv, c=c: e.scalar_tensor_tensor(out=cv, in0=G[:, 0:512], scalar=self.pp[:, o_w + c:o_w + c + 1],
                                                                     in1=cv, op0=ALU.mult, op1=ALU.add),
                  reads=[gk, (gk, "h"), "pp", ck], writes=[ck])
            S.dve(lambda e, G=G, cv=cv, c=c: e.scalar_tensor_tensor(out=cv, in0=G[:, 2:514], scalar=self.pp[:, o_w + 2 * NFF + c:o_w + 2 * NFF + c + 1],
                                                                     in1=cv, op0=ALU.mult, op1=ALU.add),
                  reads=[gk, (gk, "h"), "pp", ck], writes=[ck])
            S.act(lambda e, cv=cv: e.activation(out=cv, in_=cv, func=AF.Silu), reads=[ck], writes=[ck])
            S.dve(lambda e, cv=cv, psu=psu, c=c: e.tensor_tensor(out=act[:, c, :], in0=psu, in1=cv, op=ALU.mult), reads=[puk, ck], writes=[("act", c)])
        ak = [("act", c) for c in range(NFF)]
        for jo in range(8):
            xr, xrk = xrR.next()
            S.dma(xr, sc["xmid"][jo * 128:(jo + 1) * 128, t0:t0 + 512], writes=[xrk])
            psO, pok = psOR.next()
            self.mmg(psO, pok, [(wd[:, c, jo * 128:(jo + 1) * 128], act[:, c, :]) for c in range(NFF)], ak + ["w4"])
            S.dve(lambda e, psO=psO, xr=xr: e.tensor_tensor(out=xr, in0=psO, in1=xr, op=ALU.add), reads=[pok, xrk], writes=[xrk])
            S.dma(x_out[jo * 128:(jo + 1) * 128, t0:t0 + 512], xr, reads=[xrk])

    for gi, g in enumerate(self.groups):
        body(gi, g)


def phase5(self):
    S, A = self.S, self.A
    A.reset()
    sc = self.scr
    x_in = sc["xres"].rearrange("(c p) t -> p c t", p=128)
    xss = [A.alloc([8, 512], F32) for _ in range(2)]
    sqR = Rot([(A.alloc([512], BF16), ("sq", i)) for i in range(2)])
    lnv = A.alloc([512], F32)
    rstd = A.alloc([512], F32)
    yR = Rot([(A.alloc([512], F32), ("y", i)) for i in range(3)])
    psAs = [self.bank(0), self.bank(1)]
    onesb = self.bpv("ones")

    def body(gi, g):
        t0 = g["t0"]
        xs = xss[gi % 2]
        xk = ("xs", gi % 2)
        S.dma(xs, x_in[:, :, t0:t0 + 512], writes=[xk])
        psA = psAs[gi % 2]
        pak = ("ps", gi % 2)
        for k in range(8):
            sq, sqk = sqR.next()
            S.act(lambda e, sq=sq, k=k: e.activation(out=sq, in_=xs[:, k, :], func=AF.Square), reads=[xk], writes=[sqk])
            S.pe(lambda e, sq=sq, k=k: e.matmul(psA, lhsT=onesb, rhs=sq, start=(k == 0), stop=(k == 7)), reads=[sqk, "bp"], writes=[pak])
        self.rstd_from([(psA, pak, slice(0, 512))], 1.0 / 1024, NORM_EPS, lnv, rstd, "rstd", "lnv")
        for k in range(8):
            y, yk = yR.next()
            S.dve(lambda e, k=k, y=y: e.scalar_tensor_tensor(out=y, in0=xs[:, k, :], scalar=self.ppv("fin", k), in1=rstd,
                                                              op0=ALU.mult, op1=ALU.mult), reads=[xk, "rstd", "pp"], writes=[yk])
            S.dma(self.yT[k * 128:(k + 1) * 128, t0:t0 + 512], y, reads=[yk])

    for gi, g in enumerate(self.groups):
        body(gi, g)


MK.load_cp = load_cp
MK.phase3 = phase3
MK.phase4 = phase4
MK.phase5 = phase5


_NC_CACHE = {}
N_CORES = 8
SEQ_P, SEQ_S, DEPTH = 4096, 2048, 2


def kernel(**inputs):
    xp = np.asarray(inputs["x_prompt"], np.float32)
    xs_ = np.asarray(inputs["x_sample"], np.float32)
    seqs = [SEQ_P, SEQ_P, SEQ_S]
    key = "main"
    if key not in _NC_CACHE:
        mk = MK(seqs, DEPTH)
        _NC_CACHE[key] = mk.build()
    nc = _NC_CACHE[key]
    common = common_inputs(inputs, DEPTH, SEQ_P)
    in_maps = []
    for c in range(N_CORES):
        xT = np.concatenate([xp[2 * c].T, xp[2 * c + 1].T, xs_[c].T], axis=1)
        m = dict(common)
        m["xT"] = np.ascontiguousarray(xT)
        in_maps.append(m)
    res = run_bass_kernel_spmd(nc, in_maps, core_ids=list(range(N_CORES)))
    yp = np.empty((16, SEQ_P, D_MODEL), np.float32)
    ys = np.empty((8, SEQ_S, D_MODEL), np.float32)
    for c in range(N_CORES):
        yT = np.asarray(res.results[c]["yT"], np.float32)
        yp[2 * c] = yT[:, 0:SEQ_P].T
        yp[2 * c + 1] = yT[:, SEQ_P:2 * SEQ_P].T
        ys[c] = yT[:, 2 * SEQ_P:2 * SEQ_P + SEQ_S].T
    return (yp, ys)
```

```python
import math
from contextlib import ExitStack

import numpy as np
import ml_dtypes

import concourse.bass as bass
import concourse.mybir as mybir
from concourse.bass_utils import run_bass_kernel_spmd

F32 = mybir.dt.float32
BF16 = mybir.dt.bfloat16
AF = mybir.ActivationFunctionType
ALU = mybir.AluOpType
AX = mybir.AxisListType

PE, ACT, DVE, POOL, SP = "pe", "act", "dve", "pool", "sp"
ENGS = (PE, ACT, DVE, POOL, SP)

D_MODEL = 1024
IN_COLS = 9136
D_FF = 2816
NFF = 22
NORM_EPS = 1e-6
HEAD_EPS = 1e-5
ROPE_THETA = 10000.0

C_RQ, C_RK, C_RV, C_RG = 0, 256, 512, 1024
C_CQ, C_CKV, C_KR = 1536, 1792, 1920
C_DQ, C_DK, C_DV = 1952, 2464, 2976
C_SZ, C_XBC, C_DT, C_GATE = 3488, 4000, 5024, 5040
NA = 5040


class Op:
    __slots__ = ("eng", "fn", "deps", "is_dma", "sig", "slot", "dval", "idx", "sigval")


class Sched:
    def __init__(self, nc, ndsem=12):
        self.nc = nc
        self.ops = {e: [] for e in ENGS}
        self.res = {}
        self.dma_n = {e: 0 for e in ENGS}
        self.dma_last = {e: {} for e in ENGS}
        import os
        self.ndsem = int(os.environ.get("NDSEM", ndsem))
        rr = int(os.environ.get("NROT", "1"))
        self.nrot = {PE: 4 * rr, ACT: 2 * rr, DVE: 2 * rr, POOL: 2 * rr, SP: 1}
        self.last_real = {e: None for e in ENGS}

    def add(self, eng, fn, reads=(), writes=(), dma=False, extra=(), real=True):
        op = Op()
        op.eng = eng
        op.fn = fn
        op.is_dma = dma
        op.sig = False
        op.idx = len(self.ops[eng])
        op.slot = None
        deps = set(extra)
        res = self.res
        if any(type(k) is tuple and k[0] == "ps" for k in reads):
            writes = list(writes) + [k for k in reads if type(k) is tuple and k[0] == "ps"]
            reads = [k for k in reads if not (type(k) is tuple and k[0] == "ps")]
        for k in reads:
            st = res.get(k)
            if st is not None and st[0] is not None:
                deps.add(st[0])
        for k in writes:
            st = res.get(k)
            if st is not None:
                if st[0] is not None:
                    deps.add(st[0])
                deps.update(st[1].values())
        if dma:
            i = self.dma_n[eng]
            self.dma_n[eng] = i + 1
            op.slot = i % self.ndsem
            op.dval = 16 * (i // self.ndsem + 1)
            prev = self.dma_last[eng].get(op.slot)
            if prev is not None:
                deps.add(prev)
            self.dma_last[eng][op.slot] = op
        rk = (eng, op.slot) if dma else eng
        for k in reads:
            st = res.get(k)
            if st is None:
                st = [None, {}]
                res[k] = st
            st[1][rk] = op
        for k in writes:
            res[k] = [op, {}]
        fd = []
        work = list(deps)
        seen = set()
        while work:
            d = work.pop()
            if d is op or d is None or id(d) in seen:
                continue
            seen.add(id(d))
            if d.fn is None:
                if d.eng != eng:
                    work.extend(d.deps)
                continue
            if (not d.is_dma) and (not dma) and d.eng == eng and eng == PE:
                continue
            fd.append(d)
            if not d.is_dma:
                d.sig = True
        op.deps = fd
        self.ops[eng].append(op)
        if real and not dma:
            self.last_real[eng] = op
        return op

    def pe(self, fn, reads=(), writes=()):
        return self.add(PE, fn, reads, writes)

    def act(self, fn, reads=(), writes=()):
        return self.add(ACT, fn, reads, writes)

    def dve(self, fn, reads=(), writes=()):
        return self.add(DVE, fn, reads, writes)

    def pool(self, fn, reads=(), writes=()):
        return self.add(POOL, fn, reads, writes)

    def dma(self, out, in_, reads=(), writes=(), q=SP, **kw):
        return self.add(q, lambda e: e.dma_start(out=out, in_=in_, **kw), reads, writes, dma=True)

    def barrier(self):
        lasts = [self.last_real[e] for e in ENGS if self.last_real[e] is not None]
        dmas = []
        for e in ENGS:
            dmas.extend(self.dma_last[e].values())
        for e in ENGS:
            if not self.ops[e] and e not in (PE, ACT, DVE, POOL, SP):
                continue
            self.add(e, None, extra=[o for o in lasts if o.eng != e] + dmas, real=False)
        self.res = {}

    def emit(self, es):
        nc = self.nc
        engobj = {PE: nc.tensor, ACT: nc.scalar, DVE: nc.vector, POOL: nc.gpsimd, SP: nc.sync}
        csem = {}
        for e in ENGS:
            if any(o.sig for o in self.ops[e]):
                csem[e] = [es.enter_context(nc.semaphore(f"c_{e}_{r}")) for r in range(self.nrot[e])]
        dsem = {}
        for e in ENGS:
            if self.dma_n[e] > 0:
                dsem[e] = [es.enter_context(nc.semaphore(f"d_{e}_{r}"))
                           for r in range(min(self.ndsem, self.dma_n[e]))]
        for e in ENGS:
            n = 0
            R = self.nrot[e]
            for o in self.ops[e]:
                if o.sig:
                    o.sigval = (n % R, n // R + 1)
                    n += 1
        block = es.enter_context(nc.Block())
        self.nwaits = 0
        self.ninst = 0

        def emit_engine(e):
            eo = engobj[e]
            waited = {}
            for o in self.ops[e]:
                for d in o.deps:
                    if d.is_dma:
                        key = ("d", d.eng, d.slot)
                        val = d.dval
                        sem = dsem[d.eng][d.slot]
                    else:
                        r, val = d.sigval
                        key = ("c", d.eng, r)
                        sem = csem[d.eng][r]
                    if waited.get(key, 0) >= val:
                        continue
                    waited[key] = val
                    eo.wait_ge(sem, val)
                    self.nwaits += 1
                if o.fn is None:
                    continue
                inst = o.fn(eo)
                self.ninst += 1
                if o.is_dma:
                    inst.then_inc(dsem[e][o.slot], 16)
                elif o.sig:
                    r, _ = o.sigval
                    inst.then_inc(csem[e][r], 1)
            for slot, o in self.dma_last[e].items():
                key = ("d", e, slot)
                if waited.get(key, 0) < o.dval:
                    eo.wait_ge(dsem[e][slot], o.dval)

        @block.tensor
        def _(t):
            emit_engine(PE)

        @block.scalar
        def _(t):
            emit_engine(ACT)

        @block.vector
        def _(t):
            emit_engine(DVE)

        @block.gpsimd
        def _(t):
            emit_engine(POOL)

        @block.sync
        def _(t):
            emit_engine(SP)


class Rot:
    def __init__(self, items):
        self.items = items
        self.i = 0

    def next(self):
        it = self.items[self.i % len(self.items)]
        self.i += 1
        return it


PP = {}
_off = 0
for _n, _w in [("wn", 8), ("nf", 8), ("fin", 8), ("scw", 24), ("scb", 8), ("qnw", 2), ("kvnw", 1),
               ("rnw", 4), ("dnw", 1), ("fcw", 66), ("fcb", 22),
               ("lgd", 8), ("dtb", 16), ("alog", 16), ("ssd", 8), ("snw", 512), ("dlam", 256)]:
    PP[_n] = (_off, _w)
    _off += _w
NPP = _off

CP = {}
_off = 0
for _n, _w in [("Uf", 128), ("Ub", 128), ("SLf", 128), ("SLb", 128), ("ones", 128), ("Dpos", 128), ("Dneg", 128),
               ("Mgt", 128), ("Mlt", 128), ("I2", 128), ("Ip1", 128), ("Cmi", 128), ("Cm1j", 64), ("Jj", 64),
               ("C128", 128)]:
    CP[_n] = (_off, _w)
    _off += _w
NCP = _off

BP = {}
_off = 0
for _n, _w in [("ident", 128), ("perm64", 128), ("perm32", 128), ("ones", 128)]:
    BP[_n] = (_off, _w)
    _off += _w
NBP = _off


def make_cpack():
    c = np.zeros((128, NCP), np.float32)
    j = np.arange(128)[:, None].astype(np.float32)
    i = np.arange(128)[None, :].astype(np.float32)

    def put(name, a):
        o, w = CP[name]
        c[:, o:o + w] = a

    put("Uf", (j <= i))
    put("Ub", (j >= i))
    put("SLf", (j > i))
    put("SLb", (j < i))
    put("ones", np.ones((128, 128)))
    put("Dpos", np.maximum(i - j, 0))
    put("Dneg", np.maximum(j - i, 0))
    put("Mgt", (i > j))
    put("Mlt", (j > i))
    put("I2", 2.0 * (i == j))
    put("Ip1", np.broadcast_to(i + 1, (128, 128)))
    put("Cmi", np.broadcast_to(128 - i, (128, 128)))
    put("Cm1j", np.broadcast_to(127 - j, (128, 64)))
    put("Jj", np.broadcast_to(j, (128, 64)))
    put("C128", np.full((128, 128), 128.0))
    return c


def make_bpack():
    b = np.zeros((128, NBP), np.float32)
    k = np.arange(128)[:, None]
    m = np.arange(128)[None, :]

    def put(name, a):
        o, w = BP[name]
        b[:, o:o + w] = a

    put("ident", (k == m))
    put("perm64", (k == (m ^ 32)))
    put("perm32", (k == (m ^ 16)))
    put("ones", np.ones((128, 128)))
    return b.astype(ml_dtypes.bfloat16)


def make_rope(smax):
    out = np.zeros((4, 128, smax), np.float32)
    pos = np.arange(smax, dtype=np.float32)
    for ti, dim in ((0, 64), (2, 32)):
        inv = (1.0 / (np.float32(ROPE_THETA) ** (np.arange(0, dim, 2, dtype=np.float32) / np.float32(dim)))).astype(np.float32)
        ang = pos[:, None] * inv[None, :]
        cos = np.cos(ang).astype(np.float32)
        sin = np.sin(ang).astype(np.float32)
        for f in range(128):
            d = f % dim
            jx = d % (dim // 2)
            out[ti, f] = cos[:, jx]
            out[ti + 1, f] = sin[:, jx] * (-1.0 if d < dim // 2 else 1.0)
    return out


def make_ppack(inp, l):
    p = np.zeros((128, NPP), np.float32)

    def put(name, a):
        o, w = PP[name]
        p[:, o:o + w] = np.asarray(a, np.float32).reshape(128, w)

    def fm(v, nch):
        return np.asarray(v, np.float32).reshape(nch, 128).T

    def bc(v):
        v = np.asarray(v, np.float32).reshape(-1)
        return np.broadcast_to(v[None, :], (128, v.size))

    put("wn", fm(inp["norm_mix_w"][l], 8))
    put("nf", fm(inp["norm_ffn_w"][l], 8))
    put("fin", fm(inp["final_norm_w"], 8))
    put("scw", np.concatenate([fm(inp["ssm_conv_w"][l][t], 8) for t in range(3)], axis=1))
    put("scb", fm(inp["ssm_conv_b"][l], 8))
    put("qnw", fm(inp["mla_q_norm_w"][l], 2))
    put("kvnw", fm(inp["mla_kv_norm_w"][l], 1))
    put("rnw", fm(inp["ret_norm_w"][l], 4))
    put("dnw", fm(inp["diff_norm_w"][l], 1))
    put("fcw", np.concatenate([fm(inp["ffn_conv_w"][l][t], NFF) for t in range(3)], axis=1))
    put("fcb", fm(inp["ffn_conv_b"][l], NFF))
    put("lgd", bc(inp["ret_log_decay"][l]))
    put("dtb", bc(inp["ssm_dt_bias"][l]))
    put("alog", bc(inp["ssm_a_log"][l]))
    put("ssd", bc(inp["ssm_d"][l]))
    put("snw", bc(inp["ssm_norm_w"][l]))
    put("dlam", bc(inp["diff_lambda"][l]))
    return p


class Arena:
    def __init__(self, tensor, nwords):
        self.t = tensor
        self.n = nwords
        self.off = 0
        self.base = 0

    def mark(self):
        self.base = self.off

    def reset(self):
        self.off = self.base

    def alloc(self, shape, dt=F32):
        n = int(np.prod(shape))
        words = n if dt == F32 else (n + 1) // 2
        words = (words + 7) // 8 * 8
        assert self.off + words <= self.n, f"SBUF arena overflow {self.off}+{words}>{self.n}"
        ap = self.t[:, self.off:self.off + words]
        self.off += words
        if dt != F32:
            ap = ap.bitcast(dt)
        ap = ap[:, 0:n]
        if len(shape) == 2:
            ap = ap.rearrange("p (a b) -> p a b", a=shape[0])
        elif len(shape) == 3:
            ap = ap.rearrange("p (a b c) -> p a b c", a=shape[0], b=shape[1])
        return ap


class MK:
    def __init__(self, seqs, depth, debug=(), stop_after=None):
        self.seqs = list(seqs)
        self.T = sum(seqs)
        self.depth = depth
        self.smax = max(seqs)
        self.debug = set(debug)
        self.stop_after = stop_after
        import os
        self.dbgbar = int(os.environ.get('DBGBAR', '0'))
        T = self.T
        nc = bass.Bass("TRN2", target_bir_lowering=False)
        self.nc = nc

        def din(name, shape, dt=F32):
            return nc.dram_tensor(name, shape, dt, kind="ExternalInput").ap()

        self.xT = din("xT", [1024, T])
        self.w_in = din("w_in", [depth, 1024, IN_COLS])
        self.w_uq = din("mla_w_uq", [depth, 256, 768])
        self.w_ukv = din("mla_w_ukv", [depth, 128, 1024])
        self.w_branch = din("w_branch", [depth, 4, 512, 1024])
        self.w_out = din("w_out", [depth, 1024, 1024])
        self.w_gate = din("ffn_w_gate", [depth, 1024, D_FF])
        self.w_up = din("ffn_w_up", [depth, 1024, D_FF])
        self.w_down = din("ffn_w_down", [depth, D_FF, 1024])
        self.ppack = din("ppack", [depth, 128, NPP])
        self.cpack = din("cpack", [128, NCP])
        self.bpack = din("bpack", [128, NBP], BF16)
        self.rope = din("rope", [4, 128, self.smax])
        self.yT = nc.dram_tensor("yT", [1024, T], F32, kind="ExternalOutput").ap()
        self.scr = {}

        def scr(name, shape, dt=BF16):
            kind = "ExternalOutput" if name in self.debug else "Internal"
            self.scr[name] = nc.dram_tensor(name, shape, dt, kind=kind).ap()
            return self.scr[name]

        scr("xres", [1024, T], F32)
        scr("xmid", [1024, T], F32)
        scr("rqT", [256, T]); scr("rkT", [256, T]); scr("rv", [T, 512]); scr("rgT", [512, T])
        scr("mqnT", [512, T]); scr("mqrT", [256, T]); scr("mknT", [512, T]); scr("mkrT", [32, T])
        scr("mva", [T, 1024])
        scr("dqT", [512, T]); scr("dkT", [512, T]); scr("dv", [T, 512])
        scr("sz", [T, 512]); scr("sxB", [T, 768]); scr("sBT", [256, T]); scr("sCT", [256, T])
        scr("sdt", [T, 16], F32)
        scr("boT", [2048, T])
        self.groups = []
        base = 0
        for si, S_ in enumerate(self.seqs):
            for g in range(S_ // 512):
                self.groups.append(dict(s=si, t0=base + g * 512, pos0=g * 512, S=S_, sbase=base))
            base += S_

    def build(self):
        nc = self.nc
        with ExitStack() as es:
            self.es = es
            self.S = Sched(nc)
            ARENA_WORDS = 51 * 1024
            at = es.enter_context(nc.sbuf_tensor("arena", [128, ARENA_WORDS], F32))
            self.A = Arena(at, ARENA_WORDS)
            self.banks = [es.enter_context(nc.psum_tensor(f"bank{i}", [128, 512], F32)) for i in range(8)]
            A = self.A
            S = self.S
            self.bp = A.alloc([NBP], BF16)
            self.pp = A.alloc([NPP], F32)
            S.dma(self.bp, self.bpack, writes=["bp"])
            A.mark()
            S.barrier()
            order = ["p1", "ret", "mla", "diff", "ssd", "p3", "p4"]
            sa = self.stop_after
            only = getattr(self, "only", None)
            done = False
            for l in range(self.depth):
                S.dma(self.pp, self.ppack[l], writes=["pp"])
                S.barrier()
                for ph in order:
                    run = True
                    if sa not in (None, "all") and l == self.depth - 1:
                        if sa in ("p3", "p4"):
                            run = order.index(ph) <= order.index(sa)
                        else:
                            run = ph in ("p1", sa)
                    if run:
                        {"p1": self.phase1, "ret": self.phase_ret, "mla": self.phase_mla, "diff": self.phase_diff,
                         "ssd": self.phase_ssd, "p3": self.phase3, "p4": self.phase4}[ph](l)
                        S.barrier()
            if sa in (None, "all"):
                self.phase5()
            S.emit(es)
        return nc

    def dbg(self, name, ap, reads):
        if name not in self.debug:
            return
        shp = [int(x) for x in ap.shape]
        t = self.nc.dram_tensor(name, shp, ap.dtype, kind="ExternalOutput").ap()
        self.S.dma(t, ap, reads=reads)

    def cpv(self, name):
        o, w = CP[name]
        return self.cp[:, o:o + w]

    def bpv(self, name):
        o, w = BP[name]
        return self.bp[:, o:o + w]

    def ppv(self, name, i=0, n=1):
        o, w = PP[name]
        return self.pp[:, o + i:o + i + n]

    def bank(self, i, dt=F32):
        b = self.banks[i][:]
        if dt != F32:
            b = b.bitcast(dt)
        return b

    def load_cast(self, name, pieces, stgR):
        S = self.S
        keys = []
        for i, pc in enumerate(pieces):
            dst, src = pc[0], pc[1]
            st, sk = stgR.next()
            shp = list(src.shape)
            n = int(np.prod(shp[1:]))
            stv = st[:, 0:n]
            if len(shp) == 3:
                stv = stv.rearrange("p (a b) -> p a b", a=shp[1])
            S.dma(stv, src, writes=[sk])
            if len(pc) > 2:
                stv = pc[2](stv)
            k = (name, i)
            keys.append(k)
            eng = (ACT, DVE, POOL)[i % 3]
            if eng == ACT:
                S.act(lambda e, dst=dst, stv=stv: e.activation(out=dst, in_=stv, func=AF.Copy), reads=[sk], writes=[k])
            else:
                S.add(eng, lambda e, dst=dst, stv=stv: e.tensor_copy(out=dst, in_=stv), reads=[sk], writes=[k])
        S.add(PE, None, reads=keys, writes=[name], real=False)

    def mmg(self, out, okey, pairs, reads):
        S = self.S
        n = len(pairs)
        for i, (lhsT, rhs) in enumerate(pairs):
            S.pe(lambda e, lhsT=lhsT, rhs=rhs, i=i: e.matmul(out, lhsT=lhsT, rhs=rhs, start=(i == 0), stop=(i == n - 1)),
                 reads=reads, writes=[okey])

    def rstd_from(self, ps_list, inv_n, eps, lnv, rstd, rkey, lkey):
        S = self.S
        for ps, pk, sl in ps_list:
            S.act(lambda e, ps=ps, sl=sl: e.activation(out=lnv[:, sl], in_=ps, func=AF.Ln, scale=inv_n, bias=eps),
                  reads=[pk], writes=[lkey])
        S.act(lambda e: e.activation(out=rstd, in_=lnv, func=AF.Exp, scale=-0.5), reads=[lkey], writes=[rkey])

    def phase1(self, l):
        S, A, nc = self.S, self.A, self.nc
        A.reset()
        sc = self.scr
        x_in = (self.xT if l == 0 else sc["xres"]).rearrange("(c p) t -> p c t", p=128)
        wA = A.alloc([8, NA], BF16)
        wuqn = A.alloc([2, 8, 64], BF16)
        wuqr = A.alloc([2, 8, 32], BF16)
        wukn = A.alloc([8, 64], BF16)
        wuv = A.alloc([8, 64], BF16)
        stgR = Rot([(A.alloc([1536], F32), ("stg", i)) for i in range(2)])
        win = self.w_in[l].rearrange("(c p) n -> p c n", p=128)
        pieces = []
        for k in range(8):
            for hf in range(4):
                pieces.append((wA[:, k, hf * 1260:(hf + 1) * 1260], win[:, k, hf * 1260:(hf + 1) * 1260]))
        uq = self.w_uq[l].rearrange("(c p) n -> p c n", p=128)
        pieces.append((wuqn, uq, lambda v: v.rearrange("p k (h d) -> p k h d", h=8)[:, :, :, 0:64]))
        pieces.append((wuqr, uq, lambda v: v.rearrange("p k (h d) -> p k h d", h=8)[:, :, :, 64:96]))
        pieces.append((wukn, self.w_ukv[l], lambda v: v.rearrange("p (h d) -> p h d", h=8)[:, :, 0:64]))
        pieces.append((wuv, self.w_ukv[l], lambda v: v.rearrange("p (h d) -> p h d", h=8)[:, :, 64:128]))
        self.load_cast("wA", pieces, stgR)

        xs = A.alloc([8, 514], F32)
        hTs = [A.alloc([8, 514], BF16) for _ in range(2)]
        sqR = Rot([(A.alloc([514], BF16), ("sq", i)) for i in range(2)])
        lnv = A.alloc([514], F32)
        rstd = A.alloc([514], F32)
        tab = A.alloc([4, 512], F32)
        tk = "tab"
        xsbR = Rot([(A.alloc([512], BF16), ("xsb", i)) for i in range(3)])
        t1R = Rot([(A.alloc([512], F32), ("t1", i)) for i in range(2)])
        t2R = Rot([(A.alloc([512], F32), ("t2", i)) for i in range(2)])
        roR = Rot([(A.alloc([512], BF16), ("ro", i)) for i in range(3)])
        soR = Rot([(A.alloc([512], BF16), ("so", i)) for i in range(3)])
        GR = Rot([(A.alloc([514], F32), ("G", i)) for i in range(2)])
        cvR = Rot([(A.alloc([512], F32), ("cv", i)) for i in range(1)])
        cqf = A.alloc([2, 512], F32)
        ckvf = A.alloc([512], F32)
        cqn = A.alloc([2, 512], BF16)
        ckvn = A.alloc([512], BF16)
        lnq = lnv[:, 0:512]
        rsq = rstd[:, 0:512]
        tokR = Rot([(A.alloc([512], BF16), ("tok", i)) for i in range(3)])
        vaugs = [A.alloc([8, 128], BF16) for _ in range(2)]
        xTok = A.alloc([4, 768], BF16)
        dtx = A.alloc([4, 16], F32)
        dte = A.alloc([4, 16], F32)
        dts = A.alloc([4, 16], F32)
        for i, va in enumerate(vaugs):
            S.pool(lambda e, va=va: e.memset(va, 1.0), writes=[("vaug", i)])
        mainR = Rot([(self.bank(i), ("ps", i)) for i in range(4)])
        psA = self.bank(4)
        psH = self.bank(5)
        permR = Rot([(self.bank(6), ("ps", 6))])
        psT = self.bank(7, BF16)
        onesb = self.bpv("ones")
        ident = self.bpv("ident")
        ropeT = self.rope.rearrange("a p t -> p a t")
        tcount = [0]

        pend = []

        def flush():
            while pend:
                pend.pop(0)()

        def MM(out, okey, pairs, reads):
            self.mmg(out, okey, pairs, reads)
            flush()

        def rope_unit(ps, pk, M, scale, ci, perm, dst):
            tab, tk = self.cur_tab
            Ct = tab[0:M, ci, :]
            St = tab[0:M, ci + 1, :]
            xsb, k1 = xsbR.next()
            S.act(lambda e: e.activation(out=xsb, in_=ps, func=AF.Copy, scale=scale),
                  reads=[pk], writes=[k1])
            t1, kt1 = t1R.next()
            S.pool(lambda e: e.tensor_tensor(out=t1[0:M, :], in0=xsb[0:M, :], in1=Ct, op=ALU.mult),
                   reads=[k1, tk], writes=[kt1])

            def part_b():
                pp_, pk2 = permR.next()
                S.pe(lambda e: e.matmul(pp_, lhsT=perm, rhs=xsb, start=True, stop=True),
                     reads=[k1, "bp"], writes=[pk2])
                t2, kt2 = t2R.next()
                S.dve(lambda e: e.tensor_tensor(out=t2[0:M, :], in0=pp_[0:M, :], in1=St, op=ALU.mult),
                      reads=[pk2, tk], writes=[kt2])
                ro, kro = roR.next()
                S.dve(lambda e: e.tensor_tensor(out=ro[0:M, :], in0=t1[0:M, :], in1=t2[0:M, :], op=ALU.add),
                      reads=[kt1, kt2], writes=[kro])
                S.dma(dst, ro[0:M, :], reads=[kro])

            pend.append(part_b)

        def load_x(gi_):
            if gi_ >= len(self.groups):
                return
            g_ = self.groups[gi_]
            hasL = g_["pos0"] > 0
            hasR = g_["pos0"] + 512 < g_["S"]
            if not hasL:
                S.pool(lambda e: e.memset(xs[:, :, 0:1], 0.0), writes=["xsh"])
            if not hasR:
                S.pool(lambda e: e.memset(xs[:, :, 513:514], 0.0), writes=["xsh"])
            lo = g_["t0"] - 1 if hasL else g_["t0"]
            hi = g_["t0"] + 513 if hasR else g_["t0"] + 512
            c0 = 0 if hasL else 1
            S.dma(xs[:, :, c0:c0 + (hi - lo)], x_in[:, :, lo:hi], writes=["xs"])

        def load_tab(gi_):
            if gi_ >= len(self.groups):
                return
            p0_ = self.groups[gi_]["pos0"]
            S.dma(tab, ropeT[:, :, p0_:p0_ + 512], writes=[tk])

        def do_rms(gi_):
            if gi_ >= len(self.groups):
                return
            hT_ = hTs[gi_ % 2]
            hk_ = [("hT", gi_ % 2, k) for k in range(8)]
            for k in range(8):
                sq, sqk = sqR.next()
                S.act(lambda e, sq=sq, k=k: e.activation(out=sq, in_=xs[:, k, :], func=AF.Square), reads=["xs", "xsh"], writes=[sqk])
                S.pe(lambda e, sq=sq, k=k: e.matmul(psA, lhsT=onesb, rhs=sq[:, 0:512], start=(k == 0), stop=(k == 7)),
                     reads=[sqk, "bp"], writes=[("ps", 4)])
                S.pe(lambda e, sq=sq, k=k: e.matmul(psH[:, 0:2], lhsT=onesb, rhs=sq[:, 512:514], start=(k == 0), stop=(k == 7)),
                     reads=[sqk, "bp"], writes=[("ps", 5)])
            self.rstd_from([(psA, ("ps", 4), slice(0, 512)), (psH[:, 0:2], ("ps", 5), slice(512, 514))],
                           1.0 / 1024, NORM_EPS, lnv, rstd, "rstd", "lnv")
            for k in range(8):
                S.dve(lambda e, k=k: e.scalar_tensor_tensor(out=hT_[:, k, :], in0=xs[:, k, :], scalar=self.ppv("wn", k),
                                                             in1=rstd, op0=ALU.mult, op1=ALU.mult),
                      reads=["xs", "xsh", "rstd", "pp"], writes=[hk_[k]])
            load_x(gi_ + 1)

        def group_body(gi, g):
            t0, pos0, Sq = g["t0"], g["pos0"], g["S"]
            hT = hTs[gi % 2]
            hk = [("hT", gi % 2, k) for k in range(8)]
            if gi == 0:
                load_x(0)
                load_tab(0)
            self.cur_tab = (tab, tk)
            if gi == 0:
                do_rms(0)

            def fm_mm(c0_, M):
                ps, pk = mainR.next()
                MM(ps[0:M, :], pk, [(wA[:, k, c0_:c0_ + M], hT[:, k, 1:513]) for k in range(8)], hk + ["wA"])
                return ps, pk

            def ret_unit(c0_, j, scale, dname):
                ps, pk = fm_mm(c0_ + j * 128, 128)
                rope_unit(ps, pk, 128, scale, 0, self.bpv("perm64"), sc[dname][j * 128:(j + 1) * 128, t0:t0 + 512])

            for j in range(2):
                ps, pk = fm_mm(C_CQ + j * 128, 128)
                S.act(lambda e, ps=ps, j=j: e.activation(out=cqf[:, j, :], in_=ps, func=AF.Copy), reads=[pk], writes=[("cqf", j)])
                sq, sqk = sqR.next()
                S.act(lambda e, sq=sq, j=j: e.activation(out=sq[:, 0:512], in_=cqf[:, j, :], func=AF.Square),
                      reads=[("cqf", j)], writes=[sqk])
                pend.append(lambda sq=sq, j=j, sqk=sqk: S.pe(
                    lambda e: e.matmul(psA, lhsT=onesb, rhs=sq[:, 0:512], start=(j == 0), stop=(j == 1)),
                    reads=[sqk, "bp"], writes=[("ps", 4)]))
            ret_unit(C_RQ, 0, 1.0, "rqT")
            self.rstd_from([(psA, ("ps", 4), slice(0, 512))], 1.0 / 256, NORM_EPS, lnq, rsq, "rstd", "lnv")
            for j in range(2):
                S.dve(lambda e, j=j: e.scalar_tensor_tensor(out=cqn[:, j, :], in0=cqf[:, j, :], scalar=self.ppv("qnw", j),
                                                             in1=rsq, op0=ALU.mult, op1=ALU.mult),
                      reads=[("cqf", j), "rstd", "pp"], writes=[("cqn", j)])
            cqk = [("cqn", 0), ("cqn", 1)]
            ps, pk = fm_mm(C_CKV, 128)
            S.act(lambda e, ps=ps: e.activation(out=ckvf, in_=ps, func=AF.Copy), reads=[pk], writes=["ckvf"])
            sq, sqk = sqR.next()
            S.act(lambda e, sq=sq: e.activation(out=sq[:, 0:512], in_=ckvf, func=AF.Square), reads=["ckvf"], writes=[sqk])
            pend.append(lambda sq=sq, sqk=sqk: S.pe(lambda e: e.matmul(psA, lhsT=onesb, rhs=sq[:, 0:512], start=True, stop=True),
                                                    reads=[sqk, "bp"], writes=[("ps", 4)]))
            ret_unit(C_RQ, 1, 1.0, "rqT")
            self.rstd_from([(psA, ("ps", 4), slice(0, 512))], 1.0 / 128, NORM_EPS, lnq, rsq, "rstd", "lnv")
            S.dve(lambda e: e.scalar_tensor_tensor(out=ckvn, in0=ckvf, scalar=self.ppv("kvnw", 0), in1=rsq,
                                                   op0=ALU.mult, op1=ALU.mult),
                  reads=["ckvf", "rstd", "pp"], writes=["ckvn"])
            ret_unit(C_RK, 0, 0.125, "rkT")
            ret_unit(C_RK, 1, 0.125, "rkT")
            for i in range(4):
                ps, pk = mainR.next()
                MM(ps, pk, [(wuqn[:, k, 2 * i:2 * i + 2, :].rearrange("p a b -> p (a b)"), cqn[:, k, :]) for k in range(2)], cqk + ["wA"])
                so, sk = soR.next()
                S.act(lambda e, ps=ps, so=so: e.activation(out=so, in_=ps, func=AF.Copy), reads=[pk], writes=[sk])
                S.dma(sc["mqnT"][i * 128:(i + 1) * 128, t0:t0 + 512], so, reads=[sk])
            for i in range(2):
                ps, pk = mainR.next()
                MM(ps, pk, [(wuqr[:, k, 4 * i:4 * i + 4, :].rearrange("p a b -> p (a b)"), cqn[:, k, :]) for k in range(2)], cqk + ["wA"])
                rope_unit(ps, pk, 128, 1.0, 2, self.bpv("perm32"), sc["mqrT"][i * 128:(i + 1) * 128, t0:t0 + 512])
            for i in range(4):
                ps, pk = mainR.next()
                MM(ps, pk, [(wukn[:, 2 * i:2 * i + 2, :].rearrange("p a b -> p (a b)"), ckvn)], ["ckvn", "wA"])
                so, sk = soR.next()
                S.act(lambda e, ps=ps, so=so: e.activation(out=so, in_=ps, func=AF.Copy), reads=[pk], writes=[sk])
                S.dma(sc["mknT"][i * 128:(i + 1) * 128, t0:t0 + 512], so, reads=[sk])
            for tt in range(4):
                ps, pk = mainR.next()
                MM(ps, pk, [(ckvn[:, tt * 128:(tt + 1) * 128], wuv.rearrange("p a b -> p (a b)"))], ["ckvn", "wA"])
                va = vaugs[tt % 2]
                vk = ("vaug", tt % 2)
                S.act(lambda e, ps=ps, va=va: e.activation(out=va[:, :, 0:64], in_=ps.rearrange("p (h d) -> p h d", h=8),
                                                           func=AF.Copy), reads=[pk], writes=[vk])
                S.dma(sc["mva"][t0 + tt * 128:t0 + (tt + 1) * 128, :], va.rearrange("p h d -> p (h d)"), reads=[vk])
            ps, pk = fm_mm(C_KR, 128)
            rope_unit(ps, pk, 32, 1.0, 2, self.bpv("perm32"), sc["mkrT"][0:32, t0:t0 + 512])
            for tt in range(4):
                self.mmg(psH[:, 32 + tt * 16:48 + tt * 16], ("ps", 5),
                         [(hT[:, k, 1 + tt * 128:1 + (tt + 1) * 128], wA[:, k, C_DT:C_DT + 16]) for k in range(8)], hk + ["wA"])
            o_, w_ = PP["dtb"]
            dtb_b = self.pp[:, o_:o_ + 16].unsqueeze(1).broadcast_to([128, 4, 16])
            S.dve(lambda e: e.tensor_tensor(out=dtx, in0=psH[:, 32:96].rearrange("p (a b) -> p a b", a=4), in1=dtb_b, op=ALU.add),
                  reads=[("ps", 5), "pp"], writes=["dtx"])
            S.act(lambda e: e.activation(out=dte, in_=dtx, func=AF.Exp), reads=["dtx"], writes=["dte"])
            S.act(lambda e: e.activation(out=dts, in_=dte, func=AF.Ln, bias=1.0), reads=["dte"], writes=["dts"])
            S.dma(sc["sdt"][t0:t0 + 512, :].rearrange("(a p) c -> p a c", p=128), dts, reads=["dts"])
            for (c0_, nch, scale, dname) in ((C_DQ, 4, 1.0, "dqT"), (C_DK, 4, 1.0, "dkT")):
                for j in range(nch):
                    ret_unit(c0_, j, scale, dname)
            flush()
            load_tab(gi + 1)
            do_rms(gi + 1)
            for (c0_, dname) in ((C_RV, "rv"), (C_DV, "dv")):
                for tt in range(4):
                    ps, pk = mainR.next()
                    MM(ps, pk, [(hT[:, k, 1 + tt * 128:1 + (tt + 1) * 128], wA[:, k, c0_:c0_ + 512]) for k in range(8)], hk + ["wA"])
                    tk_, tkk = tokR.next()
                    S.act(lambda e, ps=ps, tk_=tk_: e.activation(out=tk_, in_=ps, func=AF.Copy), reads=[pk], writes=[tkk])
                    S.dma(sc[dname][t0 + tt * 128:t0 + (tt + 1) * 128, :], tk_, reads=[tkk])
            for j in range(4):
                ps, pk = fm_mm(C_RG + j * 128, 128)
                so, sk = soR.next()
                S.act(lambda e, ps=ps, so=so: e.activation(out=so, in_=ps, func=AF.Silu), reads=[pk], writes=[sk])
                S.dma(sc["rgT"][j * 128:(j + 1) * 128, t0:t0 + 512], so, reads=[sk])
            for tt in range(4):
                ps, pk = mainR.next()
                MM(ps, pk, [(hT[:, k, 1 + tt * 128:1 + (tt + 1) * 128], wA[:, k, C_SZ:C_SZ + 512]) for k in range(8)], hk + ["wA"])
                tk_, tkk = tokR.next()
                S.act(lambda e, ps=ps, tk_=tk_: e.activation(out=tk_, in_=ps, func=AF.Silu), reads=[pk], writes=[tkk])
                S.dma(sc["sz"][t0 + tt * 128:t0 + (tt + 1) * 128, :], tk_, reads=[tkk])
            for j in range(8):
                cc = C_XBC + j * 128
                ps, pk = fm_mm(cc, 128)
                hcol = 2 + 2 * j
                hkey = ("ps", 5)
                for k in range(8):
                    S.pe(lambda e, k=k, cc=cc, hcol=hcol: e.matmul(psH[:, hcol:hcol + 2], lhsT=wA[:, k, cc:cc + 128],
                                                                    rhs=hT[:, k, 0:514:513], start=(k == 0), stop=(k == 7)),
                         reads=hk + ["wA"], writes=[hkey])
                G, gk = GR.next()
                S.act(lambda e, ps=ps, G=G: e.activation(out=G[:, 1:513], in_=ps, func=AF.Copy), reads=[pk], writes=[gk])
                S.dve(lambda e, G=G, hcol=hcol: e.tensor_copy(out=G[:, 0:514:513], in_=psH[:, hcol:hcol + 2]), reads=[hkey], writes=[(gk, "h")])
                cv, ck = cvR.next()
                o_w = PP["scw"][0]
                o_b = PP["scb"][0]
                S.dve(lambda e, G=G, cv=cv, j=j: e.tensor_scalar(out=cv, in0=G[:, 1:513], scalar1=self.pp[:, o_w + 8 + j:o_w + 9 + j],
                                                                  scalar2=self.pp[:, o_b + j:o_b + j + 1], op0=ALU.mult, op1=ALU.add),
                      reads=[gk, "pp"], writes=[ck])
                S.dve(lambda e, G=G, cv=cv, j=j: e.scalar_tensor_tensor(out=cv, in0=G[:, 0:512], scalar=self.pp[:, o_w + j:o_w + j + 1],
                                                                         in1=cv, op0=ALU.mult, op1=ALU.add),
                      reads=[gk, (gk, "h"), "pp", ck], writes=[ck])
                S.dve(lambda e, G=G, cv=cv, j=j: e.scalar_tensor_tensor(out=cv, in0=G[:, 2:514], scalar=self.pp[:, o_w + 16 + j:o_w + 17 + j],
                                                                         in1=cv, op0=ALU.mult, op1=ALU.add),
                      reads=[gk, (gk, "h"), "pp", ck], writes=[ck])
                so, sk = soR.next()
                S.act(lambda e, cv=cv, so=so: e.activation(out=so, in_=cv, func=AF.Silu), reads=[ck], writes=[sk])
                if j >= 4:
                    dname, r0 = ("sBT", (j - 4) * 128) if j < 6 else ("sCT", (j - 6) * 128)
                    S.dma(sc[dname][r0:r0 + 128, t0:t0 + 512], so, reads=[sk])
                if j < 6:
                    def tr_part(so=so, sk=sk, j=j):
                        half = tcount[0] % 2
                        tcount[0] += 1
                        pT = psT[:, half * 512:(half + 1) * 512]
                        ptk = ("ps", 7)
                        for tt in range(4):
                            S.pe(lambda e, tt=tt: e.transpose(out=pT[:, tt * 128:(tt + 1) * 128], in_=so[:, tt * 128:(tt + 1) * 128],
                                                              identity=ident), reads=[sk, "bp"], writes=[ptk])
                        S.dve(lambda e: e.tensor_copy(out=xTok[:, :, j * 128:(j + 1) * 128],
                                                      in_=pT.rearrange("p (a b) -> p a b", a=4)),
                              reads=[ptk], writes=[("xTok", j)])
                    pend.append(tr_part)
            flush()
            S.dma(sc["sxB"][t0:t0 + 512, :].rearrange("(a p) c -> p a c", p=128), xTok, reads=[("xTok", j) for j in range(6)])

        for gi, g in enumerate(self.groups):
            group_body(gi, g)


def common_inputs(inp, depth, smax):
    f = lambda n: np.ascontiguousarray(np.asarray(inp[n], np.float32))
    m = {
        "w_in": f("w_in"), "mla_w_uq": f("mla_w_uq"), "mla_w_ukv": f("mla_w_ukv"), "w_branch": f("w_branch"),
        "w_out": f("w_out"), "ffn_w_gate": f("ffn_w_gate"), "ffn_w_up": f("ffn_w_up"), "ffn_w_down": f("ffn_w_down"),
        "ppack": np.stack([make_ppack(inp, l) for l in range(depth)]),
        "cpack": make_cpack(), "bpack": make_bpack(), "rope": make_rope(smax),
    }
    return m


def _seq_list(mk):
    out = []
    base = 0
    for S_ in mk.seqs:
        out.append((base, S_))
        base += S_
    return out


def phase_mla(self, l):
    S, A = self.S, self.A
    A.reset()
    sc = self.scr
    smax = self.smax
    NKCm = smax // 128
    Vaug = A.alloc([NKCm, 8, 128], BF16)
    KTs = [A.alloc([smax], BF16) for _ in range(2)]
    QTs = [A.alloc([smax], BF16) for _ in range(2)]
    pTR = Rot([(A.alloc([512], BF16), ("pT", i)) for i in range(4)])
    rsR = Rot([(A.alloc([512], F32), ("rs", i)) for i in range(2)])
    obR = Rot([(A.alloc([512], BF16), ("ob", i)) for i in range(2)])
    scR = Rot([(self.bank(i), ("ps", i)) for i in range(4)])
    accR = Rot([(self.bank(4 + i), ("ps", 4 + i)) for i in range(3)])
    scale = 96.0 ** -0.5
    LA = 2
    for (base, Sq) in _seq_list(self):
        NKC = Sq // 128
        NQG = Sq // 512
        for c0 in range(0, NKC, 4):
            S.dma(Vaug[:, c0:c0 + 4, :, :].rearrange("p c h d -> p c (h d)"),
                  sc["mva"][base + c0 * 128:base + (c0 + 4) * 128, :].rearrange("(c p) e -> p c e", p=128),
                  writes=[("V", c0)])
        vkeys = [("V", c0) for c0 in range(0, NKC, 4)]

        def load_head(h, base=base, Sq=Sq):
            KT, QT = KTs[h % 2], QTs[h % 2]
            S.dma(KT[0:64, 0:Sq], sc["mknT"][h * 64:(h + 1) * 64, base:base + Sq], writes=[("KT", h % 2, 0)])
            S.dma(KT[64:96, 0:Sq], sc["mkrT"][0:32, base:base + Sq], writes=[("KT", h % 2, 1)])
            S.dma(QT[0:64, 0:Sq], sc["mqnT"][h * 64:(h + 1) * 64, base:base + Sq], writes=[("QT", h % 2, 0)])
            S.dma(QT[64:96, 0:Sq], sc["mqrT"][h * 32:(h + 1) * 32, base:base + Sq], writes=[("QT", h % 2, 1)])

        items = [(h, qg, kc) for h in range(8) for qg in range(NQG) for kc in range(NKC)]
        state = {}

        def do_S(it):
            h, qg, kc = it
            if qg == 0 and kc == 0:
                if h == 0:
                    load_head(0)
                if h + 1 < 8:
                    load_head(h + 1)
            KT, QT = KTs[h % 2], QTs[h % 2]
            ps, pk = scR.next()
            state[it] = (ps, pk)
            S.pe(lambda e: e.matmul(ps, lhsT=KT[0:96, kc * 128:(kc + 1) * 128], rhs=QT[0:96, qg * 512:(qg + 1) * 512],
                                    start=True, stop=True),
                 reads=[("KT", h % 2, 0), ("KT", h % 2, 1), ("QT", h % 2, 0), ("QT", h % 2, 1)], writes=[pk])
            pT, ptk = pTR.next()
            S.act(lambda e: e.activation(out=pT, in_=ps, func=AF.Exp, scale=scale), reads=[pk], writes=[ptk])
            state[it] = (pT, ptk)

        def do_PV(it, base=base):
            h, qg, kc = it
            pT, ptk = state.pop(it)
            if kc == 0:
                state[("acc", h, qg)] = accR.next()
            acc, ak = state[("acc", h, qg)]
            S.pe(lambda e: e.matmul(acc, lhsT=Vaug[:, kc, h, :], rhs=pT, start=(kc == 0), stop=(kc == NKC - 1)),
                 reads=[ptk] + vkeys, writes=[ak])
            if kc == NKC - 1:
                rs, rk = rsR.next()
                S.dve(lambda e: e.reciprocal(out=rs[0:64, :], in_=acc[64:128, :]), reads=[ak], writes=[rk])
                ob, ok = obR.next()
                S.dve(lambda e: e.tensor_tensor(out=ob[0:64, :], in0=acc[0:64, :], in1=rs[0:64, :], op=ALU.mult),
                      reads=[ak, rk], writes=[ok])
                t0 = base + qg * 512
                S.dma(sc["boT"][512 + h * 64:512 + (h + 1) * 64, t0:t0 + 512], ob[0:64, :], reads=[ok])
                del state[("acc", h, qg)]

        n = len(items)
        for i in range(n + LA):
            if i < n:
                do_S(items[i])
            if i - LA >= 0:
                do_PV(items[i - LA])


def phase_diff(self, l):
    S, A = self.S, self.A
    A.reset()
    sc = self.scr
    smax = self.smax
    NKCm = smax // 128
    lam_init = 0.8 - 0.6 * math.exp(-0.3 * l)
    V = A.alloc([NKCm, 512], BF16)
    KTz = [[A.alloc([smax], BF16) for _ in range(2)] for _ in range(2)]
    QTs = [A.alloc([smax], BF16) for _ in range(2)]
    for b_ in range(2):
        S.pool(lambda e, b_=b_: e.memset(KTz[b_][0][64:128, :], 0.0), writes=[("KT", b_, 0)])
        S.pool(lambda e, b_=b_: e.memset(KTz[b_][1][0:64, :], 0.0), writes=[("KT", b_, 1)])
    pTR = Rot([(A.alloc([512], BF16), ("pT", i)) for i in range(4)])
    rR = Rot([(A.alloc([512], F32), ("r", i)) for i in range(2)])
    tR = [A.alloc([512], F32) for _ in range(2)]
    dbuf = A.alloc([512], F32)
    sqb = A.alloc([512], BF16)
    lnb = A.alloc([512], F32)
    rsb = A.alloc([512], F32)
    obR = Rot([(A.alloc([512], BF16), ("ob", i)) for i in range(2)])
    junk = A.alloc([64], F32)
    sv = A.alloc([8], F32)
    scR = Rot([(self.bank(i), ("ps", i)) for i in range(3)])
    accO = [self.bank(3), self.bank(5)]
    accS = [self.bank(4), self.bank(6)]
    psN = self.bank(7)
    onesb = self.bpv("ones")
    o_, _w = PP["dlam"]
    dl = self.pp[:, o_:o_ + 256]
    for i in range(2):
        S.dve(lambda e, i=i: e.scalar_tensor_tensor(out=junk, in0=dl[:, i * 128:i * 128 + 64], scalar=1.0,
                                                    in1=dl[:, i * 128 + 64:i * 128 + 128], op0=ALU.mult, op1=ALU.mult,
                                                    accum_out=sv[:, i:i + 1]), reads=["pp", "junk"], writes=["junk", ("sv", i)])
    S.act(lambda e: e.activation(out=sv[:, 2:4], in_=sv[:, 0:2], func=AF.Exp), reads=[("sv", 0), ("sv", 1)], writes=["sve"])
    S.dve(lambda e: e.tensor_tensor(out=sv[:, 4:5], in0=sv[:, 3:4], in1=sv[:, 2:3], op=ALU.subtract), reads=["sve"], writes=["nl0"])
    S.dve(lambda e: e.tensor_scalar(out=sv[:, 5:6], in0=sv[:, 4:5], scalar1=-lam_init, scalar2=None, op0=ALU.add),
          reads=["nl0"], writes=["neglam"])
    S.dve(lambda e: e.tensor_scalar(out=sv[:, 6:7], in0=self.ppv("dnw", 0), scalar1=(1.0 - lam_init), scalar2=None, op0=ALU.mult),
          reads=["pp"], writes=["wd"])
    neglam = sv[:, 5:6]
    wd = sv[:, 6:7]
    LA = 2
    for (base, Sq) in _seq_list(self):
        NKC = Sq // 128
        NQG = Sq // 512
        for c0 in range(0, NKC, 8):
            S.dma(V[:, c0:c0 + 8, :], sc["dv"][base + c0 * 128:base + (c0 + 8) * 128, :].rearrange("(c p) e -> p c e", p=128),
                  writes=[("V", c0)])
        vkeys = [("V", c0) for c0 in range(0, NKC, 8)]

        def load_head(h, base=base, Sq=Sq):
            S.dma(KTz[h % 2][0][0:64, 0:Sq], sc["dkT"][h * 128:h * 128 + 64, base:base + Sq], writes=[("KT", h % 2, 0)])
            S.dma(KTz[h % 2][1][64:128, 0:Sq], sc["dkT"][h * 128 + 64:(h + 1) * 128, base:base + Sq], writes=[("KT", h % 2, 1)])
            S.dma(QTs[h % 2][:, 0:Sq], sc["dqT"][h * 128:(h + 1) * 128, base:base + Sq], writes=[("QT", h % 2)])

        items = [(h, qg, m, kc) for h in range(4) for qg in range(NQG) for m in range(2) for kc in range(NKC)]
        state = {}

        def do_S(it):
            h, qg, m, kc = it
            if qg == 0 and kc == 0 and m == 0:
                if h == 0:
                    load_head(0)
                if h + 1 < 4:
                    load_head(h + 1)
            KT, QT = KTz[h % 2][m], QTs[h % 2]
            ps, pk = scR.next()
            S.pe(lambda e: e.matmul(ps, lhsT=KT[:, kc * 128:(kc + 1) * 128],
                                    rhs=QT[:, qg * 512:(qg + 1) * 512], start=True, stop=True),
                 reads=[("KT", h % 2, m), ("QT", h % 2)], writes=[pk])
            pT, ptk = pTR.next()
            S.act(lambda e: e.activation(out=pT, in_=ps, func=AF.Exp, scale=0.125), reads=[pk], writes=[ptk])
            state[it] = (pT, ptk)

        def do_PV(it, base=base):
            h, qg, m, kc = it
            pT, ptk = state.pop(it)
            aO, aS = accO[m], accS[m]
            kO, kS = ("ps", 3 + 2 * m), ("ps", 4 + 2 * m)
            S.pe(lambda e: e.matmul(aO, lhsT=V[:, kc, h * 128:(h + 1) * 128], rhs=pT, start=(kc == 0), stop=(kc == NKC - 1)),
                 reads=[ptk] + vkeys, writes=[kO])
            S.pe(lambda e: e.matmul(aS, lhsT=onesb, rhs=pT, start=(kc == 0), stop=(kc == NKC - 1)),
                 reads=[ptk, "bp"], writes=[kS])
            if kc == NKC - 1:
                r, rk = rR.next()
                S.dve(lambda e: e.reciprocal(out=r, in_=aS), reads=[kS], writes=[rk])
                t = tR[m]
                S.dve(lambda e: e.tensor_tensor(out=t, in0=aO, in1=r, op=ALU.mult), reads=[kO, rk], writes=[("t", m)])
                if m == 1:
                    S.dve(lambda e: e.scalar_tensor_tensor(out=dbuf, in0=tR[1], scalar=neglam, in1=tR[0], op0=ALU.mult, op1=ALU.add),
                          reads=[("t", 0), ("t", 1), "neglam"], writes=["dbuf"])
                    S.dve(lambda e: e.tensor_tensor(out=sqb, in0=dbuf, in1=dbuf, op=ALU.mult), reads=["dbuf"], writes=["sqb"])
                    S.pe(lambda e: e.matmul(psN, lhsT=onesb, rhs=sqb, start=True, stop=True), reads=["sqb", "bp"], writes=[("ps", 7)])
                    self.rstd_from([(psN, ("ps", 7), slice(0, 512))], 1.0 / 128, HEAD_EPS, lnb, rsb, "rsb", "lnb")
                    ob, ok = obR.next()
                    S.dve(lambda e: e.scalar_tensor_tensor(out=ob, in0=dbuf, scalar=wd, in1=rsb, op0=ALU.mult, op1=ALU.mult),
                          reads=["dbuf", "rsb", "wd"], writes=[ok])
                    t0 = base + qg * 512
                    S.dma(sc["boT"][1024 + h * 128:1024 + (h + 1) * 128, t0:t0 + 512], ob, reads=[ok])

        n = len(items)
        for i in range(n + LA):
            if i < n:
                do_S(items[i])
            if i - LA >= 0:
                do_PV(items[i - LA])


MK.phase_mla = phase_mla
MK.phase_diff = phase_diff


def phase_ret(self, l):
    S, A = self.S, self.A
    A.reset()
    self.load_cp()
    sc = self.scr
    smax = self.smax
    Nm = smax // 128
    qT = A.alloc([2, smax], BF16)
    kT = A.alloc([2, smax], BF16)
    vR = Rot([(A.alloc([4, 512], BF16), ("v", i)) for i in range(3)])
    Sfb = A.alloc([Nm, 512], BF16)
    Sbb = A.alloc([Nm, 512], BF16)
    kvbs = A.alloc([Nm, 512], BF16)
    Sf = A.alloc([512], F32)
    Sb = A.alloc([512], F32)
    MT = A.alloc([4, 128], F32)
    e1t = A.alloc([128], F32)
    Qdf = A.alloc([2, 128], F32)
    Qdb = A.alloc([2, 128], F32)
    Kdf = A.alloc([4, 64], F32)
    Kdb = A.alloc([4, 64], F32)
    Gcf = A.alloc([4, 128], F32)
    Gcb = A.alloc([4, 128], F32)
    kdfR = Rot([(A.alloc([256], BF16), ("kdf", i)) for i in range(2)])
    kdbR = Rot([(A.alloc([256], BF16), ("kdb", i)) for i in range(2)])
    qdR = Rot([(A.alloc([512], BF16), ("qd", i)) for i in range(4)])
    aTR = Rot([(A.alloc([512], BF16), ("aT", i)) for i in range(2)])
    gR = Rot([(A.alloc([512], BF16), ("g", i)) for i in range(2)])
    sqb = A.alloc([512], BF16)
    lnb = A.alloc([512], F32)
    rsb = A.alloc([512], F32)
    tb = A.alloc([512], F32)
    obR = Rot([(A.alloc([512], BF16), ("ob", i)) for i in range(2)])
    ident = self.bpv("ident")
    onesb = self.bpv("ones")
    o_lg = PP["lgd"][0]

    def lg(d, h, p0=0, p1=128):
        return self.pp[p0:p1, o_lg + d * 4 + h:o_lg + d * 4 + h + 1]

    for h in range(4):
        S.act(lambda e, h=h: e.activation(out=MT[:, h, :], in_=self.cpv("Dpos"), func=AF.Exp, scale=lg(0, h)), reads=["pp"], writes=[("MT", h)])
        S.act(lambda e, h=h: e.activation(out=e1t, in_=self.cpv("Dneg"), func=AF.Exp, scale=lg(1, h)), reads=["pp"], writes=["e1t"])
        S.dve(lambda e, h=h: e.tensor_tensor(out=MT[:, h, :], in0=MT[:, h, :], in1=self.cpv("Mgt"), op=ALU.mult), reads=[("MT", h)], writes=[("MT", h)])
        S.dve(lambda e, h=h: e.tensor_tensor(out=e1t, in0=e1t, in1=self.cpv("Mlt"), op=ALU.mult), reads=["e1t"], writes=["e1t"])
        S.dve(lambda e, h=h: e.tensor_tensor(out=MT[:, h, :], in0=MT[:, h, :], in1=e1t, op=ALU.add), reads=[("MT", h), "e1t"], writes=[("MT", h)])
        S.dve(lambda e, h=h: e.tensor_tensor(out=MT[:, h, :], in0=MT[:, h, :], in1=self.cpv("I2"), op=ALU.add), reads=[("MT", h)], writes=[("MT", h)])
        c, r0 = h // 2, (h % 2) * 64
        S.act(lambda e, h=h, c=c, r0=r0: e.activation(out=Qdf[r0:r0 + 64, c, :], in_=self.cpv("Ip1")[r0:r0 + 64, :], func=AF.Exp,
                                                       scale=lg(0, h, r0, r0 + 64)), reads=["pp"], writes=[("Qd", h)])
        S.act(lambda e, h=h, c=c, r0=r0: e.activation(out=Qdb[r0:r0 + 64, c, :], in_=self.cpv("Cmi")[r0:r0 + 64, :], func=AF.Exp,
                                                       scale=lg(1, h, r0, r0 + 64)), reads=["pp"], writes=[("Qd", h)])
        S.act(lambda e, h=h: e.activation(out=Kdf[:, h, :], in_=self.cpv("Cm1j"), func=AF.Exp, scale=lg(0, h)), reads=["pp"], writes=[("Kd", h)])
        S.act(lambda e, h=h: e.activation(out=Kdb[:, h, :], in_=self.cpv("Jj"), func=AF.Exp, scale=lg(1, h)), reads=["pp"], writes=[("Kd", h)])
        S.act(lambda e, h=h: e.activation(out=Gcf[0:64, h, :], in_=self.cpv("C128")[0:64, :], func=AF.Exp, scale=lg(0, h, 0, 64)), reads=["pp"], writes=[("Gc", h)])
        S.act(lambda e, h=h: e.activation(out=Gcb[0:64, h, :], in_=self.cpv("C128")[0:64, :], func=AF.Exp, scale=lg(1, h, 0, 64)), reads=["pp"], writes=[("Gc", h)])
    tabkeys = [("MT", h) for h in range(4)] + [("Qd", h) for h in range(4)] + [("Kd", h) for h in range(4)] + [("Gc", h) for h in range(4)]
    psT = self.bank(0, BF16)
    kvfB, kvbB = self.bank(1), self.bank(2)
    Gcf2 = Gcf.rearrange("p a b -> p (a b)")
    Gcb2 = Gcb.rearrange("p a b -> p (a b)")
    Kdf2 = Kdf.rearrange("p a b -> p (a b)")
    Kdb2 = Kdb.rearrange("p a b -> p (a b)")
    kd_keys = [("Kd", h) for h in range(4)]
    gc_keys = [("Gc", h) for h in range(4)]

    for (base, Sq) in _seq_list(self):
        N = Sq // 128
        S.dma(qT[:, :, 0:Sq], sc["rqT"][:, base:base + Sq].rearrange("(c p) t -> p c t", p=128), writes=["qT"])
        S.dma(kT[:, :, 0:Sq], sc["rkT"][:, base:base + Sq].rearrange("(c p) t -> p c t", p=128), writes=["kT"])

        def load_v(G, base=base):
            vt, vk = vR.next()
            S.dma(vt, sc["rv"][base + G * 512:base + (G + 1) * 512, :].rearrange("(c p) e -> p c e", p=128), writes=[vk])
            return vt, vk
        S.pool(lambda e: e.memset(Sf[0:64, :], 0.0), writes=["Sf"])
        S.pool(lambda e: e.memset(Sb[0:64, :], 0.0), writes=["Sb"])

        def pass1(n, vt, vk):
            vkeys = [vk]
            half = n % 2
            pT_ = psT[:, half * 512:half * 512 + 256]
            ptk = ("ps", 0)
            for c in range(2):
                S.pe(lambda e, c=c: e.transpose(out=pT_[:, c * 128:(c + 1) * 128], in_=kT[:, c, n * 128:(n + 1) * 128], identity=ident),
                     reads=["kT", "bp"], writes=[ptk])
            kdf, kfk = kdfR.next()
            kdb, kbk = kdbR.next()
            S.dve(lambda e: e.tensor_tensor(out=kdf, in0=pT_, in1=Kdf2, op=ALU.mult), reads=[ptk] + kd_keys, writes=[kfk])
            S.dve(lambda e: e.tensor_tensor(out=kdb, in0=pT_, in1=Kdb2, op=ALU.mult), reads=[ptk] + kd_keys, writes=[kbk])
            for h in range(4):
                S.pe(lambda e, h=h: e.matmul(kvfB[0:64, h * 128:(h + 1) * 128], lhsT=kdf[:, h * 64:(h + 1) * 64],
                                             rhs=vt[:, n % 4, h * 128:(h + 1) * 128], start=True, stop=True), reads=[kfk] + vkeys, writes=[("ps", 1)])
            for h in range(4):
                S.pe(lambda e, h=h: e.matmul(kvbB[0:64, h * 128:(h + 1) * 128], lhsT=kdb[:, h * 64:(h + 1) * 64],
                                             rhs=vt[:, n % 4, h * 128:(h + 1) * 128], start=True, stop=True), reads=[kbk] + vkeys, writes=[("ps", 2)])
            S.act(lambda e: e.activation(out=kvbs[0:64, n, :], in_=kvbB[0:64, :], func=AF.Copy), reads=[("ps", 2)], writes=[("kvbs", n)])
            S.pool(lambda e: e.tensor_copy(out=Sfb[0:64, n, :], in_=Sf[0:64, :]), reads=["Sf"], writes=[("Sfb", n)])
            S.pool(lambda e: e.tensor_tensor(out=Sf[0:64, :], in0=Sf[0:64, :], in1=Gcf2[0:64, :], op=ALU.mult), reads=["Sf"] + gc_keys, writes=["Sf"])
            S.dve(lambda e: e.tensor_tensor(out=Sf[0:64, :], in0=Sf[0:64, :], in1=kvfB[0:64, :], op=ALU.add), reads=["Sf", ("ps", 1)], writes=["Sf"])

        for G in range(N // 4):
            vt, vk = load_v(G)
            for ci in range(4):
                pass1(G * 4 + ci, vt, vk)

        def pass1b(n):
            S.pool(lambda e: e.tensor_copy(out=Sbb[0:64, n, :], in_=Sb[0:64, :]), reads=["Sb"], writes=[("Sbb", n)])
            S.pool(lambda e: e.tensor_tensor(out=Sb[0:64, :], in0=Sb[0:64, :], in1=Gcb2[0:64, :], op=ALU.mult), reads=["Sb"] + gc_keys, writes=["Sb"])
            S.pool(lambda e: e.tensor_tensor(out=Sb[0:64, :], in0=Sb[0:64, :], in1=kvbs[0:64, n, :], op=ALU.add), reads=["Sb", ("kvbs", n)], writes=["Sb"])


        scR = Rot([(self.bank(i), ("ps", i)) for i in (0, 1)])
        yR = Rot([(self.bank(i), ("ps", i)) for i in (2, 3, 4)])
        psN = self.bank(5)

        def pass2(G, h, vt, vk, base=base):
            vkeys = [vk]
            c, r0 = h // 2, (h % 2) * 64
            t0 = base + G * 512
            g, gk = gR.next()
            S.dma(g, sc["rgT"][h * 128:(h + 1) * 128, t0:t0 + 512], writes=[gk])
            qdf, qfk = qdR.next()
            qdb, qbk = qdR.next()
            qv = qT[r0:r0 + 64, c, G * 512:(G + 1) * 512].rearrange("p (a b) -> p a b", a=4)
            S.dve(lambda e: e.tensor_tensor(out=qdf[0:64, :].rearrange("p (a b) -> p a b", a=4), in0=qv,
                                            in1=Qdf[r0:r0 + 64, c, :].unsqueeze(1).broadcast_to([64, 4, 128]), op=ALU.mult),
                  reads=["qT", ("Qd", h)], writes=[qfk])
            S.dve(lambda e: e.tensor_tensor(out=qdb[0:64, :].rearrange("p (a b) -> p a b", a=4), in0=qv,
                                            in1=Qdb[r0:r0 + 64, c, :].unsqueeze(1).broadcast_to([64, 4, 128]), op=ALU.mult),
                  reads=["qT", ("Qd", h)], writes=[qbk])
            ps, pk = scR.next()
            for ci in range(4):
                n = G * 4 + ci
                S.pe(lambda e, ci=ci, n=n: e.matmul(ps[:, ci * 128:(ci + 1) * 128], lhsT=kT[r0:r0 + 64, c, n * 128:(n + 1) * 128],
                                                     rhs=qT[r0:r0 + 64, c, n * 128:(n + 1) * 128], start=True, stop=True),
                     reads=["kT", "qT"], writes=[pk])
            aT, ak = aTR.next()
            S.dve(lambda e: e.tensor_tensor(out=aT.rearrange("p (a b) -> p a b", a=4), in0=ps.rearrange("p (a b) -> p a b", a=4),
                                            in1=MT[:, h, :].unsqueeze(1).broadcast_to([128, 4, 128]), op=ALU.mult),
                  reads=[pk, ("MT", h)], writes=[ak])
            y, yk = yR.next()
            for ci in range(4):
                n = G * 4 + ci
                ysl = y[:, ci * 128:(ci + 1) * 128]
                S.pe(lambda e, ci=ci, n=n, ysl=ysl: e.matmul(ysl, lhsT=vt[:, ci, h * 128:(h + 1) * 128], rhs=aT[:, ci * 128:(ci + 1) * 128],
                                                              start=True, stop=False), reads=[ak] + vkeys, writes=[yk])
                S.pe(lambda e, ci=ci, n=n, ysl=ysl: e.matmul(ysl, lhsT=Sfb[0:64, n, h * 128:(h + 1) * 128], rhs=qdf[0:64, ci * 128:(ci + 1) * 128],
                                                              start=False, stop=False), reads=[qfk, ("Sfb", n)], writes=[yk])
                S.pe(lambda e, ci=ci, n=n, ysl=ysl: e.matmul(ysl, lhsT=Sbb[0:64, n, h * 128:(h + 1) * 128], rhs=qdb[0:64, ci * 128:(ci + 1) * 128],
                                                              start=False, stop=True), reads=[qbk, ("Sbb", n)], writes=[yk])
            def epi():
                S.act(lambda e: e.activation(out=sqb, in_=y, func=AF.Square), reads=[yk], writes=["sqb"])
                S.pe(lambda e: e.matmul(psN, lhsT=onesb, rhs=sqb, start=True, stop=True), reads=["sqb", "bp"], writes=[("ps", 5)])
                self.rstd_from([(psN, ("ps", 5), slice(0, 512))], 1.0 / 128, HEAD_EPS, lnb, rsb, "rsb", "lnb")
                S.dve(lambda e: e.tensor_tensor(out=tb, in0=y, in1=rsb, op=ALU.mult), reads=[yk, "rsb"], writes=["tb"])
                ob, ok = obR.next()
                S.dve(lambda e: e.scalar_tensor_tensor(out=ob, in0=tb, scalar=self.ppv("rnw", h), in1=g, op0=ALU.mult, op1=ALU.mult),
                      reads=["tb", gk, "pp"], writes=[ok])
                S.dma(sc["boT"][h * 128:(h + 1) * 128, t0:t0 + 512], ob, reads=[ok])
            return epi

        prev_epi = None
        for G in range(Sq // 512 - 1, -1, -1):
            for ci in range(3, -1, -1):
                pass1b(G * 4 + ci)
            vt, vk = load_v(G)
            for h in range(4):
                epi = pass2(G, h, vt, vk)
                if prev_epi is not None:
                    prev_epi()
                prev_epi = epi
        prev_epi()


MK.phase_ret = phase_ret


def phase_ssd(self, l):
    S, A = self.S, self.A
    A.reset()
    self.load_cp()
    sc = self.scr
    smax = self.smax
    Nm = smax // 128
    BT = A.alloc([2, smax], BF16)
    CT = A.alloc([2, smax], BF16)
    dtA = A.alloc([Nm, 16], F32)
    dta = A.alloc([Nm, 16], F32)
    cumS = A.alloc([Nm, 16], F32)
    ecum = A.alloc([Nm, 16], F32)
    sdte = A.alloc([Nm, 16], F32)
    etot = A.alloc([Nm, 16], F32)
    A16 = A.alloc([16], F32)
    prevs = [A.alloc([Nm, 512], BF16) for _ in range(2)]
    Hs = [A.alloc([512], F32) for _ in range(2)]
    xBR = Rot([(A.alloc([4, 768], BF16), ("xB", i)) for i in range(4)])
    zR = Rot([(A.alloc([4, 512], BF16), ("z", i)) for i in range(2)])
    xwR = Rot([(A.alloc([512], BF16), ("xw", i)) for i in range(2)])
    xdtR = Rot([(A.alloc([512], BF16), ("xdt", i)) for i in range(4)])
    LhR = Rot([(A.alloc([4, 128], F32), ("Lh", i)) for i in range(2)])
    ER = Rot([(A.alloc([4, 128], F32), ("E", i)) for i in range(2)])
    WR = Rot([(A.alloc([4, 128], BF16), ("W", i)) for i in range(2)])
    CBm = [A.alloc([2, 128], F32) for _ in range(2)]
    tbs = [A.alloc([512], F32) for _ in range(2)]
    ubs = [A.alloc([512], F32) for _ in range(2)]
    t2b = A.alloc([512], F32)
    junk = A.alloc([256], F32)
    ssbs = [A.alloc([4], F32) for _ in range(2)]
    yoR = Rot([(A.alloc([512], BF16), ("yo", i)) for i in range(2)])
    oTR = Rot([(A.alloc([4, 512], BF16), ("oT", i)) for i in range(2)])
    ident = self.bpv("ident")
    Uf32, Ub32 = self.cpv("Uf"), self.cpv("Ub")
    SL = [self.cpv("SLf"), self.cpv("SLb")]
    U = [Uf32, Ub32]
    ones32 = self.cpv("ones")
    o_a = PP["alog"][0]
    o_d = PP["ssd"][0]
    o_n = PP["snw"][0]
    S.act(lambda e: e.activation(out=A16, in_=self.pp[:, o_a:o_a + 16], func=AF.Exp), reads=["pp"], writes=["A16"])
    S.dve(lambda e: e.tensor_scalar(out=A16, in0=A16, scalar1=-1.0, scalar2=None, op0=ALU.mult), reads=["A16"], writes=["A16"])
    Dsk = self.pp[:, o_d:o_d + 8].unsqueeze(2).broadcast_to([128, 8, 64])
    snw = self.pp[:, o_n:o_n + 512]

    for (base, Sq) in _seq_list(self):
        N = Sq // 128
        NG = Sq // 512
        S.dma(BT[:, :, 0:Sq], sc["sBT"][:, base:base + Sq].rearrange("(c p) t -> p c t", p=128), writes=["BT"])
        S.dma(CT[:, :, 0:Sq], sc["sCT"][:, base:base + Sq].rearrange("(c p) t -> p c t", p=128), writes=["CT"])
        S.dma(dtA[:, 0:N, :], sc["sdt"][base:base + Sq, :].rearrange("(c p) k -> p c k", p=128), writes=["dtA"])
        S.dve(lambda e, N=N: e.tensor_tensor(out=dta[:, 0:N, :], in0=dtA[:, 0:N, :], in1=A16.unsqueeze(1).broadcast_to([128, N, 16]), op=ALU.mult),
              reads=["dtA", "A16"], writes=["dta"])
        psC = self.bank(0)
        psTt = self.bank(1)
        psC3 = psC.rearrange("p (n c) -> p n c", c=16)
        S.pe(lambda e, N=N: e.matmul(psC3[:, 0:N, 0:8], lhsT=Uf32, rhs=dta[:, 0:N, 0:8], start=True, stop=True), reads=["dta"], writes=[("ps", 0)])
        S.pe(lambda e, N=N: e.matmul(psC3[:, 0:N, 8:16], lhsT=Ub32, rhs=dta[:, 0:N, 8:16], start=True, stop=True), reads=["dta"], writes=[("ps", 0)])
        S.pe(lambda e, N=N: e.matmul(psTt[:, 0:N * 16], lhsT=ones32, rhs=dta[:, 0:N, :].rearrange("p n c -> p (n c)"), start=True, stop=True),
             reads=["dta"], writes=[("ps", 1)])
        cs2 = cumS.rearrange("p n c -> p (n c)")
        S.act(lambda e, N=N: e.activation(out=cs2[:, 0:N * 16], in_=psC[:, 0:N * 16], func=AF.Copy), reads=[("ps", 0)], writes=["cumS"])
        S.act(lambda e, N=N: e.activation(out=ecum.rearrange("p n c -> p (n c)")[:, 0:N * 16], in_=cs2[:, 0:N * 16], func=AF.Exp),
              reads=["cumS"], writes=["ecum"])
        sd2 = sdte.rearrange("p n c -> p (n c)")
        S.dve(lambda e, N=N: e.tensor_tensor(out=sd2[:, 0:N * 16], in0=psTt[:, 0:N * 16], in1=cs2[:, 0:N * 16], op=ALU.subtract),
              reads=[("ps", 1), "cumS"], writes=["sdte"])
        S.act(lambda e, N=N: e.activation(out=sd2[:, 0:N * 16], in_=sd2[:, 0:N * 16], func=AF.Exp), reads=["sdte"], writes=["sdte"])
        S.dve(lambda e, N=N: e.tensor_tensor(out=sd2[:, 0:N * 16], in0=sd2[:, 0:N * 16], in1=dtA.rearrange("p n c -> p (n c)")[:, 0:N * 16], op=ALU.mult),
              reads=["sdte", "dtA"], writes=["sdte"])
        S.act(lambda e, N=N: e.activation(out=etot.rearrange("p n c -> p (n c)")[:, 0:N * 16], in_=psTt[:, 0:N * 16], func=AF.Exp),
              reads=[("ps", 1)], writes=["etot"])
        for d in range(2):
            S.pool(lambda e, d=d: e.memset(Hs[d], 0.0), writes=[("H", d)])

        def load_x(G, base=base):
            xB, xk = xBR.next()
            t0 = base + G * 512
            S.dma(xB, sc["sxB"][t0:t0 + 512, :].rearrange("(c p) e -> p c e", p=128), writes=[xk])
            return xB, xk

        stR = Rot([(self.bank(i), ("ps", i)) for i in (2, 3)])

        def state_step(d, n, xB, xk):
            ci = n % 4
            H, prev = Hs[d], prevs[d]
            S.pool(lambda e: e.tensor_copy(out=prev[:, n, :], in_=H), reads=[("H", d)], writes=[("prev", d, n)])
            xw, wk = xwR.next()
            S.dve(lambda e: e.tensor_tensor(out=xw.rearrange("p (h q) -> p h q", h=8), in0=xB[:, ci, 0:512].rearrange("p (h q) -> p h q", h=8),
                                            in1=sdte[:, n, d * 8:(d + 1) * 8].unsqueeze(2).broadcast_to([128, 8, 64]), op=ALU.mult),
                  reads=[xk, "sdte"], writes=[wk])
            ps, pk = stR.next()
            for g in range(2):
                S.pe(lambda e, g=g: e.matmul(ps[:, g * 256:(g + 1) * 256], lhsT=xB[:, ci, 512 + g * 128:512 + (g + 1) * 128],
                                             rhs=xw[:, g * 256:(g + 1) * 256], start=True, stop=True), reads=[xk, wk], writes=[pk])
            S.pool(lambda e: e.tensor_tensor(out=H.rearrange("p (h q) -> p h q", h=8), in0=H.rearrange("p (h q) -> p h q", h=8),
                                             in1=etot[:, n, d * 8:(d + 1) * 8].unsqueeze(2).broadcast_to([128, 8, 64]), op=ALU.mult),
                   reads=[("H", d), "etot"], writes=[("H", d)])
            S.dve(lambda e: e.tensor_tensor(out=H, in0=H, in1=ps, op=ALU.add), reads=[("H", d), pk], writes=[("H", d)])

        curf = curb = None
        for step in range(N):
            nf, nb_ = step, N - 1 - step
            if nf % 4 == 0:
                curf = load_x(nf // 4)
            if nb_ % 4 == 3:
                curb = load_x(nb_ // 4)
            state_step(0, nf, curf[0], curf[1])
            state_step(1, nb_, curb[0], curb[1])

        psCB = self.bank(0)
        psDR = Rot([(self.bank(i), ("ps", i)) for i in (1, 2)])
        Yd = [self.bank(3), self.bank(4)]
        Yo = [self.bank(5), self.bank(6)]
        psT = self.bank(7, BF16)

        pend_tail = [None]
        store_jobs = []

        def out_chunk(n, xB, xk, z, zk, oT, otk):
            ci = n % 4
            tsl = slice(n * 128, (n + 1) * 128)
            for g in range(2):
                S.pe(lambda e, g=g: e.matmul(psCB[:, g * 128:(g + 1) * 128], lhsT=BT[:, g, tsl], rhs=CT[:, g, tsl], start=True, stop=True),
                     reads=["BT", "CT"], writes=[("ps", 0)])
            for d in range(2):
                S.dve(lambda e, d=d: e.tensor_tensor(out=CBm[d], in0=psCB[:, 0:256].rearrange("p (g i) -> p g i", g=2),
                                                     in1=U[d].unsqueeze(1).broadcast_to([128, 2, 128]), op=ALU.mult),
                      reads=[("ps", 0)], writes=[("CBm", d)])
            for d in range(2):
                xdt, xdk = xdtR.next()
                S.dve(lambda e, d=d, xdt=xdt: e.tensor_tensor(out=xdt.rearrange("p (h q) -> p h q", h=8), in0=xB[:, ci, 0:512].rearrange("p (h q) -> p h q", h=8),
                                                               in1=dtA[:, n, d * 8:(d + 1) * 8].unsqueeze(2).broadcast_to([128, 8, 64]), op=ALU.mult),
                      reads=[xk, "dtA"], writes=[xdk])
                for g in range(2):
                    c0 = d * 8 + g * 4
                    Lh, lk = LhR.next()
                    for hh in range(4):
                        S.act(lambda e, d=d, c0=c0, Lh=Lh, hh=hh: e.activation(out=Lh[:, hh, :], in_=SL[d], func=AF.Copy,
                                                                                 scale=dta[:, n, c0 + hh:c0 + hh + 1]),
                              reads=["dta"], writes=[(lk, hh)])
                    psD, pdk = psDR.next()
                    for hh in range(4):
                        S.pe(lambda e, d=d, hh=hh, Lh=Lh, psD=psD: e.matmul(psD[:, hh * 128:(hh + 1) * 128], lhsT=Lh[:, hh, :], rhs=U[d], start=True, stop=True),
                             reads=[(lk, hh)], writes=[pdk])
                    E, ek = ER.next()
                    S.act(lambda e, E=E, psD=psD: e.activation(out=E.rearrange("p a b -> p (a b)"), in_=psD, func=AF.Exp), reads=[pdk], writes=[ek])
                    W, wk = WR.next()
                    S.dve(lambda e, d=d, g=g, E=E, W=W: e.tensor_tensor(out=W, in0=E, in1=CBm[d][:, g, :].unsqueeze(1).broadcast_to([128, 4, 128]), op=ALU.mult),
                          reads=[ek, ("CBm", d)], writes=[wk])
                    for hh in range(4):
                        h = g * 4 + hh
                        S.pe(lambda e, d=d, hh=hh, h=h, W=W, xdt=xdt: e.matmul(Yd[d][:, h * 64:(h + 1) * 64], lhsT=W[:, hh, :], rhs=xdt[:, h * 64:(h + 1) * 64],
                                                                                 start=True, stop=True), reads=[wk, xdk], writes=[("ps", 3 + d)])
                    S.pe(lambda e, d=d, g=g: e.matmul(Yo[d][:, g * 256:(g + 1) * 256], lhsT=CT[:, g, tsl], rhs=prevs[d][:, n, g * 256:(g + 1) * 256],
                                                      start=True, stop=True), reads=["CT", ("prev", d, n)], writes=[("ps", 5 + d)])
            tb, ub, ssb = tbs[n % 2], ubs[n % 2], ssbs[n % 2]
            tbk, ubk = ("tb", n % 2), ("ub", n % 2)
            t3 = tb.rearrange("p (h q) -> p h q", h=8)
            u3 = ub.rearrange("p (h q) -> p h q", h=8)
            S.dve(lambda e: e.tensor_tensor(out=t3, in0=Yo[0].rearrange("p (h q) -> p h q", h=8),
                                            in1=ecum[:, n, 0:8].unsqueeze(2).broadcast_to([128, 8, 64]), op=ALU.mult),
                  reads=[("ps", 5), "ecum"], writes=[tbk])
            S.dve(lambda e: e.tensor_tensor(out=tb, in0=tb, in1=Yd[0], op=ALU.add), reads=[tbk, ("ps", 3)], writes=[tbk])
            S.dve(lambda e: e.tensor_tensor(out=u3, in0=Yo[1].rearrange("p (h q) -> p h q", h=8),
                                            in1=ecum[:, n, 8:16].unsqueeze(2).broadcast_to([128, 8, 64]), op=ALU.mult),
                  reads=[("ps", 6), "ecum"], writes=[ubk])
            S.dve(lambda e: e.tensor_tensor(out=ub, in0=ub, in1=Yd[1], op=ALU.add), reads=[ubk, ("ps", 4)], writes=[ubk])
            def tail_b():
                S.pool(lambda e: e.tensor_tensor(out=tb, in0=tb, in1=ub, op=ALU.add), reads=[tbk, ubk], writes=[tbk])
                S.pool(lambda e: e.tensor_tensor(out=t2b.rearrange("p (h q) -> p h q", h=8), in0=xB[:, ci, 0:512].rearrange("p (h q) -> p h q", h=8),
                                                 in1=Dsk, op=ALU.mult), reads=[xk, "pp"], writes=["t2b"])
                S.pool(lambda e: e.tensor_tensor(out=tb, in0=tb, in1=t2b, op=ALU.add), reads=[tbk, "t2b"], writes=[tbk])
                S.pool(lambda e: e.tensor_tensor(out=tb, in0=tb, in1=z[:, ci, :], op=ALU.mult), reads=[tbk, zk], writes=[tbk])
                for g in range(2):
                    S.act(lambda e, g=g: e.activation(out=junk, in_=tb[:, g * 256:(g + 1) * 256], func=AF.Square, accum_out=ssb[:, g:g + 1]),
                          reads=[tbk, "junk"], writes=["junk", ("ss", n % 2, g)])
                S.act(lambda e: e.activation(out=ssb[:, 2:4], in_=ssb[:, 0:2], func=AF.Ln, scale=1.0 / 256, bias=HEAD_EPS),
                      reads=[("ss", n % 2, 0), ("ss", n % 2, 1)], writes=[("ssl", n % 2)])
                S.act(lambda e: e.activation(out=ssb[:, 0:2], in_=ssb[:, 2:4], func=AF.Exp, scale=-0.5), reads=[("ssl", n % 2)], writes=[("ss", n % 2, 0), ("ss", n % 2, 1), ("rs2", n % 2)])
                yo, yk = yoR.next()
                for g in range(2):
                    S.dve(lambda e, g=g, yo=yo: e.scalar_tensor_tensor(out=yo[:, g * 256:(g + 1) * 256], in0=tb[:, g * 256:(g + 1) * 256], scalar=ssb[:, g:g + 1],
                                                                        in1=snw[:, g * 256:(g + 1) * 256], op0=ALU.mult, op1=ALU.mult),
                          reads=[tbk, ("rs2", n % 2), "pp"], writes=[(yk, g)])
                half = n % 2
                pT_ = psT[:, half * 512:(half + 1) * 512]
                ptk = ("ps", 7)
                for cc in range(4):
                    S.pe(lambda e, cc=cc, yo=yo: e.transpose(out=pT_[:, cc * 128:(cc + 1) * 128], in_=yo[:, cc * 128:(cc + 1) * 128], identity=ident),
                         reads=[(yk, 0), (yk, 1), "bp"], writes=[ptk])
                S.dve(lambda e: e.tensor_copy(out=oT[:, :, ci * 128:(ci + 1) * 128], in_=pT_.rearrange("p (a b) -> p a b", a=4)),
                      reads=[ptk], writes=[(otk, ci)])


            return tail_b

        for G in range(NG):
            xB, xk = load_x(G)
            z, zk = zR.next()
            t0 = base + G * 512
            S.dma(z, sc["sz"][t0:t0 + 512, :].rearrange("(c p) e -> p c e", p=128), writes=[zk])
            oT, otk = oTR.next()
            for ci in range(4):
                tb_fn = out_chunk(G * 4 + ci, xB, xk, z, zk, oT, otk)
                if pend_tail[0] is not None:
                    pend_tail[0]()
                pend_tail[0] = tb_fn
            store_jobs.append((t0, oT, otk))
            if len(store_jobs) > 1:
                t0_, oT_, otk_ = store_jobs.pop(0)
                S.dma(sc["boT"][1536:2048, t0_:t0_ + 512].rearrange("(c p) t -> p c t", p=128), oT_, reads=[(otk_, ci) for ci in range(4)])
        pend_tail[0]()
        pend_tail[0] = None
        while store_jobs:
            t0_, oT_, otk_ = store_jobs.pop(0)
            S.dma(sc["boT"][1536:2048, t0_:t0_ + 512].rearrange("(c p) t -> p c t", p=128), oT_, reads=[(otk_, ci) for ci in range(4)])


MK.phase_ssd = phase_ssd


def load_cp(self):
    self.cp = self.A.alloc([NCP], F32)
    self.S.dma(self.cp, self.cpack, writes=["cp"])
    for e in (PE, ACT, DVE, POOL):
        self.S.add(e, None, reads=["cp"], real=False)


def rms_group(self, xs, ncols, wname, hT, hk, sqR, lnv, rstd, psA, psH0, kA=None, kH=None):
    S = self.S
    onesb = self.bpv("ones")
    for k in range(8):
        sq, sqk = sqR.next()
        S.act(lambda e, sq=sq, k=k: e.activation(out=sq[:, 0:ncols], in_=xs[:, k, :], func=AF.Square),
              reads=["xs", "xsh"], writes=[sqk])
        S.pe(lambda e, sq=sq, k=k: e.matmul(psA, lhsT=onesb, rhs=sq[:, 0:512], start=(k == 0), stop=(k == 7)),
             reads=[sqk, "bp"], writes=[kA])
        if ncols > 512:
            S.pe(lambda e, sq=sq, k=k: e.matmul(psH0, lhsT=onesb, rhs=sq[:, 512:ncols], start=(k == 0), stop=(k == 7)),
                 reads=[sqk, "bp"], writes=[kH])
    lst = [(psA, kA, slice(0, 512))]
    if ncols > 512:
        lst.append((psH0, kH, slice(512, ncols)))
    self.rstd_from(lst, 1.0 / 1024, NORM_EPS, lnv[:, 0:ncols], rstd[:, 0:ncols], "rstd", "lnv")
    if hT is not None:
        for k in range(8):
            S.dve(lambda e, k=k: e.scalar_tensor_tensor(out=hT[:, k, :], in0=xs[:, k, :], scalar=self.ppv(wname, k),
                                                         in1=rstd[:, 0:ncols], op0=ALU.mult, op1=ALU.mult),
                  reads=["xs", "xsh", "rstd", "pp"], writes=[hk[k]])


def phase3(self, l):
    S, A = self.S, self.A
    A.reset()
    sc = self.scr
    x_in = (self.xT if l == 0 else sc["xres"]).rearrange("(c p) t -> p c t", p=128)
    boT = sc["boT"].rearrange("(c p) t -> p c t", p=128)
    wG = A.alloc([8, 4096], BF16)
    wB = A.alloc([16, 1024], BF16)
    wO = A.alloc([8, 1024], BF16)
    bo = A.alloc([16, 512], BF16)
    bo_w = bo.rearrange("p a b -> p (a b)").bitcast(F32)
    stgR = Rot([(bo_w[:, i * 2048:(i + 1) * 2048], ("stg", i)) for i in range(2)])
    win = self.w_in[l].rearrange("(c p) n -> p c n", p=128)
    pieces = []
    for k in range(8):
        for q in range(2):
            pieces.append((wG[:, k, q * 2048:(q + 1) * 2048], win[:, k, C_GATE + q * 2048:C_GATE + (q + 1) * 2048]))
    wbr = self.w_branch[l].rearrange("b (c p) n -> p (b c) n", p=128)
    for i in range(16):
        pieces.append((wB[:, i, :], wbr[:, i, :]))
    wo = self.w_out[l].rearrange("(c p) n -> p c n", p=128)
    for k in range(8):
        pieces.append((wO[:, k, :], wo[:, k, :]))
    self.load_cast("w3", pieces, stgR)
    xs = A.alloc([8, 512], F32)
    hTs = [A.alloc([8, 512], BF16) for _ in range(2)]
    sqR = Rot([(A.alloc([512], BF16), ("sq", i)) for i in range(2)])
    lnv = A.alloc([512], F32)
    rstd = A.alloc([512], F32)
    merged = A.alloc([8, 512], BF16)
    gsR = Rot([(A.alloc([512], F32), ("gs", i)) for i in range(3)])
    tmR = Rot([(A.alloc([512], F32), ("tm", i)) for i in range(2)])
    mR = Rot([(A.alloc([512], F32), ("m", i)) for i in range(2)])
    xoR = Rot([(A.alloc([512], F32), ("xo", i)) for i in range(2)])
    psGR = Rot([(self.bank(i), ("ps", i)) for i in (0, 1, 2)])
    psPR = Rot([(self.bank(i), ("ps", i)) for i in (3, 4)])
    psOR = Rot([(self.bank(i), ("ps", i)) for i in (5, 6)])
    psA = self.bank(7)
    ng = len(self.groups)

    def load_x(gi):
        if gi >= ng:
            return
        t0 = self.groups[gi]["t0"]
        S.dma(xs, x_in[:, :, t0:t0 + 512], writes=["xs"])

    def load_bo(gi):
        if gi >= ng:
            return
        t0 = self.groups[gi]["t0"]
        for q in range(4):
            S.dma(bo[:, q * 4:(q + 1) * 4, :], boT[:, q * 4:(q + 1) * 4, t0:t0 + 512], writes=[("bo", q), ("stg", 0), ("stg", 1)])

    def body(gi, g):
        t0 = g["t0"]
        hT = hTs[gi % 2]
        hk = [("hT", gi % 2, k) for k in range(8)]
        if gi == 0:
            load_x(0)
            load_bo(0)
            rms_group(self, xs, 512, "wn", hT, hk, sqR, lnv, rstd, psA, None, kA=("ps", 7))
            load_x(1)
        for j in range(8):
            if j == 5 and gi + 1 < ng:
                hk_n = [("hT", (gi + 1) % 2, k) for k in range(8)]
                rms_group(self, xs, 512, "wn", hTs[(gi + 1) % 2], hk_n, sqR, lnv, rstd, psA, None, kA=("ps", 7))
                load_x(gi + 2)
            m, mk_ = mR.next()
            for b in range(4):
                psG, pgk = psGR.next()
                cg = b * 1024 + j * 128
                self.mmg(psG, pgk, [(wG[:, k, cg:cg + 128], hT[:, k, :]) for k in range(8)], hk + ["w3"])
                gs, gk = gsR.next()
                S.act(lambda e, psG=psG, gs=gs: e.activation(out=gs, in_=psG, func=AF.Sigmoid), reads=[pgk], writes=[gk])
                psP, ppk = psPR.next()
                self.mmg(psP, ppk, [(wB[:, b * 4 + kc, j * 128:(j + 1) * 128], bo[:, b * 4 + kc, :]) for kc in range(4)], [("bo", b), "w3"])
                if b == 0:
                    S.dve(lambda e, psP=psP, gs=gs, m=m: e.tensor_tensor(out=m, in0=psP, in1=gs, op=ALU.mult), reads=[ppk, gk], writes=[mk_])
                else:
                    tm, tk = tmR.next()
                    S.dve(lambda e, psP=psP, gs=gs, tm=tm: e.tensor_tensor(out=tm, in0=psP, in1=gs, op=ALU.mult), reads=[ppk, gk], writes=[tk])
                    dst = merged[:, j, :] if b == 3 else m
                    dk = ("mg", j) if b == 3 else mk_
                    S.pool(lambda e, tm=tm, m=m, dst=dst: e.tensor_tensor(out=dst, in0=m, in1=tm, op=ALU.add), reads=[mk_, tk], writes=[dk])
        load_bo(gi + 1)
        mgk = [("mg", j) for j in range(8)]
        for jo in range(8):
            psO, pok = psOR.next()
            self.mmg(psO, pok, [(wO[:, k, jo * 128:(jo + 1) * 128], merged[:, k, :]) for k in range(8)], mgk + ["w3"])
            xo, xk = xoR.next()
            S.dma(xo, x_in[:, jo, t0:t0 + 512], writes=[xk])
            S.dve(lambda e, psO=psO, xo=xo: e.tensor_tensor(out=xo, in0=psO, in1=xo, op=ALU.add), reads=[pok, xk], writes=[xk])
            S.dma(sc["xmid"][jo * 128:(jo + 1) * 128, t0:t0 + 512], xo, reads=[xk])

    for gi, g in enumerate(self.groups):
        body(gi, g)


def phase4(self, l):
    S, A = self.S, self.A
    A.reset()
    sc = self.scr
    x_in = sc["xmid"].rearrange("(c p) t -> p c t", p=128)
    wg = A.alloc([8, D_FF], BF16)
    wu = A.alloc([8, D_FF], BF16)
    wd = A.alloc([NFF, 1024], BF16)
    act = A.alloc([NFF, 512], BF16)
    act_w = act.rearrange("p a b -> p (a b)").bitcast(F32)
    stgR = Rot([(act_w[:, i * 2816:(i + 1) * 2816], ("stg", i)) for i in range(2)])
    pieces = []
    g_ = self.w_gate[l].rearrange("(c p) n -> p c n", p=128)
    u_ = self.w_up[l].rearrange("(c p) n -> p c n", p=128)
    for k in range(8):
        pieces.append((wg[:, k, :], g_[:, k, :]))
        pieces.append((wu[:, k, :], u_[:, k, :]))
    d_ = self.w_down[l].rearrange("(c p) n -> p c n", p=128)
    for c in range(0, NFF, 2):
        pieces.append((wd[:, c:c + 2, :], d_[:, c:c + 2, :]))
    self.load_cast("w4", pieces, stgR)
    xs = A.alloc([8, 514], F32)
    hTs = [A.alloc([8, 514], BF16) for _ in range(2)]
    sqR = Rot([(A.alloc([514], BF16), ("sq", i)) for i in range(2)])
    lnv = A.alloc([514], F32)
    rstd = lnv
    GR = Rot([(A.alloc([514], F32), ("G", i)) for i in range(1)])
    cvR = Rot([(A.alloc([512], F32), ("cv", i)) for i in range(1)])
    xrR = Rot([(A.alloc([512], F32), ("xr", i)) for i in range(1)])
    psgR = Rot([(self.bank(i), ("ps", i)) for i in (0, 1)])
    psuR = Rot([(self.bank(i), ("ps", i)) for i in (2, 3)])
    psOR = Rot([(self.bank(i), ("ps", i)) for i in (4, 5)])
    psA = self.bank(6)
    psH = self.bank(7)
    ng = len(self.groups)
    o_w = PP["fcw"][0]
    o_b = PP["fcb"][0]
    x_out = sc["xres"]

    def load_x(gi_):
        if gi_ >= ng:
            return
        g_i = self.groups[gi_]
        hasL = g_i["pos0"] > 0
        hasR = g_i["pos0"] + 512 < g_i["S"]
        if not hasL:
            S.pool(lambda e: e.memset(xs[:, :, 0:1], 0.0), writes=["xsh"])
        if not hasR:
            S.pool(lambda e: e.memset(xs[:, :, 513:514], 0.0), writes=["xsh"])
        lo = g_i["t0"] - 1 if hasL else g_i["t0"]
        hi = g_i["t0"] + 513 if hasR else g_i["t0"] + 512
        c0 = 0 if hasL else 1
        S.dma(xs[:, :, c0:c0 + (hi - lo)], x_in[:, :, lo:hi], writes=["xs"])

    def body(gi, g):
        t0 = g["t0"]
        hT = hTs[gi % 2]
        hk = [("hT", gi % 2, k) for k in range(8)]
        if gi == 0:
            load_x(0)
            rms_group(self, xs, 514, "nf", hT, hk, sqR, lnv, rstd, psA, psH[:, 0:2], kA=("ps", 6), kH=("ps", 7))
            load_x(1)
        for c in range(NFF):
            if c == 13 and gi + 1 < ng:
                hk_n = [("hT", (gi + 1) % 2, k) for k in range(8)]
                rms_group(self, xs, 514, "nf", hTs[(gi + 1) % 2], hk_n, sqR, lnv, rstd, psA, psH[:, 0:2], kA=("ps", 6), kH=("ps", 7))
                load_x(gi + 2)
            cs = slice(c * 128, (c + 1) * 128)
            psg, pgk = psgR.next()
            self.mmg(psg, pgk, [(wg[:, k, cs], hT[:, k, 1:513]) for k in range(8)], hk + ["w4"])
            hcol = 2 + 2 * c
            hkey = ("ps", 7)
            for k in range(8):
                S.pe(lambda e, k=k, cs=cs, hcol=hcol: e.matmul(psH[:, hcol:hcol + 2], lhsT=wg[:, k, cs], rhs=hT[:, k, 0:514:513],
                                                                start=(k == 0), stop=(k == 7)), reads=hk + ["w4"], writes=[hkey])
            psu, puk = psuR.next()
            self.mmg(psu, puk, [(wu[:, k, cs], hT[:, k, 1:513]) for k in range(8)], hk + ["w4"])
            G, gk = GR.next()
            S.act(lambda e, psg=psg, G=G: e.activation(out=G[:, 1:513], in_=psg, func=AF.Copy), reads=[pgk], writes=[gk])
            S.dve(lambda e, G=G, hcol=hcol: e.tensor_copy(out=G[:, 0:514:513], in_=psH[:, hcol:hcol + 2]), reads=[hkey], writes=[(gk, "h")])
            cv, ck = cvR.next()
            S.dve(lambda e, G=G, cv=cv, c=c: e.tensor_scalar(out=cv, in0=G[:, 1:513], scalar1=self.pp[:, o_w + NFF + c:o_w + NFF + c + 1],
                                                              scalar2=self.pp[:, o_b + c:o_b + c + 1], op0=ALU.mult, op1=ALU.add),
                  reads=[gk, "pp"], writes=[ck])
            S.dve(lambda e, G=G, cv=cv, c=c: e.scalar_tensor_tensor(out=cv, in0=G[:, 0:512], scalar=self.pp[:, o_w + c:o_w + c + 1],
                                                                     in1=cv, op0=ALU.mult, op1=ALU.add),
                  reads=[gk, (gk, "h"), "pp", ck], writes=[ck])
            S.dve(lambda e, G=G, cv=cv, c=c: e.scalar_tensor_tensor(out=cv, in0=G[:, 2:514], scalar=self.pp[:, o_w + 2 * NFF + c:o_w + 2 * NFF + c + 1],
                                                                     in1=cv, op0=ALU.mult, op1=ALU.add),
                  reads=[gk, (gk, "h"), "pp", ck], writes=[ck])
            S.act(lambda e, cv=cv: e.activation(out=cv, in_=cv, func=AF.Silu), reads=[ck], writes=[ck])
            S.dve(lambda e, cv=cv, psu=psu, c=c: e.tensor_tensor(out=act[:, c, :], in0=psu, in1=cv, op=ALU.mult), reads=[puk, ck], writes=[("act", c)])
        ak = [("act", c) for c in range(NFF)]
        for jo in range(8):
            xr, xrk = xrR.next()
            S.dma(xr, sc["xmid"][jo * 128:(jo + 1) * 128, t0:t0 + 512], writes=[xrk])
            psO, pok = psOR.next()
            self.mmg(psO, pok, [(wd[:, c, jo * 128:(jo + 1) * 128], act[:, c, :]) for c in range(NFF)], ak + ["w4"])
            S.dve(lambda e, psO=psO, xr=xr: e.tensor_tensor(out=xr, in0=psO, in1=xr, op=ALU.add), reads=[pok, xrk], writes=[xrk])
            S.dma(x_out[jo * 128:(jo + 1) * 128, t0:t0 + 512], xr, reads=[xrk])

    for gi, g in enumerate(self.groups):
        body(gi, g)


def phase5(self):
    S, A = self.S, self.A
    A.reset()
    sc = self.scr
    x_in = sc["xres"].rearrange("(c p) t -> p c t", p=128)
    xss = [A.alloc([8, 512], F32) for _ in range(2)]
    sqR = Rot([(A.alloc([512], BF16), ("sq", i)) for i in range(2)])
    lnv = A.alloc([512], F32)
    rstd = A.alloc([512], F32)
    yR = Rot([(A.alloc([512], F32), ("y", i)) for i in range(3)])
    psAs = [self.bank(0), self.bank(1)]
    onesb = self.bpv("ones")

    def body(gi, g):
        t0 = g["t0"]
        xs = xss[gi % 2]
        xk = ("xs", gi % 2)
        S.dma(xs, x_in[:, :, t0:t0 + 512], writes=[xk])
        psA = psAs[gi % 2]
        pak = ("ps", gi % 2)
        for k in range(8):
            sq, sqk = sqR.next()
            S.act(lambda e, sq=sq, k=k: e.activation(out=sq, in_=xs[:, k, :], func=AF.Square), reads=[xk], writes=[sqk])
            S.pe(lambda e, sq=sq, k=k: e.matmul(psA, lhsT=onesb, rhs=sq, start=(k == 0), stop=(k == 7)), reads=[sqk, "bp"], writes=[pak])
        self.rstd_from([(psA, pak, slice(0, 512))], 1.0 / 1024, NORM_EPS, lnv, rstd, "rstd", "lnv")
        for k in range(8):
            y, yk = yR.next()
            S.dve(lambda e, k=k, y=y: e.scalar_tensor_tensor(out=y, in0=xs[:, k, :], scalar=self.ppv("fin", k), in1=rstd,
                                                              op0=ALU.mult, op1=ALU.mult), reads=[xk, "rstd", "pp"], writes=[yk])
            S.dma(self.yT[k * 128:(k + 1) * 128, t0:t0 + 512], y, reads=[yk])

    for gi, g in enumerate(self.groups):
        body(gi, g)


MK.load_cp = load_cp
MK.phase3 = phase3
MK.phase4 = phase4
MK.phase5 = phase5


_NC_CACHE = {}
N_CORES = 8
SEQ_P, SEQ_S, DEPTH = 4096, 2048, 2


def kernel(**inputs):
    xp = np.asarray(inputs["x_prompt"], np.float32)
    xs_ = np.asarray(inputs["x_sample"], np.float32)
    seqs = [SEQ_P, SEQ_P, SEQ_S]
    key = "main"
    if key not in _NC_CACHE:
        mk = MK(seqs, DEPTH)
        _NC_CACHE[key] = mk.build()
    nc = _NC_CACHE[key]
    common = common_inputs(inputs, DEPTH, SEQ_P)
    in_maps = []
    for c in range(N_CORES):
        xT = np.concatenate([xp[2 * c].T, xp[2 * c + 1].T, xs_[c].T], axis=1)
        m = dict(common)
        m["xT"] = np.ascontiguousarray(xT)
        in_maps.append(m)
    res = run_bass_kernel_spmd(nc, in_maps, core_ids=list(range(N_CORES)))
    yp = np.empty((16, SEQ_P, D_MODEL), np.float32)
    ys = np.empty((8, SEQ_S, D_MODEL), np.float32)
    for c in range(N_CORES):
        yT = np.asarray(res.results[c]["yT"], np.float32)
        yp[2 * c] = yT[:, 0:SEQ_P].T
        yp[2 * c + 1] = yT[:, SEQ_P:2 * SEQ_P].T
        ys[c] = yT[:, 2 * SEQ_P:2 * SEQ_P + SEQ_S].T
    return (yp, ys)
```

```python
import math
from contextlib import ExitStack

import numpy as np
import ml_dtypes

import concourse.bass as bass
import concourse.mybir as mybir
from concourse.bass_utils import run_bass_kernel_spmd

F32 = mybir.dt.float32
BF16 = mybir.dt.bfloat16
AF = mybir.ActivationFunctionType
ALU = mybir.AluOpType
AX = mybir.AxisListType

PE, ACT, DVE, POOL, SP = "pe", "act", "dve", "pool", "sp"
ENGS = (PE, ACT, DVE, POOL, SP)

D_MODEL = 1024
IN_COLS = 9136
D_FF = 2816
NFF = 22
NORM_EPS = 1e-6
HEAD_EPS = 1e-5
ROPE_THETA = 10000.0

C_RQ, C_RK, C_RV, C_RG = 0, 256, 512, 1024
C_CQ, C_CKV, C_KR = 1536, 1792, 1920
C_DQ, C_DK, C_DV = 1952, 2464, 2976
C_SZ, C_XBC, C_DT, C_GATE = 3488, 4000, 5024, 5040
NA = 5040


class Op:
    __slots__ = ("eng", "fn", "deps", "is_dma", "sig", "slot", "dval", "idx", "sigval")


class Sched:
    def __init__(self, nc, ndsem=12):
        self.nc = nc
        self.ops = {e: [] for e in ENGS}
        self.res = {}
        self.dma_n = {e: 0 for e in ENGS}
        self.dma_last = {e: {} for e in ENGS}
        import os
        self.ndsem = int(os.environ.get("NDSEM", ndsem))
        rr = int(os.environ.get("NROT", "1"))
        self.nrot = {PE: 4 * rr, ACT: 2 * rr, DVE: 2 * rr, POOL: 2 * rr, SP: 1}
        self.last_real = {e: None for e in ENGS}

    def add(self, eng, fn, reads=(), writes=(), dma=False, extra=(), real=True):
        op = Op()
        op.eng = eng
        op.fn = fn
        op.is_dma = dma
        op.sig = False
        op.idx = len(self.ops[eng])
        op.slot = None
        deps = set(extra)
        res = self.res
        if any(type(k) is tuple and k[0] == "ps" for k in reads):
            writes = list(writes) + [k for k in reads if type(k) is tuple and k[0] == "ps"]
            reads = [k for k in reads if not (type(k) is tuple and k[0] == "ps")]
        for k in reads:
            st = res.get(k)
            if st is not None and st[0] is not None:
                deps.add(st[0])
        for k in writes:
            st = res.get(k)
            if st is not None:
                if st[0] is not None:
                    deps.add(st[0])
                deps.update(st[1].values())
        if dma:
            i = self.dma_n[eng]
            self.dma_n[eng] = i + 1
            op.slot = i % self.ndsem
            op.dval = 16 * (i // self.ndsem + 1)
            prev = self.dma_last[eng].get(op.slot)
            if prev is not None:
                deps.add(prev)
            self.dma_last[eng][op.slot] = op
        rk = (eng, op.slot) if dma else eng
        for k in reads:
            st = res.get(k)
            if st is None:
                st = [None, {}]
                res[k] = st
            st[1][rk] = op
        for k in writes:
            res[k] = [op, {}]
        fd = []
        work = list(deps)
        seen = set()
        while work:
            d = work.pop()
            if d is op or d is None or id(d) in seen:
                continue
            seen.add(id(d))
            if d.fn is None:
                if d.eng != eng:
                    work.extend(d.deps)
                continue
            if (not d.is_dma) and (not dma) and d.eng == eng and eng == PE:
                continue
            fd.append(d)
            if not d.is_dma:
                d.sig = True
        op.deps = fd
        self.ops[eng].append(op)
        if real and not dma:
            self.last_real[eng] = op
        return op

    def pe(self, fn, reads=(), writes=()):
        return self.add(PE, fn, reads, writes)

    def act(self, fn, reads=(), writes=()):
        return self.add(ACT, fn, reads, writes)

    def dve(self, fn, reads=(), writes=()):
        return self.add(DVE, fn, reads, writes)

    def pool(self, fn, reads=(), writes=()):
        return self.add(POOL, fn, reads, writes)

    def dma(self, out, in_, reads=(), writes=(), q=SP, **kw):
        return self.add(q, lambda e: e.dma_start(out=out, in_=in_, **kw), reads, writes, dma=True)

    def barrier(self):
        lasts = [self.last_real[e] for e in ENGS if self.last_real[e] is not None]
        dmas = []
        for e in ENGS:
            dmas.extend(self.dma_last[e].values())
        for e in ENGS:
            if not self.ops[e] and e not in (PE, ACT, DVE, POOL, SP):
                continue
            self.add(e, None, extra=[o for o in lasts if o.eng != e] + dmas, real=False)
        self.res = {}

    def emit(self, es):
        nc = self.nc
        engobj = {PE: nc.tensor, ACT: nc.scalar, DVE: nc.vector, POOL: nc.gpsimd, SP: nc.sync}
        csem = {}
        for e in ENGS:
            if any(o.sig for o in self.ops[e]):
                csem[e] = [es.enter_context(nc.semaphore(f"c_{e}_{r}")) for r in range(self.nrot[e])]
        dsem = {}
        for e in ENGS:
            if self.dma_n[e] > 0:
                dsem[e] = [es.enter_context(nc.semaphore(f"d_{e}_{r}"))
                           for r in range(min(self.ndsem, self.dma_n[e]))]
        for e in ENGS:
            n = 0
            R = self.nrot[e]
            for o in self.ops[e]:
                if o.sig:
                    o.sigval = (n % R, n // R + 1)
                    n += 1
        block = es.enter_context(nc.Block())
        self.nwaits = 0
        self.ninst = 0

        def emit_engine(e):
            eo = engobj[e]
            waited = {}
            for o in self.ops[e]:
                for d in o.deps:
                    if d.is_dma:
                        key = ("d", d.eng, d.slot)
                        val = d.dval
                        sem = dsem[d.eng][d.slot]
                    else:
                        r, val = d.sigval
                        key = ("c", d.eng, r)
                        sem = csem[d.eng][r]
                    if waited.get(key, 0) >= val:
                        continue
                    waited[key] = val
                    eo.wait_ge(sem, val)
                    self.nwaits += 1
                if o.fn is None:
                    continue
                inst = o.fn(eo)
                self.ninst += 1
                if o.is_dma:
                    inst.then_inc(dsem[e][o.slot], 16)
                elif o.sig:
                    r, _ = o.sigval
                    inst.then_inc(csem[e][r], 1)
            for slot, o in self.dma_last[e].items():
                key = ("d", e, slot)
                if waited.get(key, 0) < o.dval:
                    eo.wait_ge(dsem[e][slot], o.dval)

        @block.tensor
        def _(t):
            emit_engine(PE)

        @block.scalar
        def _(t):
            emit_engine(ACT)

        @block.vector
        def _(t):
            emit_engine(DVE)

        @block.gpsimd
        def _(t):
            emit_engine(POOL)

        @block.sync
        def _(t):
            emit_engine(SP)


class Rot:
    def __init__(self, items):
        self.items = items
        self.i = 0

    def next(self):
        it = self.items[self.i % len(self.items)]
        self.i += 1
        return it


PP = {}
_off = 0
for _n, _w in [("wn", 8), ("nf", 8), ("fin", 8), ("scw", 24), ("scb", 8), ("qnw", 2), ("kvnw", 1),
               ("rnw", 4), ("dnw", 1), ("fcw", 66), ("fcb", 22),
               ("lgd", 8), ("dtb", 16), ("alog", 16), ("ssd", 8), ("snw", 512), ("dlam", 256)]:
    PP[_n] = (_off, _w)
    _off += _w
NPP = _off

CP = {}
_off = 0
for _n, _w in [("Uf", 128), ("Ub", 128), ("SLf", 128), ("SLb", 128), ("ones", 128), ("Dpos", 128), ("Dneg", 128),
               ("Mgt", 128), ("Mlt", 128), ("I2", 128), ("Ip1", 128), ("Cmi", 128), ("Cm1j", 64), ("Jj", 64),
               ("C128", 128)]:
    CP[_n] = (_off, _w)
    _off += _w
NCP = _off

BP = {}
_off = 0
for _n, _w in [("ident", 128), ("perm64", 128), ("perm32", 128), ("ones", 128)]:
    BP[_n] = (_off, _w)
    _off += _w
NBP = _off


def make_cpack():
    c = np.zeros((128, NCP), np.float32)
    j = np.arange(128)[:, None].astype(np.float32)
    i = np.arange(128)[None, :].astype(np.float32)

    def put(name, a):
        o, w = CP[name]
        c[:, o:o + w] = a

    put("Uf", (j <= i))
    put("Ub", (j >= i))
    put("SLf", (j > i))
    put("SLb", (j < i))
    put("ones", np.ones((128, 128)))
    put("Dpos", np.maximum(i - j, 0))
    put("Dneg", np.maximum(j - i, 0))
    put("Mgt", (i > j))
    put("Mlt", (j > i))
    put("I2", 2.0 * (i == j))
    put("Ip1", np.broadcast_to(i + 1, (128, 128)))
    put("Cmi", np.broadcast_to(128 - i, (128, 128)))
    put("Cm1j", np.broadcast_to(127 - j, (128, 64)))
    put("Jj", np.broadcast_to(j, (128, 64)))
    put("C128", np.full((128, 128), 128.0))
    return c


def make_bpack():
    b = np.zeros((128, NBP), np.float32)
    k = np.arange(128)[:, None]
    m = np.arange(128)[None, :]

    def put(name, a):
        o, w = BP[name]
        b[:, o:o + w] = a

    put("ident", (k == m))
    put("perm64", (k == (m ^ 32)))
    put("perm32", (k == (m ^ 16)))
    put("ones", np.ones((128, 128)))
    return b.astype(ml_dtypes.bfloat16)


def make_rope(smax):
    out = np.zeros((4, 128, smax), np.float32)
    pos = np.arange(smax, dtype=np.float32)
    for ti, dim in ((0, 64), (2, 32)):
        inv = (1.0 / (np.float32(ROPE_THETA) ** (np.arange(0, dim, 2, dtype=np.float32) / np.float32(dim)))).astype(np.float32)
        ang = pos[:, None] * inv[None, :]
        cos = np.cos(ang).astype(np.float32)
        sin = np.sin(ang).astype(np.float32)
        for f in range(128):
            d = f % dim
            jx = d % (dim // 2)
            out[ti, f] = cos[:, jx]
            out[ti + 1, f] = sin[:, jx] * (-1.0 if d < dim // 2 else 1.0)
    return out


def make_ppack(inp, l):
    p = np.zeros((128, NPP), np.float32)

    def put(name, a):
        o, w = PP[name]
        p[:, o:o + w] = np.asarray(a, np.float32).reshape(128, w)

    def fm(v, nch):
        return np.asarray(v, np.float32).reshape(nch, 128).T

    def bc(v):
        v = np.asarray(v, np.float32).reshape(-1)
        return np.broadcast_to(v[None, :], (128, v.size))

    put("wn", fm(inp["norm_mix_w"][l], 8))
    put("nf", fm(inp["norm_ffn_w"][l], 8))
    put("fin", fm(inp["final_norm_w"], 8))
    put("scw", np.concatenate([fm(inp["ssm_conv_w"][l][t], 8) for t in range(3)], axis=1))
    put("scb", fm(inp["ssm_conv_b"][l], 8))
    put("qnw", fm(inp["mla_q_norm_w"][l], 2))
    put("kvnw", fm(inp["mla_kv_norm_w"][l], 1))
    put("rnw", fm(inp["ret_norm_w"][l], 4))
    put("dnw", fm(inp["diff_norm_w"][l], 1))
    put("fcw", np.concatenate([fm(inp["ffn_conv_w"][l][t], NFF) for t in range(3)], axis=1))
    put("fcb", fm(inp["ffn_conv_b"][l], NFF))
    put("lgd", bc(inp["ret_log_decay"][l]))
    put("dtb", bc(inp["ssm_dt_bias"][l]))
    put("alog", bc(inp["ssm_a_log"][l]))
    put("ssd", bc(inp["ssm_d"][l]))
    put("snw", bc(inp["ssm_norm_w"][l]))
    put("dlam", bc(inp["diff_lambda"][l]))
    return p


class Arena:
    def __init__(self, tensor, nwords):
        self.t = tensor
        self.n = nwords
        self.off = 0
        self.base = 0

    def mark(self):
        self.base = self.off

    def reset(self):
        self.off = self.base

    def alloc(self, shape, dt=F32):
        n = int(np.prod(shape))
        words = n if dt == F32 else (n + 1) // 2
        words = (words + 7) // 8 * 8
        assert self.off + words <= self.n, f"SBUF arena overflow {self.off}+{words}>{self.n}"
        ap = self.t[:, self.off:self.off + words]
        self.off += words
        if dt != F32:
            ap = ap.bitcast(dt)
        ap = ap[:, 0:n]
        if len(shape) == 2:
            ap = ap.rearrange("p (a b) -> p a b", a=shape[0])
        elif len(shape) == 3:
            ap = ap.rearrange("p (a b c) -> p a b c", a=shape[0], b=shape[1])
        return ap


class MK:
    def __init__(self, seqs, depth, debug=(), stop_after=None):
        self.seqs = list(seqs)
        self.T = sum(seqs)
        self.depth = depth
        self.smax = max(seqs)
        self.debug = set(debug)
        self.stop_after = stop_after
        import os
        self.dbgbar = int(os.environ.get('DBGBAR', '0'))
        T = self.T
        nc = bass.Bass("TRN2", target_bir_lowering=False)
        self.nc = nc

        def din(name, shape, dt=F32):
            return nc.dram_tensor(name, shape, dt, kind="ExternalInput").ap()

        self.xT = din("xT", [1024, T])
        self.w_in = din("w_in", [depth, 1024, IN_COLS])
        self.w_uq = din("mla_w_uq", [depth, 256, 768])
        self.w_ukv = din("mla_w_ukv", [depth, 128, 1024])
        self.w_branch = din("w_branch", [depth, 4, 512, 1024])
        self.w_out = din("w_out", [depth, 1024, 1024])
        self.w_gate = din("ffn_w_gate", [depth, 1024, D_FF])
        self.w_up = din("ffn_w_up", [depth, 1024, D_FF])
        self.w_down = din("ffn_w_down", [depth, D_FF, 1024])
        self.ppack = din("ppack", [depth, 128, NPP])
        self.cpack = din("cpack", [128, NCP])
        self.bpack = din("bpack", [128, NBP], BF16)
        self.rope = din("rope", [4, 128, self.smax])
        self.yT = nc.dram_tensor("yT", [1024, T], F32, kind="ExternalOutput").ap()
        self.scr = {}

        def scr(name, shape, dt=BF16):
            kind = "ExternalOutput" if name in self.debug else "Internal"
            self.scr[name] = nc.dram_tensor(name, shape, dt, kind=kind).ap()
            return self.scr[name]

        scr("xres", [1024, T], F32)
        scr("xmid", [1024, T], F32)
        scr("rqT", [256, T]); scr("rkT", [256, T]); scr("rv", [T, 512]); scr("rgT", [512, T])
        scr("mqnT", [512, T]); scr("mqrT", [256, T]); scr("mknT", [512, T]); scr("mkrT", [32, T])
        scr("mva", [T, 1024])
        scr("dqT", [512, T]); scr("dkT", [512, T]); scr("dv", [T, 512])
        scr("sz", [T, 512]); scr("sxB", [T, 768]); scr("sBT", [256, T]); scr("sCT", [256, T])
        scr("sdt", [T, 16], F32)
        scr("boT", [2048, T])
        self.groups = []
        base = 0
        for si, S_ in enumerate(self.seqs):
            for g in range(S_ // 512):
                self.groups.append(dict(s=si, t0=base + g * 512, pos0=g * 512, S=S_, sbase=base))
            base += S_

    def build(self):
        nc = self.nc
        with ExitStack() as es:
            self.es = es
            self.S = Sched(nc)
            ARENA_WORDS = 51 * 1024
            at = es.enter_context(nc.sbuf_tensor("arena", [128, ARENA_WORDS], F32))
            self.A = Arena(at, ARENA_WORDS)
            self.banks = [es.enter_context(nc.psum_tensor(f"bank{i}", [128, 512], F32)) for i in range(8)]
            A = self.A
            S = self.S
            self.bp = A.alloc([NBP], BF16)
            self.pp = A.alloc([NPP], F32)
            S.dma(self.bp, self.bpack, writes=["bp"])
            A.mark()
            S.barrier()
            order = ["p1", "ret", "mla", "diff", "ssd", "p3", "p4"]
            sa = self.stop_after
            only = getattr(self, "only", None)
            done = False
            for l in range(self.depth):
                S.dma(self.pp, self.ppack[l], writes=["pp"])
                S.barrier()
                for ph in order:
                    run = True
                    if sa not in (None, "all") and l == self.depth - 1:
                        if sa in ("p3", "p4"):
                            run = order.index(ph) <= order.index(sa)
                        else:
                            run = ph in ("p1", sa)
                    if run:
                        {"p1": self.phase1, "ret": self.phase_ret, "mla": self.phase_mla, "diff": self.phase_diff,
                         "ssd": self.phase_ssd, "p3": self.phase3, "p4": self.phase4}[ph](l)
                        S.barrier()
            if sa in (None, "all"):
                self.phase5()
            S.emit(es)
        return nc

    def dbg(self, name, ap, reads):
        if name not in self.debug:
            return
        shp = [int(x) for x in ap.shape]
        t = self.nc.dram_tensor(name, shp, ap.dtype, kind="ExternalOutput").ap()
        self.S.dma(t, ap, reads=reads)

    def cpv(self, name):
        o, w = CP[name]
        return self.cp[:, o:o + w]

    def bpv(self, name):
        o, w = BP[name]
        return self.bp[:, o:o + w]

    def ppv(self, name, i=0, n=1):
        o, w = PP[name]
        return self.pp[:, o + i:o + i + n]

    def bank(self, i, dt=F32):
        b = self.banks[i][:]
        if dt != F32:
            b = b.bitcast(dt)
        return b

    def load_cast(self, name, pieces, stgR):
        S = self.S
        keys = []
        for i, pc in enumerate(pieces):
            dst, src = pc[0], pc[1]
            st, sk = stgR.next()
            shp = list(src.shape)
            n = int(np.prod(shp[1:]))
            stv = st[:, 0:n]
            if len(shp) == 3:
                stv = stv.rearrange("p (a b) -> p a b", a=shp[1])
            S.dma(stv, src, writes=[sk])
            if len(pc) > 2:
                stv = pc[2](stv)
            k = (name, i)
            keys.append(k)
            eng = (ACT, DVE, POOL)[i % 3]
            if eng == ACT:
                S.act(lambda e, dst=dst, stv=stv: e.activation(out=dst, in_=stv, func=AF.Copy), reads=[sk], writes=[k])
            else:
                S.add(eng, lambda e, dst=dst, stv=stv: e.tensor_copy(out=dst, in_=stv), reads=[sk], writes=[k])
        S.add(PE, None, reads=keys, writes=[name], real=False)

    def mmg(self, out, okey, pairs, reads):
        S = self.S
        n = len(pairs)
        for i, (lhsT, rhs) in enumerate(pairs):
            S.pe(lambda e, lhsT=lhsT, rhs=rhs, i=i: e.matmul(out, lhsT=lhsT, rhs=rhs, start=(i == 0), stop=(i == n - 1)),
                 reads=reads, writes=[okey])

    def rstd_from(self, ps_list, inv_n, eps, lnv, rstd, rkey, lkey):
        S = self.S
        for ps, pk, sl in ps_list:
            S.act(lambda e, ps=ps, sl=sl: e.activation(out=lnv[:, sl], in_=ps, func=AF.Ln, scale=inv_n, bias=eps),
                  reads=[pk], writes=[lkey])
        S.act(lambda e: e.activation(out=rstd, in_=lnv, func=AF.Exp, scale=-0.5), reads=[lkey], writes=[rkey])

    def phase1(self, l):
        S, A, nc = self.S, self.A, self.nc
        A.reset()
        sc = self.scr
        x_in = (self.xT if l == 0 else sc["xres"]).rearrange("(c p) t -> p c t", p=128)
        wA = A.alloc([8, NA], BF16)
        wuqn = A.alloc([2, 8, 64], BF16)
        wuqr = A.alloc([2, 8, 32], BF16)
        wukn = A.alloc([8, 64], BF16)
        wuv = A.alloc([8, 64], BF16)
        stgR = Rot([(A.alloc([1536], F32), ("stg", i)) for i in range(2)])
        win = self.w_in[l].rearrange("(c p) n -> p c n", p=128)
        pieces = []
        for k in range(8):
            for hf in range(4):
                pieces.append((wA[:, k, hf * 1260:(hf + 1) * 1260], win[:, k, hf * 1260:(hf + 1) * 1260]))
        uq = self.w_uq[l].rearrange("(c p) n -> p c n", p=128)
        pieces.append((wuqn, uq, lambda v: v.rearrange("p k (h d) -> p k h d", h=8)[:, :, :, 0:64]))
        pieces.append((wuqr, uq, lambda v: v.rearrange("p k (h d) -> p k h d", h=8)[:, :, :, 64:96]))
        pieces.append((wukn, self.w_ukv[l], lambda v: v.rearrange("p (h d) -> p h d", h=8)[:, :, 0:64]))
        pieces.append((wuv, self.w_ukv[l], lambda v: v.rearrange("p (h d) -> p h d", h=8)[:, :, 64:128]))
        self.load_cast("wA", pieces, stgR)

        xs = A.alloc([8, 514], F32)
        hTs = [A.alloc([8, 514], BF16) for _ in range(2)]
        sqR = Rot([(A.alloc([514], BF16), ("sq", i)) for i in range(2)])
        lnv = A.alloc([514], F32)
        rstd = A.alloc([514], F32)
        tab = A.alloc([4, 512], F32)
        tk = "tab"
        xsbR = Rot([(A.alloc([512], BF16), ("xsb", i)) for i in range(3)])
        t1R = Rot([(A.alloc([512], F32), ("t1", i)) for i in range(2)])
        t2R = Rot([(A.alloc([512], F32), ("t2", i)) for i in range(2)])
        roR = Rot([(A.alloc([512], BF16), ("ro", i)) for i in range(3)])
        soR = Rot([(A.alloc([512], BF16), ("so", i)) for i in range(3)])
        GR = Rot([(A.alloc([514], F32), ("G", i)) for i in range(2)])
        cvR = Rot([(A.alloc([512], F32), ("cv", i)) for i in range(2)])
        cqf = A.alloc([2, 512], F32)
        ckvf = A.alloc([512], F32)
        cqn = A.alloc([2, 512], BF16)
        ckvn = A.alloc([512], BF16)
        lnq = lnv[:, 0:512]
        rsq = rstd[:, 0:512]
        tokR = Rot([(A.alloc([512], BF16), ("tok", i)) for i in range(3)])
        vaugs = [A.alloc([8, 128], BF16) for _ in range(2)]
        xTok = A.alloc([4, 768], BF16)
        dtx = A.alloc([4, 16], F32)
        dte = A.alloc([4, 16], F32)
        dts = A.alloc([4, 16], F32)
        for i, va in enumerate(vaugs):
            S.pool(lambda e, va=va: e.memset(va, 1.0), writes=[("vaug", i)])
        mainR = Rot([(self.bank(i), ("ps", i)) for i in range(4)])
        psA = self.bank(4)
        psH = self.bank(5)
        permR = Rot([(self.bank(6), ("ps", 6))])
        psT = self.bank(7, BF16)
        onesb = self.bpv("ones")
        ident = self.bpv("ident")
        ropeT = self.rope.rearrange("a p t -> p a t")
        tcount = [0]

        pend = []

        def flush():
            while pend:
                pend.pop(0)()

        def MM(out, okey, pairs, reads):
            self.mmg(out, okey, pairs, reads)
            flush()

        def rope_unit(ps, pk, M, scale, ci, perm, dst):
            tab, tk = self.cur_tab
            Ct = tab[0:M, ci, :]
            St = tab[0:M, ci + 1, :]
            xsb, k1 = xsbR.next()
            S.act(lambda e: e.activation(out=xsb, in_=ps, func=AF.Copy, scale=scale),
                  reads=[pk], writes=[k1])
            t1, kt1 = t1R.next()
            S.pool(lambda e: e.tensor_tensor(out=t1[0:M, :], in0=xsb[0:M, :], in1=Ct, op=ALU.mult),
                   reads=[k1, tk], writes=[kt1])

            def part_b():
                pp_, pk2 = permR.next()
                S.pe(lambda e: e.matmul(pp_, lhsT=perm, rhs=xsb, start=True, stop=True),
                     reads=[k1, "bp"], writes=[pk2])
                t2, kt2 = t2R.next()
                S.dve(lambda e: e.tensor_tensor(out=t2[0:M, :], in0=pp_[0:M, :], in1=St, op=ALU.mult),
                      reads=[pk2, tk], writes=[kt2])
                ro, kro = roR.next()
                S.dve(lambda e: e.tensor_tensor(out=ro[0:M, :], in0=t1[0:M, :], in1=t2[0:M, :], op=ALU.add),
                      reads=[kt1, kt2], writes=[kro])
                S.dma(dst, ro[0:M, :], reads=[kro])

            pend.append(part_b)

        def load_x(gi_):
            if gi_ >= len(self.groups):
                return
            g_ = self.groups[gi_]
            hasL = g_["pos0"] > 0
            hasR = g_["pos0"] + 512 < g_["S"]
            if not hasL:
                S.pool(lambda e: e.memset(xs[:, :, 0:1], 0.0), writes=["xsh"])
            if not hasR:
                S.pool(lambda e: e.memset(xs[:, :, 513:514], 0.0), writes=["xsh"])
            lo = g_["t0"] - 1 if hasL else g_["t0"]
            hi = g_["t0"] + 513 if hasR else g_["t0"] + 512
            c0 = 0 if hasL else 1
            S.dma(xs[:, :, c0:c0 + (hi - lo)], x_in[:, :, lo:hi], writes=["xs"])

        def load_tab(gi_):
            if gi_ >= len(self.groups):
                return
            p0_ = self.groups[gi_]["pos0"]
            S.dma(tab, ropeT[:, :, p0_:p0_ + 512], writes=[tk])

        def do_rms(gi_):
            if gi_ >= len(self.groups):
                return
            hT_ = hTs[gi_ % 2]
            hk_ = [("hT", gi_ % 2, k) for k in range(8)]
            for k in range(8):
                sq, sqk = sqR.next()
                S.act(lambda e, sq=sq, k=k: e.activation(out=sq, in_=xs[:, k, :], func=AF.Square), reads=["xs", "xsh"], writes=[sqk])
                S.pe(lambda e, sq=sq, k=k: e.matmul(psA, lhsT=onesb, rhs=sq[:, 0:512], start=(k == 0), stop=(k == 7)),
                     reads=[sqk, "bp"], writes=[("ps", 4)])
                S.pe(lambda e, sq=sq, k=k: e.matmul(psH[:, 0:2], lhsT=onesb, rhs=sq[:, 512:514], start=(k == 0), stop=(k == 7)),
                     reads=[sqk, "bp"], writes=[("ps", 5)])
            self.rstd_from([(psA, ("ps", 4), slice(0, 512)), (psH[:, 0:2], ("ps", 5), slice(512, 514))],
                           1.0 / 1024, NORM_EPS, lnv, rstd, "rstd", "lnv")
            for k in range(8):
                S.dve(lambda e, k=k: e.scalar_tensor_tensor(out=hT_[:, k, :], in0=xs[:, k, :], scalar=self.ppv("wn", k),
                                                             in1=rstd, op0=ALU.mult, op1=ALU.mult),
                      reads=["xs", "xsh", "rstd", "pp"], writes=[hk_[k]])
            load_x(gi_ + 1)

        def group_body(gi, g):
            t0, pos0, Sq = g["t0"], g["pos0"], g["S"]
            hT = hTs[gi % 2]
            hk = [("hT", gi % 2, k) for k in range(8)]
            if gi == 0:
                load_x(0)
                load_tab(0)
            self.cur_tab = (tab, tk)
            if gi == 0:
                do_rms(0)

            def fm_mm(c0_, M):
                ps, pk = mainR.next()
                MM(ps[0:M, :], pk, [(wA[:, k, c0_:c0_ + M], hT[:, k, 1:513]) for k in range(8)], hk + ["wA"])
                return ps, pk

            def ret_unit(c0_, j, scale, dname):
                ps, pk = fm_mm(c0_ + j * 128, 128)
                rope_unit(ps, pk, 128, scale, 0, self.bpv("perm64"), sc[dname][j * 128:(j + 1) * 128, t0:t0 + 512])

            for j in range(2):
                ps, pk = fm_mm(C_CQ + j * 128, 128)
                S.act(lambda e, ps=ps, j=j: e.activation(out=cqf[:, j, :], in_=ps, func=AF.Copy), reads=[pk], writes=[("cqf", j)])
                sq, sqk = sqR.next()
                S.act(lambda e, sq=sq, j=j: e.activation(out=sq[:, 0:512], in_=cqf[:, j, :], func=AF.Square),
                      reads=[("cqf", j)], writes=[sqk])
                pend.append(lambda sq=sq, j=j, sqk=sqk: S.pe(
                    lambda e: e.matmul(psA, lhsT=onesb, rhs=sq[:, 0:512], start=(j == 0), stop=(j == 1)),
                    reads=[sqk, "bp"], writes=[("ps", 4)]))
            ret_unit(C_RQ, 0, 1.0, "rqT")
            self.rstd_from([(psA, ("ps", 4), slice(0, 512))], 1.0 / 256, NORM_EPS, lnq, rsq, "rstd", "lnv")
            for j in range(2):
                S.dve(lambda e, j=j: e.scalar_tensor_tensor(out=cqn[:, j, :], in0=cqf[:, j, :], scalar=self.ppv("qnw", j),
                                                             in1=rsq, op0=ALU.mult, op1=ALU.mult),
                      reads=[("cqf", j), "rstd", "pp"], writes=[("cqn", j)])
            cqk = [("cqn", 0), ("cqn", 1)]
            ps, pk = fm_mm(C_CKV, 128)
            S.act(lambda e, ps=ps: e.activation(out=ckvf, in_=ps, func=AF.Copy), reads=[pk], writes=["ckvf"])
            sq, sqk = sqR.next()
            S.act(lambda e, sq=sq: e.activation(out=sq[:, 0:512], in_=ckvf, func=AF.Square), reads=["ckvf"], writes=[sqk])
            pend.append(lambda sq=sq, sqk=sqk: S.pe(lambda e: e.matmul(psA, lhsT=onesb, rhs=sq[:, 0:512], start=True, stop=True),
                                                    reads=[sqk, "bp"], writes=[("ps", 4)]))
            ret_unit(C_RQ, 1, 1.0, "rqT")
            self.rstd_from([(psA, ("ps", 4), slice(0, 512))], 1.0 / 128, NORM_EPS, lnq, rsq, "rstd", "lnv")
            S.dve(lambda e: e.scalar_tensor_tensor(out=ckvn, in0=ckvf, scalar=self.ppv("kvnw", 0), in1=rsq,
                                                   op0=ALU.mult, op1=ALU.mult),
                  reads=["ckvf", "rstd", "pp"], writes=["ckvn"])
            ret_unit(C_RK, 0, 0.125, "rkT")
            ret_unit(C_RK, 1, 0.125, "rkT")
            for i in range(4):
                ps, pk = mainR.next()
                MM(ps, pk, [(wuqn[:, k, 2 * i:2 * i + 2, :].rearrange("p a b -> p (a b)"), cqn[:, k, :]) for k in range(2)], cqk + ["wA"])
                so, sk = soR.next()
                S.act(lambda e, ps=ps, so=so: e.activation(out=so, in_=ps, func=AF.Copy), reads=[pk], writes=[sk])
                S.dma(sc["mqnT"][i * 128:(i + 1) * 128, t0:t0 + 512], so, reads=[sk])
            for i in range(2):
                ps, pk = mainR.next()
                MM(ps, pk, [(wuqr[:, k, 4 * i:4 * i + 4, :].rearrange("p a b -> p (a b)"), cqn[:, k, :]) for k in range(2)], cqk + ["wA"])
                rope_unit(ps, pk, 128, 1.0, 2, self.bpv("perm32"), sc["mqrT"][i * 128:(i + 1) * 128, t0:t0 + 512])
            for i in range(4):
                ps, pk = mainR.next()
                MM(ps, pk, [(wukn[:, 2 * i:2 * i + 2, :].rearrange("p a b -> p (a b)"), ckvn)], ["ckvn", "wA"])
                so, sk = soR.next()
                S.act(lambda e, ps=ps, so=so: e.activation(out=so, in_=ps, func=AF.Copy), reads=[pk], writes=[sk])
                S.dma(sc["mknT"][i * 128:(i + 1) * 128, t0:t0 + 512], so, reads=[sk])
            for tt in range(4):
                ps, pk = mainR.next()
                MM(ps, pk, [(ckvn[:, tt * 128:(tt + 1) * 128], wuv.rearrange("p a b -> p (a b)"))], ["ckvn", "wA"])
                va = vaugs[tt % 2]
                vk = ("vaug", tt % 2)
                S.act(lambda e, ps=ps, va=va: e.activation(out=va[:, :, 0:64], in_=ps.rearrange("p (h d) -> p h d", h=8),
                                                           func=AF.Copy), reads=[pk], writes=[vk])
                S.dma(sc["mva"][t0 + tt * 128:t0 + (tt + 1) * 128, :], va.rearrange("p h d -> p (h d)"), reads=[vk])
            ps, pk = fm_mm(C_KR, 128)
            rope_unit(ps, pk, 32, 1.0, 2, self.bpv("perm32"), sc["mkrT"][0:32, t0:t0 + 512])
            for tt in range(4):
                self.mmg(psH[:, 32 + tt * 16:48 + tt * 16], ("ps", 5),
                         [(hT[:, k, 1 + tt * 128:1 + (tt + 1) * 128], wA[:, k, C_DT:C_DT + 16]) for k in range(8)], hk + ["wA"])
            o_, w_ = PP["dtb"]
            dtb_b = self.pp[:, o_:o_ + 16].unsqueeze(1).broadcast_to([128, 4, 16])
            S.dve(lambda e: e.tensor_tensor(out=dtx, in0=psH[:, 32:96].rearrange("p (a b) -> p a b", a=4), in1=dtb_b, op=ALU.add),
                  reads=[("ps", 5), "pp"], writes=["dtx"])
            S.act(lambda e: e.activation(out=dte, in_=dtx, func=AF.Exp), reads=["dtx"], writes=["dte"])
            S.act(lambda e: e.activation(out=dts, in_=dte, func=AF.Ln, bias=1.0), reads=["dte"], writes=["dts"])
            S.dma(sc["sdt"][t0:t0 + 512, :].rearrange("(a p) c -> p a c", p=128), dts, reads=["dts"])
            for (c0_, nch, scale, dname) in ((C_DQ, 4, 1.0, "dqT"), (C_DK, 4, 1.0, "dkT")):
                for j in range(nch):
                    ret_unit(c0_, j, scale, dname)
            flush()
            load_tab(gi + 1)
            do_rms(gi + 1)
            for (c0_, dname) in ((C_RV, "rv"), (C_DV, "dv")):
                for tt in range(4):
                    ps, pk = mainR.next()
                    MM(ps, pk, [(hT[:, k, 1 + tt * 128:1 + (tt + 1) * 128], wA[:, k, c0_:c0_ + 512]) for k in range(8)], hk + ["wA"])
                    tk_, tkk = tokR.next()
                    S.act(lambda e, ps=ps, tk_=tk_: e.activation(out=tk_, in_=ps, func=AF.Copy), reads=[pk], writes=[tkk])
                    S.dma(sc[dname][t0 + tt * 128:t0 + (tt + 1) * 128, :], tk_, reads=[tkk])
            for j in range(4):
                ps, pk = fm_mm(C_RG + j * 128, 128)
                so, sk = soR.next()
                S.act(lambda e, ps=ps, so=so: e.activation(out=so, in_=ps, func=AF.Silu), reads=[pk], writes=[sk])
                S.dma(sc["rgT"][j * 128:(j + 1) * 128, t0:t0 + 512], so, reads=[sk])
            for tt in range(4):
                ps, pk = mainR.next()
                MM(ps, pk, [(hT[:, k, 1 + tt * 128:1 + (tt + 1) * 128], wA[:, k, C_SZ:C_SZ + 512]) for k in range(8)], hk + ["wA"])
                tk_, tkk = tokR.next()
                S.act(lambda e, ps=ps, tk_=tk_: e.activation(out=tk_, in_=ps, func=AF.Silu), reads=[pk], writes=[tkk])
                S.dma(sc["sz"][t0 + tt * 128:t0 + (tt + 1) * 128, :], tk_, reads=[tkk])
            late_q = [None]
            for j in range(8):
                cc = C_XBC + j * 128
                ps, pk = fm_mm(cc, 128)
                hcol = 2 + 2 * j
                hkey = ("ps", 5)
                for k in range(8):
                    S.pe(lambda e, k=k, cc=cc, hcol=hcol: e.matmul(psH[:, hcol:hcol + 2], lhsT=wA[:, k, cc:cc + 128],
                                                                    rhs=hT[:, k, 0:514:513], start=(k == 0), stop=(k == 7)),
                         reads=hk + ["wA"], writes=[hkey])
                G, gk = GR.next()
                S.act(lambda e, ps=ps, G=G: e.activation(out=G[:, 1:513], in_=ps, func=AF.Copy), reads=[pk], writes=[gk])
                S.dve(lambda e, G=G, hcol=hcol: e.tensor_copy(out=G[:, 0:514:513], in_=psH[:, hcol:hcol + 2]), reads=[hkey], writes=[(gk, "h")])
                cv, ck = cvR.next()
                o_w = PP["scw"][0]
                o_b = PP["scb"][0]
                S.dve(lambda e, G=G, cv=cv, j=j: e.tensor_scalar(out=cv, in0=G[:, 1:513], scalar1=self.pp[:, o_w + 8 + j:o_w + 9 + j],
                                                                  scalar2=self.pp[:, o_b + j:o_b + j + 1], op0=ALU.mult, op1=ALU.add),
                      reads=[gk, "pp"], writes=[ck])
                S.dve(lambda e, G=G, cv=cv, j=j: e.scalar_tensor_tensor(out=cv, in0=G[:, 0:512], scalar=self.pp[:, o_w + j:o_w + j + 1],
                                                                         in1=cv, op0=ALU.mult, op1=ALU.add),
                      reads=[gk, (gk, "h"), "pp", ck], writes=[ck])
                S.dve(lambda e, G=G, cv=cv, j=j: e.scalar_tensor_tensor(out=cv, in0=G[:, 2:514], scalar=self.pp[:, o_w + 16 + j:o_w + 17 + j],
                                                                         in1=cv, op0=ALU.mult, op1=ALU.add),
                      reads=[gk, (gk, "h"), "pp", ck], writes=[ck])
                def late(cv=cv, ck=ck, j=j):
                    so, sk = soR.next()
                    S.act(lambda e, cv=cv, so=so: e.activation(out=so, in_=cv, func=AF.Silu), reads=[ck], writes=[sk])
                    if j >= 4:
                        dname, r0 = ("sBT", (j - 4) * 128) if j < 6 else ("sCT", (j - 6) * 128)
                        S.dma(sc[dname][r0:r0 + 128, t0:t0 + 512], so, reads=[sk])
                    if j < 6:
                        def tr_part(so=so, sk=sk, j=j):
                            half = tcount[0] % 2
                            tcount[0] += 1
                            pT = psT[:, half * 512:(half + 1) * 512]
                            ptk = ("ps", 7)
                            for tt in range(4):
                                S.pe(lambda e, tt=tt: e.transpose(out=pT[:, tt * 128:(tt + 1) * 128], in_=so[:, tt * 128:(tt + 1) * 128],
                                                                  identity=ident), reads=[sk, "bp"], writes=[ptk])
                            S.dve(lambda e: e.tensor_copy(out=xTok[:, :, j * 128:(j + 1) * 128],
                                                          in_=pT.rearrange("p (a b) -> p a b", a=4)),
                                  reads=[ptk], writes=[("xTok", j)])
                        pend.append(tr_part)
                if late_q[0] is not None:
                    late_q[0]()
                late_q[0] = late
            late_q[0]()
            flush()
            S.dma(sc["sxB"][t0:t0 + 512, :].rearrange("(a p) c -> p a c", p=128), xTok, reads=[("xTok", j) for j in range(6)])

        for gi, g in enumerate(self.groups):
            group_body(gi, g)


def common_inputs(inp, depth, smax):
    f = lambda n: np.ascontiguousarray(np.asarray(inp[n], np.float32))
    m = {
        "w_in": f("w_in"), "mla_w_uq": f("mla_w_uq"), "mla_w_ukv": f("mla_w_ukv"), "w_branch": f("w_branch"),
        "w_out": f("w_out"), "ffn_w_gate": f("ffn_w_gate"), "ffn_w_up": f("ffn_w_up"), "ffn_w_down": f("ffn_w_down"),
        "ppack": np.stack([make_ppack(inp, l) for l in range(depth)]),
        "cpack": make_cpack(), "bpack": make_bpack(), "rope": make_rope(smax),
    }
    return m


def _seq_list(mk):
    out = []
    base = 0
    for S_ in mk.seqs:
        out.append((base, S_))
        base += S_
    return out


def phase_mla(self, l):
    S, A = self.S, self.A
    A.reset()
    sc = self.scr
    smax = self.smax
    NKCm = smax // 128
    Vaug = A.alloc([NKCm, 8, 128], BF16)
    KTs = [A.alloc([smax], BF16) for _ in range(2)]
    QTs = [A.alloc([smax], BF16) for _ in range(2)]
    pTR = Rot([(A.alloc([512], BF16), ("pT", i)) for i in range(4)])
    rsR = Rot([(A.alloc([512], F32), ("rs", i)) for i in range(2)])
    obR = Rot([(A.alloc([512], BF16), ("ob", i)) for i in range(2)])
    scR = Rot([(self.bank(i), ("ps", i)) for i in range(4)])
    accR = Rot([(self.bank(4 + i), ("ps", 4 + i)) for i in range(3)])
    scale = 96.0 ** -0.5
    LA = 2
    for (base, Sq) in _seq_list(self):
        NKC = Sq // 128
        NQG = Sq // 512
        for c0 in range(0, NKC, 4):
            S.dma(Vaug[:, c0:c0 + 4, :, :].rearrange("p c h d -> p c (h d)"),
                  sc["mva"][base + c0 * 128:base + (c0 + 4) * 128, :].rearrange("(c p) e -> p c e", p=128),
                  writes=[("V", c0)])
        vkeys = [("V", c0) for c0 in range(0, NKC, 4)]

        def load_head(h, base=base, Sq=Sq):
            KT, QT = KTs[h % 2], QTs[h % 2]
            S.dma(KT[0:64, 0:Sq], sc["mknT"][h * 64:(h + 1) * 64, base:base + Sq], writes=[("KT", h % 2, 0)])
            S.dma(KT[64:96, 0:Sq], sc["mkrT"][0:32, base:base + Sq], writes=[("KT", h % 2, 1)])
            S.dma(QT[0:64, 0:Sq], sc["mqnT"][h * 64:(h + 1) * 64, base:base + Sq], writes=[("QT", h % 2, 0)])
            S.dma(QT[64:96, 0:Sq], sc["mqrT"][h * 32:(h + 1) * 32, base:base + Sq], writes=[("QT", h % 2, 1)])

        items = [(h, qg, kc) for h in range(8) for qg in range(NQG) for kc in range(NKC)]
        state = {}

        def do_S(it):
            h, qg, kc = it
            if qg == 0 and kc == 0:
                if h == 0:
                    load_head(0)
                if h + 1 < 8:
                    load_head(h + 1)
            KT, QT = KTs[h % 2], QTs[h % 2]
            ps, pk = scR.next()
            state[it] = (ps, pk)
            S.pe(lambda e: e.matmul(ps, lhsT=KT[0:96, kc * 128:(kc + 1) * 128], rhs=QT[0:96, qg * 512:(qg + 1) * 512],
                                    start=True, stop=True),
                 reads=[("KT", h % 2, 0), ("KT", h % 2, 1), ("QT", h % 2, 0), ("QT", h % 2, 1)], writes=[pk])
            pT, ptk = pTR.next()
            S.act(lambda e: e.activation(out=pT, in_=ps, func=AF.Exp, scale=scale), reads=[pk], writes=[ptk])
            state[it] = (pT, ptk)

        def do_PV(it, base=base):
            h, qg, kc = it
            pT, ptk = state.pop(it)
            if kc == 0:
                state[("acc", h, qg)] = accR.next()
            acc, ak = state[("acc", h, qg)]
            S.pe(lambda e: e.matmul(acc, lhsT=Vaug[:, kc, h, :], rhs=pT, start=(kc == 0), stop=(kc == NKC - 1)),
                 reads=[ptk] + vkeys, writes=[ak])
            if kc == NKC - 1:
                rs, rk = rsR.next()
                S.dve(lambda e: e.reciprocal(out=rs[0:64, :], in_=acc[64:128, :]), reads=[ak], writes=[rk])
                ob, ok = obR.next()
                S.dve(lambda e: e.tensor_tensor(out=ob[0:64, :], in0=acc[0:64, :], in1=rs[0:64, :], op=ALU.mult),
                      reads=[ak, rk], writes=[ok])
                t0 = base + qg * 512
                S.dma(sc["boT"][512 + h * 64:512 + (h + 1) * 64, t0:t0 + 512], ob[0:64, :], reads=[ok])
                del state[("acc", h, qg)]

        n = len(items)
        for i in range(n + LA):
            if i < n:
                do_S(items[i])
            if i - LA >= 0:
                do_PV(items[i - LA])


def phase_diff(self, l):
    S, A = self.S, self.A
    A.reset()
    sc = self.scr
    smax = self.smax
    NKCm = smax // 128
    lam_init = 0.8 - 0.6 * math.exp(-0.3 * l)
    V = A.alloc([NKCm, 512], BF16)
    KTz = [[A.alloc([smax], BF16) for _ in range(2)] for _ in range(2)]
    QTs = [A.alloc([smax], BF16) for _ in range(2)]
    for b_ in range(2):
        S.pool(lambda e, b_=b_: e.memset(KTz[b_][0][64:128, :], 0.0), writes=[("KT", b_, 0)])
        S.pool(lambda e, b_=b_: e.memset(KTz[b_][1][0:64, :], 0.0), writes=[("KT", b_, 1)])
    pTR = Rot([(A.alloc([512], BF16), ("pT", i)) for i in range(4)])
    rR = Rot([(A.alloc([512], F32), ("r", i)) for i in range(2)])
    tR = [A.alloc([512], F32) for _ in range(2)]
    dbuf = A.alloc([512], F32)
    sqb = A.alloc([512], BF16)
    lnb = A.alloc([512], F32)
    rsb = A.alloc([512], F32)
    obR = Rot([(A.alloc([512], BF16), ("ob", i)) for i in range(2)])
    junk = A.alloc([64], F32)
    sv = A.alloc([8], F32)
    scR = Rot([(self.bank(i), ("ps", i)) for i in range(3)])
    accO = [self.bank(3), self.bank(5)]
    accS = [self.bank(4), self.bank(6)]
    psN = self.bank(7)
    onesb = self.bpv("ones")
    o_, _w = PP["dlam"]
    dl = self.pp[:, o_:o_ + 256]
    for i in range(2):
        S.dve(lambda e, i=i: e.scalar_tensor_tensor(out=junk, in0=dl[:, i * 128:i * 128 + 64], scalar=1.0,
                                                    in1=dl[:, i * 128 + 64:i * 128 + 128], op0=ALU.mult, op1=ALU.mult,
                                                    accum_out=sv[:, i:i + 1]), reads=["pp", "junk"], writes=["junk", ("sv", i)])
    S.act(lambda e: e.activation(out=sv[:, 2:4], in_=sv[:, 0:2], func=AF.Exp), reads=[("sv", 0), ("sv", 1)], writes=["sve"])
    S.dve(lambda e: e.tensor_tensor(out=sv[:, 4:5], in0=sv[:, 3:4], in1=sv[:, 2:3], op=ALU.subtract), reads=["sve"], writes=["nl0"])
    S.dve(lambda e: e.tensor_scalar(out=sv[:, 5:6], in0=sv[:, 4:5], scalar1=-lam_init, scalar2=None, op0=ALU.add),
          reads=["nl0"], writes=["neglam"])
    S.dve(lambda e: e.tensor_scalar(out=sv[:, 6:7], in0=self.ppv("dnw", 0), scalar1=(1.0 - lam_init), scalar2=None, op0=ALU.mult),
          reads=["pp"], writes=["wd"])
    neglam = sv[:, 5:6]
    wd = sv[:, 6:7]
    LA = 2
    for (base, Sq) in _seq_list(self):
        NKC = Sq // 128
        NQG = Sq // 512
        for c0 in range(0, NKC, 8):
            S.dma(V[:, c0:c0 + 8, :], sc["dv"][base + c0 * 128:base + (c0 + 8) * 128, :].rearrange("(c p) e -> p c e", p=128),
                  writes=[("V", c0)])
        vkeys = [("V", c0) for c0 in range(0, NKC, 8)]

        def load_head(h, base=base, Sq=Sq):
            S.dma(KTz[h % 2][0][0:64, 0:Sq], sc["dkT"][h * 128:h * 128 + 64, base:base + Sq], writes=[("KT", h % 2, 0)])
            S.dma(KTz[h % 2][1][64:128, 0:Sq], sc["dkT"][h * 128 + 64:(h + 1) * 128, base:base + Sq], writes=[("KT", h % 2, 1)])
            S.dma(QTs[h % 2][:, 0:Sq], sc["dqT"][h * 128:(h + 1) * 128, base:base + Sq], writes=[("QT", h % 2)])

        items = [(h, qg, m, kc) for h in range(4) for qg in range(NQG) for m in range(2) for kc in range(NKC)]
        state = {}

        def do_S(it):
            h, qg, m, kc = it
            if qg == 0 and kc == 0 and m == 0:
                if h == 0:
                    load_head(0)
                if h + 1 < 4:
                    load_head(h + 1)
            KT, QT = KTz[h % 2][m], QTs[h % 2]
            ps, pk = scR.next()
            S.pe(lambda e: e.matmul(ps, lhsT=KT[:, kc * 128:(kc + 1) * 128],
                                    rhs=QT[:, qg * 512:(qg + 1) * 512], start=True, stop=True),
                 reads=[("KT", h % 2, m), ("QT", h % 2)], writes=[pk])
            pT, ptk = pTR.next()
            S.act(lambda e: e.activation(out=pT, in_=ps, func=AF.Exp, scale=0.125), reads=[pk], writes=[ptk])
            state[it] = (pT, ptk)

        def do_PV(it, base=base):
            h, qg, m, kc = it
            pT, ptk = state.pop(it)
            aO, aS = accO[m], accS[m]
            kO, kS = ("ps", 3 + 2 * m), ("ps", 4 + 2 * m)
            S.pe(lambda e: e.matmul(aO, lhsT=V[:, kc, h * 128:(h + 1) * 128], rhs=pT, start=(kc == 0), stop=(kc == NKC - 1)),
                 reads=[ptk] + vkeys, writes=[kO])
            S.pe(lambda e: e.matmul(aS, lhsT=onesb, rhs=pT, start=(kc == 0), stop=(kc == NKC - 1)),
                 reads=[ptk, "bp"], writes=[kS])
            if kc == NKC - 1:
                r, rk = rR.next()
                S.dve(lambda e: e.reciprocal(out=r, in_=aS), reads=[kS], writes=[rk])
                t = tR[m]
                S.dve(lambda e: e.tensor_tensor(out=t, in0=aO, in1=r, op=ALU.mult), reads=[kO, rk], writes=[("t", m)])
                if m == 1:
                    S.dve(lambda e: e.scalar_tensor_tensor(out=dbuf, in0=tR[1], scalar=neglam, in1=tR[0], op0=ALU.mult, op1=ALU.add),
                          reads=[("t", 0), ("t", 1), "neglam"], writes=["dbuf"])
                    S.dve(lambda e: e.tensor_tensor(out=sqb, in0=dbuf, in1=dbuf, op=ALU.mult), reads=["dbuf"], writes=["sqb"])
                    S.pe(lambda e: e.matmul(psN, lhsT=onesb, rhs=sqb, start=True, stop=True), reads=["sqb", "bp"], writes=[("ps", 7)])
                    self.rstd_from([(psN, ("ps", 7), slice(0, 512))], 1.0 / 128, HEAD_EPS, lnb, rsb, "rsb", "lnb")
                    ob, ok = obR.next()
                    S.dve(lambda e: e.scalar_tensor_tensor(out=ob, in0=dbuf, scalar=wd, in1=rsb, op0=ALU.mult, op1=ALU.mult),
                          reads=["dbuf", "rsb", "wd"], writes=[ok])
                    t0 = base + qg * 512
                    S.dma(sc["boT"][1024 + h * 128:1024 + (h + 1) * 128, t0:t0 + 512], ob, reads=[ok])

        n = len(items)
        for i in range(n + LA):
            if i < n:
                do_S(items[i])
            if i - LA >= 0:
                do_PV(items[i - LA])


MK.phase_mla = phase_mla
MK.phase_diff = phase_diff


def phase_ret(self, l):
    S, A = self.S, self.A
    A.reset()
    self.load_cp()
    sc = self.scr
    smax = self.smax
    Nm = smax // 128
    qT = A.alloc([2, smax], BF16)
    kT = A.alloc([2, smax], BF16)
    vR = Rot([(A.alloc([4, 512], BF16), ("v", i)) for i in range(3)])
    Sfb = A.alloc([Nm, 512], BF16)
    Sbb = A.alloc([Nm, 512], BF16)
    kvbs = A.alloc([Nm, 512], BF16)
    Sf = A.alloc([512], F32)
    Sb = A.alloc([512], F32)
    MT = A.alloc([4, 128], F32)
    e1t = A.alloc([128], F32)
    Qdf = A.alloc([2, 128], F32)
    Qdb = A.alloc([2, 128], F32)
    Kdf = A.alloc([4, 64], F32)
    Kdb = A.alloc([4, 64], F32)
    Gcf = A.alloc([4, 128], F32)
    Gcb = A.alloc([4, 128], F32)
    kdfR = Rot([(A.alloc([256], BF16), ("kdf", i)) for i in range(2)])
    kdbR = Rot([(A.alloc([256], BF16), ("kdb", i)) for i in range(2)])
    qdR = Rot([(A.alloc([512], BF16), ("qd", i)) for i in range(4)])
    aTR = Rot([(A.alloc([512], BF16), ("aT", i)) for i in range(2)])
    gR = Rot([(A.alloc([512], BF16), ("g", i)) for i in range(2)])
    sqb = A.alloc([512], BF16)
    lnb = A.alloc([512], F32)
    rsb = A.alloc([512], F32)
    tb = A.alloc([512], F32)
    obR = Rot([(A.alloc([512], BF16), ("ob", i)) for i in range(2)])
    ident = self.bpv("ident")
    onesb = self.bpv("ones")
    o_lg = PP["lgd"][0]

    def lg(d, h, p0=0, p1=128):
        return self.pp[p0:p1, o_lg + d * 4 + h:o_lg + d * 4 + h + 1]

    for h in range(4):
        S.act(lambda e, h=h: e.activation(out=MT[:, h, :], in_=self.cpv("Dpos"), func=AF.Exp, scale=lg(0, h)), reads=["pp"], writes=[("MT", h)])
        S.act(lambda e, h=h: e.activation(out=e1t, in_=self.cpv("Dneg"), func=AF.Exp, scale=lg(1, h)), reads=["pp"], writes=["e1t"])
        S.dve(lambda e, h=h: e.tensor_tensor(out=MT[:, h, :], in0=MT[:, h, :], in1=self.cpv("Mgt"), op=ALU.mult), reads=[("MT", h)], writes=[("MT", h)])
        S.dve(lambda e, h=h: e.tensor_tensor(out=e1t, in0=e1t, in1=self.cpv("Mlt"), op=ALU.mult), reads=["e1t"], writes=["e1t"])
        S.dve(lambda e, h=h: e.tensor_tensor(out=MT[:, h, :], in0=MT[:, h, :], in1=e1t, op=ALU.add), reads=[("MT", h), "e1t"], writes=[("MT", h)])
        S.dve(lambda e, h=h: e.tensor_tensor(out=MT[:, h, :], in0=MT[:, h, :], in1=self.cpv("I2"), op=ALU.add), reads=[("MT", h)], writes=[("MT", h)])
        c, r0 = h // 2, (h % 2) * 64
        S.act(lambda e, h=h, c=c, r0=r0: e.activation(out=Qdf[r0:r0 + 64, c, :], in_=self.cpv("Ip1")[r0:r0 + 64, :], func=AF.Exp,
                                                       scale=lg(0, h, r0, r0 + 64)), reads=["pp"], writes=[("Qd", h)])
        S.act(lambda e, h=h, c=c, r0=r0: e.activation(out=Qdb[r0:r0 + 64, c, :], in_=self.cpv("Cmi")[r0:r0 + 64, :], func=AF.Exp,
                                                       scale=lg(1, h, r0, r0 + 64)), reads=["pp"], writes=[("Qd", h)])
        S.act(lambda e, h=h: e.activation(out=Kdf[:, h, :], in_=self.cpv("Cm1j"), func=AF.Exp, scale=lg(0, h)), reads=["pp"], writes=[("Kd", h)])
        S.act(lambda e, h=h: e.activation(out=Kdb[:, h, :], in_=self.cpv("Jj"), func=AF.Exp, scale=lg(1, h)), reads=["pp"], writes=[("Kd", h)])
        S.act(lambda e, h=h: e.activation(out=Gcf[0:64, h, :], in_=self.cpv("C128")[0:64, :], func=AF.Exp, scale=lg(0, h, 0, 64)), reads=["pp"], writes=[("Gc", h)])
        S.act(lambda e, h=h: e.activation(out=Gcb[0:64, h, :], in_=self.cpv("C128")[0:64, :], func=AF.Exp, scale=lg(1, h, 0, 64)), reads=["pp"], writes=[("Gc", h)])
    tabkeys = [("MT", h) for h in range(4)] + [("Qd", h) for h in range(4)] + [("Kd", h) for h in range(4)] + [("Gc", h) for h in range(4)]
    psT = self.bank(0, BF16)
    kvfB, kvbB = self.bank(1), self.bank(2)
    Gcf2 = Gcf.rearrange("p a b -> p (a b)")
    Gcb2 = Gcb.rearrange("p a b -> p (a b)")
    Kdf2 = Kdf.rearrange("p a b -> p (a b)")
    Kdb2 = Kdb.rearrange("p a b -> p (a b)")
    kd_keys = [("Kd", h) for h in range(4)]
    gc_keys = [("Gc", h) for h in range(4)]

    for (base, Sq) in _seq_list(self):
        N = Sq // 128
        S.dma(qT[:, :, 0:Sq], sc["rqT"][:, base:base + Sq].rearrange("(c p) t -> p c t", p=128), writes=["qT"])
        S.dma(kT[:, :, 0:Sq], sc["rkT"][:, base:base + Sq].rearrange("(c p) t -> p c t", p=128), writes=["kT"])

        def load_v(G, base=base):
            vt, vk = vR.next()
            S.dma(vt, sc["rv"][base + G * 512:base + (G + 1) * 512, :].rearrange("(c p) e -> p c e", p=128), writes=[vk])
            return vt, vk
        S.pool(lambda e: e.memset(Sf[0:64, :], 0.0), writes=["Sf"])
        S.pool(lambda e: e.memset(Sb[0:64, :], 0.0), writes=["Sb"])

        def pass1(n, vt, vk):
            vkeys = [vk]
            half = n % 2
            pT_ = psT[:, half * 512:half * 512 + 256]
            ptk = ("ps", 0)
            for c in range(2):
                S.pe(lambda e, c=c: e.transpose(out=pT_[:, c * 128:(c + 1) * 128], in_=kT[:, c, n * 128:(n + 1) * 128], identity=ident),
                     reads=["kT", "bp"], writes=[ptk])
            kdf, kfk = kdfR.next()
            kdb, kbk = kdbR.next()
            S.dve(lambda e: e.tensor_tensor(out=kdf, in0=pT_, in1=Kdf2, op=ALU.mult), reads=[ptk] + kd_keys, writes=[kfk])
            S.dve(lambda e: e.tensor_tensor(out=kdb, in0=pT_, in1=Kdb2, op=ALU.mult), reads=[ptk] + kd_keys, writes=[kbk])
            for h in range(4):
                S.pe(lambda e, h=h: e.matmul(kvfB[0:64, h * 128:(h + 1) * 128], lhsT=kdf[:, h * 64:(h + 1) * 64],
                                             rhs=vt[:, n % 4, h * 128:(h + 1) * 128], start=True, stop=True), reads=[kfk] + vkeys, writes=[("ps", 1)])
            for h in range(4):
                S.pe(lambda e, h=h: e.matmul(kvbB[0:64, h * 128:(h + 1) * 128], lhsT=kdb[:, h * 64:(h + 1) * 64],
                                             rhs=vt[:, n % 4, h * 128:(h + 1) * 128], start=True, stop=True), reads=[kbk] + vkeys, writes=[("ps", 2)])
            S.act(lambda e: e.activation(out=kvbs[0:64, n, :], in_=kvbB[0:64, :], func=AF.Copy), reads=[("ps", 2)], writes=[("kvbs", n)])
            S.pool(lambda e: e.tensor_copy(out=Sfb[0:64, n, :], in_=Sf[0:64, :]), reads=["Sf"], writes=[("Sfb", n)])
            S.pool(lambda e: e.tensor_tensor(out=Sf[0:64, :], in0=Sf[0:64, :], in1=Gcf2[0:64, :], op=ALU.mult), reads=["Sf"] + gc_keys, writes=["Sf"])
            S.dve(lambda e: e.tensor_tensor(out=Sf[0:64, :], in0=Sf[0:64, :], in1=kvfB[0:64, :], op=ALU.add), reads=["Sf", ("ps", 1)], writes=["Sf"])

        for G in range(N // 4):
            vt, vk = load_v(G)
            for ci in range(4):
                pass1(G * 4 + ci, vt, vk)

        def pass1b(n):
            S.pool(lambda e: e.tensor_copy(out=Sbb[0:64, n, :], in_=Sb[0:64, :]), reads=["Sb"], writes=[("Sbb", n)])
            S.pool(lambda e: e.tensor_tensor(out=Sb[0:64, :], in0=Sb[0:64, :], in1=Gcb2[0:64, :], op=ALU.mult), reads=["Sb"] + gc_keys, writes=["Sb"])
            S.pool(lambda e: e.tensor_tensor(out=Sb[0:64, :], in0=Sb[0:64, :], in1=kvbs[0:64, n, :], op=ALU.add), reads=["Sb", ("kvbs", n)], writes=["Sb"])


        scR = Rot([(self.bank(i), ("ps", i)) for i in (0, 1)])
        yR = Rot([(self.bank(i), ("ps", i)) for i in (2, 3, 4)])
        psN = self.bank(5)

        def pass2(G, h, vt, vk, base=base):
            vkeys = [vk]
            c, r0 = h // 2, (h % 2) * 64
            t0 = base + G * 512
            g, gk = gR.next()
            S.dma(g, sc["rgT"][h * 128:(h + 1) * 128, t0:t0 + 512], writes=[gk])
            qdf, qfk = qdR.next()
            qdb, qbk = qdR.next()
            qv = qT[r0:r0 + 64, c, G * 512:(G + 1) * 512].rearrange("p (a b) -> p a b", a=4)
            S.dve(lambda e: e.tensor_tensor(out=qdf[0:64, :].rearrange("p (a b) -> p a b", a=4), in0=qv,
                                            in1=Qdf[r0:r0 + 64, c, :].unsqueeze(1).broadcast_to([64, 4, 128]), op=ALU.mult),
                  reads=["qT", ("Qd", h)], writes=[qfk])
            S.dve(lambda e: e.tensor_tensor(out=qdb[0:64, :].rearrange("p (a b) -> p a b", a=4), in0=qv,
                                            in1=Qdb[r0:r0 + 64, c, :].unsqueeze(1).broadcast_to([64, 4, 128]), op=ALU.mult),
                  reads=["qT", ("Qd", h)], writes=[qbk])
            ps, pk = scR.next()
            for ci in range(4):
                n = G * 4 + ci
                S.pe(lambda e, ci=ci, n=n: e.matmul(ps[:, ci * 128:(ci + 1) * 128], lhsT=kT[r0:r0 + 64, c, n * 128:(n + 1) * 128],
                                                     rhs=qT[r0:r0 + 64, c, n * 128:(n + 1) * 128], start=True, stop=True),
                     reads=["kT", "qT"], writes=[pk])
            aT, ak = aTR.next()
            S.dve(lambda e: e.tensor_tensor(out=aT.rearrange("p (a b) -> p a b", a=4), in0=ps.rearrange("p (a b) -> p a b", a=4),
                                            in1=MT[:, h, :].unsqueeze(1).broadcast_to([128, 4, 128]), op=ALU.mult),
                  reads=[pk, ("MT", h)], writes=[ak])
            y, yk = yR.next()
            for ci in range(4):
                n = G * 4 + ci
                ysl = y[:, ci * 128:(ci + 1) * 128]
                S.pe(lambda e, ci=ci, n=n, ysl=ysl: e.matmul(ysl, lhsT=vt[:, ci, h * 128:(h + 1) * 128], rhs=aT[:, ci * 128:(ci + 1) * 128],
                                                              start=True, stop=False), reads=[ak] + vkeys, writes=[yk])
                S.pe(lambda e, ci=ci, n=n, ysl=ysl: e.matmul(ysl, lhsT=Sfb[0:64, n, h * 128:(h + 1) * 128], rhs=qdf[0:64, ci * 128:(ci + 1) * 128],
                                                              start=False, stop=False), reads=[qfk, ("Sfb", n)], writes=[yk])
                S.pe(lambda e, ci=ci, n=n, ysl=ysl: e.matmul(ysl, lhsT=Sbb[0:64, n, h * 128:(h + 1) * 128], rhs=qdb[0:64, ci * 128:(ci + 1) * 128],
                                                              start=False, stop=True), reads=[qbk, ("Sbb", n)], writes=[yk])
            def epi():
                S.act(lambda e: e.activation(out=sqb, in_=y, func=AF.Square), reads=[yk], writes=["sqb"])
                S.pe(lambda e: e.matmul(psN, lhsT=onesb, rhs=sqb, start=True, stop=True), reads=["sqb", "bp"], writes=[("ps", 5)])
                self.rstd_from([(psN, ("ps", 5), slice(0, 512))], 1.0 / 128, HEAD_EPS, lnb, rsb, "rsb", "lnb")
                S.dve(lambda e: e.tensor_tensor(out=tb, in0=y, in1=rsb, op=ALU.mult), reads=[yk, "rsb"], writes=["tb"])
                ob, ok = obR.next()
                S.dve(lambda e: e.scalar_tensor_tensor(out=ob, in0=tb, scalar=self.ppv("rnw", h), in1=g, op0=ALU.mult, op1=ALU.mult),
                      reads=["tb", gk, "pp"], writes=[ok])
                S.dma(sc["boT"][h * 128:(h + 1) * 128, t0:t0 + 512], ob, reads=[ok])
            return epi

        prev_epi = None
        for G in range(Sq // 512 - 1, -1, -1):
            for ci in range(3, -1, -1):
                pass1b(G * 4 + ci)
            vt, vk = load_v(G)
            for h in range(4):
                epi = pass2(G, h, vt, vk)
                if prev_epi is not None:
                    prev_epi()
                prev_epi = epi
        prev_epi()


MK.phase_ret = phase_ret


def phase_ssd(self, l):
    S, A = self.S, self.A
    A.reset()
    self.load_cp()
    sc = self.scr
    smax = self.smax
    Nm = smax // 128
    BT = A.alloc([2, smax], BF16)
    CT = A.alloc([2, smax], BF16)
    dtA = A.alloc([Nm, 16], F32)
    dta = A.alloc([Nm, 16], F32)
    cumS = A.alloc([Nm, 16], F32)
    ecum = A.alloc([Nm, 16], F32)
    sdte = A.alloc([Nm, 16], F32)
    etot = A.alloc([Nm, 16], F32)
    A16 = A.alloc([16], F32)
    prevs = [A.alloc([Nm, 512], BF16) for _ in range(2)]
    Hs = [A.alloc([512], F32) for _ in range(2)]
    xBR = Rot([(A.alloc([4, 768], BF16), ("xB", i)) for i in range(4)])
    zR = Rot([(A.alloc([4, 512], BF16), ("z", i)) for i in range(2)])
    xwR = Rot([(A.alloc([512], BF16), ("xw", i)) for i in range(2)])
    xdtR = Rot([(A.alloc([512], BF16), ("xdt", i)) for i in range(4)])
    LhR = Rot([(A.alloc([4, 128], F32), ("Lh", i)) for i in range(2)])
    ER = Rot([(A.alloc([4, 128], F32), ("E", i)) for i in range(2)])
    WR = Rot([(A.alloc([4, 128], BF16), ("W", i)) for i in range(2)])
    CBm = [A.alloc([2, 128], F32) for _ in range(2)]
    tbs = [A.alloc([512], F32) for _ in range(2)]
    ubs = [A.alloc([512], F32) for _ in range(2)]
    t2b = A.alloc([512], F32)
    junk = A.alloc([256], F32)
    ssbs = [A.alloc([4], F32) for _ in range(2)]
    yoR = Rot([(A.alloc([512], BF16), ("yo", i)) for i in range(2)])
    oTR = Rot([(A.alloc([4, 512], BF16), ("oT", i)) for i in range(2)])
    ident = self.bpv("ident")
    Uf32, Ub32 = self.cpv("Uf"), self.cpv("Ub")
    SL = [self.cpv("SLf"), self.cpv("SLb")]
    U = [Uf32, Ub32]
    ones32 = self.cpv("ones")
    o_a = PP["alog"][0]
    o_d = PP["ssd"][0]
    o_n = PP["snw"][0]
    S.act(lambda e: e.activation(out=A16, in_=self.pp[:, o_a:o_a + 16], func=AF.Exp), reads=["pp"], writes=["A16"])
    S.dve(lambda e: e.tensor_scalar(out=A16, in0=A16, scalar1=-1.0, scalar2=None, op0=ALU.mult), reads=["A16"], writes=["A16"])
    Dsk = self.pp[:, o_d:o_d + 8].unsqueeze(2).broadcast_to([128, 8, 64])
    snw = self.pp[:, o_n:o_n + 512]

    for (base, Sq) in _seq_list(self):
        N = Sq // 128
        NG = Sq // 512
        S.dma(BT[:, :, 0:Sq], sc["sBT"][:, base:base + Sq].rearrange("(c p) t -> p c t", p=128), writes=["BT"])
        S.dma(CT[:, :, 0:Sq], sc["sCT"][:, base:base + Sq].rearrange("(c p) t -> p c t", p=128), writes=["CT"])
        S.dma(dtA[:, 0:N, :], sc["sdt"][base:base + Sq, :].rearrange("(c p) k -> p c k", p=128), writes=["dtA"])
        S.dve(lambda e, N=N: e.tensor_tensor(out=dta[:, 0:N, :], in0=dtA[:, 0:N, :], in1=A16.unsqueeze(1).broadcast_to([128, N, 16]), op=ALU.mult),
              reads=["dtA", "A16"], writes=["dta"])
        psC = self.bank(0)
        psTt = self.bank(1)
        psC3 = psC.rearrange("p (n c) -> p n c", c=16)
        S.pe(lambda e, N=N: e.matmul(psC3[:, 0:N, 0:8], lhsT=Uf32, rhs=dta[:, 0:N, 0:8], start=True, stop=True), reads=["dta"], writes=[("ps", 0)])
        S.pe(lambda e, N=N: e.matmul(psC3[:, 0:N, 8:16], lhsT=Ub32, rhs=dta[:, 0:N, 8:16], start=True, stop=True), reads=["dta"], writes=[("ps", 0)])
        S.pe(lambda e, N=N: e.matmul(psTt[:, 0:N * 16], lhsT=ones32, rhs=dta[:, 0:N, :].rearrange("p n c -> p (n c)"), start=True, stop=True),
             reads=["dta"], writes=[("ps", 1)])
        cs2 = cumS.rearrange("p n c -> p (n c)")
        S.act(lambda e, N=N: e.activation(out=cs2[:, 0:N * 16], in_=psC[:, 0:N * 16], func=AF.Copy), reads=[("ps", 0)], writes=["cumS"])
        S.act(lambda e, N=N: e.activation(out=ecum.rearrange("p n c -> p (n c)")[:, 0:N * 16], in_=cs2[:, 0:N * 16], func=AF.Exp),
              reads=["cumS"], writes=["ecum"])
        sd2 = sdte.rearrange("p n c -> p (n c)")
        S.dve(lambda e, N=N: e.tensor_tensor(out=sd2[:, 0:N * 16], in0=psTt[:, 0:N * 16], in1=cs2[:, 0:N * 16], op=ALU.subtract),
              reads=[("ps", 1), "cumS"], writes=["sdte"])
        S.act(lambda e, N=N: e.activation(out=sd2[:, 0:N * 16], in_=sd2[:, 0:N * 16], func=AF.Exp), reads=["sdte"], writes=["sdte"])
        S.dve(lambda e, N=N: e.tensor_tensor(out=sd2[:, 0:N * 16], in0=sd2[:, 0:N * 16], in1=dtA.rearrange("p n c -> p (n c)")[:, 0:N * 16], op=ALU.mult),
              reads=["sdte", "dtA"], writes=["sdte"])
        S.act(lambda e, N=N: e.activation(out=etot.rearrange("p n c -> p (n c)")[:, 0:N * 16], in_=psTt[:, 0:N * 16], func=AF.Exp),
              reads=[("ps", 1)], writes=["etot"])
        for d in range(2):
            S.pool(lambda e, d=d: e.memset(Hs[d], 0.0), writes=[("H", d)])

        def load_x(G, base=base):
            xB, xk = xBR.next()
            t0 = base + G * 512
            S.dma(xB, sc["sxB"][t0:t0 + 512, :].rearrange("(c p) e -> p c e", p=128), writes=[xk])
            return xB, xk

        stR = Rot([(self.bank(i), ("ps", i)) for i in (2, 3)])

        def state_step(d, n, xB, xk):
            ci = n % 4
            H, prev = Hs[d], prevs[d]
            S.pool(lambda e: e.tensor_copy(out=prev[:, n, :], in_=H), reads=[("H", d)], writes=[("prev", d, n)])
            xw, wk = xwR.next()
            S.dve(lambda e: e.tensor_tensor(out=xw.rearrange("p (h q) -> p h q", h=8), in0=xB[:, ci, 0:512].rearrange("p (h q) -> p h q", h=8),
                                            in1=sdte[:, n, d * 8:(d + 1) * 8].unsqueeze(2).broadcast_to([128, 8, 64]), op=ALU.mult),
                  reads=[xk, "sdte"], writes=[wk])
            ps, pk = stR.next()
            for g in range(2):
                S.pe(lambda e, g=g: e.matmul(ps[:, g * 256:(g + 1) * 256], lhsT=xB[:, ci, 512 + g * 128:512 + (g + 1) * 128],
                                             rhs=xw[:, g * 256:(g + 1) * 256], start=True, stop=True), reads=[xk, wk], writes=[pk])
            S.pool(lambda e: e.tensor_tensor(out=H.rearrange("p (h q) -> p h q", h=8), in0=H.rearrange("p (h q) -> p h q", h=8),
                                             in1=etot[:, n, d * 8:(d + 1) * 8].unsqueeze(2).broadcast_to([128, 8, 64]), op=ALU.mult),
                   reads=[("H", d), "etot"], writes=[("H", d)])
            S.dve(lambda e: e.tensor_tensor(out=H, in0=H, in1=ps, op=ALU.add), reads=[("H", d), pk], writes=[("H", d)])

        curf = curb = None
        for step in range(N):
            nf, nb_ = step, N - 1 - step
            if nf % 4 == 0:
                curf = load_x(nf // 4)
            if nb_ % 4 == 3:
                curb = load_x(nb_ // 4)
            state_step(0, nf, curf[0], curf[1])
            state_step(1, nb_, curb[0], curb[1])

        psCB = self.bank(0)
        psDR = Rot([(self.bank(i), ("ps", i)) for i in (1, 2)])
        Yd = [self.bank(3), self.bank(4)]
        Yo = [self.bank(5), self.bank(6)]
        psT = self.bank(7, BF16)

        pend_tail = [None]
        store_jobs = []

        def out_chunk(n, xB, xk, z, zk, oT, otk):
            ci = n % 4
            tsl = slice(n * 128, (n + 1) * 128)
            for g in range(2):
                S.pe(lambda e, g=g: e.matmul(psCB[:, g * 128:(g + 1) * 128], lhsT=BT[:, g, tsl], rhs=CT[:, g, tsl], start=True, stop=True),
                     reads=["BT", "CT"], writes=[("ps", 0)])
            for d in range(2):
                S.dve(lambda e, d=d: e.tensor_tensor(out=CBm[d], in0=psCB[:, 0:256].rearrange("p (g i) -> p g i", g=2),
                                                     in1=U[d].unsqueeze(1).broadcast_to([128, 2, 128]), op=ALU.mult),
                      reads=[("ps", 0)], writes=[("CBm", d)])
            for d in range(2):
                xdt, xdk = xdtR.next()
                S.dve(lambda e, d=d, xdt=xdt: e.tensor_tensor(out=xdt.rearrange("p (h q) -> p h q", h=8), in0=xB[:, ci, 0:512].rearrange("p (h q) -> p h q", h=8),
                                                               in1=dtA[:, n, d * 8:(d + 1) * 8].unsqueeze(2).broadcast_to([128, 8, 64]), op=ALU.mult),
                      reads=[xk, "dtA"], writes=[xdk])
                for g in range(2):
                    c0 = d * 8 + g * 4
                    Lh, lk = LhR.next()
                    for hh in range(4):
                        S.act(lambda e, d=d, c0=c0, Lh=Lh, hh=hh: e.activation(out=Lh[:, hh, :], in_=SL[d], func=AF.Copy,
                                                                                 scale=dta[:, n, c0 + hh:c0 + hh + 1]),
                              reads=["dta"], writes=[(lk, hh)])
                    psD, pdk = psDR.next()
                    for hh in range(4):
                        S.pe(lambda e, d=d, hh=hh, Lh=Lh, psD=psD: e.matmul(psD[:, hh * 128:(hh + 1) * 128], lhsT=Lh[:, hh, :], rhs=U[d], start=True, stop=True),
                             reads=[(lk, hh)], writes=[pdk])
                    E, ek = ER.next()
                    S.act(lambda e, E=E, psD=psD: e.activation(out=E.rearrange("p a b -> p (a b)"), in_=psD, func=AF.Exp), reads=[pdk], writes=[ek])
                    W, wk = WR.next()
                    S.dve(lambda e, d=d, g=g, E=E, W=W: e.tensor_tensor(out=W, in0=E, in1=CBm[d][:, g, :].unsqueeze(1).broadcast_to([128, 4, 128]), op=ALU.mult),
                          reads=[ek, ("CBm", d)], writes=[wk])
                    for hh in range(4):
                        h = g * 4 + hh
                        S.pe(lambda e, d=d, hh=hh, h=h, W=W, xdt=xdt: e.matmul(Yd[d][:, h * 64:(h + 1) * 64], lhsT=W[:, hh, :], rhs=xdt[:, h * 64:(h + 1) * 64],
                                                                                 start=True, stop=True), reads=[wk, xdk], writes=[("ps", 3 + d)])
                    S.pe(lambda e, d=d, g=g: e.matmul(Yo[d][:, g * 256:(g + 1) * 256], lhsT=CT[:, g, tsl], rhs=prevs[d][:, n, g * 256:(g + 1) * 256],
                                                      start=True, stop=True), reads=["CT", ("prev", d, n)], writes=[("ps", 5 + d)])
            tb, ub, ssb = tbs[n % 2], ubs[n % 2], ssbs[n % 2]
            tbk, ubk = ("tb", n % 2), ("ub", n % 2)
            t3 = tb.rearrange("p (h q) -> p h q", h=8)
            u3 = ub.rearrange("p (h q) -> p h q", h=8)
            S.dve(lambda e: e.tensor_tensor(out=t3, in0=Yo[0].rearrange("p (h q) -> p h q", h=8),
                                            in1=ecum[:, n, 0:8].unsqueeze(2).broadcast_to([128, 8, 64]), op=ALU.mult),
                  reads=[("ps", 5), "ecum"], writes=[tbk])
            S.dve(lambda e: e.tensor_tensor(out=tb, in0=tb, in1=Yd[0], op=ALU.add), reads=[tbk, ("ps", 3)], writes=[tbk])
            S.dve(lambda e: e.tensor_tensor(out=u3, in0=Yo[1].rearrange("p (h q) -> p h q", h=8),
                                            in1=ecum[:, n, 8:16].unsqueeze(2).broadcast_to([128, 8, 64]), op=ALU.mult),
                  reads=[("ps", 6), "ecum"], writes=[ubk])
            S.dve(lambda e: e.tensor_tensor(out=ub, in0=ub, in1=Yd[1], op=ALU.add), reads=[ubk, ("ps", 4)], writes=[ubk])
            def tail_b():
                S.pool(lambda e: e.tensor_tensor(out=tb, in0=tb, in1=ub, op=ALU.add), reads=[tbk, ubk], writes=[tbk])
                S.pool(lambda e: e.tensor_tensor(out=t2b.rearrange("p (h q) -> p h q", h=8), in0=xB[:, ci, 0:512].rearrange("p (h q) -> p h q", h=8),
                                                 in1=Dsk, op=ALU.mult), reads=[xk, "pp"], writes=["t2b"])
                S.pool(lambda e: e.tensor_tensor(out=tb, in0=tb, in1=t2b, op=ALU.add), reads=[tbk, "t2b"], writes=[tbk])
                S.pool(lambda e: e.tensor_tensor(out=tb, in0=tb, in1=z[:, ci, :], op=ALU.mult), reads=[tbk, zk], writes=[tbk])
                for g in range(2):
                    S.act(lambda e, g=g: e.activation(out=junk, in_=tb[:, g * 256:(g + 1) * 256], func=AF.Square, accum_out=ssb[:, g:g + 1]),
                          reads=[tbk, "junk"], writes=["junk", ("ss", n % 2, g)])
                S.act(lambda e: e.activation(out=ssb[:, 2:4], in_=ssb[:, 0:2], func=AF.Ln, scale=1.0 / 256, bias=HEAD_EPS),
                      reads=[("ss", n % 2, 0), ("ss", n % 2, 1)], writes=[("ssl", n % 2)])
                S.act(lambda e: e.activation(out=ssb[:, 0:2], in_=ssb[:, 2:4], func=AF.Exp, scale=-0.5), reads=[("ssl", n % 2)], writes=[("ss", n % 2, 0), ("ss", n % 2, 1), ("rs2", n % 2)])
                yo, yk = yoR.next()
                for g in range(2):
                    S.dve(lambda e, g=g, yo=yo: e.scalar_tensor_tensor(out=yo[:, g * 256:(g + 1) * 256], in0=tb[:, g * 256:(g + 1) * 256], scalar=ssb[:, g:g + 1],
                                                                        in1=snw[:, g * 256:(g + 1) * 256], op0=ALU.mult, op1=ALU.mult),
                          reads=[tbk, ("rs2", n % 2), "pp"], writes=[(yk, g)])
                half = n % 2
                pT_ = psT[:, half * 512:(half + 1) * 512]
                ptk = ("ps", 7)
                for cc in range(4):
                    S.pe(lambda e, cc=cc, yo=yo: e.transpose(out=pT_[:, cc * 128:(cc + 1) * 128], in_=yo[:, cc * 128:(cc + 1) * 128], identity=ident),
                         reads=[(yk, 0), (yk, 1), "bp"], writes=[ptk])
                S.dve(lambda e: e.tensor_copy(out=oT[:, :, ci * 128:(ci + 1) * 128], in_=pT_.rearrange("p (a b) -> p a b", a=4)),
                      reads=[ptk], writes=[(otk, ci)])


            return tail_b

        for G in range(NG):
            xB, xk = load_x(G)
            z, zk = zR.next()
            t0 = base + G * 512
            S.dma(z, sc["sz"][t0:t0 + 512, :].rearrange("(c p) e -> p c e", p=128), writes=[zk])
            oT, otk = oTR.next()
            for ci in range(4):
                tb_fn = out_chunk(G * 4 + ci, xB, xk, z, zk, oT, otk)
                if pend_tail[0] is not None:
                    pend_tail[0]()
                pend_tail[0] = tb_fn
            store_jobs.append((t0, oT, otk))
            if len(store_jobs) > 1:
                t0_, oT_, otk_ = store_jobs.pop(0)
                S.dma(sc["boT"][1536:2048, t0_:t0_ + 512].rearrange("(c p) t -> p c t", p=128), oT_, reads=[(otk_, ci) for ci in range(4)])
        pend_tail[0]()
        pend_tail[0] = None
        while store_jobs:
            t0_, oT_, otk_ = store_jobs.pop(0)
            S.dma(sc["boT"][1536:2048, t0_:t0_ + 512].rearrange("(c p) t -> p c t", p=128), oT_, reads=[(otk_, ci) for ci in range(4)])


MK.phase_ssd = phase_ssd


def load_cp(self):
    self.cp = self.A.alloc([NCP], F32)
    self.S.dma(self.cp, self.cpack, writes=["cp"])
    for e in (PE, ACT, DVE, POOL):
        self.S.add(e, None, reads=["cp"], real=False)


def rms_group(self, xs, ncols, wname, hT, hk, sqR, lnv, rstd, psA, psH0, kA=None, kH=None):
    S = self.S
    onesb = self.bpv("ones")
    for k in range(8):
        sq, sqk = sqR.next()
        S.act(lambda e, sq=sq, k=k: e.activation(out=sq[:, 0:ncols], in_=xs[:, k, :], func=AF.Square),
              reads=["xs", "xsh"], writes=[sqk])
        S.pe(lambda e, sq=sq, k=k: e.matmul(psA, lhsT=onesb, rhs=sq[:, 0:512], start=(k == 0), stop=(k == 7)),
             reads=[sqk, "bp"], writes=[kA])
        if ncols > 512:
            S.pe(lambda e, sq=sq, k=k: e.matmul(psH0, lhsT=onesb, rhs=sq[:, 512:ncols], start=(k == 0), stop=(k == 7)),
                 reads=[sqk, "bp"], writes=[kH])
    lst = [(psA, kA, slice(0, 512))]
    if ncols > 512:
        lst.append((psH0, kH, slice(512, ncols)))
    self.rstd_from(lst, 1.0 / 1024, NORM_EPS, lnv[:, 0:ncols], rstd[:, 0:ncols], "rstd", "lnv")
    if hT is not None:
        for k in range(8):
            S.dve(lambda e, k=k: e.scalar_tensor_tensor(out=hT[:, k, :], in0=xs[:, k, :], scalar=self.ppv(wname, k),
                                                         in1=rstd[:, 0:ncols], op0=ALU.mult, op1=ALU.mult),
                  reads=["xs", "xsh", "rstd", "pp"], writes=[hk[k]])


def phase3(self, l):
    S, A = self.S, self.A
    A.reset()
    sc = self.scr
    x_in = (self.xT if l == 0 else sc["xres"]).rearrange("(c p) t -> p c t", p=128)
    boT = sc["boT"].rearrange("(c p) t -> p c t", p=128)
    wG = A.alloc([8, 4096], BF16)
    wB = A.alloc([16, 1024], BF16)
    wO = A.alloc([8, 1024], BF16)
    bo = A.alloc([16, 512], BF16)
    bo_w = bo.rearrange("p a b -> p (a b)").bitcast(F32)
    stgR = Rot([(bo_w[:, i * 2048:(i + 1) * 2048], ("stg", i)) for i in range(2)])
    win = self.w_in[l].rearrange("(c p) n -> p c n", p=128)
    pieces = []
    for k in range(8):
        for q in range(2):
            pieces.append((wG[:, k, q * 2048:(q + 1) * 2048], win[:, k, C_GATE + q * 2048:C_GATE + (q + 1) * 2048]))
    wbr = self.w_branch[l].rearrange("b (c p) n -> p (b c) n", p=128)
    for i in range(16):
        pieces.append((wB[:, i, :], wbr[:, i, :]))
    wo = self.w_out[l].rearrange("(c p) n -> p c n", p=128)
    for k in range(8):
        pieces.append((wO[:, k, :], wo[:, k, :]))
    self.load_cast("w3", pieces, stgR)
    xs = A.alloc([8, 512], F32)
    hTs = [A.alloc([8, 512], BF16) for _ in range(2)]
    sqR = Rot([(A.alloc([512], BF16), ("sq", i)) for i in range(2)])
    lnv = A.alloc([512], F32)
    rstd = A.alloc([512], F32)
    merged = A.alloc([8, 512], BF16)
    gsR = Rot([(A.alloc([512], F32), ("gs", i)) for i in range(3)])
    tmR = Rot([(A.alloc([512], F32), ("tm", i)) for i in range(2)])
    mR = Rot([(A.alloc([512], F32), ("m", i)) for i in range(2)])
    xoR = Rot([(A.alloc([512], F32), ("xo", i)) for i in range(2)])
    psGR = Rot([(self.bank(i), ("ps", i)) for i in (0, 1, 2)])
    psPR = Rot([(self.bank(i), ("ps", i)) for i in (3, 4)])
    psOR = Rot([(self.bank(i), ("ps", i)) for i in (5, 6)])
    psA = self.bank(7)
    ng = len(self.groups)

    def load_x(gi):
        if gi >= ng:
            return
        t0 = self.groups[gi]["t0"]
        S.dma(xs, x_in[:, :, t0:t0 + 512], writes=["xs"])

    def load_bo(gi):
        if gi >= ng:
            return
        t0 = self.groups[gi]["t0"]
        for q in range(4):
            S.dma(bo[:, q * 4:(q + 1) * 4, :], boT[:, q * 4:(q + 1) * 4, t0:t0 + 512], writes=[("bo", q), ("stg", 0), ("stg", 1)])

    def body(gi, g):
        t0 = g["t0"]
        hT = hTs[gi % 2]
        hk = [("hT", gi % 2, k) for k in range(8)]
        if gi == 0:
            load_x(0)
            load_bo(0)
            rms_group(self, xs, 512, "wn", hT, hk, sqR, lnv, rstd, psA, None, kA=("ps", 7))
            load_x(1)
        for j in range(8):
            if j == 5 and gi + 1 < ng:
                hk_n = [("hT", (gi + 1) % 2, k) for k in range(8)]
                rms_group(self, xs, 512, "wn", hTs[(gi + 1) % 2], hk_n, sqR, lnv, rstd, psA, None, kA=("ps", 7))
                load_x(gi + 2)
            m, mk_ = mR.next()
            for b in range(4):
                psG, pgk = psGR.next()
                cg = b * 1024 + j * 128
                self.mmg(psG, pgk, [(wG[:, k, cg:cg + 128], hT[:, k, :]) for k in range(8)], hk + ["w3"])
                gs, gk = gsR.next()
                S.act(lambda e, psG=psG, gs=gs: e.activation(out=gs, in_=psG, func=AF.Sigmoid), reads=[pgk], writes=[gk])
                psP, ppk = psPR.next()
                self.mmg(psP, ppk, [(wB[:, b * 4 + kc, j * 128:(j + 1) * 128], bo[:, b * 4 + kc, :]) for kc in range(4)], [("bo", b), "w3"])
                if b == 0:
                    S.dve(lambda e, psP=psP, gs=gs, m=m: e.tensor_tensor(out=m, in0=psP, in1=gs, op=ALU.mult), reads=[ppk, gk], writes=[mk_])
                else:
                    tm, tk = tmR.next()
                    S.dve(lambda e, psP=psP, gs=gs, tm=tm: e.tensor_tensor(out=tm, in0=psP, in1=gs, op=ALU.mult), reads=[ppk, gk], writes=[tk])
                    dst = merged[:, j, :] if b == 3 else m
                    dk = ("mg", j) if b == 3 else mk_
                    S.pool(lambda e, tm=tm, m=m, dst=dst: e.tensor_tensor(out=dst, in0=m, in1=tm, op=ALU.add), reads=[mk_, tk], writes=[dk])
        load_bo(gi + 1)
        mgk = [("mg", j) for j in range(8)]
        for jo in range(8):
            psO, pok = psOR.next()
            for k in range(8):
                S.pe(lambda e, k=k, jo=jo, psO=psO: e.matmul(psO, lhsT=wO[:, k, jo * 128:(jo + 1) * 128], rhs=merged[:, k, :],
                                                             start=(k == 0), stop=(k == 7)), reads=[("mg", k), "w3"], writes=[pok])
            xo, xk = xoR.next()
            S.dma(xo, x_in[:, jo, t0:t0 + 512], writes=[xk])
            S.dve(lambda e, psO=psO, xo=xo: e.tensor_tensor(out=xo, in0=psO, in1=xo, op=ALU.add), reads=[pok, xk], writes=[xk])
            S.dma(sc["xmid"][jo * 128:(jo + 1) * 128, t0:t0 + 512], xo, reads=[xk])

    for gi, g in enumerate(self.groups):
        body(gi, g)


def phase4(self, l):
    S, A = self.S, self.A
    A.reset()
    sc = self.scr
    x_in = sc["xmid"].rearrange("(c p) t -> p c t", p=128)
    wg = A.alloc([8, D_FF], BF16)
    wu = A.alloc([8, D_FF], BF16)
    wd = A.alloc([NFF, 1024], BF16)
    act = A.alloc([NFF, 512], BF16)
    act_w = act.rearrange("p a b -> p (a b)").bitcast(F32)
    stgR = Rot([(act_w[:, i * 2816:(i + 1) * 2816], ("stg", i)) for i in range(2)])
    pieces = []
    g_ = self.w_gate[l].rearrange("(c p) n -> p c n", p=128)
    u_ = self.w_up[l].rearrange("(c p) n -> p c n", p=128)
    for k in range(8):
        pieces.append((wg[:, k, :], g_[:, k, :]))
        pieces.append((wu[:, k, :], u_[:, k, :]))
    d_ = self.w_down[l].rearrange("(c p) n -> p c n", p=128)
    for c in range(0, NFF, 2):
        pieces.append((wd[:, c:c + 2, :], d_[:, c:c + 2, :]))
    self.load_cast("w4", pieces, stgR)
    xs = A.alloc([8, 514], F32)
    hTs = [A.alloc([8, 514], BF16) for _ in range(2)]
    sqR = Rot([(A.alloc([514], BF16), ("sq", i)) for i in range(2)])
    lnv = A.alloc([514], F32)
    rstd = lnv
    GR = Rot([(A.alloc([514], F32), ("G", i)) for i in range(1)])
    cvR = Rot([(A.alloc([512], F32), ("cv", i)) for i in range(1)])
    xrR = Rot([(A.alloc([512], F32), ("xr", i)) for i in range(1)])
    psgR = Rot([(self.bank(i), ("ps", i)) for i in (0, 1)])
    psuR = Rot([(self.bank(i), ("ps", i)) for i in (2, 3)])
    psOR = Rot([(self.bank(i), ("ps", i)) for i in (4, 5)])
    psA = self.bank(6)
    psH = self.bank(7)
    ng = len(self.groups)
    o_w = PP["fcw"][0]
    o_b = PP["fcb"][0]
    x_out = sc["xres"]

    def load_x(gi_):
        if gi_ >= ng:
            return
        g_i = self.groups[gi_]
        hasL = g_i["pos0"] > 0
        hasR = g_i["pos0"] + 512 < g_i["S"]
        if not hasL:
            S.pool(lambda e: e.memset(xs[:, :, 0:1], 0.0), writes=["xsh"])
        if not hasR:
            S.pool(lambda e: e.memset(xs[:, :, 513:514], 0.0), writes=["xsh"])
        lo = g_i["t0"] - 1 if hasL else g_i["t0"]
        hi = g_i["t0"] + 513 if hasR else g_i["t0"] + 512
        c0 = 0 if hasL else 1
        S.dma(xs[:, :, c0:c0 + (hi - lo)], x_in[:, :, lo:hi], writes=["xs"])

    def body(gi, g):
        t0 = g["t0"]
        hT = hTs[gi % 2]
        hk = [("hT", gi % 2, k) for k in range(8)]
        if gi == 0:
            load_x(0)
            rms_group(self, xs, 514, "nf", hT, hk, sqR, lnv, rstd, psA, psH[:, 0:2], kA=("ps", 6), kH=("ps", 7))
            load_x(1)
        for c in range(NFF):
            if c == 13 and gi + 1 < ng:
                hk_n = [("hT", (gi + 1) % 2, k) for k in range(8)]
                rms_group(self, xs, 514, "nf", hTs[(gi + 1) % 2], hk_n, sqR, lnv, rstd, psA, psH[:, 0:2], kA=("ps", 6), kH=("ps", 7))
                load_x(gi + 2)
            cs = slice(c * 128, (c + 1) * 128)
            psg, pgk = psgR.next()
            self.mmg(psg, pgk, [(wg[:, k, cs], hT[:, k, 1:513]) for k in range(8)], hk + ["w4"])
            hcol = 2 + 2 * c
            hkey = ("ps", 7)
            for k in range(8):
                S.pe(lambda e, k=k, cs=cs, hcol=hcol: e.matmul(psH[:, hcol:hcol + 2], lhsT=wg[:, k, cs], rhs=hT[:, k, 0:514:513],
                                                                start=(k == 0), stop=(k == 7)), reads=hk + ["w4"], writes=[hkey])
            psu, puk = psuR.next()
            self.mmg(psu, puk, [(wu[:, k, cs], hT[:, k, 1:513]) for k in range(8)], hk + ["w4"])
            G, gk = GR.next()
            S.act(lambda e, psg=psg, G=G: e.activation(out=G[:, 1:513], in_=psg, func=AF.Copy), reads=[pgk], writes=[gk])
            S.dve(lambda e, G=G, hcol=hcol: e.tensor_copy(out=G[:, 0:514:513], in_=psH[:, hcol:hcol + 2]), reads=[hkey], writes=[(gk, "h")])
            cv, ck = cvR.next()
            S.dve(lambda e, G=G, cv=cv, c=c: e.tensor_scalar(out=cv, in0=G[:, 1:513], scalar1=self.pp[:, o_w + NFF + c:o_w + NFF + c + 1],
                                                              scalar2=self.pp[:, o_b + c:o_b + c + 1], op0=ALU.mult, op1=ALU.add),
                  reads=[gk, "pp"], writes=[ck])
            S.dve(lambda e, G=G, cv=cv, c=c: e.scalar_tensor_tensor(out=cv, in0=G[:, 0:512], scalar=self.pp[:, o_w + c:o_w + c + 1],
                                                                     in1=cv, op0=ALU.mult, op1=ALU.add),
                  reads=[gk, (gk, "h"), "pp", ck], writes=[ck])
            S.dve(lambda e, G=G, cv=cv, c=c: e.scalar_tensor_tensor(out=cv, in0=G[:, 2:514], scalar=self.pp[:, o_w + 2 * NFF + c:o_w + 2 * NFF + c + 1],
                                                                     in1=cv, op0=ALU.mult, op1=ALU.add),
                  reads=[gk, (gk, "h"), "pp", ck], writes=[ck])
            S.act(lambda e, cv=cv: e.activation(out=cv, in_=cv, func=AF.Silu), reads=[ck], writes=[ck])
            S.dve(lambda e, cv=cv, psu=psu, c=c: e.tensor_tensor(out=act[:, c, :], in0=psu, in1=cv, op=ALU.mult), reads=[puk, ck], writes=[("act", c)])
        ak = [("act", c) for c in range(NFF)]
        for jo in range(8):
            xr, xrk = xrR.next()
            S.dma(xr, sc["xmid"][jo * 128:(jo + 1) * 128, t0:t0 + 512], writes=[xrk])
            psO, pok = psOR.next()
            for c in range(NFF):
                S.pe(lambda e, c=c, jo=jo, psO=psO: e.matmul(psO, lhsT=wd[:, c, jo * 128:(jo + 1) * 128], rhs=act[:, c, :],
                                                             start=(c == 0), stop=(c == NFF - 1)), reads=[("act", c), "w4"], writes=[pok])
            S.dve(lambda e, psO=psO, xr=xr: e.tensor_tensor(out=xr, in0=psO, in1=xr, op=ALU.add), reads=[pok, xrk], writes=[xrk])
            S.dma(x_out[jo * 128:(jo + 1) * 128, t0:t0 + 512], xr, reads=[xrk])

    for gi, g in enumerate(self.groups):
        body(gi, g)


def phase5(self):
    S, A = self.S, self.A
    A.reset()
    sc = self.scr
    x_in = sc["xres"].rearrange("(c p) t -> p c t", p=128)
    xss = [A.alloc([8, 512], F32) for _ in range(2)]
    sqR = Rot([(A.alloc([512], BF16), ("sq", i)) for i in range(2)])
    lnv = A.alloc([512], F32)
    rstd = A.alloc([512], F32)
    yR = Rot([(A.alloc([512], F32), ("y", i)) for i in range(3)])
    psAs = [self.bank(0), self.bank(1)]
    onesb = self.bpv("ones")

    def body(gi, g):
        t0 = g["t0"]
        xs = xss[gi % 2]
        xk = ("xs", gi % 2)
        S.dma(xs, x_in[:, :, t0:t0 + 512], writes=[xk])
        psA = psAs[gi % 2]
        pak = ("ps", gi % 2)
        for k in range(8):
            sq, sqk = sqR.next()
            S.act(lambda e, sq=sq, k=k: e.activation(out=sq, in_=xs[:, k, :], func=AF.Square), reads=[xk], writes=[sqk])
            S.pe(lambda e, sq=sq, k=k: e.matmul(psA, lhsT=onesb, rhs=sq, start=(k == 0), stop=(k == 7)), reads=[sqk, "bp"], writes=[pak])
        self.rstd_from([(psA, pak, slice(0, 512))], 1.0 / 1024, NORM_EPS, lnv, rstd, "rstd", "lnv")
        for k in range(8):
            y, yk = yR.next()
            S.dve(lambda e, k=k, y=y: e.scalar_tensor_tensor(out=y, in0=xs[:, k, :], scalar=self.ppv("fin", k), in1=rstd,
                                                              op0=ALU.mult, op1=ALU.mult), reads=[xk, "rstd", "pp"], writes=[yk])
            S.dma(self.yT[k * 128:(k + 1) * 128, t0:t0 + 512], y, reads=[yk])

    for gi, g in enumerate(self.groups):
        body(gi, g)


MK.load_cp = load_cp
MK.phase3 = phase3
MK.phase4 = phase4
MK.phase5 = phase5


_NC_CACHE = {}
N_CORES = 8
SEQ_P, SEQ_S, DEPTH = 4096, 2048, 2


def kernel(**inputs):
    xp = np.asarray(inputs["x_prompt"], np.float32)
    xs_ = np.asarray(inputs["x_sample"], np.float32)
    seqs = [SEQ_P, SEQ_P, SEQ_S]
    key = "main"
    if key not in _NC_CACHE:
        mk = MK(seqs, DEPTH)
        _NC_CACHE[key] = mk.build()
    nc = _NC_CACHE[key]
    common = common_inputs(inputs, DEPTH, SEQ_P)
    in_maps = []
    for c in range(N_CORES):
        xT = np.concatenate([xp[2 * c].T, xp[2 * c + 1].T, xs_[c].T], axis=1)
        m = dict(common)
        m["xT"] = np.ascontiguousarray(xT)
        in_maps.append(m)
    res = run_bass_kernel_spmd(nc, in_maps, core_ids=list(range(N_CORES)))
    yp = np.empty((16, SEQ_P, D_MODEL), np.float32)
    ys = np.empty((8, SEQ_S, D_MODEL), np.float32)
    for c in range(N_CORES):
        yT = np.asarray(res.results[c]["yT"], np.float32)
        yp[2 * c] = yT[:, 0:SEQ_P].T
        yp[2 * c + 1] = yT[:, SEQ_P:2 * SEQ_P].T
        ys[c] = yT[:, 2 * SEQ_P:2 * SEQ_P + SEQ_S].T
    return (yp, ys)
```

```python
import math
from contextlib import ExitStack

import numpy as np
import ml_dtypes

import concourse.bass as bass
import concourse.mybir as mybir
from concourse.bass_utils import run_bass_kernel_spmd

F32 = mybir.dt.float32
BF16 = mybir.dt.bfloat16
AF = mybir.ActivationFunctionType
ALU = mybir.AluOpType
AX = mybir.AxisListType

PE, ACT, DVE, POOL, SP = "pe", "act", "dve", "pool", "sp"
ENGS = (PE, ACT, DVE, POOL, SP)

D_MODEL = 1024
IN_COLS = 9136
D_FF = 2816
NFF = 22
NORM_EPS = 1e-6
HEAD_EPS = 1e-5
ROPE_THETA = 10000.0

C_RQ, C_RK, C_RV, C_RG = 0, 256, 512, 1024
C_CQ, C_CKV, C_KR = 1536, 1792, 1920
C_DQ, C_DK, C_DV = 1952, 2464, 2976
C_SZ, C_XBC, C_DT, C_GATE = 3488, 4000, 5024, 5040
NA = 5040


class Op:
    __slots__ = ("eng", "fn", "deps", "is_dma", "sig", "slot", "dval", "idx", "sigval")


class Sched:
    def __init__(self, nc, ndsem=12):
        self.nc = nc
        self.ops = {e: [] for e in ENGS}
        self.res = {}
        self.dma_n = {e: 0 for e in ENGS}
        self.dma_last = {e: {} for e in ENGS}
        import os
        self.ndsem = int(os.environ.get("NDSEM", ndsem))
        rr = int(os.environ.get("NROT", "1"))
        self.nrot = {PE: 4 * rr, ACT: 2 * rr, DVE: 2 * rr, POOL: 2 * rr, SP: 1}
        self.last_real = {e: None for e in ENGS}

    def add(self, eng, fn, reads=(), writes=(), dma=False, extra=(), real=True):
        op = Op()
        op.eng = eng
        op.fn = fn
        op.is_dma = dma
        op.sig = False
        op.idx = len(self.ops[eng])
        op.slot = None
        deps = set(extra)
        res = self.res
        if any(type(k) is tuple and k[0] == "ps" for k in reads):
            writes = list(writes) + [k for k in reads if type(k) is tuple and k[0] == "ps"]
            reads = [k for k in reads if not (type(k) is tuple and k[0] == "ps")]
        for k in reads:
            st = res.get(k)
            if st is not None and st[0] is not None:
                deps.add(st[0])
        for k in writes:
            st = res.get(k)
            if st is not None:
                if st[0] is not None:
                    deps.add(st[0])
                deps.update(st[1].values())
        if dma:
            i = self.dma_n[eng]
            self.dma_n[eng] = i + 1
            op.slot = i % self.ndsem
            op.dval = 16 * (i // self.ndsem + 1)
            prev = self.dma_last[eng].get(op.slot)
            if prev is not None:
                deps.add(prev)
            self.dma_last[eng][op.slot] = op
        rk = (eng, op.slot) if dma else eng
        for k in reads:
            st = res.get(k)
            if st is None:
                st = [None, {}]
                res[k] = st
            st[1][rk] = op
        for k in writes:
            res[k] = [op, {}]
        fd = []
        work = list(deps)
        seen = set()
        while work:
            d = work.pop()
            if d is op or d is None or id(d) in seen:
                continue
            seen.add(id(d))
            if d.fn is None:
                if d.eng != eng:
                    work.extend(d.deps)
                continue
            if (not d.is_dma) and (not dma) and d.eng == eng and eng == PE:
                continue
            fd.append(d)
            if not d.is_dma:
                d.sig = True
        op.deps = fd
        self.ops[eng].append(op)
        if real and not dma:
            self.last_real[eng] = op
        return op

    def pe(self, fn, reads=(), writes=()):
        return self.add(PE, fn, reads, writes)

    def act(self, fn, reads=(), writes=()):
        return self.add(ACT, fn, reads, writes)

    def dve(self, fn, reads=(), writes=()):
        return self.add(DVE, fn, reads, writes)

    def pool(self, fn, reads=(), writes=()):
        return self.add(POOL, fn, reads, writes)

    def dma(self, out, in_, reads=(), writes=(), q=SP, **kw):
        return self.add(q, lambda e: e.dma_start(out=out, in_=in_, **kw), reads, writes, dma=True)

    def barrier(self):
        lasts = [self.last_real[e] for e in ENGS if self.last_real[e] is not None]
        dmas = []
        for e in ENGS:
            dmas.extend(self.dma_last[e].values())
        for e in ENGS:
            if not self.ops[e] and e not in (PE, ACT, DVE, POOL, SP):
                continue
            self.add(e, None, extra=[o for o in lasts if o.eng != e] + dmas, real=False)
        self.res = {}

    def emit(self, es):
        nc = self.nc
        engobj = {PE: nc.tensor, ACT: nc.scalar, DVE: nc.vector, POOL: nc.gpsimd, SP: nc.sync}
        csem = {}
        for e in ENGS:
            if any(o.sig for o in self.ops[e]):
                csem[e] = [es.enter_context(nc.semaphore(f"c_{e}_{r}")) for r in range(self.nrot[e])]
        dsem = {}
        for e in ENGS:
            if self.dma_n[e] > 0:
                dsem[e] = [es.enter_context(nc.semaphore(f"d_{e}_{r}"))
                           for r in range(min(self.ndsem, self.dma_n[e]))]
        for e in ENGS:
            n = 0
            R = self.nrot[e]
            for o in self.ops[e]:
                if o.sig:
                    o.sigval = (n % R, n // R + 1)
                    n += 1
        block = es.enter_context(nc.Block())
        self.nwaits = 0
        self.ninst = 0

        def emit_engine(e):
            eo = engobj[e]
            waited = {}
            for o in self.ops[e]:
                for d in o.deps:
                    if d.is_dma:
                        key = ("d", d.eng, d.slot)
                        val = d.dval
                        sem = dsem[d.eng][d.slot]
                    else:
                        r, val = d.sigval
                        key = ("c", d.eng, r)
                        sem = csem[d.eng][r]
                    if waited.get(key, 0) >= val:
                        continue
                    waited[key] = val
                    eo.wait_ge(sem, val)
                    self.nwaits += 1
                if o.fn is None:
                    continue
                inst = o.fn(eo)
                self.ninst += 1
                if o.is_dma:
                    inst.then_inc(dsem[e][o.slot], 16)
                elif o.sig:
                    r, _ = o.sigval
                    inst.then_inc(csem[e][r], 1)
            for slot, o in self.dma_last[e].items():
                key = ("d", e, slot)
                if waited.get(key, 0) < o.dval:
                    eo.wait_ge(dsem[e][slot], o.dval)

        @block.tensor
        def _(t):
            emit_engine(PE)

        @block.scalar
        def _(t):
            emit_engine(ACT)

        @block.vector
        def _(t):
            emit_engine(DVE)

        @block.gpsimd
        def _(t):
            emit_engine(POOL)

        @block.sync
        def _(t):
            emit_engine(SP)


class Rot:
    def __init__(self, items):
        self.items = items
        self.i = 0

    def next(self):
        it = self.items[self.i % len(self.items)]
        self.i += 1
        return it


PP = {}
_off = 0
for _n, _w in [("wn", 8), ("nf", 8), ("fin", 8), ("scw", 24), ("scb", 8), ("qnw", 2), ("kvnw", 1),
               ("rnw", 4), ("dnw", 1), ("fcw", 66), ("fcb", 22),
               ("lgd", 8), ("dtb", 16), ("alog", 16), ("ssd", 8), ("snw", 512), ("dlam", 256)]:
    PP[_n] = (_off, _w)
    _off += _w
NPP = _off

CP = {}
_off = 0
for _n, _w in [("Uf", 128), ("Ub", 128), ("SLf", 128), ("SLb", 128), ("ones", 128), ("Dpos", 128), ("Dneg", 128),
               ("Mgt", 128), ("Mlt", 128), ("I2", 128), ("Ip1", 128), ("Cmi", 128), ("Cm1j", 64), ("Jj", 64),
               ("C128", 128)]:
    CP[_n] = (_off, _w)
    _off += _w
NCP = _off

BP = {}
_off = 0
for _n, _w in [("ident", 128), ("perm64", 128), ("perm32", 128), ("ones", 128)]:
    BP[_n] = (_off, _w)
    _off += _w
NBP = _off


def make_cpack():
    c = np.zeros((128, NCP), np.float32)
    j = np.arange(128)[:, None].astype(np.float32)
    i = np.arange(128)[None, :].astype(np.float32)

    def put(name, a):
        o, w = CP[name]
        c[:, o:o + w] = a

    put("Uf", (j <= i))
    put("Ub", (j >= i))
    put("SLf", (j > i))
    put("SLb", (j < i))
    put("ones", np.ones((128, 128)))
    put("Dpos", np.maximum(i - j, 0))
    put("Dneg", np.maximum(j - i, 0))
    put("Mgt", (i > j))
    put("Mlt", (j > i))
    put("I2", 2.0 * (i == j))
    put("Ip1", np.broadcast_to(i + 1, (128, 128)))
    put("Cmi", np.broadcast_to(128 - i, (128, 128)))
    put("Cm1j", np.broadcast_to(127 - j, (128, 64)))
    put("Jj", np.broadcast_to(j, (128, 64)))
    put("C128", np.full((128, 128), 128.0))
    return c


def make_bpack():
    b = np.zeros((128, NBP), np.float32)
    k = np.arange(128)[:, None]
    m = np.arange(128)[None, :]

    def put(name, a):
        o, w = BP[name]
        b[:, o:o + w] = a

    put("ident", (k == m))
    put("perm64", (k == (m ^ 32)))
    put("perm32", (k == (m ^ 16)))
    put("ones", np.ones((128, 128)))
    return b.astype(ml_dtypes.bfloat16)


def make_rope(smax):
    out = np.zeros((4, 128, smax), np.float32)
    pos = np.arange(smax, dtype=np.float32)
    for ti, dim in ((0, 64), (2, 32)):
        inv = (1.0 / (np.float32(ROPE_THETA) ** (np.arange(0, dim, 2, dtype=np.float32) / np.float32(dim)))).astype(np.float32)
        ang = pos[:, None] * inv[None, :]
        cos = np.cos(ang).astype(np.float32)
        sin = np.sin(ang).astype(np.float32)
        for f in range(128):
            d = f % dim
            jx = d % (dim // 2)
            out[ti, f] = cos[:, jx]
            out[ti + 1, f] = sin[:, jx] * (-1.0 if d < dim // 2 else 1.0)
    return out


def make_ppack(inp, l):
    p = np.zeros((128, NPP), np.float32)

    def put(name, a):
        o, w = PP[name]
        p[:, o:o + w] = np.asarray(a, np.float32).reshape(128, w)

    def fm(v, nch):
        return np.asarray(v, np.float32).reshape(nch, 128).T

    def bc(v):
        v = np.asarray(v, np.float32).reshape(-1)
        return np.broadcast_to(v[None, :], (128, v.size))

    put("wn", fm(inp["norm_mix_w"][l], 8))
    put("nf", fm(inp["norm_ffn_w"][l], 8))
    put("fin", fm(inp["final_norm_w"], 8))
    put("scw", np.concatenate([fm(inp["ssm_conv_w"][l][t], 8) for t in range(3)], axis=1))
    put("scb", fm(inp["ssm_conv_b"][l], 8))
    put("qnw", fm(inp["mla_q_norm_w"][l], 2))
    put("kvnw", fm(inp["mla_kv_norm_w"][l], 1))
    put("rnw", fm(inp["ret_norm_w"][l], 4))
    put("dnw", fm(inp["diff_norm_w"][l], 1))
    put("fcw", np.concatenate([fm(inp["ffn_conv_w"][l][t], NFF) for t in range(3)], axis=1))
    put("fcb", fm(inp["ffn_conv_b"][l], NFF))
    put("lgd", bc(inp["ret_log_decay"][l]))
    put("dtb", bc(inp["ssm_dt_bias"][l]))
    put("alog", bc(inp["ssm_a_log"][l]))
    put("ssd", bc(inp["ssm_d"][l]))
    put("snw", bc(inp["ssm_norm_w"][l]))
    put("dlam", bc(inp["diff_lambda"][l]))
    return p


class Arena:
    def __init__(self, tensor, nwords):
        self.t = tensor
        self.n = nwords
        self.off = 0
        self.base = 0

    def mark(self):
        self.base = self.off

    def reset(self):
        self.off = self.base

    def alloc(self, shape, dt=F32):
        n = int(np.prod(shape))
        words = n if dt == F32 else (n + 1) // 2
        words = (words + 7) // 8 * 8
        assert self.off + words <= self.n, f"SBUF arena overflow {self.off}+{words}>{self.n}"
        ap = self.t[:, self.off:self.off + words]
        self.off += words
        if dt != F32:
            ap = ap.bitcast(dt)
        ap = ap[:, 0:n]
        if len(shape) == 2:
            ap = ap.rearrange("p (a b) -> p a b", a=shape[0])
        elif len(shape) == 3:
            ap = ap.rearrange("p (a b c) -> p a b c", a=shape[0], b=shape[1])
        return ap


class MK:
    def __init__(self, seqs, depth, debug=(), stop_after=None):
        self.seqs = list(seqs)
        self.T = sum(seqs)
        self.depth = depth
        self.smax = max(seqs)
        self.debug = set(debug)
        self.stop_after = stop_after
        import os
        self.dbgbar = int(os.environ.get('DBGBAR', '0'))
        T = self.T
        nc = bass.Bass("TRN2", target_bir_lowering=False)
        self.nc = nc

        def din(name, shape, dt=F32):
            return nc.dram_tensor(name, shape, dt, kind="ExternalInput").ap()

        self.xT = din("xT", [1024, T])
        self.w_in = din("w_in", [depth, 1024, IN_COLS])
        self.w_uq = din("mla_w_uq", [depth, 256, 768])
        self.w_ukv = din("mla_w_ukv", [depth, 128, 1024])
        self.w_branch = din("w_branch", [depth, 4, 512, 1024])
        self.w_out = din("w_out", [depth, 1024, 1024])
        self.w_gate = din("ffn_w_gate", [depth, 1024, D_FF])
        self.w_up = din("ffn_w_up", [depth, 1024, D_FF])
        self.w_down = din("ffn_w_down", [depth, D_FF, 1024])
        self.ppack = din("ppack", [depth, 128, NPP])
        self.cpack = din("cpack", [128, NCP])
        self.bpack = din("bpack", [128, NBP], BF16)
        self.rope = din("rope", [4, 128, self.smax])
        self.yT = nc.dram_tensor("yT", [1024, T], F32, kind="ExternalOutput").ap()
        self.scr = {}

        def scr(name, shape, dt=BF16):
            kind = "ExternalOutput" if name in self.debug else "Internal"
            self.scr[name] = nc.dram_tensor(name, shape, dt, kind=kind).ap()
            return self.scr[name]

        scr("xres", [1024, T], F32)
        scr("xmid", [1024, T], F32)
        scr("rqT", [256, T]); scr("rkT", [256, T]); scr("rv", [T, 512]); scr("rgT", [512, T])
        scr("mqnT", [512, T]); scr("mqrT", [256, T]); scr("mknT", [512, T]); scr("mkrT", [32, T])
        scr("mva", [T, 1024])
        scr("dqT", [512, T]); scr("dkT", [512, T]); scr("dv", [T, 512])
        scr("sz", [T, 512]); scr("sxB", [T, 768]); scr("sBT", [256, T]); scr("sCT", [256, T])
        scr("sdt", [T, 16], F32)
        scr("boT", [2048, T])
        self.groups = []
        base = 0
        for si, S_ in enumerate(self.seqs):
            for g in range(S_ // 512):
                self.groups.append(dict(s=si, t0=base + g * 512, pos0=g * 512, S=S_, sbase=base))
            base += S_

    def build(self):
        nc = self.nc
        with ExitStack() as es:
            self.es = es
            self.S = Sched(nc)
            ARENA_WORDS = 51 * 1024
            at = es.enter_context(nc.sbuf_tensor("arena", [128, ARENA_WORDS], F32))
            self.A = Arena(at, ARENA_WORDS)
            self.banks = [es.enter_context(nc.psum_tensor(f"bank{i}", [128, 512], F32)) for i in range(8)]
            A = self.A
            S = self.S
            self.bp = A.alloc([NBP], BF16)
            self.pp = A.alloc([NPP], F32)
            S.dma(self.bp, self.bpack, writes=["bp"])
            A.mark()
            S.barrier()
            order = ["p1", "ret", "mla", "diff", "ssd", "p3", "p4"]
            sa = self.stop_after
            only = getattr(self, "only", None)
            done = False
            for l in range(self.depth):
                S.dma(self.pp, self.ppack[l], writes=["pp"])
                S.barrier()
                for ph in order:
                    run = True
                    if sa not in (None, "all") and l == self.depth - 1:
                        if sa in ("p3", "p4"):
                            run = order.index(ph) <= order.index(sa)
                        else:
                            run = ph in ("p1", sa)
                    if run:
                        {"p1": self.phase1, "ret": self.phase_ret, "mla": self.phase_mla, "diff": self.phase_diff,
                         "ssd": self.phase_ssd, "p3": self.phase3, "p4": self.phase4}[ph](l)
                        S.barrier()
            if sa in (None, "all"):
                self.phase5()
            S.emit(es)
        return nc

    def dbg(self, name, ap, reads):
        if name not in self.debug:
            return
        shp = [int(x) for x in ap.shape]
        t = self.nc.dram_tensor(name, shp, ap.dtype, kind="ExternalOutput").ap()
        self.S.dma(t, ap, reads=reads)

    def cpv(self, name):
        o, w = CP[name]
        return self.cp[:, o:o + w]

    def bpv(self, name):
        o, w = BP[name]
        return self.bp[:, o:o + w]

    def ppv(self, name, i=0, n=1):
        o, w = PP[name]
        return self.pp[:, o + i:o + i + n]

    def bank(self, i, dt=F32):
        b = self.banks[i][:]
        if dt != F32:
            b = b.bitcast(dt)
        return b

    def load_cast(self, name, pieces, stgR):
        S = self.S
        keys = []
        for i, pc in enumerate(pieces):
            dst, src = pc[0], pc[1]
            st, sk = stgR.next()
            shp = list(src.shape)
            n = int(np.prod(shp[1:]))
            stv = st[:, 0:n]
            if len(shp) == 3:
                stv = stv.rearrange("p (a b) -> p a b", a=shp[1])
            S.dma(stv, src, writes=[sk])
            if len(pc) > 2:
                stv = pc[2](stv)
            k = (name, i)
            keys.append(k)
            eng = (ACT, DVE, POOL)[i % 3]
            if eng == ACT:
                S.act(lambda e, dst=dst, stv=stv: e.activation(out=dst, in_=stv, func=AF.Copy), reads=[sk], writes=[k])
            else:
                S.add(eng, lambda e, dst=dst, stv=stv: e.tensor_copy(out=dst, in_=stv), reads=[sk], writes=[k])
        S.add(PE, None, reads=keys, writes=[name], real=False)

    def mmg(self, out, okey, pairs, reads):
        S = self.S
        n = len(pairs)
        for i, (lhsT, rhs) in enumerate(pairs):
            S.pe(lambda e, lhsT=lhsT, rhs=rhs, i=i: e.matmul(out, lhsT=lhsT, rhs=rhs, start=(i == 0), stop=(i == n - 1)),
                 reads=reads, writes=[okey])

    def rstd_from(self, ps_list, inv_n, eps, lnv, rstd, rkey, lkey):
        S = self.S
        for ps, pk, sl in ps_list:
            S.act(lambda e, ps=ps, sl=sl: e.activation(out=lnv[:, sl], in_=ps, func=AF.Ln, scale=inv_n, bias=eps),
                  reads=[pk], writes=[lkey])
        S.act(lambda e: e.activation(out=rstd, in_=lnv, func=AF.Exp, scale=-0.5), reads=[lkey], writes=[rkey])

    def phase1(self, l):
        S, A, nc = self.S, self.A, self.nc
        A.reset()
        sc = self.scr
        x_in = (self.xT if l == 0 else sc["xres"]).rearrange("(c p) t -> p c t", p=128)
        wA = A.alloc([8, NA], BF16)
        wuqn = A.alloc([2, 8, 64], BF16)
        wuqr = A.alloc([2, 8, 32], BF16)
        wukn = A.alloc([8, 64], BF16)
        wuv = A.alloc([8, 64], BF16)
        stgR = Rot([(A.alloc([1536], F32), ("stg", i)) for i in range(2)])
        win = self.w_in[l].rearrange("(c p) n -> p c n", p=128)
        pieces = []
        for k in range(8):
            for hf in range(4):
                pieces.append((wA[:, k, hf * 1260:(hf + 1) * 1260], win[:, k, hf * 1260:(hf + 1) * 1260]))
        uq = self.w_uq[l].rearrange("(c p) n -> p c n", p=128)
        pieces.append((wuqn, uq, lambda v: v.rearrange("p k (h d) -> p k h d", h=8)[:, :, :, 0:64]))
        pieces.append((wuqr, uq, lambda v: v.rearrange("p k (h d) -> p k h d", h=8)[:, :, :, 64:96]))
        pieces.append((wukn, self.w_ukv[l], lambda v: v.rearrange("p (h d) -> p h d", h=8)[:, :, 0:64]))
        pieces.append((wuv, self.w_ukv[l], lambda v: v.rearrange("p (h d) -> p h d", h=8)[:, :, 64:128]))
        self.load_cast("wA", pieces, stgR)

        xs = A.alloc([8, 514], F32)
        hTs = [A.alloc([8, 514], BF16) for _ in range(2)]
        sqR = Rot([(A.alloc([514], BF16), ("sq", i)) for i in range(2)])
        lnv = A.alloc([514], F32)
        rstd = A.alloc([514], F32)
        tab = A.alloc([4, 512], F32)
        tk = "tab"
        xsbR = Rot([(A.alloc([512], BF16), ("xsb", i)) for i in range(3)])
        t1R = Rot([(A.alloc([512], F32), ("t1", i)) for i in range(2)])
        t2R = Rot([(A.alloc([512], F32), ("t2", i)) for i in range(2)])
        roR = Rot([(A.alloc([512], BF16), ("ro", i)) for i in range(3)])
        soR = Rot([(A.alloc([512], BF16), ("so", i)) for i in range(3)])
        GR = Rot([(A.alloc([514], F32), ("G", i)) for i in range(2)])
        cvR = Rot([(A.alloc([512], F32), ("cv", i)) for i in range(2)])
        cqf = A.alloc([2, 512], F32)
        ckvf = A.alloc([512], F32)
        cqn = A.alloc([2, 512], BF16)
        ckvn = A.alloc([512], BF16)
        lnq = lnv[:, 0:512]
        rsq = rstd[:, 0:512]
        tokR = Rot([(A.alloc([512], BF16), ("tok", i)) for i in range(3)])
        vaugs = [A.alloc([8, 128], BF16) for _ in range(2)]
        xTok = A.alloc([4, 768], BF16)
        dtx = A.alloc([4, 16], F32)
        dte = A.alloc([4, 16], F32)
        dts = A.alloc([4, 16], F32)
        for i, va in enumerate(vaugs):
            S.pool(lambda e, va=va: e.memset(va, 1.0), writes=[("vaug", i)])
        mainR = Rot([(self.bank(i), ("ps", i)) for i in range(4)])
        psA = self.bank(4)
        psH = self.bank(5)
        permR = Rot([(self.bank(6), ("ps", 6))])
        psT = self.bank(7, BF16)
        onesb = self.bpv("ones")
        ident = self.bpv("ident")
        ropeT = self.rope.rearrange("a p t -> p a t")
        tcount = [0]

        pend = []

        def flush():
            while pend:
                pend.pop(0)()

        def MM(out, okey, pairs, reads):
            self.mmg(out, okey, pairs, reads)
            flush()

        def rope_unit(ps, pk, M, scale, ci, perm, dst):
            tab, tk = self.cur_tab
            Ct = tab[0:M, ci, :]
            St = tab[0:M, ci + 1, :]
            xsb, k1 = xsbR.next()
            S.act(lambda e: e.activation(out=xsb, in_=ps, func=AF.Copy, scale=scale),
                  reads=[pk], writes=[k1])
            t1, kt1 = t1R.next()
            S.pool(lambda e: e.tensor_tensor(out=t1[0:M, :], in0=xsb[0:M, :], in1=Ct, op=ALU.mult),
                   reads=[k1, tk], writes=[kt1])

            def part_b():
                pp_, pk2 = permR.next()
                S.pe(lambda e: e.matmul(pp_, lhsT=perm, rhs=xsb, start=True, stop=True),
                     reads=[k1, "bp"], writes=[pk2])
                t2, kt2 = t2R.next()
                S.dve(lambda e: e.tensor_tensor(out=t2[0:M, :], in0=pp_[0:M, :], in1=St, op=ALU.mult),
                      reads=[pk2, tk], writes=[kt2])
                ro, kro = roR.next()
                S.dve(lambda e: e.tensor_tensor(out=ro[0:M, :], in0=t1[0:M, :], in1=t2[0:M, :], op=ALU.add),
                      reads=[kt1, kt2], writes=[kro])
                S.dma(dst, ro[0:M, :], reads=[kro])

            pend.append(part_b)

        def load_x(gi_):
            if gi_ >= len(self.groups):
                return
            g_ = self.groups[gi_]
            hasL = g_["pos0"] > 0
            hasR = g_["pos0"] + 512 < g_["S"]
            if not hasL:
                S.pool(lambda e: e.memset(xs[:, :, 0:1], 0.0), writes=["xsh"])
            if not hasR:
                S.pool(lambda e: e.memset(xs[:, :, 513:514], 0.0), writes=["xsh"])
            lo = g_["t0"] - 1 if hasL else g_["t0"]
            hi = g_["t0"] + 513 if hasR else g_["t0"] + 512
            c0 = 0 if hasL else 1
            S.dma(xs[:, :, c0:c0 + (hi - lo)], x_in[:, :, lo:hi], writes=["xs"])

        def load_tab(gi_):
            if gi_ >= len(self.groups):
                return
            p0_ = self.groups[gi_]["pos0"]
            S.dma(tab, ropeT[:, :, p0_:p0_ + 512], writes=[tk])

        def do_rms(gi_):
            if gi_ >= len(self.groups):
                return
            hT_ = hTs[gi_ % 2]
            hk_ = [("hT", gi_ % 2, k) for k in range(8)]
            for k in range(8):
                sq, sqk = sqR.next()
                S.act(lambda e, sq=sq, k=k: e.activation(out=sq, in_=xs[:, k, :], func=AF.Square), reads=["xs", "xsh"], writes=[sqk])
                S.pe(lambda e, sq=sq, k=k: e.matmul(psA, lhsT=onesb, rhs=sq[:, 0:512], start=(k == 0), stop=(k == 7)),
                     reads=[sqk, "bp"], writes=[("ps", 4)])
                S.pe(lambda e, sq=sq, k=k: e.matmul(psH[:, 0:2], lhsT=onesb, rhs=sq[:, 512:514], start=(k == 0), stop=(k == 7)),
                     reads=[sqk, "bp"], writes=[("ps", 5)])
            self.rstd_from([(psA, ("ps", 4), slice(0, 512)), (psH[:, 0:2], ("ps", 5), slice(512, 514))],
                           1.0 / 1024, NORM_EPS, lnv, rstd, "rstd", "lnv")
            for k in range(8):
                S.dve(lambda e, k=k: e.scalar_tensor_tensor(out=hT_[:, k, :], in0=xs[:, k, :], scalar=self.ppv("wn", k),
                                                             in1=rstd, op0=ALU.mult, op1=ALU.mult),
                      reads=["xs", "xsh", "rstd", "pp"], writes=[hk_[k]])
            load_x(gi_ + 1)

        def group_body(gi, g):
            t0, pos0, Sq = g["t0"], g["pos0"], g["S"]
            hT = hTs[gi % 2]
            hk = [("hT", gi % 2, k) for k in range(8)]
            if gi == 0:
                load_x(0)
                load_tab(0)
            self.cur_tab = (tab, tk)
            if gi == 0:
                do_rms(0)

            def fm_mm(c0_, M):
                ps, pk = mainR.next()
                MM(ps[0:M, :], pk, [(wA[:, k, c0_:c0_ + M], hT[:, k, 1:513]) for k in range(8)], hk + ["wA"])
                return ps, pk

            def ret_unit(c0_, j, scale, dname):
                ps, pk = fm_mm(c0_ + j * 128, 128)
                rope_unit(ps, pk, 128, scale, 0, self.bpv("perm64"), sc[dname][j * 128:(j + 1) * 128, t0:t0 + 512])

            for j in range(2):
                ps, pk = fm_mm(C_CQ + j * 128, 128)
                S.act(lambda e, ps=ps, j=j: e.activation(out=cqf[:, j, :], in_=ps, func=AF.Copy), reads=[pk], writes=[("cqf", j)])
                sq, sqk = sqR.next()
                S.act(lambda e, sq=sq, j=j: e.activation(out=sq[:, 0:512], in_=cqf[:, j, :], func=AF.Square),
                      reads=[("cqf", j)], writes=[sqk])
                pend.append(lambda sq=sq, j=j, sqk=sqk: S.pe(
                    lambda e: e.matmul(psA, lhsT=onesb, rhs=sq[:, 0:512], start=(j == 0), stop=(j == 1)),
                    reads=[sqk, "bp"], writes=[("ps", 4)]))
            ret_unit(C_RQ, 0, 1.0, "rqT")
            self.rstd_from([(psA, ("ps", 4), slice(0, 512))], 1.0 / 256, NORM_EPS, lnq, rsq, "rstd", "lnv")
            for j in range(2):
                S.dve(lambda e, j=j: e.scalar_tensor_tensor(out=cqn[:, j, :], in0=cqf[:, j, :], scalar=self.ppv("qnw", j),
                                                             in1=rsq, op0=ALU.mult, op1=ALU.mult),
                      reads=[("cqf", j), "rstd", "pp"], writes=[("cqn", j)])
            cqk = [("cqn", 0), ("cqn", 1)]
            ps, pk = fm_mm(C_CKV, 128)
            S.act(lambda e, ps=ps: e.activation(out=ckvf, in_=ps, func=AF.Copy), reads=[pk], writes=["ckvf"])
            sq, sqk = sqR.next()
            S.act(lambda e, sq=sq: e.activation(out=sq[:, 0:512], in_=ckvf, func=AF.Square), reads=["ckvf"], writes=[sqk])
            pend.append(lambda sq=sq, sqk=sqk: S.pe(lambda e: e.matmul(psA, lhsT=onesb, rhs=sq[:, 0:512], start=True, stop=True),
                                                    reads=[sqk, "bp"], writes=[("ps", 4)]))
            ret_unit(C_RQ, 1, 1.0, "rqT")
            self.rstd_from([(psA, ("ps", 4), slice(0, 512))], 1.0 / 128, NORM_EPS, lnq, rsq, "rstd", "lnv")
            S.dve(lambda e: e.scalar_tensor_tensor(out=ckvn, in0=ckvf, scalar=self.ppv("kvnw", 0), in1=rsq,
                                                   op0=ALU.mult, op1=ALU.mult),
                  reads=["ckvf", "rstd", "pp"], writes=["ckvn"])
            ret_unit(C_RK, 0, 0.125, "rkT")
            ret_unit(C_RK, 1, 0.125, "rkT")
            for i in range(4):
                ps, pk = mainR.next()
                MM(ps, pk, [(wuqn[:, k, 2 * i:2 * i + 2, :].rearrange("p a b -> p (a b)"), cqn[:, k, :]) for k in range(2)], cqk + ["wA"])
                so, sk = soR.next()
                S.act(lambda e, ps=ps, so=so: e.activation(out=so, in_=ps, func=AF.Copy), reads=[pk], writes=[sk])
                S.dma(sc["mqnT"][i * 128:(i + 1) * 128, t0:t0 + 512], so, reads=[sk])
            for i in range(2):
                ps, pk = mainR.next()
                MM(ps, pk, [(wuqr[:, k, 4 * i:4 * i + 4, :].rearrange("p a b -> p (a b)"), cqn[:, k, :]) for k in range(2)], cqk + ["wA"])
                rope_unit(ps, pk, 128, 1.0, 2, self.bpv("perm32"), sc["mqrT"][i * 128:(i + 1) * 128, t0:t0 + 512])
            for i in range(4):
                ps, pk = mainR.next()
                MM(ps, pk, [(wukn[:, 2 * i:2 * i + 2, :].rearrange("p a b -> p (a b)"), ckvn)], ["ckvn", "wA"])
                so, sk = soR.next()
                S.act(lambda e, ps=ps, so=so: e.activation(out=so, in_=ps, func=AF.Copy), reads=[pk], writes=[sk])
                S.dma(sc["mknT"][i * 128:(i + 1) * 128, t0:t0 + 512], so, reads=[sk])
            for tt in range(4):
                ps, pk = mainR.next()
                MM(ps, pk, [(ckvn[:, tt * 128:(tt + 1) * 128], wuv.rearrange("p a b -> p (a b)"))], ["ckvn", "wA"])
                va = vaugs[tt % 2]
                vk = ("vaug", tt % 2)
                S.act(lambda e, ps=ps, va=va: e.activation(out=va[:, :, 0:64], in_=ps.rearrange("p (h d) -> p h d", h=8),
                                                           func=AF.Copy), reads=[pk], writes=[vk])
                S.dma(sc["mva"][t0 + tt * 128:t0 + (tt + 1) * 128, :], va.rearrange("p h d -> p (h d)"), reads=[vk])
            ps, pk = fm_mm(C_KR, 128)
            rope_unit(ps, pk, 32, 1.0, 2, self.bpv("perm32"), sc["mkrT"][0:32, t0:t0 + 512])
            for tt in range(4):
                self.mmg(psH[:, 32 + tt * 16:48 + tt * 16], ("ps", 5),
                         [(hT[:, k, 1 + tt * 128:1 + (tt + 1) * 128], wA[:, k, C_DT:C_DT + 16]) for k in range(8)], hk + ["wA"])
            o_, w_ = PP["dtb"]
            dtb_b = self.pp[:, o_:o_ + 16].unsqueeze(1).broadcast_to([128, 4, 16])
            S.dve(lambda e: e.tensor_tensor(out=dtx, in0=psH[:, 32:96].rearrange("p (a b) -> p a b", a=4), in1=dtb_b, op=ALU.add),
                  reads=[("ps", 5), "pp"], writes=["dtx"])
            S.act(lambda e: e.activation(out=dte, in_=dtx, func=AF.Exp), reads=["dtx"], writes=["dte"])
            S.act(lambda e: e.activation(out=dts, in_=dte, func=AF.Ln, bias=1.0), reads=["dte"], writes=["dts"])
            S.dma(sc["sdt"][t0:t0 + 512, :].rearrange("(a p) c -> p a c", p=128), dts, reads=["dts"])
            for (c0_, nch, scale, dname) in ((C_DQ, 4, 1.0, "dqT"), (C_DK, 4, 1.0, "dkT")):
                for j in range(nch):
                    ret_unit(c0_, j, scale, dname)
            flush()
            load_tab(gi + 1)
            do_rms(gi + 1)
            for (c0_, dname) in ((C_RV, "rv"), (C_DV, "dv")):
                for tt in range(4):
                    ps, pk = mainR.next()
                    MM(ps, pk, [(hT[:, k, 1 + tt * 128:1 + (tt + 1) * 128], wA[:, k, c0_:c0_ + 512]) for k in range(8)], hk + ["wA"])
                    tk_, tkk = tokR.next()
                    S.act(lambda e, ps=ps, tk_=tk_: e.activation(out=tk_, in_=ps, func=AF.Copy), reads=[pk], writes=[tkk])
                    S.dma(sc[dname][t0 + tt * 128:t0 + (tt + 1) * 128, :], tk_, reads=[tkk])
            for j in range(4):
                ps, pk = fm_mm(C_RG + j * 128, 128)
                so, sk = soR.next()
                S.act(lambda e, ps=ps, so=so: e.activation(out=so, in_=ps, func=AF.Silu), reads=[pk], writes=[sk])
                S.dma(sc["rgT"][j * 128:(j + 1) * 128, t0:t0 + 512], so, reads=[sk])
            for tt in range(4):
                ps, pk = mainR.next()
                MM(ps, pk, [(hT[:, k, 1 + tt * 128:1 + (tt + 1) * 128], wA[:, k, C_SZ:C_SZ + 512]) for k in range(8)], hk + ["wA"])
                tk_, tkk = tokR.next()
                S.act(lambda e, ps=ps, tk_=tk_: e.activation(out=tk_, in_=ps, func=AF.Silu), reads=[pk], writes=[tkk])
                S.dma(sc["sz"][t0 + tt * 128:t0 + (tt + 1) * 128, :], tk_, reads=[tkk])
            late_q = [None]
            for j in range(8):
                cc = C_XBC + j * 128
                ps, pk = fm_mm(cc, 128)
                hcol = 2 + 2 * j
                hkey = ("ps", 5)
                for k in range(8):
                    S.pe(lambda e, k=k, cc=cc, hcol=hcol: e.matmul(psH[:, hcol:hcol + 2], lhsT=wA[:, k, cc:cc + 128],
                                                                    rhs=hT[:, k, 0:514:513], start=(k == 0), stop=(k == 7)),
                         reads=hk + ["wA"], writes=[hkey])
                G, gk = GR.next()
                S.act(lambda e, ps=ps, G=G: e.activation(out=G[:, 1:513], in_=ps, func=AF.Copy), reads=[pk], writes=[gk])
                S.dve(lambda e, G=G, hcol=hcol: e.tensor_copy(out=G[:, 0:514:513], in_=psH[:, hcol:hcol + 2]), reads=[hkey], writes=[(gk, "h")])
                cv, ck = cvR.next()
                o_w = PP["scw"][0]
                o_b = PP["scb"][0]
                S.dve(lambda e, G=G, cv=cv, j=j: e.tensor_scalar(out=cv, in0=G[:, 1:513], scalar1=self.pp[:, o_w + 8 + j:o_w + 9 + j],
                                                                  scalar2=self.pp[:, o_b + j:o_b + j + 1], op0=ALU.mult, op1=ALU.add),
                      reads=[gk, "pp"], writes=[ck])
                S.dve(lambda e, G=G, cv=cv, j=j: e.scalar_tensor_tensor(out=cv, in0=G[:, 0:512], scalar=self.pp[:, o_w + j:o_w + j + 1],
                                                                         in1=cv, op0=ALU.mult, op1=ALU.add),
                      reads=[gk, (gk, "h"), "pp", ck], writes=[ck])
                S.dve(lambda e, G=G, cv=cv, j=j: e.scalar_tensor_tensor(out=cv, in0=G[:, 2:514], scalar=self.pp[:, o_w + 16 + j:o_w + 17 + j],
                                                                         in1=cv, op0=ALU.mult, op1=ALU.add),
                      reads=[gk, (gk, "h"), "pp", ck], writes=[ck])
                def late(cv=cv, ck=ck, j=j):
                    so, sk = soR.next()
                    S.act(lambda e, cv=cv, so=so: e.activation(out=so, in_=cv, func=AF.Silu), reads=[ck], writes=[sk])
                    if j >= 4:
                        dname, r0 = ("sBT", (j - 4) * 128) if j < 6 else ("sCT", (j - 6) * 128)
                        S.dma(sc[dname][r0:r0 + 128, t0:t0 + 512], so, reads=[sk])
                    if j < 6:
                        def tr_part(so=so, sk=sk, j=j):
                            half = tcount[0] % 2
                            tcount[0] += 1
                            pT = psT[:, half * 512:(half + 1) * 512]
                            ptk = ("ps", 7)
                            for tt in range(4):
                                S.pe(lambda e, tt=tt: e.transpose(out=pT[:, tt * 128:(tt + 1) * 128], in_=so[:, tt * 128:(tt + 1) * 128],
                                                                  identity=ident), reads=[sk, "bp"], writes=[ptk])
                            S.dve(lambda e: e.tensor_copy(out=xTok[:, :, j * 128:(j + 1) * 128],
                                                          in_=pT.rearrange("p (a b) -> p a b", a=4)),
                                  reads=[ptk], writes=[("xTok", j)])
                        pend.append(tr_part)
                if late_q[0] is not None:
                    late_q[0]()
                late_q[0] = late
            late_q[0]()
            flush()
            S.dma(sc["sxB"][t0:t0 + 512, :].rearrange("(a p) c -> p a c", p=128), xTok, reads=[("xTok", j) for j in range(6)])

        for gi, g in enumerate(self.groups):
            group_body(gi, g)


def common_inputs(inp, depth, smax):
    f = lambda n: np.ascontiguousarray(np.asarray(inp[n], np.float32))
    m = {
        "w_in": f("w_in"), "mla_w_uq": f("mla_w_uq"), "mla_w_ukv": f("mla_w_ukv"), "w_branch": f("w_branch"),
        "w_out": f("w_out"), "ffn_w_gate": f("ffn_w_gate"), "ffn_w_up": f("ffn_w_up"), "ffn_w_down": f("ffn_w_down"),
        "ppack": np.stack([make_ppack(inp, l) for l in range(depth)]),
        "cpack": make_cpack(), "bpack": make_bpack(), "rope": make_rope(smax),
    }
    return m


def _seq_list(mk):
    out = []
    base = 0
    for S_ in mk.seqs:
        out.append((base, S_))
        base += S_
    return out


def phase_mla(self, l):
    S, A = self.S, self.A
    A.reset()
    sc = self.scr
    smax = self.smax
    NKCm = smax // 128
    Vaug = A.alloc([NKCm, 8, 128], BF16)
    KTs = [A.alloc([smax], BF16) for _ in range(2)]
    QTs = [A.alloc([smax], BF16) for _ in range(2)]
    pTR = Rot([(A.alloc([512], BF16), ("pT", i)) for i in range(4)])
    rsR = Rot([(A.alloc([512], F32), ("rs", i)) for i in range(2)])
    obR = Rot([(A.alloc([512], BF16), ("ob", i)) for i in range(2)])
    scR = Rot([(self.bank(i), ("ps", i)) for i in range(4)])
    accR = Rot([(self.bank(4 + i), ("ps", 4 + i)) for i in range(3)])
    scale = 96.0 ** -0.5
    LA = 2
    for (base, Sq) in _seq_list(self):
        NKC = Sq // 128
        NQG = Sq // 512
        for c0 in range(0, NKC, 4):
            S.dma(Vaug[:, c0:c0 + 4, :, :].rearrange("p c h d -> p c (h d)"),
                  sc["mva"][base + c0 * 128:base + (c0 + 4) * 128, :].rearrange("(c p) e -> p c e", p=128),
                  writes=[("V", c0)])
        vkeys = [("V", c0) for c0 in range(0, NKC, 4)]

        def load_head(h, base=base, Sq=Sq):
            KT, QT = KTs[h % 2], QTs[h % 2]
            S.dma(KT[0:64, 0:Sq], sc["mknT"][h * 64:(h + 1) * 64, base:base + Sq], writes=[("KT", h % 2, 0)])
            S.dma(KT[64:96, 0:Sq], sc["mkrT"][0:32, base:base + Sq], writes=[("KT", h % 2, 1)])
            S.dma(QT[0:64, 0:Sq], sc["mqnT"][h * 64:(h + 1) * 64, base:base + Sq], writes=[("QT", h % 2, 0)])
            S.dma(QT[64:96, 0:Sq], sc["mqrT"][h * 32:(h + 1) * 32, base:base + Sq], writes=[("QT", h % 2, 1)])

        items = [(h, qg, kc) for h in range(8) for qg in range(NQG) for kc in range(NKC)]
        state = {}

        def do_S(it):
            h, qg, kc = it
            if qg == 0 and kc == 0:
                if h == 0:
                    load_head(0)
                if h + 1 < 8:
                    load_head(h + 1)
            KT, QT = KTs[h % 2], QTs[h % 2]
            ps, pk = scR.next()
            state[it] = (ps, pk)
            S.pe(lambda e: e.matmul(ps, lhsT=KT[0:96, kc * 128:(kc + 1) * 128], rhs=QT[0:96, qg * 512:(qg + 1) * 512],
                                    start=True, stop=True),
                 reads=[("KT", h % 2, 0), ("KT", h % 2, 1), ("QT", h % 2, 0), ("QT", h % 2, 1)], writes=[pk])
            pT, ptk = pTR.next()
            S.act(lambda e: e.activation(out=pT, in_=ps, func=AF.Exp, scale=scale), reads=[pk], writes=[ptk])
            state[it] = (pT, ptk)

        def do_PV(it, base=base):
            h, qg, kc = it
            pT, ptk = state.pop(it)
            if kc == 0:
                state[("acc", h, qg)] = accR.next()
            acc, ak = state[("acc", h, qg)]
            S.pe(lambda e: e.matmul(acc, lhsT=Vaug[:, kc, h, :], rhs=pT, start=(kc == 0), stop=(kc == NKC - 1)),
                 reads=[ptk] + vkeys, writes=[ak])
            if kc == NKC - 1:
                rs, rk = rsR.next()
                S.dve(lambda e: e.reciprocal(out=rs[0:64, :], in_=acc[64:128, :]), reads=[ak], writes=[rk])
                ob, ok = obR.next()
                S.dve(lambda e: e.tensor_tensor(out=ob[0:64, :], in0=acc[0:64, :], in1=rs[0:64, :], op=ALU.mult),
                      reads=[ak, rk], writes=[ok])
                t0 = base + qg * 512
                S.dma(sc["boT"][512 + h * 64:512 + (h + 1) * 64, t0:t0 + 512], ob[0:64, :], reads=[ok])
                del state[("acc", h, qg)]

        n = len(items)
        for i in range(n + LA):
            if i < n:
                do_S(items[i])
            if i - LA >= 0:
                do_PV(items[i - LA])


def phase_diff(self, l):
    S, A = self.S, self.A
    A.reset()
    sc = self.scr
    smax = self.smax
    NKCm = smax // 128
    lam_init = 0.8 - 0.6 * math.exp(-0.3 * l)
    V = A.alloc([NKCm, 512], BF16)
    KTz = [[A.alloc([smax], BF16) for _ in range(2)] for _ in range(2)]
    QTs = [A.alloc([smax], BF16) for _ in range(2)]
    for b_ in range(2):
        S.pool(lambda e, b_=b_: e.memset(KTz[b_][0][64:128, :], 0.0), writes=[("KT", b_, 0)])
        S.pool(lambda e, b_=b_: e.memset(KTz[b_][1][0:64, :], 0.0), writes=[("KT", b_, 1)])
    pTR = Rot([(A.alloc([512], BF16), ("pT", i)) for i in range(4)])
    rR = Rot([(A.alloc([512], F32), ("r", i)) for i in range(2)])
    tR = [A.alloc([512], F32) for _ in range(2)]
    dbuf = A.alloc([512], F32)
    sqb = A.alloc([512], BF16)
    lnb = A.alloc([512], F32)
    rsb = A.alloc([512], F32)
    obR = Rot([(A.alloc([512], BF16), ("ob", i)) for i in range(2)])
    junk = A.alloc([64], F32)
    sv = A.alloc([8], F32)
    scR = Rot([(self.bank(i), ("ps", i)) for i in range(3)])
    accO = [self.bank(3), self.bank(5)]
    accS = [self.bank(4), self.bank(6)]
    psN = self.bank(7)
    onesb = self.bpv("ones")
    o_, _w = PP["dlam"]
    dl = self.pp[:, o_:o_ + 256]
    for i in range(2):
        S.dve(lambda e, i=i: e.scalar_tensor_tensor(out=junk, in0=dl[:, i * 128:i * 128 + 64], scalar=1.0,
                                                    in1=dl[:, i * 128 + 64:i * 128 + 128], op0=ALU.mult, op1=ALU.mult,
                                                    accum_out=sv[:, i:i + 1]), reads=["pp", "junk"], writes=["junk", ("sv", i)])
    S.act(lambda e: e.activation(out=sv[:, 2:4], in_=sv[:, 0:2], func=AF.Exp), reads=[("sv", 0), ("sv", 1)], writes=["sve"])
    S.dve(lambda e: e.tensor_tensor(out=sv[:, 4:5], in0=sv[:, 3:4], in1=sv[:, 2:3], op=ALU.subtract), reads=["sve"], writes=["nl0"])
    S.dve(lambda e: e.tensor_scalar(out=sv[:, 5:6], in0=sv[:, 4:5], scalar1=-lam_init, scalar2=None, op0=ALU.add),
          reads=["nl0"], writes=["neglam"])
    S.dve(lambda e: e.tensor_scalar(out=sv[:, 6:7], in0=self.ppv("dnw", 0), scalar1=(1.0 - lam_init), scalar2=None, op0=ALU.mult),
          reads=["pp"], writes=["wd"])
    neglam = sv[:, 5:6]
    wd = sv[:, 6:7]
    LA = 2
    for (base, Sq) in _seq_list(self):
        NKC = Sq // 128
        NQG = Sq // 512
        for c0 in range(0, NKC, 8):
            S.dma(V[:, c0:c0 + 8, :], sc["dv"][base + c0 * 128:base + (c0 + 8) * 128, :].rearrange("(c p) e -> p c e", p=128),
                  writes=[("V", c0)])
        vkeys = [("V", c0) for c0 in range(0, NKC, 8)]

        def load_head(h, base=base, Sq=Sq):
            S.dma(KTz[h % 2][0][0:64, 0:Sq], sc["dkT"][h * 128:h * 128 + 64, base:base + Sq], writes=[("KT", h % 2, 0)])
            S.dma(KTz[h % 2][1][64:128, 0:Sq], sc["dkT"][h * 128 + 64:(h + 1) * 128, base:base + Sq], writes=[("KT", h % 2, 1)])
            S.dma(QTs[h % 2][:, 0:Sq], sc["dqT"][h * 128:(h + 1) * 128, base:base + Sq], writes=[("QT", h % 2)])

        items = [(h, qg, m, kc) for h in range(4) for qg in range(NQG) for m in range(2) for kc in range(NKC)]
        state = {}

        def do_S(it):
            h, qg, m, kc = it
            if qg == 0 and kc == 0 and m == 0:
                if h == 0:
                    load_head(0)
                if h + 1 < 4:
                    load_head(h + 1)
            KT, QT = KTz[h % 2][m], QTs[h % 2]
            ps, pk = scR.next()
            S.pe(lambda e: e.matmul(ps, lhsT=KT[:, kc * 128:(kc + 1) * 128],
                                    rhs=QT[:, qg * 512:(qg + 1) * 512], start=True, stop=True),
                 reads=[("KT", h % 2, m), ("QT", h % 2)], writes=[pk])
            pT, ptk = pTR.next()
            S.act(lambda e: e.activation(out=pT, in_=ps, func=AF.Exp, scale=0.125), reads=[pk], writes=[ptk])
            state[it] = (pT, ptk)

        def do_PV(it, base=base):
            h, qg, m, kc = it
            pT, ptk = state.pop(it)
            aO, aS = accO[m], accS[m]
            kO, kS = ("ps", 3 + 2 * m), ("ps", 4 + 2 * m)
            S.pe(lambda e: e.matmul(aO, lhsT=V[:, kc, h * 128:(h + 1) * 128], rhs=pT, start=(kc == 0), stop=(kc == NKC - 1)),
                 reads=[ptk] + vkeys, writes=[kO])
            S.pe(lambda e: e.matmul(aS, lhsT=onesb, rhs=pT, start=(kc == 0), stop=(kc == NKC - 1)),
                 reads=[ptk, "bp"], writes=[kS])
            if kc == NKC - 1:
                r, rk = rR.next()
                S.dve(lambda e: e.reciprocal(out=r, in_=aS), reads=[kS], writes=[rk])
                t = tR[m]
                S.dve(lambda e: e.tensor_tensor(out=t, in0=aO, in1=r, op=ALU.mult), reads=[kO, rk], writes=[("t", m)])
                if m == 1:
                    S.dve(lambda e: e.scalar_tensor_tensor(out=dbuf, in0=tR[1], scalar=neglam, in1=tR[0], op0=ALU.mult, op1=ALU.add),
                          reads=[("t", 0), ("t", 1), "neglam"], writes=["dbuf"])
                    S.dve(lambda e: e.tensor_tensor(out=sqb, in0=dbuf, in1=dbuf, op=ALU.mult), reads=["dbuf"], writes=["sqb"])
                    S.pe(lambda e: e.matmul(psN, lhsT=onesb, rhs=sqb, start=True, stop=True), reads=["sqb", "bp"], writes=[("ps", 7)])
                    self.rstd_from([(psN, ("ps", 7), slice(0, 512))], 1.0 / 128, HEAD_EPS, lnb, rsb, "rsb", "lnb")
                    ob, ok = obR.next()
                    S.dve(lambda e: e.scalar_tensor_tensor(out=ob, in0=dbuf, scalar=wd, in1=rsb, op0=ALU.mult, op1=ALU.mult),
                          reads=["dbuf", "rsb", "wd"], writes=[ok])
                    t0 = base + qg * 512
                    S.dma(sc["boT"][1024 + h * 128:1024 + (h + 1) * 128, t0:t0 + 512], ob, reads=[ok])

        n = len(items)
        for i in range(n + LA):
            if i < n:
                do_S(items[i])
            if i - LA >= 0:
                do_PV(items[i - LA])


MK.phase_mla = phase_mla
MK.phase_diff = phase_diff


def phase_ret(self, l):
    S, A = self.S, self.A
    A.reset()
    self.load_cp()
    sc = self.scr
    smax = self.smax
    Nm = smax // 128
    qT = A.alloc([2, smax], BF16)
    kT = A.alloc([2, smax], BF16)
    vR = Rot([(A.alloc([4, 512], BF16), ("v", i)) for i in range(3)])
    Sfb = A.alloc([Nm, 512], BF16)
    Sbb = A.alloc([Nm, 512], BF16)
    kvbs = A.alloc([Nm, 512], BF16)
    Sf = A.alloc([512], F32)
    Sb = A.alloc([512], F32)
    MT = A.alloc([4, 128], F32)
    e1t = A.alloc([128], F32)
    Qdf = A.alloc([2, 128], F32)
    Qdb = A.alloc([2, 128], F32)
    Kdf = A.alloc([4, 64], F32)
    Kdb = A.alloc([4, 64], F32)
    Gcf = A.alloc([4, 128], F32)
    Gcb = A.alloc([4, 128], F32)
    kdfR = Rot([(A.alloc([256], BF16), ("kdf", i)) for i in range(2)])
    kdbR = Rot([(A.alloc([256], BF16), ("kdb", i)) for i in range(2)])
    qdR = Rot([(A.alloc([512], BF16), ("qd", i)) for i in range(4)])
    aTR = Rot([(A.alloc([512], BF16), ("aT", i)) for i in range(2)])
    gR = Rot([(A.alloc([512], BF16), ("g", i)) for i in range(2)])
    sqb = A.alloc([512], BF16)
    lnb = A.alloc([512], F32)
    rsb = A.alloc([512], F32)
    tb = A.alloc([512], F32)
    obR = Rot([(A.alloc([512], BF16), ("ob", i)) for i in range(2)])
    ident = self.bpv("ident")
    onesb = self.bpv("ones")
    o_lg = PP["lgd"][0]

    def lg(d, h, p0=0, p1=128):
        return self.pp[p0:p1, o_lg + d * 4 + h:o_lg + d * 4 + h + 1]

    for h in range(4):
        S.act(lambda e, h=h: e.activation(out=MT[:, h, :], in_=self.cpv("Dpos"), func=AF.Exp, scale=lg(0, h)), reads=["pp"], writes=[("MT", h)])
        S.act(lambda e, h=h: e.activation(out=e1t, in_=self.cpv("Dneg"), func=AF.Exp, scale=lg(1, h)), reads=["pp"], writes=["e1t"])
        S.dve(lambda e, h=h: e.tensor_tensor(out=MT[:, h, :], in0=MT[:, h, :], in1=self.cpv("Mgt"), op=ALU.mult), reads=[("MT", h)], writes=[("MT", h)])
        S.dve(lambda e, h=h: e.tensor_tensor(out=e1t, in0=e1t, in1=self.cpv("Mlt"), op=ALU.mult), reads=["e1t"], writes=["e1t"])
        S.dve(lambda e, h=h: e.tensor_tensor(out=MT[:, h, :], in0=MT[:, h, :], in1=e1t, op=ALU.add), reads=[("MT", h), "e1t"], writes=[("MT", h)])
        S.dve(lambda e, h=h: e.tensor_tensor(out=MT[:, h, :], in0=MT[:, h, :], in1=self.cpv("I2"), op=ALU.add), reads=[("MT", h)], writes=[("MT", h)])
        c, r0 = h // 2, (h % 2) * 64
        S.act(lambda e, h=h, c=c, r0=r0: e.activation(out=Qdf[r0:r0 + 64, c, :], in_=self.cpv("Ip1")[r0:r0 + 64, :], func=AF.Exp,
                                                       scale=lg(0, h, r0, r0 + 64)), reads=["pp"], writes=[("Qd", h)])
        S.act(lambda e, h=h, c=c, r0=r0: e.activation(out=Qdb[r0:r0 + 64, c, :], in_=self.cpv("Cmi")[r0:r0 + 64, :], func=AF.Exp,
                                                       scale=lg(1, h, r0, r0 + 64)), reads=["pp"], writes=[("Qd", h)])
        S.act(lambda e, h=h: e.activation(out=Kdf[:, h, :], in_=self.cpv("Cm1j"), func=AF.Exp, scale=lg(0, h)), reads=["pp"], writes=[("Kd", h)])
        S.act(lambda e, h=h: e.activation(out=Kdb[:, h, :], in_=self.cpv("Jj"), func=AF.Exp, scale=lg(1, h)), reads=["pp"], writes=[("Kd", h)])
        S.act(lambda e, h=h: e.activation(out=Gcf[0:64, h, :], in_=self.cpv("C128")[0:64, :], func=AF.Exp, scale=lg(0, h, 0, 64)), reads=["pp"], writes=[("Gc", h)])
        S.act(lambda e, h=h: e.activation(out=Gcb[0:64, h, :], in_=self.cpv("C128")[0:64, :], func=AF.Exp, scale=lg(1, h, 0, 64)), reads=["pp"], writes=[("Gc", h)])
    tabkeys = [("MT", h) for h in range(4)] + [("Qd", h) for h in range(4)] + [("Kd", h) for h in range(4)] + [("Gc", h) for h in range(4)]
    psT = self.bank(0, BF16)
    kvfB, kvbB = self.bank(1), self.bank(2)
    Gcf2 = Gcf.rearrange("p a b -> p (a b)")
    Gcb2 = Gcb.rearrange("p a b -> p (a b)")
    Kdf2 = Kdf.rearrange("p a b -> p (a b)")
    Kdb2 = Kdb.rearrange("p a b -> p (a b)")
    kd_keys = [("Kd", h) for h in range(4)]
    gc_keys = [("Gc", h) for h in range(4)]

    for (base, Sq) in _seq_list(self):
        N = Sq // 128
        S.dma(qT[:, :, 0:Sq], sc["rqT"][:, base:base + Sq].rearrange("(c p) t -> p c t", p=128), writes=["qT"])
        S.dma(kT[:, :, 0:Sq], sc["rkT"][:, base:base + Sq].rearrange("(c p) t -> p c t", p=128), writes=["kT"])

        def load_v(G, base=base):
            vt, vk = vR.next()
            S.dma(vt, sc["rv"][base + G * 512:base + (G + 1) * 512, :].rearrange("(c p) e -> p c e", p=128), writes=[vk])
            return vt, vk
        S.pool(lambda e: e.memset(Sf[0:64, :], 0.0), writes=["Sf"])
        S.pool(lambda e: e.memset(Sb[0:64, :], 0.0), writes=["Sb"])

        def pass1(n, vt, vk):
            vkeys = [vk]
            half = n % 2
            pT_ = psT[:, half * 512:half * 512 + 256]
            ptk = ("ps", 0)
            for c in range(2):
                S.pe(lambda e, c=c: e.transpose(out=pT_[:, c * 128:(c + 1) * 128], in_=kT[:, c, n * 128:(n + 1) * 128], identity=ident),
                     reads=["kT", "bp"], writes=[ptk])
            kdf, kfk = kdfR.next()
            kdb, kbk = kdbR.next()
            S.dve(lambda e: e.tensor_tensor(out=kdf, in0=pT_, in1=Kdf2, op=ALU.mult), reads=[ptk] + kd_keys, writes=[kfk])
            S.dve(lambda e: e.tensor_tensor(out=kdb, in0=pT_, in1=Kdb2, op=ALU.mult), reads=[ptk] + kd_keys, writes=[kbk])
            for h in range(4):
                S.pe(lambda e, h=h: e.matmul(kvfB[0:64, h * 128:(h + 1) * 128], lhsT=kdf[:, h * 64:(h + 1) * 64],
                                             rhs=vt[:, n % 4, h * 128:(h + 1) * 128], start=True, stop=True), reads=[kfk] + vkeys, writes=[("ps", 1)])
            for h in range(4):
                S.pe(lambda e, h=h: e.matmul(kvbB[0:64, h * 128:(h + 1) * 128], lhsT=kdb[:, h * 64:(h + 1) * 64],
                                             rhs=vt[:, n % 4, h * 128:(h + 1) * 128], start=True, stop=True), reads=[kbk] + vkeys, writes=[("ps", 2)])
            S.act(lambda e: e.activation(out=kvbs[0:64, n, :], in_=kvbB[0:64, :], func=AF.Copy), reads=[("ps", 2)], writes=[("kvbs", n)])
            S.pool(lambda e: e.tensor_copy(out=Sfb[0:64, n, :], in_=Sf[0:64, :]), reads=["Sf"], writes=[("Sfb", n)])
            S.pool(lambda e: e.tensor_tensor(out=Sf[0:64, :], in0=Sf[0:64, :], in1=Gcf2[0:64, :], op=ALU.mult), reads=["Sf"] + gc_keys, writes=["Sf"])
            S.dve(lambda e: e.tensor_tensor(out=Sf[0:64, :], in0=Sf[0:64, :], in1=kvfB[0:64, :], op=ALU.add), reads=["Sf", ("ps", 1)], writes=["Sf"])

        for G in range(N // 4):
            vt, vk = load_v(G)
            for ci in range(4):
                pass1(G * 4 + ci, vt, vk)

        def pass1b(n):
            S.pool(lambda e: e.tensor_copy(out=Sbb[0:64, n, :], in_=Sb[0:64, :]), reads=["Sb"], writes=[("Sbb", n)])
            S.pool(lambda e: e.tensor_tensor(out=Sb[0:64, :], in0=Sb[0:64, :], in1=Gcb2[0:64, :], op=ALU.mult), reads=["Sb"] + gc_keys, writes=["Sb"])
            S.pool(lambda e: e.tensor_tensor(out=Sb[0:64, :], in0=Sb[0:64, :], in1=kvbs[0:64, n, :], op=ALU.add), reads=["Sb", ("kvbs", n)], writes=["Sb"])


        scR = Rot([(self.bank(i), ("ps", i)) for i in (0, 1)])
        yR = Rot([(self.bank(i), ("ps", i)) for i in (2, 3, 4)])
        psN = self.bank(5)

        def pass2(G, h, vt, vk, base=base):
            vkeys = [vk]
            c, r0 = h // 2, (h % 2) * 64
            t0 = base + G * 512
            g, gk = gR.next()
            S.dma(g, sc["rgT"][h * 128:(h + 1) * 128, t0:t0 + 512], writes=[gk])
            qdf, qfk = qdR.next()
            qdb, qbk = qdR.next()
            qv = qT[r0:r0 + 64, c, G * 512:(G + 1) * 512].rearrange("p (a b) -> p a b", a=4)
            S.dve(lambda e: e.tensor_tensor(out=qdf[0:64, :].rearrange("p (a b) -> p a b", a=4), in0=qv,
                                            in1=Qdf[r0:r0 + 64, c, :].unsqueeze(1).broadcast_to([64, 4, 128]), op=ALU.mult),
                  reads=["qT", ("Qd", h)], writes=[qfk])
            S.dve(lambda e: e.tensor_tensor(out=qdb[0:64, :].rearrange("p (a b) -> p a b", a=4), in0=qv,
                                            in1=Qdb[r0:r0 + 64, c, :].unsqueeze(1).broadcast_to([64, 4, 128]), op=ALU.mult),
                  reads=["qT", ("Qd", h)], writes=[qbk])
            ps, pk = scR.next()
            for ci in range(4):
                n = G * 4 + ci
                S.pe(lambda e, ci=ci, n=n: e.matmul(ps[:, ci * 128:(ci + 1) * 128], lhsT=kT[r0:r0 + 64, c, n * 128:(n + 1) * 128],
                                                     rhs=qT[r0:r0 + 64, c, n * 128:(n + 1) * 128], start=True, stop=True),
                     reads=["kT", "qT"], writes=[pk])
            aT, ak = aTR.next()
            S.dve(lambda e: e.tensor_tensor(out=aT.rearrange("p (a b) -> p a b", a=4), in0=ps.rearrange("p (a b) -> p a b", a=4),
                                            in1=MT[:, h, :].unsqueeze(1).broadcast_to([128, 4, 128]), op=ALU.mult),
                  reads=[pk, ("MT", h)], writes=[ak])
            y, yk = yR.next()
            for ci in range(4):
                n = G * 4 + ci
                ysl = y[:, ci * 128:(ci + 1) * 128]
                S.pe(lambda e, ci=ci, n=n, ysl=ysl: e.matmul(ysl, lhsT=vt[:, ci, h * 128:(h + 1) * 128], rhs=aT[:, ci * 128:(ci + 1) * 128],
                                                              start=True, stop=False), reads=[ak] + vkeys, writes=[yk])
                S.pe(lambda e, ci=ci, n=n, ysl=ysl: e.matmul(ysl, lhsT=Sfb[0:64, n, h * 128:(h + 1) * 128], rhs=qdf[0:64, ci * 128:(ci + 1) * 128],
                                                              start=False, stop=False), reads=[qfk, ("Sfb", n)], writes=[yk])
                S.pe(lambda e, ci=ci, n=n, ysl=ysl: e.matmul(ysl, lhsT=Sbb[0:64, n, h * 128:(h + 1) * 128], rhs=qdb[0:64, ci * 128:(ci + 1) * 128],
                                                              start=False, stop=True), reads=[qbk, ("Sbb", n)], writes=[yk])
            def epi():
                S.act(lambda e: e.activation(out=sqb, in_=y, func=AF.Square), reads=[yk], writes=["sqb"])
                S.pe(lambda e: e.matmul(psN, lhsT=onesb, rhs=sqb, start=True, stop=True), reads=["sqb", "bp"], writes=[("ps", 5)])
                self.rstd_from([(psN, ("ps", 5), slice(0, 512))], 1.0 / 128, HEAD_EPS, lnb, rsb, "rsb", "lnb")
                S.dve(lambda e: e.tensor_tensor(out=tb, in0=y, in1=rsb, op=ALU.mult), reads=[yk, "rsb"], writes=["tb"])
                ob, ok = obR.next()
                S.dve(lambda e: e.scalar_tensor_tensor(out=ob, in0=tb, scalar=self.ppv("rnw", h), in1=g, op0=ALU.mult, op1=ALU.mult),
                      reads=["tb", gk, "pp"], writes=[ok])
                S.dma(sc["boT"][h * 128:(h + 1) * 128, t0:t0 + 512], ob, reads=[ok])
            return epi

        prev_epi = None
        for G in range(Sq // 512 - 1, -1, -1):
            for ci in range(3, -1, -1):
                pass1b(G * 4 + ci)
            vt, vk = load_v(G)
            for h in range(4):
                epi = pass2(G, h, vt, vk)
                if prev_epi is not None:
                    prev_epi()
                prev_epi = epi
        prev_epi()


MK.phase_ret = phase_ret


def phase_ssd(self, l):
    S, A = self.S, self.A
    A.reset()
    self.load_cp()
    sc = self.scr
    smax = self.smax
    Nm = smax // 128
    BT = A.alloc([2, smax], BF16)
    CT = A.alloc([2, smax], BF16)
    dtA = A.alloc([Nm, 16], F32)
    dta = A.alloc([Nm, 16], F32)
    cumS = A.alloc([Nm, 16], F32)
    ecum = A.alloc([Nm, 16], F32)
    sdte = A.alloc([Nm, 16], F32)
    etot = A.alloc([Nm, 16], F32)
    A16 = A.alloc([16], F32)
    prevs = [A.alloc([Nm, 512], BF16) for _ in range(2)]
    Hs = [A.alloc([512], F32) for _ in range(2)]
    xBR = Rot([(A.alloc([4, 768], BF16), ("xB", i)) for i in range(4)])
    zR = Rot([(A.alloc([4, 512], BF16), ("z", i)) for i in range(2)])
    xwR = Rot([(A.alloc([512], BF16), ("xw", i)) for i in range(2)])
    xdtR = Rot([(A.alloc([512], BF16), ("xdt", i)) for i in range(4)])
    LhR = Rot([(A.alloc([4, 128], F32), ("Lh", i)) for i in range(2)])
    ER = Rot([(A.alloc([4, 128], F32), ("E", i)) for i in range(2)])
    WR = Rot([(A.alloc([4, 128], BF16), ("W", i)) for i in range(2)])
    CBm = [A.alloc([2, 128], F32) for _ in range(2)]
    tbs = [A.alloc([512], F32) for _ in range(2)]
    ubs = [A.alloc([512], F32) for _ in range(2)]
    t2b = A.alloc([512], F32)
    junk = A.alloc([256], F32)
    ssbs = [A.alloc([4], F32) for _ in range(2)]
    yoR = Rot([(A.alloc([512], BF16), ("yo", i)) for i in range(2)])
    oTR = Rot([(A.alloc([4, 512], BF16), ("oT", i)) for i in range(2)])
    ident = self.bpv("ident")
    Uf32, Ub32 = self.cpv("Uf"), self.cpv("Ub")
    SL = [self.cpv("SLf"), self.cpv("SLb")]
    U = [Uf32, Ub32]
    ones32 = self.cpv("ones")
    o_a = PP["alog"][0]
    o_d = PP["ssd"][0]
    o_n = PP["snw"][0]
    S.act(lambda e: e.activation(out=A16, in_=self.pp[:, o_a:o_a + 16], func=AF.Exp), reads=["pp"], writes=["A16"])
    S.dve(lambda e: e.tensor_scalar(out=A16, in0=A16, scalar1=-1.0, scalar2=None, op0=ALU.mult), reads=["A16"], writes=["A16"])
    Dsk = self.pp[:, o_d:o_d + 8].unsqueeze(2).broadcast_to([128, 8, 64])
    snw = self.pp[:, o_n:o_n + 512]

    for (base, Sq) in _seq_list(self):
        N = Sq // 128
        NG = Sq // 512
        S.dma(BT[:, :, 0:Sq], sc["sBT"][:, base:base + Sq].rearrange("(c p) t -> p c t", p=128), writes=["BT"])
        S.dma(CT[:, :, 0:Sq], sc["sCT"][:, base:base + Sq].rearrange("(c p) t -> p c t", p=128), writes=["CT"])
        S.dma(dtA[:, 0:N, :], sc["sdt"][base:base + Sq, :].rearrange("(c p) k -> p c k", p=128), writes=["dtA"])
        S.dve(lambda e, N=N: e.tensor_tensor(out=dta[:, 0:N, :], in0=dtA[:, 0:N, :], in1=A16.unsqueeze(1).broadcast_to([128, N, 16]), op=ALU.mult),
              reads=["dtA", "A16"], writes=["dta"])
        psC = self.bank(0)
        psTt = self.bank(1)
        psC3 = psC.rearrange("p (n c) -> p n c", c=16)
        S.pe(lambda e, N=N: e.matmul(psC3[:, 0:N, 0:8], lhsT=Uf32, rhs=dta[:, 0:N, 0:8], start=True, stop=True), reads=["dta"], writes=[("ps", 0)])
        S.pe(lambda e, N=N: e.matmul(psC3[:, 0:N, 8:16], lhsT=Ub32, rhs=dta[:, 0:N, 8:16], start=True, stop=True), reads=["dta"], writes=[("ps", 0)])
        S.pe(lambda e, N=N: e.matmul(psTt[:, 0:N * 16], lhsT=ones32, rhs=dta[:, 0:N, :].rearrange("p n c -> p (n c)"), start=True, stop=True),
             reads=["dta"], writes=[("ps", 1)])
        cs2 = cumS.rearrange("p n c -> p (n c)")
        S.act(lambda e, N=N: e.activation(out=cs2[:, 0:N * 16], in_=psC[:, 0:N * 16], func=AF.Copy), reads=[("ps", 0)], writes=["cumS"])
        S.act(lambda e, N=N: e.activation(out=ecum.rearrange("p n c -> p (n c)")[:, 0:N * 16], in_=cs2[:, 0:N * 16], func=AF.Exp),
              reads=["cumS"], writes=["ecum"])
        sd2 = sdte.rearrange("p n c -> p (n c)")
        S.dve(lambda e, N=N: e.tensor_tensor(out=sd2[:, 0:N * 16], in0=psTt[:, 0:N * 16], in1=cs2[:, 0:N * 16], op=ALU.subtract),
              reads=[("ps", 1), "cumS"], writes=["sdte"])
        S.act(lambda e, N=N: e.activation(out=sd2[:, 0:N * 16], in_=sd2[:, 0:N * 16], func=AF.Exp), reads=["sdte"], writes=["sdte"])
        S.dve(lambda e, N=N: e.tensor_tensor(out=sd2[:, 0:N * 16], in0=sd2[:, 0:N * 16], in1=dtA.rearrange("p n c -> p (n c)")[:, 0:N * 16], op=ALU.mult),
              reads=["sdte", "dtA"], writes=["sdte"])
        S.act(lambda e, N=N: e.activation(out=etot.rearrange("p n c -> p (n c)")[:, 0:N * 16], in_=psTt[:, 0:N * 16], func=AF.Exp),
              reads=[("ps", 1)], writes=["etot"])
        for d in range(2):
            S.pool(lambda e, d=d: e.memset(Hs[d], 0.0), writes=[("H", d)])

        def load_x(G, base=base):
            xB, xk = xBR.next()
            t0 = base + G * 512
            S.dma(xB, sc["sxB"][t0:t0 + 512, :].rearrange("(c p) e -> p c e", p=128), writes=[xk])
            return xB, xk

        stR = Rot([(self.bank(i), ("ps", i)) for i in (2, 3)])

        def state_step(d, n, xB, xk):
            ci = n % 4
            H, prev = Hs[d], prevs[d]
            S.pool(lambda e: e.tensor_copy(out=prev[:, n, :], in_=H), reads=[("H", d)], writes=[("prev", d, n)])
            xw, wk = xwR.next()
            S.dve(lambda e: e.tensor_tensor(out=xw.rearrange("p (h q) -> p h q", h=8), in0=xB[:, ci, 0:512].rearrange("p (h q) -> p h q", h=8),
                                            in1=sdte[:, n, d * 8:(d + 1) * 8].unsqueeze(2).broadcast_to([128, 8, 64]), op=ALU.mult),
                  reads=[xk, "sdte"], writes=[wk])
            ps, pk = stR.next()
            for g in range(2):
                S.pe(lambda e, g=g: e.matmul(ps[:, g * 256:(g + 1) * 256], lhsT=xB[:, ci, 512 + g * 128:512 + (g + 1) * 128],
                                             rhs=xw[:, g * 256:(g + 1) * 256], start=True, stop=True), reads=[xk, wk], writes=[pk])
            S.pool(lambda e: e.tensor_tensor(out=H.rearrange("p (h q) -> p h q", h=8), in0=H.rearrange("p (h q) -> p h q", h=8),
                                             in1=etot[:, n, d * 8:(d + 1) * 8].unsqueeze(2).broadcast_to([128, 8, 64]), op=ALU.mult),
                   reads=[("H", d), "etot"], writes=[("H", d)])
            S.dve(lambda e: e.tensor_tensor(out=H, in0=H, in1=ps, op=ALU.add), reads=[("H", d), pk], writes=[("H", d)])

        curf = curb = None
        for step in range(N):
            nf, nb_ = step, N - 1 - step
            if nf % 4 == 0:
                curf = load_x(nf // 4)
            if nb_ % 4 == 3:
                curb = load_x(nb_ // 4)
            state_step(0, nf, curf[0], curf[1])
            state_step(1, nb_, curb[0], curb[1])

        psCB = self.bank(0)
        psDR = Rot([(self.bank(i), ("ps", i)) for i in (1, 2)])
        Yd = [self.bank(3), self.bank(4)]
        Yo = [self.bank(5), self.bank(6)]
        psT = self.bank(7, BF16)

        pend_tail = [None]
        store_jobs = []

        def out_chunk(n, xB, xk, z, zk, oT, otk):
            ci = n % 4
            tsl = slice(n * 128, (n + 1) * 128)
            for g in range(2):
                S.pe(lambda e, g=g: e.matmul(psCB[:, g * 128:(g + 1) * 128], lhsT=BT[:, g, tsl], rhs=CT[:, g, tsl], start=True, stop=True),
                     reads=["BT", "CT"], writes=[("ps", 0)])
            for d in range(2):
                S.dve(lambda e, d=d: e.tensor_tensor(out=CBm[d], in0=psCB[:, 0:256].rearrange("p (g i) -> p g i", g=2),
                                                     in1=U[d].unsqueeze(1).broadcast_to([128, 2, 128]), op=ALU.mult),
                      reads=[("ps", 0)], writes=[("CBm", d)])
            for d in range(2):
                xdt, xdk = xdtR.next()
                S.dve(lambda e, d=d, xdt=xdt: e.tensor_tensor(out=xdt.rearrange("p (h q) -> p h q", h=8), in0=xB[:, ci, 0:512].rearrange("p (h q) -> p h q", h=8),
                                                               in1=dtA[:, n, d * 8:(d + 1) * 8].unsqueeze(2).broadcast_to([128, 8, 64]), op=ALU.mult),
                      reads=[xk, "dtA"], writes=[xdk])
                for g in range(2):
                    c0 = d * 8 + g * 4
                    Lh, lk = LhR.next()
                    for hh in range(4):
                        S.act(lambda e, d=d, c0=c0, Lh=Lh, hh=hh: e.activation(out=Lh[:, hh, :], in_=SL[d], func=AF.Copy,
                                                                                 scale=dta[:, n, c0 + hh:c0 + hh + 1]),
                              reads=["dta"], writes=[(lk, hh)])
                    psD, pdk = psDR.next()
                    for hh in range(4):
                        S.pe(lambda e, d=d, hh=hh, Lh=Lh, psD=psD: e.matmul(psD[:, hh * 128:(hh + 1) * 128], lhsT=Lh[:, hh, :], rhs=U[d], start=True, stop=True),
                             reads=[(lk, hh)], writes=[pdk])
                    E, ek = ER.next()
                    S.act(lambda e, E=E, psD=psD: e.activation(out=E.rearrange("p a b -> p (a b)"), in_=psD, func=AF.Exp), reads=[pdk], writes=[ek])
                    W, wk = WR.next()
                    S.dve(lambda e, d=d, g=g, E=E, W=W: e.tensor_tensor(out=W, in0=E, in1=CBm[d][:, g, :].unsqueeze(1).broadcast_to([128, 4, 128]), op=ALU.mult),
                          reads=[ek, ("CBm", d)], writes=[wk])
                    for hh in range(4):
                        h = g * 4 + hh
                        S.pe(lambda e, d=d, hh=hh, h=h, W=W, xdt=xdt: e.matmul(Yd[d][:, h * 64:(h + 1) * 64], lhsT=W[:, hh, :], rhs=xdt[:, h * 64:(h + 1) * 64],
                                                                                 start=True, stop=True), reads=[wk, xdk], writes=[("ps", 3 + d)])
                    S.pe(lambda e, d=d, g=g: e.matmul(Yo[d][:, g * 256:(g + 1) * 256], lhsT=CT[:, g, tsl], rhs=prevs[d][:, n, g * 256:(g + 1) * 256],
                                                      start=True, stop=True), reads=["CT", ("prev", d, n)], writes=[("ps", 5 + d)])
            tb, ub, ssb = tbs[n % 2], ubs[n % 2], ssbs[n % 2]
            tbk, ubk = ("tb", n % 2), ("ub", n % 2)
            t3 = tb.rearrange("p (h q) -> p h q", h=8)
            u3 = ub.rearrange("p (h q) -> p h q", h=8)
            S.dve(lambda e: e.tensor_tensor(out=t3, in0=Yo[0].rearrange("p (h q) -> p h q", h=8),
                                            in1=ecum[:, n, 0:8].unsqueeze(2).broadcast_to([128, 8, 64]), op=ALU.mult),
                  reads=[("ps", 5), "ecum"], writes=[tbk])
            S.dve(lambda e: e.tensor_tensor(out=tb, in0=tb, in1=Yd[0], op=ALU.add), reads=[tbk, ("ps", 3)], writes=[tbk])
            S.dve(lambda e: e.tensor_tensor(out=u3, in0=Yo[1].rearrange("p (h q) -> p h q", h=8),
                                            in1=ecum[:, n, 8:16].unsqueeze(2).broadcast_to([128, 8, 64]), op=ALU.mult),
                  reads=[("ps", 6), "ecum"], writes=[ubk])
            S.dve(lambda e: e.tensor_tensor(out=ub, in0=ub, in1=Yd[1], op=ALU.add), reads=[ubk, ("ps", 4)], writes=[ubk])
            def tail_b():
                S.pool(lambda e: e.tensor_tensor(out=tb, in0=tb, in1=ub, op=ALU.add), reads=[tbk, ubk], writes=[tbk])
                S.pool(lambda e: e.tensor_tensor(out=t2b.rearrange("p (h q) -> p h q", h=8), in0=xB[:, ci, 0:512].rearrange("p (h q) -> p h q", h=8),
                                                 in1=Dsk, op=ALU.mult), reads=[xk, "pp"], writes=["t2b"])
                S.pool(lambda e: e.tensor_tensor(out=tb, in0=tb, in1=t2b, op=ALU.add), reads=[tbk, "t2b"], writes=[tbk])
                S.pool(lambda e: e.tensor_tensor(out=tb, in0=tb, in1=z[:, ci, :], op=ALU.mult), reads=[tbk, zk], writes=[tbk])
                for g in range(2):
                    S.act(lambda e, g=g: e.activation(out=junk, in_=tb[:, g * 256:(g + 1) * 256], func=AF.Square, accum_out=ssb[:, g:g + 1]),
                          reads=[tbk, "junk"], writes=["junk", ("ss", n % 2, g)])
                S.act(lambda e: e.activation(out=ssb[:, 2:4], in_=ssb[:, 0:2], func=AF.Ln, scale=1.0 / 256, bias=HEAD_EPS),
                      reads=[("ss", n % 2, 0), ("ss", n % 2, 1)], writes=[("ssl", n % 2)])
                S.act(lambda e: e.activation(out=ssb[:, 0:2], in_=ssb[:, 2:4], func=AF.Exp, scale=-0.5), reads=[("ssl", n % 2)], writes=[("ss", n % 2, 0), ("ss", n % 2, 1), ("rs2", n % 2)])
                yo, yk = yoR.next()
                for g in range(2):
                    S.dve(lambda e, g=g, yo=yo: e.scalar_tensor_tensor(out=yo[:, g * 256:(g + 1) * 256], in0=tb[:, g * 256:(g + 1) * 256], scalar=ssb[:, g:g + 1],
                                                                        in1=snw[:, g * 256:(g + 1) * 256], op0=ALU.mult, op1=ALU.mult),
                          reads=[tbk, ("rs2", n % 2), "pp"], writes=[(yk, g)])
                half = n % 2
                pT_ = psT[:, half * 512:(half + 1) * 512]
                ptk = ("ps", 7)
                for cc in range(4):
                    S.pe(lambda e, cc=cc, yo=yo: e.transpose(out=pT_[:, cc * 128:(cc + 1) * 128], in_=yo[:, cc * 128:(cc + 1) * 128], identity=ident),
                         reads=[(yk, 0), (yk, 1), "bp"], writes=[ptk])
                S.dve(lambda e: e.tensor_copy(out=oT[:, :, ci * 128:(ci + 1) * 128], in_=pT_.rearrange("p (a b) -> p a b", a=4)),
                      reads=[ptk], writes=[(otk, ci)])


            return tail_b

        for G in range(NG):
            xB, xk = load_x(G)
            z, zk = zR.next()
            t0 = base + G * 512
            S.dma(z, sc["sz"][t0:t0 + 512, :].rearrange("(c p) e -> p c e", p=128), writes=[zk])
            oT, otk = oTR.next()
            for ci in range(4):
                tb_fn = out_chunk(G * 4 + ci, xB, xk, z, zk, oT, otk)
                if pend_tail[0] is not None:
                    pend_tail[0]()
                pend_tail[0] = tb_fn
            store_jobs.append((t0, oT, otk))
            if len(store_jobs) > 1:
                t0_, oT_, otk_ = store_jobs.pop(0)
                S.dma(sc["boT"][1536:2048, t0_:t0_ + 512].rearrange("(c p) t -> p c t", p=128), oT_, reads=[(otk_, ci) for ci in range(4)])
        pend_tail[0]()
        pend_tail[0] = None
        while store_jobs:
            t0_, oT_, otk_ = store_jobs.pop(0)
            S.dma(sc["boT"][1536:2048, t0_:t0_ + 512].rearrange("(c p) t -> p c t", p=128), oT_, reads=[(otk_, ci) for ci in range(4)])


MK.phase_ssd = phase_ssd


def load_cp(self):
    self.cp = self.A.alloc([NCP], F32)
    self.S.dma(self.cp, self.cpack, writes=["cp"])
    for e in (PE, ACT, DVE, POOL):
        self.S.add(e, None, reads=["cp"], real=False)


def rms_group(self, xs, ncols, wname, hT, hk, sqR, lnv, rstd, psA, psH0, kA=None, kH=None):
    S = self.S
    onesb = self.bpv("ones")
    for k in range(8):
        sq, sqk = sqR.next()
        S.act(lambda e, sq=sq, k=k: e.activation(out=sq[:, 0:ncols], in_=xs[:, k, :], func=AF.Square),
              reads=["xs", "xsh"], writes=[sqk])
        S.pe(lambda e, sq=sq, k=k: e.matmul(psA, lhsT=onesb, rhs=sq[:, 0:512], start=(k == 0), stop=(k == 7)),
             reads=[sqk, "bp"], writes=[kA])
        if ncols > 512:
            S.pe(lambda e, sq=sq, k=k: e.matmul(psH0, lhsT=onesb, rhs=sq[:, 512:ncols], start=(k == 0), stop=(k == 7)),
                 reads=[sqk, "bp"], writes=[kH])
    lst = [(psA, kA, slice(0, 512))]
    if ncols > 512:
        lst.append((psH0, kH, slice(512, ncols)))
    self.rstd_from(lst, 1.0 / 1024, NORM_EPS, lnv[:, 0:ncols], rstd[:, 0:ncols], "rstd", "lnv")
    if hT is not None:
        for k in range(8):
            S.dve(lambda e, k=k: e.scalar_tensor_tensor(out=hT[:, k, :], in0=xs[:, k, :], scalar=self.ppv(wname, k),
                                                         in1=rstd[:, 0:ncols], op0=ALU.mult, op1=ALU.mult),
                  reads=["xs", "xsh", "rstd", "pp"], writes=[hk[k]])


def phase3(self, l):
    S, A = self.S, self.A
    A.reset()
    sc = self.scr
    x_in = (self.xT if l == 0 else sc["xres"]).rearrange("(c p) t -> p c t", p=128)
    boT = sc["boT"].rearrange("(c p) t -> p c t", p=128)
    wG = A.alloc([8, 4096], BF16)
    wB = A.alloc([16, 1024], BF16)
    wO = A.alloc([8, 1024], BF16)
    bo = A.alloc([16, 512], BF16)
    bo_w = bo.rearrange("p a b -> p (a b)").bitcast(F32)
    stgR = Rot([(bo_w[:, i * 1024:(i + 1) * 1024], ("stg", i)) for i in range(4)])
    win = self.w_in[l].rearrange("(c p) n -> p c n", p=128)
    pieces = []
    for k in range(8):
        for q in range(4):
            pieces.append((wG[:, k, q * 1024:(q + 1) * 1024], win[:, k, C_GATE + q * 1024:C_GATE + (q + 1) * 1024]))
    wbr = self.w_branch[l].rearrange("b (c p) n -> p (b c) n", p=128)
    for i in range(16):
        pieces.append((wB[:, i, :], wbr[:, i, :]))
    wo = self.w_out[l].rearrange("(c p) n -> p c n", p=128)
    for k in range(8):
        pieces.append((wO[:, k, :], wo[:, k, :]))
    self.load_cast("w3", pieces, stgR)
    xs = A.alloc([8, 512], F32)
    hTs = [A.alloc([8, 512], BF16) for _ in range(2)]
    sqR = Rot([(A.alloc([512], BF16), ("sq", i)) for i in range(2)])
    lnv = A.alloc([512], F32)
    rstd = A.alloc([512], F32)
    merged = A.alloc([8, 512], BF16)
    gsR = Rot([(A.alloc([512], F32), ("gs", i)) for i in range(3)])
    tmR = Rot([(A.alloc([512], F32), ("tm", i)) for i in range(2)])
    mR = Rot([(A.alloc([512], F32), ("m", i)) for i in range(2)])
    xoR = Rot([(A.alloc([512], F32), ("xo", i)) for i in range(2)])
    psGR = Rot([(self.bank(i), ("ps", i)) for i in (0, 1, 2)])
    psPR = Rot([(self.bank(i), ("ps", i)) for i in (3, 4)])
    psOR = Rot([(self.bank(i), ("ps", i)) for i in (5, 6)])
    psA = self.bank(7)
    ng = len(self.groups)

    def load_x(gi):
        if gi >= ng:
            return
        t0 = self.groups[gi]["t0"]
        S.dma(xs, x_in[:, :, t0:t0 + 512], writes=["xs"])

    def load_bo(gi):
        if gi >= ng:
            return
        t0 = self.groups[gi]["t0"]
        for q in range(4):
            S.dma(bo[:, q * 4:(q + 1) * 4, :], boT[:, q * 4:(q + 1) * 4, t0:t0 + 512], writes=[("bo", q), ("stg", 0), ("stg", 1), ("stg", 2), ("stg", 3)])

    def body(gi, g):
        t0 = g["t0"]
        hT = hTs[gi % 2]
        hk = [("hT", gi % 2, k) for k in range(8)]
        if gi == 0:
            load_x(0)
            load_bo(0)
            rms_group(self, xs, 512, "wn", hT, hk, sqR, lnv, rstd, psA, None, kA=("ps", 7))
            load_x(1)
        for j in range(8):
            if j == 5 and gi + 1 < ng:
                hk_n = [("hT", (gi + 1) % 2, k) for k in range(8)]
                rms_group(self, xs, 512, "wn", hTs[(gi + 1) % 2], hk_n, sqR, lnv, rstd, psA, None, kA=("ps", 7))
                load_x(gi + 2)
            m, mk_ = mR.next()
            for b in range(4):
                psG, pgk = psGR.next()
                cg = b * 1024 + j * 128
                self.mmg(psG, pgk, [(wG[:, k, cg:cg + 128], hT[:, k, :]) for k in range(8)], hk + ["w3"])
                gs, gk = gsR.next()
                S.act(lambda e, psG=psG, gs=gs: e.activation(out=gs, in_=psG, func=AF.Sigmoid), reads=[pgk], writes=[gk])
                psP, ppk = psPR.next()
                self.mmg(psP, ppk, [(wB[:, b * 4 + kc, j * 128:(j + 1) * 128], bo[:, b * 4 + kc, :]) for kc in range(4)], [("bo", b), "w3"])
                if b == 0:
                    S.dve(lambda e, psP=psP, gs=gs, m=m: e.tensor_tensor(out=m, in0=psP, in1=gs, op=ALU.mult), reads=[ppk, gk], writes=[mk_])
                else:
                    tm, tk = tmR.next()
                    S.dve(lambda e, psP=psP, gs=gs, tm=tm: e.tensor_tensor(out=tm, in0=psP, in1=gs, op=ALU.mult), reads=[ppk, gk], writes=[tk])
                    dst = merged[:, j, :] if b == 3 else m
                    dk = ("mg", j) if b == 3 else mk_
                    S.pool(lambda e, tm=tm, m=m, dst=dst: e.tensor_tensor(out=dst, in0=m, in1=tm, op=ALU.add), reads=[mk_, tk], writes=[dk])
        load_bo(gi + 1)
        mgk = [("mg", j) for j in range(8)]
        for jo in range(8):
            psO, pok = psOR.next()
            for k in range(8):
                S.pe(lambda e, k=k, jo=jo, psO=psO: e.matmul(psO, lhsT=wO[:, k, jo * 128:(jo + 1) * 128], rhs=merged[:, k, :],
                                                             start=(k == 0), stop=(k == 7)), reads=[("mg", k), "w3"], writes=[pok])
            xo, xk = xoR.next()
            S.dma(xo, x_in[:, jo, t0:t0 + 512], writes=[xk])
            S.dve(lambda e, psO=psO, xo=xo: e.tensor_tensor(out=xo, in0=psO, in1=xo, op=ALU.add), reads=[pok, xk], writes=[xk])
            S.dma(sc["xmid"][jo * 128:(jo + 1) * 128, t0:t0 + 512], xo, reads=[xk])

    for gi, g in enumerate(self.groups):
        body(gi, g)


def phase4(self, l):
    S, A = self.S, self.A
    A.reset()
    sc = self.scr
    x_in = sc["xmid"].rearrange("(c p) t -> p c t", p=128)
    wg = A.alloc([8, D_FF], BF16)
    wu = A.alloc([8, D_FF], BF16)
    wd = A.alloc([NFF, 1024], BF16)
    act = A.alloc([NFF, 512], BF16)
    act_w = act.rearrange("p a b -> p (a b)").bitcast(F32)
    stgR = Rot([(act_w[:, i * 1408:(i + 1) * 1408], ("stg", i)) for i in range(4)])
    pieces = []
    g_ = self.w_gate[l].rearrange("(c p) n -> p c n", p=128)
    u_ = self.w_up[l].rearrange("(c p) n -> p c n", p=128)
    for k in range(8):
        for hf in range(2):
            pieces.append((wg[:, k, hf * 1408:(hf + 1) * 1408], g_[:, k, hf * 1408:(hf + 1) * 1408]))
            pieces.append((wu[:, k, hf * 1408:(hf + 1) * 1408], u_[:, k, hf * 1408:(hf + 1) * 1408]))
    d_ = self.w_down[l].rearrange("(c p) n -> p c n", p=128)
    for c in range(NFF):
        pieces.append((wd[:, c, :], d_[:, c, :]))
    self.load_cast("w4", pieces, stgR)
    xs = A.alloc([8, 514], F32)
    hTs = [A.alloc([8, 514], BF16) for _ in range(2)]
    sqR = Rot([(A.alloc([514], BF16), ("sq", i)) for i in range(2)])
    lnv = A.alloc([514], F32)
    rstd = lnv
    GR = Rot([(A.alloc([514], F32), ("G", i)) for i in range(1)])
    cvR = Rot([(A.alloc([512], F32), ("cv", i)) for i in range(1)])
    xrR = Rot([(A.alloc([512], F32), ("xr", i)) for i in range(1)])
    psgR = Rot([(self.bank(i), ("ps", i)) for i in (0, 1)])
    psuR = Rot([(self.bank(i), ("ps", i)) for i in (2, 3)])
    psOR = Rot([(self.bank(i), ("ps", i)) for i in (4, 5)])
    psA = self.bank(6)
    psH = self.bank(7)
    ng = len(self.groups)
    o_w = PP["fcw"][0]
    o_b = PP["fcb"][0]
    x_out = sc["xres"]

    def load_x(gi_):
        if gi_ >= ng:
            return
        g_i = self.groups[gi_]
        hasL = g_i["pos0"] > 0
        hasR = g_i["pos0"] + 512 < g_i["S"]
        if not hasL:
            S.pool(lambda e: e.memset(xs[:, :, 0:1], 0.0), writes=["xsh"])
        if not hasR:
            S.pool(lambda e: e.memset(xs[:, :, 513:514], 0.0), writes=["xsh"])
        lo = g_i["t0"] - 1 if hasL else g_i["t0"]
        hi = g_i["t0"] + 513 if hasR else g_i["t0"] + 512
        c0 = 0 if hasL else 1
        S.dma(xs[:, :, c0:c0 + (hi - lo)], x_in[:, :, lo:hi], writes=["xs"])

    def body(gi, g):
        t0 = g["t0"]
        hT = hTs[gi % 2]
        hk = [("hT", gi % 2, k) for k in range(8)]
        if gi == 0:
            load_x(0)
            rms_group(self, xs, 514, "nf", hT, hk, sqR, lnv, rstd, psA, psH[:, 0:2], kA=("ps", 6), kH=("ps", 7))
            load_x(1)
        for c in range(NFF):
            if c == 13 and gi + 1 < ng:
                hk_n = [("hT", (gi + 1) % 2, k) for k in range(8)]
                rms_group(self, xs, 514, "nf", hTs[(gi + 1) % 2], hk_n, sqR, lnv, rstd, psA, psH[:, 0:2], kA=("ps", 6), kH=("ps", 7))
                load_x(gi + 2)
            cs = slice(c * 128, (c + 1) * 128)
            psg, pgk = psgR.next()
            self.mmg(psg, pgk, [(wg[:, k, cs], hT[:, k, 1:513]) for k in range(8)], hk + ["w4"])
            hcol = 2 + 2 * c
            hkey = ("ps", 7)
            for k in range(8):
                S.pe(lambda e, k=k, cs=cs, hcol=hcol: e.matmul(psH[:, hcol:hcol + 2], lhsT=wg[:, k, cs], rhs=hT[:, k, 0:514:513],
                                                                start=(k == 0), stop=(k == 7)), reads=hk + ["w4"], writes=[hkey])
            psu, puk = psuR.next()
            self.mmg(psu, puk, [(wu[:, k, cs], hT[:, k, 1:513]) for k in range(8)], hk + ["w4"])
            G, gk = GR.next()
            S.act(lambda e, psg=psg, G=G: e.activation(out=G[:, 1:513], in_=psg, func=AF.Copy), reads=[pgk], writes=[gk])
            S.dve(lambda e, G=G, hcol=hcol: e.tensor_copy(out=G[:, 0:514:513], in_=psH[:, hcol:hcol + 2]), reads=[hkey], writes=[(gk, "h")])
            cv, ck = cvR.next()
            S.dve(lambda e, G=G, cv=cv, c=c: e.tensor_scalar(out=cv, in0=G[:, 1:513], scalar1=self.pp[:, o_w + NFF + c:o_w + NFF + c + 1],
                                                              scalar2=self.pp[:, o_b + c:o_b + c + 1], op0=ALU.mult, op1=ALU.add),
                  reads=[gk, "pp"], writes=[ck])
            S.dve(lambda e, G=G, cv=cv, c=c: e.scalar_tensor_tensor(out=cv, in0=G[:, 0:512], scalar=self.pp[:, o_w + c:o_w + c + 1],
                                                                     in1=cv, op0=ALU.mult, op1=ALU.add),
                  reads=[gk, (gk, "h"), "pp", ck], writes=[ck])
            S.dve(lambda e, G=G, cv=cv, c=c: e.scalar_tensor_tensor(out=cv, in0=G[:, 2:514], scalar=self.pp[:, o_w + 2 * NFF + c:o_w + 2 * NFF + c + 1],
                                                                     in1=cv, op0=ALU.mult, op1=ALU.add),
                  reads=[gk, (gk, "h"), "pp", ck], writes=[ck])
            S.act(lambda e, cv=cv: e.activation(out=cv, in_=cv, func=AF.Silu), reads=[ck], writes=[ck])
            S.dve(lambda e, cv=cv, psu=psu, c=c: e.tensor_tensor(out=act[:, c, :], in0=psu, in1=cv, op=ALU.mult), reads=[puk, ck], writes=[("act", c)])
        ak = [("act", c) for c in range(NFF)]
        for jo in range(8):
            xr, xrk = xrR.next()
            S.dma(xr, sc["xmid"][jo * 128:(jo + 1) * 128, t0:t0 + 512], writes=[xrk])
            psO, pok = psOR.next()
            for c in range(NFF):
                S.pe(lambda e, c=c, jo=jo, psO=psO: e.matmul(psO, lhsT=wd[:, c, jo * 128:(jo + 1) * 128], rhs=act[:, c, :],
                                                             start=(c == 0), stop=(c == NFF - 1)), reads=[("act", c), "w4"], writes=[pok])
            S.dve(lambda e, psO=psO, xr=xr: e.tensor_tensor(out=xr, in0=psO, in1=xr, op=ALU.add), reads=[pok, xrk], writes=[xrk])
            S.dma(x_out[jo * 128:(jo + 1) * 128, t0:t0 + 512], xr, reads=[xrk])

    for gi, g in enumerate(self.groups):
        body(gi, g)


def phase5(self):
    S, A = self.S, self.A
    A.reset()
    sc = self.scr
    x_in = sc["xres"].rearrange("(c p) t -> p c t", p=128)
    xss = [A.alloc([8, 512], F32) for _ in range(2)]
    sqR = Rot([(A.alloc([512], BF16), ("sq", i)) for i in range(2)])
    lnv = A.alloc([512], F32)
    rstd = A.alloc([512], F32)
    yR = Rot([(A.alloc([512], F32), ("y", i)) for i in range(3)])
    psAs = [self.bank(0), self.bank(1)]
    onesb = self.bpv("ones")

    def body(gi, g):
        t0 = g["t0"]
        xs = xss[gi % 2]
        xk = ("xs", gi % 2)
        S.dma(xs, x_in[:, :, t0:t0 + 512], writes=[xk])
        psA = psAs[gi % 2]
        pak = ("ps", gi % 2)
        for k in range(8):
            sq, sqk = sqR.next()
            S.act(lambda e, sq=sq, k=k: e.activation(out=sq, in_=xs[:, k, :], func=AF.Square), reads=[xk], writes=[sqk])
            S.pe(lambda e, sq=sq, k=k: e.matmul(psA, lhsT=onesb, rhs=sq, start=(k == 0), stop=(k == 7)), reads=[sqk, "bp"], writes=[pak])
        self.rstd_from([(psA, pak, slice(0, 512))], 1.0 / 1024, NORM_EPS, lnv, rstd, "rstd", "lnv")
        for k in range(8):
            y, yk = yR.next()
            S.dve(lambda e, k=k, y=y: e.scalar_tensor_tensor(out=y, in0=xs[:, k, :], scalar=self.ppv("fin", k), in1=rstd,
                                                              op0=ALU.mult, op1=ALU.mult), reads=[xk, "rstd", "pp"], writes=[yk])
            S.dma(self.yT[k * 128:(k + 1) * 128, t0:t0 + 512], y, reads=[yk])

    for gi, g in enumerate(self.groups):
        body(gi, g)


MK.load_cp = load_cp
MK.phase3 = phase3
MK.phase4 = phase4
MK.phase5 = phase5


_NC_CACHE = {}
N_CORES = 8
SEQ_P, SEQ_S, DEPTH = 4096, 2048, 2


def kernel(**inputs):
    xp = np.asarray(inputs["x_prompt"], np.float32)
    xs_ = np.asarray(inputs["x_sample"], np.float32)
    seqs = [SEQ_P, SEQ_P, SEQ_S]
    key = "main"
    if key not in _NC_CACHE:
        mk = MK(seqs, DEPTH)
        _NC_CACHE[key] = mk.build()
    nc = _NC_CACHE[key]
    common = common_inputs(inputs, DEPTH, SEQ_P)
    in_maps = []
    for c in range(N_CORES):
        xT = np.concatenate([xp[2 * c].T, xp[2 * c + 1].T, xs_[c].T], axis=1)
        m = dict(common)
        m["xT"] = np.ascontiguousarray(xT)
        in_maps.append(m)
    res = run_bass_kernel_spmd(nc, in_maps, core_ids=list(range(N_CORES)))
    yp = np.empty((16, SEQ_P, D_MODEL), np.float32)
    ys = np.empty((8, SEQ_S, D_MODEL), np.float32)
    for c in range(N_CORES):
        yT = np.asarray(res.results[c]["yT"], np.float32)
        yp[2 * c] = yT[:, 0:SEQ_P].T
        yp[2 * c + 1] = yT[:, SEQ_P:2 * SEQ_P].T
        ys[c] = yT[:, 2 * SEQ_P:2 * SEQ_P + SEQ_S].T
    return (yp, ys)
```
